# Optimizing a Trainium2 kernel written in Bass

```python
import math
import jax, jax.numpy as jnp
from jax import lax
import numpy as np

D_MODEL = 1024
BATCH = 2
SEQ = 8192
DEPTH = 4
DEC_BATCH = 128
DEC_SEQ = 8
PAST_LEN = 8192
PAGE_SIZE = 128

N_EVEN = (DEPTH + 1) // 2
N_ODD = DEPTH // 2

ML_HEADS = 4
ML_DK = 128
ML_DV = 128
ML_WIDTH = ML_HEADS * ML_DV
ML_CHUNK = 64

SC_WIDTH = D_MODEL // 2
CONV_W = 3

ATT_HEADS = 16
KV_HEADS = 4
HEAD_DIM = 64
GROUP = ATT_HEADS // KV_HEADS
WINDOW = 128
WIN_BUF = min(WINDOW, PAST_LEN)
ROPE_THETA = 10000.0

D_FF = 2816

EPS = 1e-6
IN_A = 4 * ML_WIDTH + 2 * ML_HEADS + 3 * SC_WIDTH
QKV_W = (ATT_HEADS + 2 * KV_HEADS) * HEAD_DIM

kernel_name = 'hybrid_mlstm_shortconv_swa_decoder_step'


def rms_norm(x, g):
    xf = x.astype(jnp.float32)
    y = xf * lax.rsqrt(jnp.mean(xf * xf, axis=-1, keepdims=True) + EPS)
    return (y * g.astype(jnp.float32)).astype(x.dtype)


def rope(x, pos):
    half = HEAD_DIM // 2
    inv = ROPE_THETA ** (-jnp.arange(half, dtype=jnp.float32) / half)
    ang = pos.astype(jnp.float32)[:, None] * inv[None, :]
    cos = jnp.cos(ang)[:, None, :]
    sin = jnp.sin(ang)[:, None, :]
    xf = x.astype(jnp.float32)
    x1, x2 = xf[..., :half], xf[..., half:]
    return jnp.concatenate([x1 * cos - x2 * sin, x2 * cos + x1 * sin], axis=-1).astype(x.dtype)


def causal_conv(u, prev, w):
    S = u.shape[1]
    up = jnp.concatenate([prev.astype(u.dtype), u], axis=1)
    y = up[:, 0:S] * w[0]
    for j in range(1, CONV_W):
        y = y + up[:, j:j + S] * w[j]
    return y, up[:, S:]


def mlstm_chunk_step(carry, inp):
    C, n, m = carry
    q, k, v, ig, lf = inp
    L = q.shape[-2]
    causal = jnp.tril(jnp.ones((L, L), dtype=bool))
    b = jnp.cumsum(lf, axis=-1)
    dlog = jnp.where(causal, b[..., :, None] - b[..., None, :] + ig[..., None, :], -jnp.inf)
    g = b + m[..., None]
    m_t = jnp.maximum(jnp.max(dlog, axis=-1), g)
    w = jnp.exp(dlog - m_t[..., None])
    wg = jnp.exp(g - m_t)
    s = jnp.einsum('bhtd,bhsd->bhts', q, k) * w
    num = wg[..., None] * jnp.einsum('bhtd,bhde->bhte', q, C) + jnp.einsum('bhts,bhse->bhte', s, v)
    den = wg * jnp.einsum('bhtd,bhd->bht', q, n) + jnp.sum(s, axis=-1)
    h = num / jnp.maximum(jnp.abs(den), jnp.exp(-m_t))[..., None]
    bL = b[..., -1]
    a = bL[..., None] - b + ig
    m_new = jnp.maximum(bL + m, jnp.max(a, axis=-1))
    wc = jnp.exp(bL + m - m_new)
    ws = jnp.exp(a - m_new[..., None])
    C_new = wc[..., None, None] * C + jnp.einsum('bhs,bhsd,bhse->bhde', ws, k, v)
    n_new = wc[..., None] * n + jnp.einsum('bhs,bhsd->bhd', ws, k)
    return (C_new, n_new, m_new), h


def mlstm(q, k, v, ig, lf, C0, n0, m0):
    B, H, S, _ = q.shape
    L = math.gcd(S, ML_CHUNK)
    NC = S // L

    def chunks(a):
        return jnp.moveaxis(a.reshape((B, H, NC, L) + a.shape[3:]), 2, 0)

    (C, n, m), h = lax.scan(mlstm_chunk_step, (C0, n0, m0),
                            (chunks(q), chunks(k), chunks(v), chunks(ig), chunks(lf)))
    h = jnp.moveaxis(h, 0, 2).reshape(B, H, S, h.shape[-1])
    return h, C, n, m


def mixer_ab(h, C0, n0, m0, sc_prev, w_in, b_if, out_norm, conv_w, w_out):
    B, S, _ = h.shape
    f32 = jnp.float32
    z = h @ w_in
    splits = [int(i) for i in np.cumsum([ML_WIDTH] * 4 + [2 * ML_HEADS] + [SC_WIDTH] * 2)]
    zq, zk, zv, zo, zg, zb, zc, zx = jnp.split(z, splits, axis=-1)

    def heads(a):
        return a.reshape(B, S, ML_HEADS, -1).transpose(0, 2, 1, 3).astype(f32)

    gates = (zg.astype(f32) + b_if.astype(f32)).transpose(0, 2, 1)
    ig = gates[:, :ML_HEADS]
    lf = jax.nn.log_sigmoid(gates[:, ML_HEADS:])
    hm, C, n, m = mlstm(heads(zq), heads(zk) * (ML_DK ** -0.5), heads(zv), ig, lf,
                        C0.astype(f32), n0.astype(f32), m0.astype(f32))
    hm = rms_norm(hm.transpose(0, 2, 1, 3).astype(h.dtype), out_norm.reshape(ML_HEADS, ML_DV))
    hm = (hm * jax.nn.sigmoid(zo).reshape(B, S, ML_HEADS, ML_DV)).reshape(B, S, ML_WIDTH)
    u, sc_new = causal_conv(zc * zx, sc_prev, conv_w)
    y = jnp.concatenate([hm, zb * u], axis=-1) @ w_out
    return y, C, n, m, sc_new


def qkv_heads(h, w_qkv, q_norm, k_norm, pos):
    B, S, _ = h.shape
    z = h @ w_qkv
    q, k, v = jnp.split(z, [ATT_HEADS * HEAD_DIM, (ATT_HEADS + KV_HEADS) * HEAD_DIM], axis=-1)
    q = rope(rms_norm(q.reshape(B, S, ATT_HEADS, HEAD_DIM), q_norm), pos)
    k = rope(rms_norm(k.reshape(B, S, KV_HEADS, HEAD_DIM), k_norm), pos)
    v = v.reshape(B, S, KV_HEADS, HEAD_DIM)
    return q, k, v


def sink_attention(q, k, v, mask, sink):
    s = jnp.einsum('...tkgd,...skd->...kgts', q, k).astype(jnp.float32) * (HEAD_DIM ** -0.5)
    s = jnp.where(mask, s, -jnp.inf)
    sk = jnp.broadcast_to(sink.astype(jnp.float32).reshape(KV_HEADS, GROUP, 1, 1), s.shape[:-1] + (1,))
    p = jax.nn.softmax(jnp.concatenate([s, sk], axis=-1), axis=-1)[..., :-1]
    return jnp.einsum('...kgts,...skd->...tkgd', p.astype(v.dtype), v)


def attn_prompt(h, w_qkv, q_norm, k_norm, sink, w_out):
    B, S, _ = h.shape
    pos = jnp.arange(S, dtype=jnp.int32)
    q, k, v = qkv_heads(h, w_qkv, q_norm, k_norm, pos)
    NB = S // WINDOW
    qb = q.reshape(B, NB, WINDOW, KV_HEADS, GROUP, HEAD_DIM)

    def with_prev(a):
        ab = a.reshape(B, NB, WINDOW, KV_HEADS, HEAD_DIM)
        prev = jnp.concatenate([jnp.zeros_like(ab[:, :1]), ab[:, :-1]], axis=1)
        return jnp.concatenate([prev, ab], axis=2)

    blk = jnp.arange(NB)[:, None, None]
    qpos = blk * WINDOW + jnp.arange(WINDOW)[None, :, None]
    kpos = (blk - 1) * WINDOW + jnp.arange(2 * WINDOW)[None, None, :]
    d = qpos - kpos
    mask = (d >= 0) & (d < WINDOW) & (kpos >= 0)
    o = sink_attention(qb, with_prev(k), with_prev(v), mask[:, None, None], sink)
    y = o.reshape(B, S, ATT_HEADS * HEAD_DIM) @ w_out
    return y, k[:, S - WIN_BUF:], v[:, S - WIN_BUF:]


def attn_sample(h, buf_k, buf_v, w_qkv, q_norm, k_norm, sink, w_out):
    B, L, _ = h.shape
    WB = buf_k.shape[1]
    pos = PAST_LEN + jnp.arange(L, dtype=jnp.int32)
    q, k, v = qkv_heads(h, w_qkv, q_norm, k_norm, pos)
    kk = jnp.concatenate([buf_k.astype(k.dtype), k], axis=1)
    vv = jnp.concatenate([buf_v.astype(v.dtype), v], axis=1)
    kpos = PAST_LEN - WB + jnp.arange(WB + L, dtype=jnp.int32)
    d = pos[:, None] - kpos[None, :]
    mask = (d >= 0) & (d < WINDOW)
    o = sink_attention(q.reshape(B, L, KV_HEADS, GROUP, HEAD_DIM), kk, vv, mask, sink)
    y = o.reshape(B, L, ATT_HEADS * HEAD_DIM) @ w_out
    return y, kk[:, L:], vv[:, L:]


def conv_ffn(h, prev, w_up, conv_w, w_down):
    u = h @ w_up
    u, new_prev = causal_conv(u, prev, conv_w)
    g, a = jnp.split(u, 2, axis=-1)
    return (jax.nn.silu(g) * a) @ w_down, new_prev


def run_trunk(x, c, prompt, st_C, st_n, st_m, st_sc, st_wk, st_wv, st_ffn,
              norm1, norm2, w_ada, b_ada, a_w_in, a_b_if, a_out_norm, a_conv_w, a_w_out,
              c_w_qkv, c_q_norm, c_k_norm, c_sink, c_w_out, f_w_up, f_conv_w, f_w_down):
    Cs, ns, ms, scs, wks, wvs, ffs = [], [], [], [], [], [], []
    for l in range(DEPTH):
        mod = jax.nn.silu(c) @ w_ada[l] + b_ada[l]
        sh1, sc1, g1, sh2, sc2, g2 = jnp.split(mod[:, None, :], 6, axis=-1)
        h = rms_norm(x, norm1[l]) * (1 + sc1) + sh1
        if l % 2 == 0:
            i = l // 2
            y, C, n, m, sc_new = mixer_ab(h, st_C[i], st_n[i], st_m[i], st_sc[i], a_w_in[i], a_b_if[i],
                                          a_out_norm[i], a_conv_w[i], a_w_out[i])
            Cs.append(C)
            ns.append(n)
            ms.append(m)
            scs.append(sc_new)
        else:
            j = l // 2
            if prompt:
                y, wk, wv = attn_prompt(h, c_w_qkv[j], c_q_norm[j], c_k_norm[j], c_sink[j], c_w_out[j])
            else:
                y, wk, wv = attn_sample(h, st_wk[j], st_wv[j], c_w_qkv[j], c_q_norm[j], c_k_norm[j],
                                        c_sink[j], c_w_out[j])
            wks.append(wk)
            wvs.append(wv)
        x = x + g1 * y
        h = rms_norm(x, norm2[l]) * (1 + sc2) + sh2
        y, fp = conv_ffn(h, st_ffn[l], f_w_up[l], f_conv_w[l], f_w_down[l])
        ffs.append(fp)
        x = x + g2 * y
    return (x, jnp.stack(Cs), jnp.stack(ns), jnp.stack(ms), jnp.stack(scs),
            jnp.stack(wks), jnp.stack(wvs), jnp.stack(ffs))


def setup_inputs(seed: int = 0) -> dict:
    key = jax.random.key(seed)
    ks = iter(jax.random.split(key, 40))
    nrm = lambda shape, s=1.0: jax.random.normal(next(ks), shape, jnp.float32) * s
    D = D_MODEL
    b_if = jnp.concatenate([nrm((N_EVEN, ML_HEADS), 0.1),
                            3.0 + nrm((N_EVEN, ML_HEADS), 0.5)], axis=-1)
    return {
        'x_prompt': nrm((BATCH, SEQ, D)),
        'x_sample': nrm((DEC_BATCH, DEC_SEQ, D)),
        'c_prompt': nrm((BATCH, D)),
        'c_sample': nrm((DEC_BATCH, D)),
        'state_mlstm_C': nrm((N_EVEN, DEC_BATCH, ML_HEADS, ML_DK, ML_DV), 0.1),
        'state_mlstm_n': nrm((N_EVEN, DEC_BATCH, ML_HEADS, ML_DK), 0.3),
        'state_mlstm_m': nrm((N_EVEN, DEC_BATCH, ML_HEADS), 1.0),
        'state_sconv': nrm((N_EVEN, DEC_BATCH, CONV_W - 1, SC_WIDTH)),
        'cache_win_k': nrm((N_ODD, DEC_BATCH, WIN_BUF, KV_HEADS, HEAD_DIM)),
        'cache_win_v': nrm((N_ODD, DEC_BATCH, WIN_BUF, KV_HEADS, HEAD_DIM)),
        'state_ffn_conv': nrm((DEPTH, DEC_BATCH, CONV_W - 1, 2 * D_FF)),
        'norm1': 1.0 + nrm((DEPTH, D), 0.02),
        'norm2': 1.0 + nrm((DEPTH, D), 0.02),
        'w_ada': nrm((DEPTH, D, 6 * D), 0.5 * D ** -0.5),
        'b_ada': nrm((DEPTH, 6 * D), 0.02),
        'a_w_in': nrm((N_EVEN, D, IN_A), D ** -0.5),
        'a_b_if': b_if,
        'a_out_norm': 1.0 + nrm((N_EVEN, ML_WIDTH), 0.02),
        'a_conv_w': nrm((N_EVEN, CONV_W, SC_WIDTH), CONV_W ** -0.5),
        'a_w_out': nrm((N_EVEN, ML_WIDTH + SC_WIDTH, D), (ML_WIDTH + SC_WIDTH) ** -0.5),
        'c_w_qkv': nrm((N_ODD, D, QKV_W), D ** -0.5),
        'c_q_norm': 1.0 + nrm((N_ODD, HEAD_DIM), 0.02),
        'c_k_norm': 1.0 + nrm((N_ODD, HEAD_DIM), 0.02),
        'c_sink': nrm((N_ODD, ATT_HEADS), 0.5),
        'c_w_out': nrm((N_ODD, ATT_HEADS * HEAD_DIM, D), (ATT_HEADS * HEAD_DIM) ** -0.5),
        'f_w_up': nrm((DEPTH, D, 2 * D_FF), D ** -0.5),
        'f_conv_w': nrm((DEPTH, CONV_W, 2 * D_FF), CONV_W ** -0.5),
        'f_w_down': nrm((DEPTH, D_FF, D), D_FF ** -0.5),
    }


def reference(x_prompt, x_sample, c_prompt, c_sample, state_mlstm_C, state_mlstm_n, state_mlstm_m,
              state_sconv, cache_win_k, cache_win_v, state_ffn_conv,
              norm1, norm2, w_ada, b_ada, a_w_in, a_b_if, a_out_norm, a_conv_w, a_w_out,
              c_w_qkv, c_q_norm, c_k_norm, c_sink, c_w_out, f_w_up, f_conv_w, f_w_down):
    B = x_prompt.shape[0]
    f32 = jnp.float32
    z_C = jnp.zeros((N_EVEN, B, ML_HEADS, ML_DK, ML_DV), f32)
    z_n = jnp.zeros((N_EVEN, B, ML_HEADS, ML_DK), f32)
    z_m = jnp.zeros((N_EVEN, B, ML_HEADS), f32)
    z_sc = jnp.zeros((N_EVEN, B, CONV_W - 1, SC_WIDTH), x_prompt.dtype)
    z_ffn = jnp.zeros((DEPTH, B, CONV_W - 1, 2 * D_FF), x_prompt.dtype)

    y_prompt, p_C, p_n, p_m, p_sc, p_wk, p_wv, p_ffn = run_trunk(
        x_prompt, c_prompt, True, z_C, z_n, z_m, z_sc, None, None, z_ffn,
        norm1, norm2, w_ada, b_ada, a_w_in, a_b_if, a_out_norm, a_conv_w, a_w_out,
        c_w_qkv, c_q_norm, c_k_norm, c_sink, c_w_out, f_w_up, f_conv_w, f_w_down)

    y_sample, s_C, s_n, s_m, s_sc, s_wk, s_wv, s_ffn = run_trunk(
        x_sample, c_sample, False, state_mlstm_C, state_mlstm_n, state_mlstm_m, state_sconv,
        cache_win_k, cache_win_v, state_ffn_conv,
        norm1, norm2, w_ada, b_ada, a_w_in, a_b_if, a_out_norm, a_conv_w, a_w_out,
        c_w_qkv, c_q_norm, c_k_norm, c_sink, c_w_out, f_w_up, f_conv_w, f_w_down)

    return (y_prompt, y_sample, p_C, p_n, p_m, p_sc, p_wk, p_wv, p_ffn,
            s_C, s_n, s_m, s_sc, s_wk, s_wv, s_ffn)
```

```python
import contextlib
import os
import numpy as np
import concourse.bass as bass
import concourse.mybir as mybir
from concourse.bass_utils import run_bass_kernel_spmd

F32 = mybir.dt.float32
BF16 = mybir.dt.bfloat16
AF = mybir.ActivationFunctionType
ALU = mybir.AluOpType
AX = mybir.AxisListType

D = 1024
KC = 8
DFF = 2816
NF = 22
INA = 3592
SEQ = 8192
TP = 512
NPASS = SEQ // TP
EPS = 1e-6
NEG = -1.0e30
ANEG = -1.0e9
NCORE = 8
SLOT = 13312


class Sch:
    ROT = 30000

    def __init__(self, nc, stack):
        self.nc = nc
        self.stack = stack
        self.names = ('pe', 'act', 'dve', 'pool', 'sync')
        self.prog = {e: [] for e in self.names}
        self.cnt = {e: 0 for e in ('pe', 'act', 'dve', 'pool')}
        self.sems = {}
        self.seen = {e: {} for e in self.names}
        self.lastw = {}
        self.readers = {}
        self.dtot = {}

    def sem(self, key):
        if key not in self.sems:
            nm = "s" + str(len(self.sems))
            self.sems[key] = self.stack.enter_context(self.nc.semaphore(nm))
        return self.sems[key]

    def _deps(self, r, w):
        ev = []
        for k in r:
            if k in self.lastw:
                ev.append(self.lastw[k])
            if k[0] == 'P':
                ev.extend(self.readers.get(k, []))
        for k in w:
            if k in self.lastw:
                ev.append(self.lastw[k])
            ev.extend(self.readers.get(k, []))
        return ev

    def _wait(self, en, evs, skip=None):
        need = {}
        for (sk, v) in evs:
            if skip is not None and sk == skip:
                continue
            if en == 'pe' and sk[0] == 'pe':
                continue
            if need.get(sk, 0) < v:
                need[sk] = v
        for sk, v in need.items():
            if self.seen[en].get(sk, 0) < v:
                self.prog[en].append(('w', self.sem(sk), v))
                self.seen[en][sk] = v

    def _commit(self, ev, r, w):
        for k in r:
            self.readers.setdefault(k, []).append(ev)
        for k in w:
            self.lastw[k] = ev
            self.readers[k] = []

    def op(self, en, fn, r=(), w=(), inc=True):
        self._wait(en, self._deps(r, w))
        c = self.cnt[en] + 1
        sk = (en, (c - 1) // self.ROT)
        v = (c - 1) % self.ROT + 1
        if inc:
            self.cnt[en] = c
            self.prog[en].append(('i', fn, self.sem(sk), 1))
        else:
            self.prog[en].append(('i', fn, None, 0))
        self._commit((sk, v), r, w)

    def dma(self, q, out, in_, r=(), w=(), chan=None, **kw):
        if chan is None:
            chan = w[0] if len(w) else ('o',) + tuple(r[0])
        sk = ('d', chan)
        self._wait(q, self._deps(r, w), skip=sk)
        self.dtot[chan] = self.dtot.get(chan, 0) + 16
        fn = (lambda e, out=out, in_=in_, kw=kw: e.dma_start(out=out, in_=in_, allow_slow_non_contiguous=True, **kw))
        self.prog[q].append(('i', fn, self.sem(sk), 16))
        self._commit((sk, self.dtot[chan]), r, w)

    def barrier(self):
        evs = []
        for e in ('pe', 'act', 'dve', 'pool'):
            c = self.cnt[e]
            if c > 0:
                evs.append(((e, (c - 1) // self.ROT), (c - 1) % self.ROT + 1))
        for ch, t in self.dtot.items():
            evs.append((('d', ch), t))
        for e in self.names:
            need = [x for x in evs if not (x[0][0] == e)]
            for (sk, v) in need:
                if self.seen[e].get(sk, 0) < v:
                    self.prog[e].append(('w', self.sem(sk), v))
                    self.seen[e][sk] = v

    def finish(self):
        evs = []
        for ch, t in self.dtot.items():
            evs.append((('d', ch), t))
        for e in ('pe', 'act', 'dve', 'pool'):
            c = self.cnt[e]
            if c > 0:
                evs.append(((e, (c - 1) // self.ROT), (c - 1) % self.ROT + 1))
        for (sk, v) in evs:
            self.prog['sync'].append(('w', self.sem(sk), v))

    def replay(self, en, eng):
        for it in self.prog[en]:
            if it[0] == 'w':
                eng.wait_ge(it[1], it[2])
            else:
                ins = it[1](eng)
                if it[2] is not None:
                    ins.then_inc(it[2], it[3])


class Arena:
    def __init__(self, t, words):
        self.t = t
        self.words = words
        self.off = 0
        self.gen = 0

    def reset(self):
        self.off = 0
        self.gen += 1

    def get(self, name, shape, dt=F32, parts=128):
        n = 1
        for s in shape[1:]:
            n *= s
        if dt == BF16:
            w = (n + 1) // 2
        else:
            w = n
        w = (w + 7) // 8 * 8
        assert self.off + w <= self.words, (name, self.off, w, self.words)
        ap = self.t[0:shape[0], self.off:self.off + w]
        self.off += w
        if dt == BF16:
            ap = ap.bitcast(BF16)[:, 0:n]
        else:
            ap = ap[:, 0:n]
        if len(shape) == 3:
            ap = ap.rearrange("p (a b) -> p a b", b=shape[2])
        elif len(shape) == 4:
            ap = ap.rearrange("p (a b c) -> p a b c", b=shape[2], c=shape[3])
        return ap, (name, self.gen)


def build():
    nc = bass.Bass("TRN2", target_bir_lowering=False)
    din = lambda n, s: nc.dram_tensor(n, list(s), F32, kind="ExternalInput").ap()
    dout = lambda n, s: nc.dram_tensor(n, list(s), F32, kind="ExternalOutput").ap()
    I = {}
    for n, s in [("xp", (SEQ, D)), ("xs", (128, D)), ("call", (17, D)),
                 ("stC", (2, 16, 4, 128, 128)), ("stn", (2, 64, 128)), ("stmT", (4, 2, 16)),
                 ("stsc", (2, 32, 512)), ("ck", (2, 16, 128, 256)), ("cv", (2, 16, 128, 256)),
                 ("stffn", (4, 32, 5632)),
                 ("w_ada", (4, D, 6144)), ("b_adaT", (128, 4, 48)), ("norm1T", (128, 4, 8)),
                 ("norm2T", (128, 4, 8)), ("a_w_in", (2, D, INA)), ("a_bif", (4, 2, 2)),
                 ("a_onT", (128, 2, 4)), ("a_cwT", (128, 2, 3, 4)), ("a_w_out", (2, D, D)),
                 ("c_w_qkv", (2, D, 1536)), ("c_qn", (2, 64)), ("c_kn", (2, 64)), ("c_sink", (2, 16)),
                 ("c_w_out", (2, D, D)), ("f_w_up", (4, D, 2 * DFF)), ("f_cwT", (128, 4, 3, 44)),
                 ("f_w_down", (4, DFF, D)),
                 ("ident", (128, 128)), ("mP", (128, 128)), ("mS", (128, 128)),
                 ("acur", (128, 512)), ("aprev", (128, 512)), ("anew", (128, 512)),
                 ("acache", (128, 16, 128)), ("onehot", (128, 16)), ("sel4", (4, 4, 128)),
                 ("cosP", (SEQ, 32)), ("sinP", (SEQ, 32)), ("cosS", (128, 32)), ("sinS", (128, 32))]:
        I[n] = din(n, s)
    O = {}
    for n, s in [("yp", (SEQ, D)), ("ys", (128, D)), ("pC", (2, 4, 128, 128)), ("pn", (2, 4, 128)),
                 ("pm", (2, 4)), ("psc", (2, 2, 512)), ("pwk", (2, 128, 256)), ("pwv", (2, 128, 256)),
                 ("pffn", (4, 2, 5632)), ("sC", (2, 16, 4, 128, 128)), ("sn", (2, 64, 128)),
                 ("smT", (4, 2, 16)), ("ssc", (2, 32, 512)), ("swk", (2, 16, 128, 256)),
                 ("swv", (2, 16, 128, 256)), ("sffn", (4, 32, 5632))]:
        O[n] = dout(n, s)

    with contextlib.ExitStack() as st:
        S = Sch(nc, st)
        sbt = lambda n, s, dt=F32: st.enter_context(nc.sbuf_tensor("sb_" + n, list(s), dt))
        PS = [st.enter_context(nc.psum_tensor("ps%d" % i, [128, 512], F32)) for i in range(8)]
        PK = [('P', i) for i in range(8)]
        psb = lambda i: PS[i][:, :].bitcast(BF16)

        xT = sbt("xT", [128, 8, TP]); kX = ('xT',)
        hT = sbt("hT", [128, 8, TP], BF16); kH = ('hT',)
        MOD = sbt("MOD", [128, 4, 48, 17]); kM = ('MOD',)
        WB = sbt("WB", [128, 3 * SLOT], BF16)
        WKt = sbt("WK", [128, 15360])
        AR = Arena(WKt, 15360)
        ident = sbt("ident", [128, 128]); identb = sbt("identb", [128, 128], BF16)
        onesb = sbt("onesb", [128, 128], BF16)
        mPb = sbt("mPb", [128, 128], BF16); mSb = sbt("mSb", [128, 128], BF16)
        acurb = sbt("acurb", [128, 512], BF16); aprevb = sbt("aprevb", [128, 512], BF16)
        anewb = sbt("anewb", [128, 512], BF16); acacheb = sbt("acacheb", [128, 16, 128], BF16)
        onehot = sbt("onehot", [128, 16]); sel4 = sbt("sel4", [4, 4, 128])
        onesrow = sbt("onesrow", [4, TP])
        cosT = sbt("cosT", [128, 4, 32]); sinT = sbt("sinT", [128, 4, 32])
        n1T = sbt("n1T", [128, 4, 8]); n2T = sbt("n2T", [128, 4, 8]); badaT = sbt("badaT", [128, 4, 48])
        fcw = sbt("fcw", [128, 4, 3, 44]); acw = sbt("acw", [128, 2, 3, 4]); aon = sbt("aon", [128, 2, 4])
        bif = sbt("bif", [4, 2, 2]); nbif = sbt("nbif", [4, 2, 2])
        GQ = sbt("GQ", [128, 2, 64]); GK = sbt("GK", [128, 2, 64]); SK = sbt("SK", [128, 2, 16])
        SINKE = sbt("SINKE", [128, 2, 16]); NEGMA = sbt("NEGMA", [128, 2]); tmpc = sbt("tmpc", [128, 4])
        Cst = sbt("Cst", [128, 2, 4, 129]); MROW = sbt("MROW", [4, 2])
        ffc = sbt("ffc", [128, 4, 44, 2]); scc = sbt("scc", [128, 2, 4, 2])
        KTC = sbt("KTC", [128, 2, 2, 128], BF16); VC = sbt("VC", [128, 2, 4, 65], BF16)
        scT = sbt("scT", [128, 8, 17], BF16)
        stage = sbt("stage", [128, D]); call_sb = stage[0:17, :]
        kC = ('const',)

        V, A_, P_, PE_ = 'dve', 'act', 'pool', 'pe'

        def ld(q, dst, src, key, **kw):
            S.dma(q, dst, src, w=[key], chan=key if key != kC else ('c0',), **kw)
        for dst, nm in [(ident, "ident"), (onehot, "onehot"), (sel4, "sel4"), (n1T, "norm1T"), (n2T, "norm2T"),
                        (badaT, "b_adaT"), (fcw, "f_cwT"), (acw, "a_cwT"), (aon, "a_onT"), (bif, "a_bif")]:
            S.dma('sync', dst[:], I[nm], w=[kC], chan=('c0',))
        S.dma('sync', call_sb, I["call"], w=[('stage',)], chan=('istage',))
        for l in range(2):
            S.dma('sync', GQ[:, l, :], I["c_qn"][l].partition_broadcast(128), w=[kC], chan=('c0',))
            S.dma('sync', GK[:, l, :], I["c_kn"][l].partition_broadcast(128), w=[kC], chan=('c0',))
            S.dma('sync', SK[:, l, :], I["c_sink"][l].partition_broadcast(128), w=[kC], chan=('c0',))
        for dst, nm in [(identb, "ident"), (mPb, "mP"), (mSb, "mS"), (acurb, "acur"), (aprevb, "aprev"),
                        (anewb, "anew"), (acacheb, "acache")]:
            S.dma('pool', dst[:], I[nm], w=[kC], chan=('c1',))
        kC2 = ('const2',)
        S.op(V, lambda e: e.memset(onesb[:], 1.0), r=[kC], w=[kC2])
        S.op(V, lambda e: e.memset(onesrow[:], 1.0), w=[kC2])
        S.op(V, lambda e: e.tensor_scalar(out=nbif[:], in0=bif[:], scalar1=-1.0, scalar2=None, op0=ALU.mult), r=[kC], w=[kC2])
        S.op(V, lambda e: e.memset(Cst[:], 0.0), w=[('Cst', 0), ('Cst', 1)])
        S.op(V, lambda e: e.memset(MROW[:], 0.0), w=[('MROW',)])
        S.op(V, lambda e: e.memset(ffc[:], 0.0), w=[('ffc',)])
        S.op(V, lambda e: e.memset(scc[:], 0.0), w=[('scc',)])
        S.op(V, lambda e: e.memset(VC[:], 1.0), w=[('VC',)])
        for l in range(2):
            S.op(V, lambda e, l=l: e.tensor_reduce(out=tmpc[:, 0:1], in_=GQ[:, l, :], axis=AX.X, op=ALU.max, apply_absolute_value=True), r=[kC], w=[('tmpc',)])
            S.op(V, lambda e, l=l: e.tensor_reduce(out=tmpc[:, 1:2], in_=GK[:, l, :], axis=AX.X, op=ALU.max, apply_absolute_value=True), r=[kC], w=[('tmpc',)])
            S.op(V, lambda e, l=l: e.scalar_tensor_tensor(out=NEGMA[:, l:l + 1], in0=tmpc[:, 0:1], scalar=-8.0, in1=tmpc[:, 1:2], op0=ALU.mult, op1=ALU.mult), r=[('tmpc',)], w=[kC2])
            S.op(A_, lambda e, l=l: e.activation(out=SINKE[:, l, :], in_=SK[:, l, :], func=AF.Exp, bias=NEGMA[:, l:l + 1], scale=1.0), r=[kC, kC2], w=[('SINKE',)])

        S.op(A_, lambda e: e.activation(out=call_sb, in_=call_sb, func=AF.Silu), r=[('stage',)], w=[('stage',)])
        for kc in range(8):
            S.op(PE_, lambda e, kc=kc: e.transpose(out=PS[0][:, kc * 17:(kc + 1) * 17], in_=call_sb[:, kc * 128:(kc + 1) * 128], identity=ident[0:17, 0:17]),
                 r=[('stage',), kC], w=[PK[0]])
        S.op(V, lambda e: e.tensor_copy(out=scT[:], in_=PS[0][:, 0:136].rearrange("p (a b) -> p a b", b=17)), r=[PK[0]], w=[('scT',)])
        wa = [WB[:, i * 6144:(i + 1) * 6144] for i in range(2)]
        for l in range(4):
            for kc in range(8):
                sl = (l * 8 + kc) % 2
                S.dma('pool', wa[sl], I["w_ada"][l, kc * 128:(kc + 1) * 128, :], w=[('W', sl)], max_dma_last_dim=4096)
                for fc in range(48):
                    b = 1 + fc // 24
                    o = (fc % 24) * 17
                    S.op(PE_, lambda e, sl=sl, fc=fc, b=b, o=o, kc=kc: e.matmul(PS[b][:, o:o + 17], wa[sl][:, fc * 128:(fc + 1) * 128], scT[:, kc, :],
                                                                            start=(kc == 0 and fc % 24 == 0), stop=(kc == 7), skip_group_check=True),
                         r=[('W', sl), ('scT',)], w=[PK[b]], inc=(fc == 47))
            for b in range(2):
                S.op(V, lambda e, l=l, b=b: e.tensor_tensor(out=MOD[:, l, b * 24:(b + 1) * 24, :], in0=PS[1 + b][:, 0:408].rearrange("p (a b) -> p a b", b=17),
                                                           in1=badaT[:, l, b * 24:(b + 1) * 24].unsqueeze(2).broadcast_to([128, 24, 17]), op=ALU.add),
                     r=[PK[1 + b], kC], w=[kM])
            for (c0, nT) in [(8, n1T), (32, n2T)]:
                S.op(V, lambda e, l=l, c0=c0, nT=nT: e.scalar_tensor_tensor(out=MOD[:, l, c0:c0 + 8, :], in0=MOD[:, l, c0:c0 + 8, :], scalar=1.0,
                                                                           in1=nT[:, l, :].unsqueeze(2).broadcast_to([128, 8, 17]), op0=ALU.add, op1=ALU.mult),
                     r=[kM, kC], w=[kM])

        def fm_rows_out(src_fn, nrows, dst_rows_fn, nchunks, rkeys):
            for c0 in range(0, nchunks, 4):
                n = min(4, nchunks - c0)
                for j in range(n):
                    S.op(PE_, lambda e, j=j, c0=c0: e.transpose(out=PS[3][0:nrows, j * 128:(j + 1) * 128], in_=src_fn(c0 + j), identity=ident[:, :]),
                         r=rkeys + [kC], w=[PK[3]])
                S.op(A_, lambda e, n=n: e.copy(out=stage[0:nrows, 0:n * 128], in_=PS[3][0:nrows, 0:n * 128]), r=[PK[3]], w=[('stage',)])
                S.dma('sync', dst_rows_fn(c0 * 128, n * 128), stage[0:nrows, 0:n * 128], r=[('stage',)], chan=('ostage',))

        def rows_to_fm(src_rows_fn, nrows, dst_fn, nchunks, wkeys):
            for c0 in range(0, nchunks, 8):
                n = min(8, nchunks - c0)
                S.dma('sync', stage[0:nrows, 0:n * 128], src_rows_fn(c0 * 128, n * 128), w=[('stage',)], chan=('istage',))
                for j in range(n):
                    S.op(PE_, lambda e, j=j: e.transpose(out=PS[3][:, j * 32:j * 32 + nrows], in_=stage[0:nrows, j * 128:(j + 1) * 128], identity=ident[0:nrows, 0:nrows]),
                         r=[('stage',), kC], w=[PK[3]])
                for j in range(n):
                    S.op(V, lambda e, j=j, c0=c0: e.tensor_copy(out=dst_fn(c0 + j), in_=PS[3][:, j * 32:j * 32 + nrows]), r=[PK[3]], w=wkeys)

        def run_pass(sample, pi):
            T = 128 if sample else TP
            nseq = 16 if sample else 1
            L = 8 if sample else 128
            LT = 8 if sample else TP
            nch = T // 128
            tok0 = 0 if sample else pi * TP
            last = sample or (pi == NPASS - 1)
            xin = I["xs"] if sample else I["xp"]
            yout = O["ys"] if sample else O["yp"]
            sq0 = 1 if sample else 0
            v3 = lambda ap: ap.rearrange("p (s l) -> p s l", l=LT)

            def modbc(l, ch, kc):
                return MOD[:, l, ch * 8 + kc, sq0:sq0 + nseq].unsqueeze(2).broadcast_to([128, nseq, LT])

            for c in range(nch):
                S.dma('sync', stage[:, :], xin[tok0 + c * 128: tok0 + (c + 1) * 128, :], w=[('stage',)], chan=('istage',))
                for b in range(2):
                    for j in range(4):
                        kc = b * 4 + j
                        S.op(PE_, lambda e, b=b, j=j, kc=kc: e.transpose(out=PS[b][:, j * 128:(j + 1) * 128], in_=stage[:, kc * 128:(kc + 1) * 128], identity=ident[:, :]),
                             r=[('stage',), kC], w=[PK[b]])
                    S.op(V if b == 0 else A_, (lambda e, b=b, c=c: e.tensor_copy(out=xT[:, b * 4:(b + 1) * 4, c * 128:(c + 1) * 128], in_=PS[b][:, :].rearrange("p (a b) -> p a b", b=128))) if b == 0 else
                         (lambda e, b=b, c=c: e.copy(out=xT[:, b * 4:(b + 1) * 4, c * 128:(c + 1) * 128], in_=PS[b][:, :].rearrange("p (a b) -> p a b", b=128))),
                         r=[PK[b]], w=[kX])
            if not sample:
                S.dma('sync', cosT[:, 0:nch, :], I["cosP"][tok0:tok0 + T, :].rearrange("(c p) i -> p c i", p=128), w=[('rope',)])
                S.dma('sync', sinT[:, 0:nch, :], I["sinP"][tok0:tok0 + T, :].rearrange("(c p) i -> p c i", p=128), w=[('rope',)])
            else:
                S.dma('sync', cosT[:, 0, :], I["cosS"], w=[('rope',)])
                S.dma('sync', sinT[:, 0, :], I["sinS"], w=[('rope',)])

            def norm_mod(l, which):
                cA, cB = (1, 0) if which == 1 else (4, 3)
                S.barrier(); AR.reset()
                sqb, ksq = AR.get("sq", [128, 8, T], BF16)
                rs, krs = AR.get("rs", [128, T])
                tmp, ktmp = AR.get("tmp", [128, T])
                S.op(A_, lambda e: e.activation(out=sqb, in_=xT[:, :, 0:T], func=AF.Square), r=[kX], w=[ksq])
                for kc in range(8):
                    S.op(PE_, lambda e, kc=kc: e.matmul(PS[0][:, 0:T], onesb[:, :], sqb[:, kc, :], start=(kc == 0), stop=(kc == 7)), r=[ksq, kC2], w=[PK[0]], inc=(kc == 7))
                S.op(A_, lambda e: e.activation(out=rs, in_=PS[0][:, 0:T], func=AF.Sqrt, bias=EPS, scale=1.0 / D), r=[PK[0]], w=[krs])
                S.op(V, lambda e: e.reciprocal(out=rs, in_=rs), r=[krs], w=[krs])
                for kc in range(8):
                    S.op(V, lambda e, kc=kc: e.tensor_tensor(out=tmp, in0=xT[:, kc, 0:T], in1=rs, op=ALU.mult), r=[kX, krs], w=[ktmp])
                    S.op(V, lambda e, kc=kc: e.tensor_tensor(out=v3(tmp), in0=v3(tmp), in1=modbc(l, cA, kc), op=ALU.mult), r=[ktmp, kM], w=[ktmp])
                    S.op(V, lambda e, kc=kc: e.tensor_tensor(out=v3(hT[:, kc, 0:T]), in0=v3(tmp), in1=modbc(l, cB, kc), op=ALU.add), r=[ktmp, kM], w=[kH])

            def resid_add(l, gch, d, psrc, pkey, tkey_ap):
                tmp, ktmp = tkey_ap
                S.op(V, lambda e: e.tensor_tensor(out=v3(tmp), in0=v3(psrc), in1=modbc(l, gch, d), op=ALU.mult), r=[pkey, kM], w=[ktmp])
                S.op(V, lambda e: e.tensor_tensor(out=xT[:, d, 0:T], in0=xT[:, d, 0:T], in1=tmp, op=ALU.add), r=[ktmp, kX], w=[kX])

            def conv3(dst3, src3, wfn, keys_r, keys_w):
                S.op(V, lambda e: e.tensor_scalar(out=dst3, in0=src3[:, :, 0:LT], scalar1=wfn(0), scalar2=None, op0=ALU.mult), r=keys_r, w=keys_w)
                S.op(V, lambda e: e.scalar_tensor_tensor(out=dst3, in0=src3[:, :, 1:LT + 1], scalar=wfn(1), in1=dst3, op0=ALU.mult, op1=ALU.add), r=keys_r + keys_w, w=keys_w)
                S.op(V, lambda e: e.scalar_tensor_tensor(out=dst3, in0=src3[:, :, 2:LT + 2], scalar=wfn(2), in1=dst3, op0=ALU.mult, op1=ALU.add), r=keys_r + keys_w, w=keys_w)

            def ffn(l):
                norm_mod(l, 2)
                if os.environ.get("KD_FFN", "") == "n":
                    return
                S.barrier(); AR.reset()
                UB, kUB = AR.get("UB", [128, 4, nseq * (LT + 2)])
                Y, kY = AR.get("Y", [128, 4, T])
                ACTB, kAB = AR.get("ACTB", [128, 2, T], BF16)
                rt = AR.get("rt", [128, T])
                if sample:
                    car, kcar = AR.get("car", [128, 44, 32])
                    rows_to_fm(lambda o, n: I["stffn"][l, :, o:o + n], 32, lambda c: car[:, c, :], 44, [kcar])
                    carv = lambda idx: car[:, idx, :].rearrange("p (s j) -> p s j", j=2)
                else:
                    kcar = ('ffc',)
                    carv = lambda idx: ffc[:, l, idx, :].unsqueeze(1)
                ub3 = lambda i: UB[:, i, :].rearrange("p (s k) -> p s k", k=LT + 2)

                def wslot(g):
                    base = (g % 3) * SLOT
                    ug = WB[:, base:base + 2048].rearrange("p (k c) -> p k c", c=256)
                    ua = WB[:, base + 2048:base + 4096].rearrange("p (k c) -> p k c", c=256)
                    dn = WB[:, base + 4096:base + 6144].rearrange("p (j d) -> p j d", d=D)
                    return ug, ua, dn

                def issue(g):
                    ug, ua, dn = wslot(g)
                    k = ('W', g % 3)
                    wu = I["f_w_up"][l].rearrange("(k p) c -> p k c", p=128)
                    S.dma('pool', ug, wu[:, :, g * 256:(g + 1) * 256], w=[k], max_dma_last_dim=4096)
                    S.dma('pool', ua, wu[:, :, DFF + g * 256:DFF + (g + 1) * 256], w=[k], max_dma_last_dim=4096)
                    S.dma('pool', dn, I["f_w_down"][l, g * 256:(g + 1) * 256, :].rearrange("(j p) d -> p j d", p=128), w=[k], max_dma_last_dim=4096)
                issue(0); issue(1)
                for g in range(int(os.environ.get("KD_NG", 11))):
                    if g + 2 < 11:
                        issue(g + 2)
                    ug, ua, dn = wslot(g)
                    kW = ('W', g % 3)
                    for j in range(2):
                        for part in range(2):
                            i = 2 * j + part
                            wv = ug if part == 0 else ua
                            idx = (0 if part == 0 else 22) + 2 * g + j
                            for kc in range(8):
                                S.op(PE_, lambda e, i=i, wv=wv, j=j, kc=kc: e.matmul(PS[i][:, 0:T], wv[:, kc, j * 128:(j + 1) * 128], hT[:, kc, 0:T], start=(kc == 0), stop=(kc == 7)),
                                     r=[kW, kH], w=[PK[i]], inc=(kc == 7))
                            S.op(A_, lambda e, i=i, idx=idx: e.copy(out=ub3(i)[:, :, 0:2], in_=carv(idx)), r=[kcar], w=[(kUB, i)])
                            S.op(A_, lambda e, i=i: e.copy(out=ub3(i)[:, :, 2:LT + 2], in_=v3(PS[i][:, 0:T])), r=[PK[i]], w=[(kUB, i)])
                            S.op(A_, lambda e, i=i, idx=idx: e.copy(out=carv(idx), in_=ub3(i)[:, :, LT:LT + 2]), r=[(kUB, i)], w=[kcar])
                            conv3(v3(Y[:, i, :]), ub3(i), lambda t, idx=idx: fcw[:, l, t, idx:idx + 1], [(kUB, i), kC], [(kY, i)])
                        S.op(A_, lambda e, j=j: e.activation(out=Y[:, 2 * j, :], in_=Y[:, 2 * j, :], func=AF.Silu), r=[(kY, 2 * j)], w=[(kY, 2 * j)])
                        S.op(V, lambda e, j=j: e.tensor_tensor(out=ACTB[:, j, :], in0=Y[:, 2 * j, :], in1=Y[:, 2 * j + 1, :], op=ALU.mult), r=[(kY, 2 * j), (kY, 2 * j + 1)], w=[(kAB, j)])
                    for dh in range(2):
                        for d4 in range(4):
                            d = dh * 4 + d4
                            for j in range(2):
                                S.op(PE_, lambda e, d4=d4, d=d, j=j, dn=dn: e.matmul(PS[4 + d4][:, 0:T], dn[:, j, d * 128:(d + 1) * 128], ACTB[:, j, :], start=(j == 0), stop=(j == 1)),
                                     r=[kW, (kAB, j)], w=[PK[4 + d4]], inc=(j == 1))
                            resid_add(l, 5, d, PS[4 + d4][:, 0:T], PK[4 + d4], rt)
                if last:
                    nr = 32 if sample else 2
                    dst = O["sffn"] if sample else O["pffn"]
                    if sample:
                        fm_rows_out(lambda c: car[:, c, :], 32, lambda o, n: dst[l, :, o:o + n], 44, [kcar])
                    else:
                        fm_rows_out(lambda c: ffc[:, l, c, :], 2, lambda o, n: dst[l, :, o:o + n], 44, [kcar])

            def mixer_ab(l):
                i = l // 2
                norm_mod(l, 1)
                S.barrier(); AR.reset()
                win = WB[:, 0:8 * INA].rearrange("p (k c) -> p k c", c=INA)
                wout = WB[:, 8 * INA:8 * INA + 8 * D].rearrange("p (k c) -> p k c", c=D)
                kW = [('W', 0), ('W', 1), ('W', 2)]
                for kc in range(8):
                    S.dma('pool', win[:, kc, :], I["a_w_in"][i, kc * 128:(kc + 1) * 128, :], w=kW, chan=('W', 0), max_dma_last_dim=4096)
                S.dma('pool', wout, I["a_w_out"][i].rearrange("(k p) c -> p k c", p=128), w=kW, chan=('W', 0), max_dma_last_dim=4096)
                qT, kqT = AR.get("qT", [128, 4, T], BF16)
                kTt, kkT = AR.get("kT", [128, 4, T], BF16)
                sgT, ksg = AR.get("sgT", [128, 4, T], BF16)
                scTt, ksc = AR.get("scTt", [128, 4, T], BF16)
                hmT, khm = AR.get("hmT", [128, 4, T], BF16)
                KTOK, kkt = AR.get("KTOK", [128, nch, 512], BF16)
                VT, kvt = AR.get("VT", [128, nch * 4, 129], BF16)
                PB, kPB = AR.get("PB", [128, nseq * (LT + 2)])
                ZX, kZX = AR.get("ZX", [128, T])
                U, kU = AR.get("U", [128, T])
                rt = AR.get("rt", [128, T])
                rows = {}
                for nm in ["ig", "l1", "cs", "aa", "A", "negm", "prev", "lastr"]:
                    rows[nm] = AR.get("r_" + nm, [4, T], parts=4)
                NEGAX, kNX = AR.get("NEGAX", [4, nseq + T], parts=4)
                MIN, kMIN = AR.get("MIN", [4, 16], parts=4)
                COLS, kCOLS = AR.get("COLS", [128, 20])
                SA, kSA = AR.get("SA", [128, nseq + 128])
                WT, kWT = AR.get("WT", [128, 128])
                PT, kPT = AR.get("PT", [128, 128], BF16)
                PVs, kPVs = AR.get("PVs", [128, 129])
                NUM, kNUM = AR.get("NUM", [128, 129])
                JK, kJK = AR.get("JK", [128, 128])
                HN, kHN = AR.get("HN", [128, 128], BF16)
                VS, kVS = AR.get("VS", [128, 129], BF16)
                SM, kSM = AR.get("SM", [128, 16])
                WSI, kWSI = AR.get("WSI", [128, 16])
                WCB, kWCB = AR.get("WCB", [128, 16])
                QPAD, kQP = AR.get("QPAD", [128, nseq * 136])
                if sample:
                    CS_, kCS = AR.get("CSs", [128, 64, 129])
                    S.dma('sync', CS_[:, :, 0:128], I["stC"][i].rearrange("s h k v -> k (s h) v"), w=[kCS])
                    rows_to_fm(lambda o, n: I["stn"][i, :, o:o + n], 64, lambda c: CS_[:, :, 128], 1, [kCS])
                    S.dma('sync', MIN, I["stmT"][:, i, :], w=[kMIN])
                    csv = lambda s_, h: CS_[:, s_ * 4 + h, :]
                    pcar, kpc = AR.get("pcar", [128, 4, 32])
                    rows_to_fm(lambda o, n: I["stsc"][i, :, o:o + n], 32, lambda c: pcar[:, c, :], 4, [kpc])
                    pcv = lambda ch: pcar[:, ch, :].rearrange("p (s j) -> p s j", j=2)
                else:
                    kCS = ('Cst', i)
                    csv = lambda s_, h: Cst[:, i, h, :]
                    kMIN = ('MROW',)
                    kpc = ('scc',)
                    pcv = lambda ch: scc[:, i, ch, :].unsqueeze(1)
                S.op(V, lambda e: e.memset(VT, 1.0), w=[kvt])
                S.op(V, lambda e: e.memset(QPAD, 0.0), w=[kQP])

                def proj_fm(c0, handler, tag):
                    b = proj_fm.n % 2
                    proj_fm.n += 1
                    for kc in range(8):
                        S.op(PE_, lambda e, b=b, kc=kc: e.matmul(PS[b][:, 0:T], win[:, kc, c0:c0 + 128], hT[:, kc, 0:T], start=(kc == 0), stop=(kc == 7)),
                             r=kW + [kH], w=[PK[b]], inc=(kc == 7))
                    handler(PS[b][:, 0:T], PK[b])
                proj_fm.n = 0
                for h in range(4):
                    proj_fm(h * 128, lambda p, k, h=h: S.op(A_, lambda e: e.copy(out=qT[:, h, :], in_=p), r=[k], w=[kqT]), "q")
                    proj_fm(512 + h * 128, lambda p, k, h=h: S.op(A_, lambda e: e.activation(out=kTt[:, h, :], in_=p, func=AF.Copy, scale=128.0 ** -0.5), r=[k], w=[kkT]), "k")
                    proj_fm(1536 + h * 128, lambda p, k, h=h: S.op(A_, lambda e: e.activation(out=sgT[:, h, :], in_=p, func=AF.Sigmoid), r=[k], w=[ksg]), "o")
                pb3 = PB.rearrange("p (s k) -> p s k", k=LT + 2)
                for ch in range(4):
                    proj_fm(3080 + ch * 128, lambda p, k: S.op(A_, lambda e: e.copy(out=ZX, in_=p), r=[k], w=[kZX]), "zx")
                    S.op(A_, lambda e, ch=ch: e.copy(out=pb3[:, :, 0:2], in_=pcv(ch)), r=[kpc], w=[kPB])
                    proj_fm(2568 + ch * 128, lambda p, k: S.op(V, lambda e: e.tensor_tensor(out=pb3[:, :, 2:LT + 2], in0=v3(p), in1=v3(ZX), op=ALU.mult), r=[k, kZX], w=[kPB]), "zc")
                    S.op(A_, lambda e, ch=ch: e.copy(out=pcv(ch), in_=pb3[:, :, LT:LT + 2]), r=[kPB], w=[kpc])
                    conv3(v3(U), pb3, lambda t, ch=ch: acw[:, i, t, ch:ch + 1], [kPB, kC], [kU])
                    proj_fm(2056 + ch * 128, lambda p, k, ch=ch: S.op(V, lambda e: e.tensor_tensor(out=scTt[:, ch, :], in0=p, in1=U, op=ALU.mult), r=[k, kU], w=[ksc]), "zb")
                if last:
                    if sample:
                        fm_rows_out(lambda c: pcar[:, c, :], 32, lambda o, n: O["ssc"][i, :, o:o + n], 4, [kpc])
                    else:
                        fm_rows_out(lambda c: scc[:, i, c, :], 2, lambda o, n: O["psc"][i, :, o:o + n], 4, [kpc])
                rg = lambda nm: rows[nm][0]
                kg = lambda nm: rows[nm][1]
                for gi, (c0, nm) in enumerate([(2048, "ig"), (2052, "l1")]):
                    for kc in range(8):
                        S.op(PE_, lambda e, kc=kc, c0=c0: e.matmul(PS[2][0:4, 0:T], win[:, kc, c0:c0 + 4], hT[:, kc, 0:T], start=(kc == 0), stop=(kc == 7)),
                             r=kW + [kH], w=[PK[2]], inc=(kc == 7))
                    if gi == 0:
                        S.op(A_, lambda e: e.activation(out=rg("ig"), in_=PS[2][0:4, 0:T], func=AF.Identity, bias=bif[:, i, 0:1], scale=1.0), r=[PK[2], kC], w=[kg("ig")])
                    else:
                        S.op(A_, lambda e: e.activation(out=rg("l1"), in_=PS[2][0:4, 0:T], func=AF.Exp, bias=nbif[:, i, 1:2], scale=-1.0), r=[PK[2], kC2], w=[kg("l1")])
                        S.op(A_, lambda e: e.activation(out=rg("l1"), in_=rg("l1"), func=AF.Ln, bias=1.0, scale=1.0), r=[kg("l1")], w=[kg("l1")])
                r3 = lambda ap: ap.rearrange("p (s l) -> p s l", l=LT)
                for s_ in range(nseq):
                    sl = slice(s_ * LT, (s_ + 1) * LT)
                    S.op(V, lambda e, sl=sl: e.tensor_tensor_scan(out=rg("cs")[:, sl], data0=onesrow[:, 0:LT], data1=rg("l1")[:, sl], initial=0.0, op0=ALU.mult, op1=ALU.add),
                         r=[kg("l1"), kC2], w=[kg("cs")])
                S.op(V, lambda e: e.tensor_tensor(out=rg("aa"), in0=rg("ig"), in1=rg("cs"), op=ALU.add), r=[kg("ig"), kg("cs")], w=[kg("aa")])
                minap = (lambda s_: MIN[:, s_:s_ + 1]) if sample else (lambda s_: MROW[:, i:i + 1])
                for s_ in range(nseq):
                    sl = slice(s_ * LT, (s_ + 1) * LT)
                    S.op(V, lambda e, sl=sl, s_=s_: e.tensor_tensor_scan(out=rg("A")[:, sl], data0=rg("aa")[:, sl], data1=rg("aa")[:, sl], initial=minap(s_), op0=ALU.max, op1=ALU.max),
                         r=[kg("aa"), kMIN], w=[kg("A")])
                minall = MIN[:, 0:16] if sample else MROW[:, i:i + 1]
                S.op(V, lambda e: e.tensor_scalar(out=NEGAX[:, 0:nseq], in0=minall, scalar1=-1.0, scalar2=None, op0=ALU.mult), r=[kMIN], w=[kNX])
                S.op(V, lambda e: e.tensor_scalar(out=NEGAX[:, nseq:nseq + T], in0=rg("A"), scalar1=-1.0, scalar2=None, op0=ALU.mult), r=[kg("A")], w=[kNX])
                S.op(V, lambda e: e.tensor_tensor(out=rg("negm"), in0=rg("cs"), in1=rg("A"), op=ALU.subtract), r=[kg("cs"), kg("A")], w=[kg("negm")])
                if sample:
                    S.op(V, lambda e: e.tensor_copy(out=r3(rg("prev")), in_=MIN[:, 0:16].unsqueeze(2).broadcast_to([4, 16, 8])), r=[kMIN], w=[kg("prev")])
                    S.op(V, lambda e: e.tensor_copy(out=r3(rg("lastr")), in_=r3(NEGAX[:, 16:16 + T])[:, :, 7:8].broadcast_to([4, 16, 8])), r=[kNX], w=[kg("lastr")])
                else:
                    c3 = lambda ap: ap.rearrange("p (c l) -> p c l", l=128)
                    S.op(V, lambda e: e.tensor_scalar(out=c3(rg("prev")), in0=c3(NEGAX[:, 0:T])[:, :, 0:1].broadcast_to([4, nch, 128]), scalar1=-1.0, scalar2=None, op0=ALU.mult), r=[kNX], w=[kg("prev")])
                    S.op(V, lambda e: e.tensor_copy(out=c3(rg("lastr")), in_=c3(NEGAX[:, 1:T + 1])[:, :, 127:128].broadcast_to([4, nch, 128])), r=[kNX], w=[kg("lastr")])
                if sample:
                    if last:
                        MOUT, kMO = AR.get("MOUT", [4, 16], parts=4)
                        S.op(V, lambda e: e.tensor_scalar(out=MOUT, in0=r3(rg("negm"))[:, :, 7], scalar1=-1.0, scalar2=None, op0=ALU.mult), r=[kg("negm")], w=[kMO])
                        S.dma('sync', O["smT"][:, i, :], MOUT, r=[kMO], chan=('osm',))
                else:
                    S.op(V, lambda e: e.tensor_scalar(out=MROW[:, i:i + 1], in0=rg("negm")[:, T - 1:T], scalar1=-1.0, scalar2=None, op0=ALU.mult), r=[kg("negm")], w=[kMIN])
                    if last:
                        S.dma('sync', O["pm"][i].unsqueeze(1), MROW[:, i:i + 1], r=[kMIN], chan=('opm',))
                for c in range(nch):
                    cs_ = slice(c * 128, (c + 1) * 128)
                    for part, c0 in [(0, 512), (1, 1024)]:
                        b = 4 + part
                        for kc in range(8):
                            S.op(PE_, lambda e, kc=kc, b=b, c0=c0, cs_=cs_: e.matmul(PS[b][:, 0:512], hT[:, kc, cs_], win[:, kc, c0:c0 + 512], start=(kc == 0), stop=(kc == 7)),
                                 r=kW + [kH], w=[PK[b]], inc=(kc == 7))
                    S.op(A_, lambda e, c=c: e.activation(out=KTOK[:, c, :], in_=PS[4][:, 0:512], func=AF.Copy, scale=128.0 ** -0.5), r=[PK[4]], w=[kkt])
                    S.op(A_, lambda e, c=c: e.copy(out=VT[:, c * 4:(c + 1) * 4, 0:128], in_=PS[5][:, 0:512].rearrange("p (h v) -> p h v", v=128)), r=[PK[5]], w=[kvt])
                maskb = mSb if sample else mPb
                qpd = QPAD.rearrange("p (s k) -> p s k", k=136)[:, :, 0:L]
                for c in range(nch):
                    cs_ = slice(c * 128, (c + 1) * 128)
                    for bi, nm in enumerate(["aa", "A", "negm", "prev", "lastr"]):
                        src = NEGAX[:, nseq + c * 128:nseq + (c + 1) * 128] if nm == "A" else rg(nm)[:, cs_]
                        S.op(PE_, lambda e, bi=bi, src=src: e.transpose(out=PS[3][:, bi * 4:(bi + 1) * 4], in_=src, identity=ident[0:4, 0:4]),
                             r=[kg(nm) if nm != "A" else kNX, kC], w=[PK[3]])
                    S.op(V, lambda e: e.tensor_copy(out=COLS, in_=PS[3][:, 0:20]), r=[PK[3]], w=[kCOLS])
                    for h in range(4):
                        ca = COLS[:, 0 + h:1 + h]; cnA = COLS[:, 4 + h:5 + h]; cnm = COLS[:, 8 + h:9 + h]
                        cpv = COLS[:, 12 + h:13 + h]; cla = COLS[:, 16 + h:17 + h]
                        if sample:
                            S.op(PE_, lambda e, h=h: e.matmul(PS[2][:, 0:16], sel4[:, h, :], NEGAX[:, 0:16], start=True, stop=False, skip_group_check=True), r=[kNX, kC], w=[PK[2]], inc=False)
                            S.op(PE_, lambda e, h=h: e.matmul(PS[2][:, 16:144], sel4[:, h, :], NEGAX[:, 16:144], start=False, stop=True, skip_group_check=True), r=[kNX, kC], w=[PK[2]])
                        else:
                            S.op(PE_, lambda e, h=h, c=c: e.matmul(PS[2][:, 0:129], sel4[:, h, :], NEGAX[:, c * 128:c * 128 + 129], start=True, stop=True), r=[kNX, kC], w=[PK[2]])
                        S.op(PE_, lambda e, h=h, c=c: e.matmul(PS[3][:, 0:128], sel4[:, h, :], NEGAX[:, nseq + c * 128:nseq + (c + 1) * 128], start=True, stop=False), r=[kNX, kC, kCOLS], w=[PK[3]], inc=False)
                        S.op(PE_, lambda e: e.matmul(PS[3][:, 0:128], identb[:, :], maskb[:, :], start=False, stop=True), r=[kC], w=[PK[3]])
                        S.op(PE_, lambda e, h=h, cs_=cs_: e.matmul(PS[4][:, 0:128], kTt[:, h, cs_], qT[:, h, cs_], start=True, stop=True), r=[kkT, kqT], w=[PK[4]])
                        S.op(A_, lambda e: e.copy(out=SA, in_=PS[2][:, 0:nseq + 128]), r=[PK[2]], w=[kSA])
                        S.op(A_, lambda e, ca=ca: e.activation(out=WT, in_=PS[3][:, 0:128], func=AF.Exp, bias=ca, scale=1.0), r=[PK[3], kCOLS], w=[kWT])
                        S.op(V, lambda e: e.tensor_tensor(out=PT, in0=PS[4][:, 0:128], in1=WT, op=ALU.mult), r=[PK[4], kWT], w=[kPT])
                        S.op(PE_, lambda e, c=c, h=h: e.matmul(PS[6][:, 0:129], PT, VT[:, c * 4 + h, :], start=True, stop=True), r=[kPT, kvt], w=[PK[6]])
                        S.op(V, lambda e, h=h, cs_=cs_: e.tensor_copy(out=qpd, in_=qT[:, h, cs_].rearrange("p (s l) -> p s l", l=L)), r=[kqT], w=[kQP])
                        for s_ in range(nseq):
                            S.op(PE_, lambda e, s_=s_, h=h: e.matmul(PS[5][:, 0:129], QPAD[:, s_ * 128:(s_ + 1) * 128], csv(s_, h), start=(s_ == 0), stop=(s_ == nseq - 1)),
                                 r=[kQP, kCS], w=[PK[5]], inc=(s_ == nseq - 1))
                        S.op(A_, lambda e, cnA=cnA, cpv=cpv: e.activation(out=SM[:, 0:1], in_=cnA, func=AF.Exp, bias=cpv, scale=1.0), r=[kCOLS], w=[(kSM, 0)])
                        S.op(A_, lambda e, ca=ca, cla=cla: e.activation(out=SM[:, 1:2], in_=ca, func=AF.Exp, bias=cla, scale=1.0), r=[kCOLS], w=[(kSM, 1)])
                        S.op(A_, lambda e, cnm=cnm: e.activation(out=SM[:, 2:3], in_=cnm, func=AF.Exp), r=[kCOLS], w=[(kSM, 2)])
                        S.op(A_, lambda e: e.copy(out=PVs, in_=PS[6][:, 0:129]), r=[PK[6]], w=[kPVs])
                        S.op(V, lambda e: e.scalar_tensor_tensor(out=NUM, in0=PS[5][:, 0:129], scalar=SM[:, 0:1], in1=PVs, op0=ALU.mult, op1=ALU.add), r=[PK[5], (kSM, 0), kPVs], w=[kNUM])
                        S.op(A_, lambda e: e.activation(out=SM[:, 3:4], in_=NUM[:, 128:129], func=AF.Abs), r=[kNUM], w=[(kSM, 3)])
                        S.op(V, lambda e: e.tensor_tensor(out=SM[:, 3:4], in0=SM[:, 3:4], in1=SM[:, 2:3], op=ALU.max), r=[(kSM, 3), (kSM, 2)], w=[(kSM, 3)])
                        S.op(V, lambda e: e.reciprocal(out=SM[:, 4:5], in_=SM[:, 3:4]), r=[(kSM, 3)], w=[(kSM, 4)])
                        S.op(A_, lambda e: e.activation(out=JK, in_=NUM[:, 0:128], func=AF.Square, scale=SM[:, 4:5], accum_out=SM[:, 5:6]), r=[kNUM, (kSM, 4)], w=[kJK, (kSM, 5)])
                        S.op(A_, lambda e: e.activation(out=SM[:, 6:7], in_=SM[:, 5:6], func=AF.Sqrt, bias=EPS, scale=1.0 / 128), r=[(kSM, 5)], w=[(kSM, 6)])
                        S.op(V, lambda e: e.reciprocal(out=SM[:, 6:7], in_=SM[:, 6:7]), r=[(kSM, 6)], w=[(kSM, 6)])
                        S.op(V, lambda e: e.tensor_tensor(out=SM[:, 7:8], in0=SM[:, 6:7], in1=SM[:, 4:5], op=ALU.mult), r=[(kSM, 6), (kSM, 4)], w=[(kSM, 7)])
                        S.op(V, lambda e: e.tensor_scalar(out=HN, in0=NUM[:, 0:128], scalar1=SM[:, 7:8], scalar2=None, op0=ALU.mult), r=[kNUM, (kSM, 7)], w=[kHN])
                        S.op(PE_, lambda e: e.transpose(out=psb(7)[:, 0:128], in_=HN, identity=identb[:, :]), r=[kHN, kC], w=[PK[7]])
                        S.op(V, lambda e, h=h, cs_=cs_: e.scalar_tensor_tensor(out=hmT[:, h, cs_], in0=psb(7)[:, 0:128], scalar=aon[:, i, h:h + 1], in1=sgT[:, h, cs_], op0=ALU.mult, op1=ALU.mult),
                             r=[PK[7], kC, ksg], w=[khm])
                        S.op(V, lambda e: e.tensor_scalar(out=WSI[:, 0:nseq], in0=onehot[:, 0:nseq] if sample else onesb[:, 0:1], scalar1=SM[:, 1:2], scalar2=None, op0=ALU.mult), r=[(kSM, 1), kC, kC2], w=[kWSI])
                        sa_last = SA[:, nseq:nseq + 128].rearrange("p (s l) -> p s l", l=L)[:, :, L - 1]
                        S.op(V, lambda e, sa_last=sa_last: e.tensor_tensor(out=WCB[:, 0:nseq], in0=sa_last, in1=SA[:, 0:nseq], op=ALU.subtract), r=[kSA], w=[kWCB])
                        S.op(A_, lambda e: e.activation(out=WCB[:, 0:nseq], in_=WCB[:, 0:nseq], func=AF.Exp), r=[kWCB], w=[kWCB])
                        for s_ in range(nseq):
                            S.op(V, lambda e, s_=s_, c=c, h=h: e.tensor_scalar(out=VS, in0=VT[:, c * 4 + h, :], scalar1=WSI[:, s_:s_ + 1], scalar2=None, op0=ALU.mult), r=[kvt, kWSI], w=[kVS])
                            S.op(PE_, lambda e, c=c, h=h: e.matmul(PS[7][:, 0:129], KTOK[:, c, h * 128:(h + 1) * 128], VS, start=True, stop=True), r=[kkt, kVS], w=[PK[7]])
                            S.op(V, lambda e, s_=s_, h=h: e.scalar_tensor_tensor(out=csv(s_, h), in0=csv(s_, h), scalar=WCB[:, s_:s_ + 1], in1=PS[7][:, 0:129], op0=ALU.mult, op1=ALU.add),
                                 r=[PK[7], kWCB, kCS], w=[kCS])
                if last:
                    if sample:
                        S.dma('sync', O["sC"][i].rearrange("s h k v -> k (s h) v"), CS_[:, :, 0:128], r=[kCS], chan=('osC',))
                        fm_rows_out(lambda c: CS_[:, :, 128], 64, lambda o, n: O["sn"][i, :, o:o + n], 1, [kCS])
                    else:
                        S.dma('sync', O["pC"][i].rearrange("h k v -> k h v"), Cst[:, i, :, 0:128], r=[kCS], chan=('opC',))
                        fm_rows_out(lambda c: Cst[:, i, :, 128], 4, lambda o, n: O["pn"][i, :, o:o + n], 1, [kCS])
                for d in range(8):
                    b = d % 2
                    for j in range(8):
                        rhs = hmT[:, j, :] if j < 4 else scTt[:, j - 4, :]
                        S.op(PE_, lambda e, b=b, j=j, d=d, rhs=rhs: e.matmul(PS[b][:, 0:T], wout[:, j, d * 128:(d + 1) * 128], rhs, start=(j == 0), stop=(j == 7)),
                             r=kW + [khm, ksc], w=[PK[b]], inc=(j == 7))
                    resid_add(l, 2, d, PS[b][:, 0:T], PK[b], rt)

            def mixer_c(l):
                jl = l // 2
                norm_mod(l, 1)
                S.barrier(); AR.reset()
                wq = WB[:, 0:8 * 1536].rearrange("p (k c) -> p k c", c=1536)
                wo = WB[:, 8 * 1536:8 * 1536 + 8 * D].rearrange("p (k c) -> p k c", c=D)
                kW = [('W', 0), ('W', 1), ('W', 2)]
                for kc in range(8):
                    for b3 in range(3):
                        S.dma('pool', wq[:, kc, b3 * 512:(b3 + 1) * 512], I["c_w_qkv"][jl, kc * 128:(kc + 1) * 128, b3 * 512:(b3 + 1) * 512], w=kW, chan=('W', 0))
                S.dma('pool', wo, I["c_w_out"][jl].rearrange("(k p) c -> p k c", p=128), w=kW, chan=('W', 0), max_dma_last_dim=4096)
                QK, kQK = AR.get("QK", [128, 1280])
                SQ, kSQ = AR.get("SQ", [128, 1280])
                QR, kQR = AR.get("QR", [128, 1280])
                QRb, kQRb = AR.get("QRb", [128, 1280], BF16)
                T1, kT1 = AR.get("T1", [128, 640])
                T2, kT2 = AR.get("T2", [128, 640])
                VV, kVV = AR.get("VV", [128, 256])
                SS, kSS = AR.get("SS", [128, 20])
                QT, kQT = AR.get("QTz", [128, 16, T], BF16)
                S.op(V, lambda e: e.memset(QT, 0.0), w=[kQT])
                QT5 = QT.rearrange("p (a b g) t -> p a b g t", a=2, b=2)
                KT, kKT = AR.get("KT", [128, 2, T], BF16)
                VTa, kVTa = AR.get("VTa", [128, nch * 4, 65], BF16)
                PTb = [AR.get("PTb%d" % z, [128, 512], BF16) for z in range(2)]
                DEN, kDEN = AR.get("DEN", [128, 4])
                OTOK, kOT = AR.get("OTOK", [128, 16, 64], BF16)
                OTT, kOTT = AR.get("OTT", [128, 8, T], BF16)
                rt = AR.get("rt", [128, T])
                if sample:
                    CKb, kCKb = AR.get("CKb", [128, 16, 256], BF16)
                    KCT, kKCT = AR.get("KCT", [128, 32, 128], BF16)
                    VCs, kVCs = AR.get("VCs", [128, 64, 65], BF16)
                    S.op(V, lambda e: e.memset(VCs, 1.0), w=[kVCs])
                    S.dma('pool', CKb, I["ck"][jl].rearrange("s p c -> p s c"), w=[kCKb], max_dma_last_dim=1024)
                    for s_ in range(16):
                        S.dma('pool', VCs[:, s_ * 4:(s_ + 1) * 4, 0:64], I["cv"][jl, s_].rearrange("p (h d) -> p h d", d=64), w=[kVCs], max_dma_last_dim=256)
                    for s_ in range(16):
                        for j in range(2):
                            S.op(PE_, lambda e, s_=s_, j=j: e.transpose(out=psb(3)[:, j * 128:(j + 1) * 128], in_=CKb[:, s_, j * 128:(j + 1) * 128], identity=identb[:, :]), r=[kCKb, kC], w=[PK[3]])
                        S.op(V, lambda e, s_=s_: e.tensor_copy(out=KCT[:, 2 * s_:2 * s_ + 2, :], in_=psb(3)[:, 0:256].rearrange("p (a b) -> p a b", b=128)), r=[PK[3]], w=[kKCT])
                    S.dma('sync', O["swk"][jl, :, 0:120, :], I["ck"][jl, :, 8:128, :], chan=('dd',))
                    S.dma('sync', O["swv"][jl, :, 0:120, :], I["cv"][jl, :, 8:128, :], chan=('dd',))
                S.op(V, lambda e: e.memset(VTa, 1.0), w=[kVTa])
                if int(os.environ.get('KD_C', 9)) < 1:
                    return
                qk3 = lambda ap: ap.rearrange("p (h d) -> p h d", d=64)
                for c in range(nch):
                    cs_ = slice(c * 128, (c + 1) * 128)
                    for b in range(3):
                        for kc in range(8):
                            S.op(PE_, lambda e, b=b, kc=kc, cs_=cs_: e.matmul(PS[b][:, 0:512], hT[:, kc, cs_], wq[:, kc, b * 512:(b + 1) * 512], start=(kc == 0), stop=(kc == 7)),
                                 r=kW + [kH], w=[PK[b]], inc=(kc == 7))
                    S.op(A_, lambda e: e.copy(out=QK[:, 0:512], in_=PS[0][:, 0:512]), r=[PK[0]], w=[kQK])
                    S.op(A_, lambda e: e.copy(out=QK[:, 512:1024], in_=PS[1][:, 0:512]), r=[PK[1]], w=[kQK])
                    S.op(A_, lambda e: e.copy(out=QK[:, 1024:1280], in_=PS[2][:, 0:256]), r=[PK[2]], w=[kQK])
                    if os.environ.get('KD_X', '') == 'a':
                        continue
                    S.op(A_, lambda e: e.copy(out=VV, in_=PS[2][:, 256:512]), r=[PK[2]], w=[kVV])
                    if os.environ.get('KD_X', '') == 'b':
                        continue
                    S.op(V, lambda e, c=c: e.tensor_copy(out=VTa[:, c * 4:(c + 1) * 4, 0:64], in_=VV.rearrange("p (h d) -> p h d", d=64)), r=[kVV], w=[kVTa])
                    if int(os.environ.get('KD_C', 9)) < 2:
                        continue
                    S.op(A_, lambda e: e.activation(out=SQ, in_=QK, func=AF.Square), r=[kQK], w=[kSQ])
                    S.op(V, lambda e: e.tensor_reduce(out=SS, in_=qk3(SQ), axis=AX.X, op=ALU.add), r=[kSQ], w=[kSS])
                    S.op(A_, lambda e: e.activation(out=SS, in_=SS, func=AF.Sqrt, bias=EPS, scale=1.0 / 64), r=[kSS], w=[kSS])
                    S.op(V, lambda e: e.reciprocal(out=SS, in_=SS), r=[kSS], w=[kSS])
                    S.op(V, lambda e: e.tensor_tensor(out=qk3(QK), in0=qk3(QK), in1=SS.unsqueeze(2).broadcast_to([128, 20, 64]), op=ALU.mult), r=[kSS, kQK], w=[kQK])
                    S.op(V, lambda e: e.tensor_tensor(out=qk3(QK[:, 0:1024]), in0=qk3(QK[:, 0:1024]), in1=GQ[:, jl, :].unsqueeze(1).broadcast_to([128, 16, 64]), op=ALU.mult), r=[kC, kQK], w=[kQK])
                    S.op(V, lambda e: e.tensor_tensor(out=qk3(QK[:, 1024:1280]), in0=qk3(QK[:, 1024:1280]), in1=GK[:, jl, :].unsqueeze(1).broadcast_to([128, 4, 64]), op=ALU.mult), r=[kC, kQK], w=[kQK])
                    if int(os.environ.get('KD_C', 9)) < 3:
                        continue
                    cosb = cosT[:, c, :].unsqueeze(1).broadcast_to([128, 20, 32])
                    sinb = sinT[:, c, :].unsqueeze(1).broadcast_to([128, 20, 32])
                    x1 = qk3(QK)[:, :, 0:32]; x2 = qk3(QK)[:, :, 32:64]
                    t1 = T1.rearrange("p (h d) -> p h d", d=32); t2 = T2.rearrange("p (h d) -> p h d", d=32)
                    S.op(V, lambda e, x1=x1, cosb=cosb: e.tensor_tensor(out=t1, in0=x1, in1=cosb, op=ALU.mult), r=[kQK, ('rope',)], w=[kT1])
                    S.op(V, lambda e, x2=x2, sinb=sinb: e.tensor_tensor(out=t2, in0=x2, in1=sinb, op=ALU.mult), r=[kQK, ('rope',)], w=[kT2])
                    S.op(V, lambda e: e.tensor_tensor(out=qk3(QR)[:, :, 0:32], in0=t1, in1=t2, op=ALU.subtract), r=[kT1, kT2], w=[kQR])
                    S.op(V, lambda e, x2=x2, cosb=cosb: e.tensor_tensor(out=t1, in0=x2, in1=cosb, op=ALU.mult), r=[kQK, ('rope',)], w=[kT1])
                    S.op(V, lambda e, x1=x1, sinb=sinb: e.tensor_tensor(out=t2, in0=x1, in1=sinb, op=ALU.mult), r=[kQK, ('rope',)], w=[kT2])
                    S.op(V, lambda e: e.tensor_tensor(out=qk3(QR)[:, :, 32:64], in0=t1, in1=t2, op=ALU.add), r=[kT1, kT2], w=[kQR])
                    S.op(A_, lambda e: e.copy(out=QRb, in_=QR), r=[kQR], w=[kQRb])
                    if int(os.environ.get('KD_C', 9)) < 4:
                        continue
                    for j in range(8):
                        S.op(PE_, lambda e, j=j: e.transpose(out=psb(3)[:, j * 128:(j + 1) * 128], in_=QRb[:, j * 128:(j + 1) * 128], identity=identb[:, :]), r=[kQRb, kC], w=[PK[3]])
                    S.op(V, lambda e, cs_=cs_: e.tensor_copy(out=QT5[0:64, :, 0, :, cs_], in_=psb(3)[0:64, 0:1024].rearrange("p (a g t) -> p a g t", a=2, g=4)), r=[PK[3]], w=[kQT])
                    S.op(V, lambda e, cs_=cs_: e.tensor_copy(out=QT5[64:128, :, 1, :, cs_], in_=psb(3)[64:128, 0:1024].rearrange("p (a g t) -> p a g t", a=2, g=4)), r=[PK[3]], w=[kQT])
                    for j in range(2):
                        S.op(PE_, lambda e, j=j: e.transpose(out=psb(4)[:, j * 128:(j + 1) * 128], in_=QRb[:, 1024 + j * 128:1024 + (j + 1) * 128], identity=identb[:, :]), r=[kQRb, kC], w=[PK[4]])
                    S.op(V, lambda e, cs_=cs_: e.tensor_copy(out=KT[:, :, cs_], in_=psb(4)[:, 0:256].rearrange("p (a b) -> p a b", b=128)), r=[PK[4]], w=[kKT])
                    if sample:
                        for s_ in range(16):
                            S.dma('sync', O["swk"][jl, s_, 120:128, :], QR[s_ * 8:(s_ + 1) * 8, 1024:1280], r=[kQR], chan=('oswk',))
                            S.dma('sync', O["swv"][jl, s_, 120:128, :], VV[s_ * 8:(s_ + 1) * 8, :], r=[kVV], chan=('oswv',))
                    elif last and c == nch - 1:
                        S.dma('sync', O["pwk"][jl], QR[:, 1024:1280], r=[kQR], chan=('opwk',))
                        S.dma('sync', O["pwv"][jl], VV, r=[kVV], chan=('opwv',))
                    if int(os.environ.get('KD_C', 9)) < 5:
                        continue
                    for kap in range(4):
                        base = 64 * (kap % 2)
                        pr = kap // 2
                        blocks = []
                        if sample:
                            for s_ in range(16):
                                blocks.append((KCT[:, 2 * s_ + pr, :], VCs[:, s_ * 4 + kap, :],
                                               acacheb[:, s_, :].unsqueeze(1).broadcast_to([128, 4, 128]), [kKCT], [kVCs]))
                            blocks.append((KT[:, pr, cs_], VTa[:, c * 4 + kap, :], anewb[:, :].rearrange("p (g t) -> p g t", t=128), [kKT], [kVTa]))
                        else:
                            blocks.append((KT[:, pr, cs_], VTa[:, c * 4 + kap, :], acurb[:, :].rearrange("p (g t) -> p g t", t=128), [kKT], [kVTa]))
                            if c > 0:
                                ps_ = slice((c - 1) * 128, c * 128)
                                blocks.append((KT[:, pr, ps_], VTa[:, (c - 1) * 4 + kap, :], aprevb[:, :].rearrange("p (g t) -> p g t", t=128), [kKT], [kVTa]))
                            elif pi > 0:
                                blocks.append((KTC[:, jl, pr, :], VC[:, jl, kap, :], aprevb[:, :].rearrange("p (g t) -> p g t", t=128), [('KTC',)], [('VC',)]))
                        qrhs = QT[:, 4 * kap:4 * kap + 4, cs_]
                        for bi, (kb, vb, mb, kr, vr) in enumerate(blocks):
                            pt, kpt = PTb[bi % 2]
                            ps5 = PS[5][:, 0:512].rearrange("p (g t) -> p g t", t=128)
                            S.op(PE_, lambda e, kb=kb, qrhs=qrhs, ps5=ps5: e.matmul(ps5, kb, qrhs, start=True, stop=False), r=kr + [kQT], w=[PK[5]], inc=False)
                            S.op(PE_, lambda e, mb=mb, ps5=ps5: e.matmul(ps5, identb[:, :], mb, start=False, stop=True), r=[kC], w=[PK[5]])
                            S.op(A_, lambda e, pt=pt: e.activation(out=pt, in_=PS[5][:, 0:512], func=AF.Exp, bias=NEGMA[:, jl:jl + 1], scale=0.125), r=[PK[5], kC2], w=[kpt])
                            for g in range(4):
                                S.op(PE_, lambda e, g=g, pt=pt, vb=vb, bi=bi, nb=len(blocks): e.matmul(PS[6][:, g * 65:(g + 1) * 65], pt[:, g * 128:(g + 1) * 128], vb,
                                                                                    start=(bi == 0 and g == 0), stop=(bi == nb - 1), skip_group_check=True),
                                     r=[kpt] + vr, w=[PK[6]], inc=(g == 3))
                        s0 = 8 * pr + (kap % 2)
                        o3 = PS[6][:, 0:260].rearrange("p (g d) -> p g d", d=65)
                        S.op(V, lambda e, s0=s0, o3=o3: e.tensor_tensor(out=DEN, in0=o3[:, :, 64], in1=SINKE[:, jl, s0:s0 + 7:2], op=ALU.add), r=[PK[6], ('SINKE',)], w=[kDEN])
                        S.op(V, lambda e: e.reciprocal(out=DEN, in_=DEN), r=[kDEN], w=[kDEN])
                        S.op(V, lambda e, s0=s0, o3=o3: e.tensor_tensor(out=OTOK[:, s0:s0 + 7:2, :], in0=o3[:, :, 0:64], in1=DEN.unsqueeze(2).broadcast_to([128, 4, 64]), op=ALU.mult),
                             r=[PK[6], kDEN], w=[kOT])
                    if int(os.environ.get('KD_C', 9)) < 6:
                        continue
                    otf = OTOK.rearrange("p h d -> p (h d)")
                    for j in range(8):
                        S.op(PE_, lambda e, j=j: e.transpose(out=psb(7)[:, j * 128:(j + 1) * 128], in_=otf[:, j * 128:(j + 1) * 128], identity=identb[:, :]), r=[kOT, kC], w=[PK[7]])
                    S.op(V, lambda e, cs_=cs_: e.tensor_copy(out=OTT[:, :, cs_], in_=psb(7)[:, 0:1024].rearrange("p (a b) -> p a b", b=128)), r=[PK[7]], w=[kOTT])
                if int(os.environ.get('KD_C', 9)) < 7:
                    return
                if not sample:
                    ls_ = slice((nch - 1) * 128, nch * 128)
                    S.op(V, lambda e: e.tensor_copy(out=KTC[:, jl, :, :], in_=KT[:, :, ls_]), r=[kKT], w=[('KTC',)])
                    S.op(V, lambda e: e.tensor_copy(out=VC[:, jl, :, :], in_=VTa[:, (nch - 1) * 4:nch * 4, :]), r=[kVTa], w=[('VC',)])
                for d in range(8):
                    b = d % 2
                    for j in range(8):
                        S.op(PE_, lambda e, b=b, j=j, d=d: e.matmul(PS[b][:, 0:T], wo[:, j, d * 128:(d + 1) * 128], OTT[:, j, :], start=(j == 0), stop=(j == 7)),
                             r=kW + [kOTT], w=[PK[b]], inc=(j == 7))
                    resid_add(l, 2, d, PS[b][:, 0:T], PK[b], rt)

            for l in range(4):
                if str(l) not in os.environ.get("KD_LAYERS", "0123"):
                    continue
                if "m" in os.environ.get("KD_PARTS", "mf"):
                    if l % 2 == 0:
                        mixer_ab(l)
                    else:
                        mixer_c(l)
                if "f" in os.environ.get("KD_PARTS", "mf"):
                    ffn(l)
            S.barrier()
            for c in range(nch):
                for b in range(2):
                    for j in range(4):
                        kc = b * 4 + j
                        S.op(PE_, lambda e, b=b, j=j, kc=kc, c=c: e.transpose(out=PS[b][:, j * 128:(j + 1) * 128], in_=xT[:, kc, c * 128:(c + 1) * 128], identity=ident[:, :]),
                             r=[kX, kC], w=[PK[b]])
                    S.op(A_, lambda e, b=b: e.copy(out=stage[:, b * 512:(b + 1) * 512], in_=PS[b][:, :]), r=[PK[b]], w=[('stage',)])
                S.dma('sync', yout[tok0 + c * 128: tok0 + (c + 1) * 128, :], stage[:, :], r=[('stage',)], chan=('ostage',))

        for pi in range(int(os.environ.get("KD_NPASS", NPASS))):
            run_pass(False, pi)
        if os.environ.get("KD_SAMPLE", "1") == "1":
            run_pass(True, 0)
        S.finish()

        with nc.Block() as block:
            @block.sync
            def _(e):
                S.replay('sync', e)

            @block.tensor
            def _(e):
                S.replay('pe', e)

            @block.scalar
            def _(e):
                S.replay('act', e)

            @block.vector
            def _(e):
                S.replay('dve', e)

            @block.gpsimd
            def _(e):
                S.replay('pool', e)
    return nc


def _slot_perm():
    perm = np.zeros(16, np.int64)
    for kap in range(4):
        for g in range(4):
            s = 2 * (g + 4 * (kap // 2)) + (kap % 2)
            perm[s] = 4 * kap + g
    return perm


def _consts():
    c = {}
    c["ident"] = np.eye(128, dtype=np.float32)
    s = np.arange(128)[:, None]
    t = np.arange(128)[None, :]
    c["mP"] = np.where(s <= t, 0.0, NEG).astype(np.float32)
    same = (s // 8) == (t // 8)
    c["mS"] = np.where(same & (s <= t), 0.0, NEG).astype(np.float32)
    c["acur"] = np.tile(np.where(s <= t, 0.0, ANEG).astype(np.float32), (1, 4))
    c["aprev"] = np.tile(np.where(s > t, 0.0, ANEG).astype(np.float32), (1, 4))
    c["anew"] = np.tile(np.where(same & (s <= t), 0.0, ANEG).astype(np.float32), (1, 4))
    ac = np.full((128, 16, 128), ANEG, np.float32)
    for i in range(16):
        for j in range(8):
            tt = 8 * i + j
            ac[j + 1:, i, tt] = 0.0
    c["acache"] = ac
    oh = np.zeros((128, 16), np.float32)
    oh[np.arange(128), np.arange(128) // 8] = 1.0
    c["onehot"] = oh
    sel = np.zeros((4, 4, 128), np.float32)
    for h in range(4):
        sel[h, h, :] = 1.0
    c["sel4"] = sel
    inv = 10000.0 ** (-np.arange(32, dtype=np.float64) / 32)
    pos = np.arange(SEQ, dtype=np.float64)[:, None] * inv[None, :]
    c["cosP"] = np.cos(pos).astype(np.float32)
    c["sinP"] = np.sin(pos).astype(np.float32)
    ps = (8192 + (np.arange(128) % 8)).astype(np.float64)[:, None] * inv[None, :]
    c["cosS"] = np.cos(ps).astype(np.float32)
    c["sinS"] = np.sin(ps).astype(np.float32)
    return c


_NC = None


def kernel(x_prompt, x_sample, c_prompt, c_sample, state_mlstm_C, state_mlstm_n, state_mlstm_m,
           state_sconv, cache_win_k, cache_win_v, state_ffn_conv,
           norm1, norm2, w_ada, b_ada, a_w_in, a_b_if, a_out_norm, a_conv_w, a_w_out,
           c_w_qkv, c_q_norm, c_k_norm, c_sink, c_w_out, f_w_up, f_conv_w, f_w_down):
    global _NC
    f = lambda a: np.ascontiguousarray(np.asarray(a, dtype=np.float32))
    perm = _slot_perm()
    common = dict(_consts())
    common["w_ada"] = f(w_ada)
    common["b_adaT"] = f(np.asarray(b_ada).reshape(4, 48, 128).transpose(2, 0, 1))
    common["norm1T"] = f(np.asarray(norm1).reshape(4, 8, 128).transpose(2, 0, 1))
    common["norm2T"] = f(np.asarray(norm2).reshape(4, 8, 128).transpose(2, 0, 1))
    common["a_w_in"] = f(a_w_in)
    common["a_bif"] = f(np.asarray(a_b_if).reshape(2, 2, 4).transpose(2, 0, 1))
    common["a_onT"] = f(np.asarray(a_out_norm).reshape(2, 4, 128).transpose(2, 0, 1))
    common["a_cwT"] = f(np.asarray(a_conv_w).reshape(2, 3, 4, 128).transpose(3, 0, 1, 2))
    common["a_w_out"] = f(a_w_out)
    wq = np.asarray(c_w_qkv)
    qcols = np.concatenate([np.arange(64) + 64 * h for h in perm])
    common["c_w_qkv"] = f(np.concatenate([wq[:, :, qcols], wq[:, :, 1024:]], axis=2))
    common["c_qn"] = f(c_q_norm)
    common["c_kn"] = f(c_k_norm)
    common["c_sink"] = f(np.asarray(c_sink)[:, perm])
    common["c_w_out"] = f(np.asarray(c_w_out)[:, qcols, :])
    common["f_w_up"] = f(f_w_up)
    common["f_cwT"] = f(np.asarray(f_conv_w).reshape(4, 3, 44, 128).transpose(3, 0, 1, 2))
    common["f_w_down"] = f(f_w_down)
    in_maps = []
    for c in range(NCORE):
        b = c // 4
        sl = slice(16 * c, 16 * c + 16)
        m = dict(common)
        m["xp"] = f(np.asarray(x_prompt)[b])
        m["xs"] = f(np.asarray(x_sample)[sl].reshape(128, D))
        m["call"] = f(np.concatenate([np.asarray(c_prompt)[b:b + 1], np.asarray(c_sample)[sl]], axis=0))
        m["stC"] = f(np.asarray(state_mlstm_C)[:, sl])
        m["stn"] = f(np.asarray(state_mlstm_n)[:, sl].reshape(2, 64, 128))
        m["stmT"] = f(np.asarray(state_mlstm_m)[:, sl].transpose(2, 0, 1))
        m["stsc"] = f(np.asarray(state_sconv)[:, sl].reshape(2, 32, 512))
        m["ck"] = f(np.asarray(cache_win_k)[:, sl].reshape(2, 16, 128, 256))
        m["cv"] = f(np.asarray(cache_win_v)[:, sl].reshape(2, 16, 128, 256))
        m["stffn"] = f(np.asarray(state_ffn_conv)[:, sl].reshape(4, 32, 5632))
        in_maps.append(m)
    if _NC is None:
        _NC = build()
    res = run_bass_kernel_spmd(_NC, in_maps, core_ids=list(range(NCORE)))
    R = res.results
    pc = [0, 4]
    cat_p = lambda nm, shp: np.stack([R[c][nm] for c in pc], axis=1).reshape(shp)
    y_prompt = np.stack([R[c]["yp"] for c in pc], axis=0)
    y_sample = np.concatenate([R[c]["ys"].reshape(16, 8, D) for c in range(NCORE)], axis=0)
    p_C = cat_p("pC", (2, 2, 4, 128, 128))
    p_n = cat_p("pn", (2, 2, 4, 128))
    p_m = cat_p("pm", (2, 2, 4))
    p_sc = cat_p("psc", (2, 2, 2, 512))
    p_wk = cat_p("pwk", (2, 2, 128, 4, 64))
    p_wv = cat_p("pwv", (2, 2, 128, 4, 64))
    p_ffn = cat_p("pffn", (4, 2, 2, 5632))
    cat_s = lambda nm, shp: np.concatenate([R[c][nm].reshape(shp) for c in range(NCORE)], axis=1)
    s_C = cat_s("sC", (2, 16, 4, 128, 128))
    s_n = cat_s("sn", (2, 16, 4, 128))
    s_m = np.concatenate([R[c]["smT"].transpose(1, 2, 0) for c in range(NCORE)], axis=1)
    s_sc = cat_s("ssc", (2, 16, 2, 512))
    s_wk = cat_s("swk", (2, 16, 128, 4, 64))
    s_wv = cat_s("swv", (2, 16, 128, 4, 64))
    s_ffn = cat_s("sffn", (4, 16, 2, 5632))
    outs = (y_prompt, y_sample, p_C, p_n, p_m, p_sc, p_wk, p_wv, p_ffn,
            s_C, s_n, s_m, s_sc, s_wk, s_wv, s_ffn)
    return tuple(np.ascontiguousarray(o, dtype=np.float32) for o in outs)
```

```python
import contextlib
import os
import numpy as np
import concourse.bass as bass
import concourse.mybir as mybir
from concourse.bass_utils import run_bass_kernel_spmd

F32 = mybir.dt.float32
BF16 = mybir.dt.bfloat16
AF = mybir.ActivationFunctionType
ALU = mybir.AluOpType
AX = mybir.AxisListType

D = 1024
KC = 8
DFF = 2816
NF = 22
INA = 3592
SEQ = 8192
TP = 512
NPASS = SEQ // TP
EPS = 1e-6
NEG = -1.0e30
ANEG = -1.0e9
NCORE = 8
SLOT = 13312


class Sch:
    ROT = 30000

    def __init__(self, nc, stack):
        self.nc = nc
        self.stack = stack
        self.names = ('pe', 'act', 'dve', 'pool', 'sync')
        self.prog = {e: [] for e in self.names}
        self.cnt = {e: 0 for e in ('pe', 'act', 'dve', 'pool')}
        self.sems = {}
        self.seen = {e: {} for e in self.names}
        self.lastw = {}
        self.readers = {}
        self.dtot = {}

    def sem(self, key):
        if key not in self.sems:
            nm = "s" + str(len(self.sems))
            self.sems[key] = self.stack.enter_context(self.nc.semaphore(nm))
        return self.sems[key]

    def _deps(self, r, w):
        ev = []
        for k in r:
            if k in self.lastw:
                ev.append(self.lastw[k])
            if k[0] == 'P':
                ev.extend(self.readers.get(k, []))
        for k in w:
            if k in self.lastw:
                ev.append(self.lastw[k])
            ev.extend(self.readers.get(k, []))
        return ev

    def _wait(self, en, evs, skip=None):
        need = {}
        for (sk, v) in evs:
            if skip is not None and sk == skip:
                continue
            if en == 'pe' and sk[0] == 'pe':
                continue
            if need.get(sk, 0) < v:
                need[sk] = v
        for sk, v in need.items():
            if self.seen[en].get(sk, 0) < v:
                self.prog[en].append(('w', self.sem(sk), v))
                self.seen[en][sk] = v

    def _commit(self, ev, r, w):
        for k in r:
            self.readers.setdefault(k, []).append(ev)
        for k in w:
            self.lastw[k] = ev
            self.readers[k] = []

    def op(self, en, fn, r=(), w=(), inc=True):
        self._wait(en, self._deps(r, w))
        c = self.cnt[en] + 1
        sk = (en, (c - 1) // self.ROT)
        v = (c - 1) % self.ROT + 1
        if inc:
            self.cnt[en] = c
            self.prog[en].append(('i', fn, self.sem(sk), 1))
        else:
            self.prog[en].append(('i', fn, None, 0))
        self._commit((sk, v), r, w)

    def dma(self, q, out, in_, r=(), w=(), chan=None, **kw):
        if chan is None:
            chan = w[0] if len(w) else ('o',) + tuple(r[0])
        sk = ('d', chan)
        self._wait(q, self._deps(r, w), skip=sk)
        self.dtot[chan] = self.dtot.get(chan, 0) + 16
        fn = (lambda e, out=out, in_=in_, kw=kw: e.dma_start(out=out, in_=in_, allow_slow_non_contiguous=True, **kw))
        self.prog[q].append(('i', fn, self.sem(sk), 16))
        self._commit((sk, self.dtot[chan]), r, w)

    def barrier(self):
        evs = []
        for e in ('pe', 'act', 'dve', 'pool'):
            c = self.cnt[e]
            if c > 0:
                evs.append(((e, (c - 1) // self.ROT), (c - 1) % self.ROT + 1))
        for ch, t in self.dtot.items():
            if ch[0] == 'W':
                continue
            evs.append((('d', ch), t))
        for e in self.names:
            need = [x for x in evs if not (x[0][0] == e)]
            for (sk, v) in need:
                if self.seen[e].get(sk, 0) < v:
                    self.prog[e].append(('w', self.sem(sk), v))
                    self.seen[e][sk] = v

    def finish(self):
        evs = []
        for ch, t in self.dtot.items():
            evs.append((('d', ch), t))
        for e in ('pe', 'act', 'dve', 'pool'):
            c = self.cnt[e]
            if c > 0:
                evs.append(((e, (c - 1) // self.ROT), (c - 1) % self.ROT + 1))
        for (sk, v) in evs:
            self.prog['sync'].append(('w', self.sem(sk), v))

    def replay(self, en, eng):
        for it in self.prog[en]:
            if it[0] == 'w':
                eng.wait_ge(it[1], it[2])
            else:
                ins = it[1](eng)
                if it[2] is not None:
                    ins.then_inc(it[2], it[3])


class Arena:
    def __init__(self, t, words):
        self.t = t
        self.words = words
        self.off = 0
        self.gen = 0

    def reset(self):
        self.off = 0
        self.gen += 1

    def get(self, name, shape, dt=F32, parts=128):
        n = 1
        for s in shape[1:]:
            n *= s
        if dt == BF16:
            w = (n + 1) // 2
        else:
            w = n
        w = (w + 7) // 8 * 8
        assert self.off + w <= self.words, (name, self.off, w, self.words)
        ap = self.t[0:shape[0], self.off:self.off + w]
        self.off += w
        if dt == BF16:
            ap = ap.bitcast(BF16)[:, 0:n]
        else:
            ap = ap[:, 0:n]
        if len(shape) == 3:
            ap = ap.rearrange("p (a b) -> p a b", b=shape[2])
        elif len(shape) == 4:
            ap = ap.rearrange("p (a b c) -> p a b c", b=shape[2], c=shape[3])
        return ap, (name, self.gen)


def build():
    nc = bass.Bass("TRN2", target_bir_lowering=False)
    din = lambda n, s: nc.dram_tensor(n, list(s), F32, kind="ExternalInput").ap()
    dout = lambda n, s: nc.dram_tensor(n, list(s), F32, kind="ExternalOutput").ap()
    I = {}
    for n, s in [("xp", (SEQ, D)), ("xs", (128, D)), ("call", (17, D)),
                 ("stC", (2, 16, 4, 128, 128)), ("stn", (2, 64, 128)), ("stmT", (4, 2, 16)),
                 ("stsc", (2, 32, 512)), ("ck", (2, 16, 128, 256)), ("cv", (2, 16, 128, 256)),
                 ("stffn", (4, 32, 5632)),
                 ("w_ada", (4, D, 6144)), ("b_adaT", (128, 4, 48)), ("norm1T", (128, 4, 8)),
                 ("norm2T", (128, 4, 8)), ("a_w_in", (2, D, INA)), ("a_bif", (4, 2, 2)),
                 ("a_onT", (128, 2, 4)), ("a_cwT", (128, 2, 3, 4)), ("a_w_out", (2, D, D)),
                 ("c_w_qkv", (2, D, 1536)), ("c_qn", (2, 64)), ("c_kn", (2, 64)), ("c_sink", (2, 16)),
                 ("c_w_out", (2, D, D)), ("f_w_up", (4, D, 2 * DFF)), ("f_cwT", (128, 4, 3, 44)),
                 ("f_w_down", (4, DFF, D)),
                 ("ident", (128, 128)), ("mP", (128, 128)), ("mS", (128, 128)),
                 ("acur", (128, 512)), ("aprev", (128, 512)), ("anew", (128, 512)),
                 ("acache", (128, 16, 128)), ("onehot", (128, 16)), ("sel4", (4, 4, 128)),
                 ("cosP", (SEQ, 32)), ("sinP", (SEQ, 32)), ("cosS", (128, 32)), ("sinS", (128, 32))]:
        I[n] = din(n, s)
    O = {}
    for n, s in [("yp", (SEQ, D)), ("ys", (128, D)), ("pC", (2, 4, 128, 128)), ("pn", (2, 4, 128)),
                 ("pm", (2, 4)), ("psc", (2, 2, 512)), ("pwk", (2, 128, 256)), ("pwv", (2, 128, 256)),
                 ("pffn", (4, 2, 5632)), ("sC", (2, 16, 4, 128, 128)), ("sn", (2, 64, 128)),
                 ("smT", (4, 2, 16)), ("ssc", (2, 32, 512)), ("swk", (2, 16, 128, 256)),
                 ("swv", (2, 16, 128, 256)), ("sffn", (4, 32, 5632))]:
        O[n] = dout(n, s)

    with contextlib.ExitStack() as st:
        S = Sch(nc, st)
        sbt = lambda n, s, dt=F32: st.enter_context(nc.sbuf_tensor("sb_" + n, list(s), dt))
        PS = [st.enter_context(nc.psum_tensor("ps%d" % i, [128, 512], F32)) for i in range(8)]
        PK = [('P', i) for i in range(8)]
        psb = lambda i: PS[i][:, :].bitcast(BF16)

        xT = sbt("xT", [128, 8, TP]); kX = ('xT',)
        hT = sbt("hT", [128, 8, TP], BF16); kH = ('hT',)
        MOD = sbt("MOD", [128, 4, 48, 17]); kM = ('MOD',)
        WB = sbt("WB", [128, 3 * SLOT], BF16)
        WKt = sbt("WK", [128, 15360])
        AR = Arena(WKt, 15360)
        ident = sbt("ident", [128, 128]); identb = sbt("identb", [128, 128], BF16)
        onesb = sbt("onesb", [128, 128], BF16)
        mPb = sbt("mPb", [128, 128], BF16); mSb = sbt("mSb", [128, 128], BF16)
        acurb = sbt("acurb", [128, 512], BF16); aprevb = sbt("aprevb", [128, 512], BF16)
        anewb = sbt("anewb", [128, 512], BF16); acacheb = sbt("acacheb", [128, 16, 128], BF16)
        onehot = sbt("onehot", [128, 16]); sel4 = sbt("sel4", [4, 4, 128])
        onesrow = sbt("onesrow", [4, TP])
        cosT = sbt("cosT", [128, 4, 32]); sinT = sbt("sinT", [128, 4, 32])
        n1T = sbt("n1T", [128, 4, 8]); n2T = sbt("n2T", [128, 4, 8]); badaT = sbt("badaT", [128, 4, 48])
        fcw = sbt("fcw", [128, 4, 3, 44]); acw = sbt("acw", [128, 2, 3, 4]); aon = sbt("aon", [128, 2, 4])
        bif = sbt("bif", [4, 2, 2]); nbif = sbt("nbif", [4, 2, 2])
        GQ = sbt("GQ", [128, 2, 64]); GK = sbt("GK", [128, 2, 64]); SK = sbt("SK", [128, 2, 16])
        SINKE = sbt("SINKE", [128, 2, 16]); NEGMA = sbt("NEGMA", [128, 2]); tmpc = sbt("tmpc", [128, 4])
        Cst = sbt("Cst", [128, 2, 4, 129]); MROW = sbt("MROW", [4, 2])
        ffc = sbt("ffc", [128, 4, 44, 2]); scc = sbt("scc", [128, 2, 4, 2])
        KTC = sbt("KTC", [128, 2, 2, 128], BF16); VC = sbt("VC", [128, 2, 4, 65], BF16)
        scT = sbt("scT", [128, 8, 17], BF16)
        stage = sbt("stage", [128, D]); call_sb = stage[0:17, :]
        kC = ('const',)

        V, A_, P_, PE_ = 'dve', 'act', 'pool', 'pe'

        def ld(q, dst, src, key, **kw):
            S.dma(q, dst, src, w=[key], chan=key if key != kC else ('c0',), **kw)
        for dst, nm in [(ident, "ident"), (onehot, "onehot"), (sel4, "sel4"), (n1T, "norm1T"), (n2T, "norm2T"),
                        (badaT, "b_adaT"), (fcw, "f_cwT"), (acw, "a_cwT"), (aon, "a_onT"), (bif, "a_bif")]:
            S.dma('sync', dst[:], I[nm], w=[kC], chan=('c0',))
        S.dma('sync', call_sb, I["call"], w=[('stage',)], chan=('istage',))
        for l in range(2):
            S.dma('sync', GQ[:, l, :], I["c_qn"][l].partition_broadcast(128), w=[kC], chan=('c0',))
            S.dma('sync', GK[:, l, :], I["c_kn"][l].partition_broadcast(128), w=[kC], chan=('c0',))
            S.dma('sync', SK[:, l, :], I["c_sink"][l].partition_broadcast(128), w=[kC], chan=('c0',))
        for dst, nm in [(identb, "ident"), (mPb, "mP"), (mSb, "mS"), (acurb, "acur"), (aprevb, "aprev"),
                        (anewb, "anew"), (acacheb, "acache")]:
            S.dma('pool', dst[:], I[nm], w=[kC], chan=('c1',))
        kC2 = ('const2',)
        S.op(V, lambda e: e.memset(onesb[:], 1.0), r=[kC], w=[kC2])
        S.op(V, lambda e: e.memset(onesrow[:], 1.0), w=[kC2])
        S.op(V, lambda e: e.tensor_scalar(out=nbif[:], in0=bif[:], scalar1=-1.0, scalar2=None, op0=ALU.mult), r=[kC], w=[kC2])
        S.op(V, lambda e: e.memset(Cst[:], 0.0), w=[('Cst', 0), ('Cst', 1)])
        S.op(V, lambda e: e.memset(MROW[:], 0.0), w=[('MROW',)])
        S.op(V, lambda e: e.memset(ffc[:], 0.0), w=[('ffc',)])
        S.op(V, lambda e: e.memset(scc[:], 0.0), w=[('scc',)])
        S.op(V, lambda e: e.memset(VC[:], 1.0), w=[('VC',)])
        for l in range(2):
            S.op(V, lambda e, l=l: e.tensor_reduce(out=tmpc[:, 0:1], in_=GQ[:, l, :], axis=AX.X, op=ALU.max, apply_absolute_value=True), r=[kC], w=[('tmpc',)])
            S.op(V, lambda e, l=l: e.tensor_reduce(out=tmpc[:, 1:2], in_=GK[:, l, :], axis=AX.X, op=ALU.max, apply_absolute_value=True), r=[kC], w=[('tmpc',)])
            S.op(V, lambda e, l=l: e.scalar_tensor_tensor(out=NEGMA[:, l:l + 1], in0=tmpc[:, 0:1], scalar=-8.0, in1=tmpc[:, 1:2], op0=ALU.mult, op1=ALU.mult), r=[('tmpc',)], w=[kC2])
            S.op(A_, lambda e, l=l: e.activation(out=SINKE[:, l, :], in_=SK[:, l, :], func=AF.Exp, bias=NEGMA[:, l:l + 1], scale=1.0), r=[kC, kC2], w=[('SINKE',)])

        S.op(A_, lambda e: e.activation(out=call_sb, in_=call_sb, func=AF.Silu), r=[('stage',)], w=[('stage',)])
        for kc in range(8):
            S.op(PE_, lambda e, kc=kc: e.transpose(out=PS[0][:, kc * 17:(kc + 1) * 17], in_=call_sb[:, kc * 128:(kc + 1) * 128], identity=ident[0:17, 0:17]),
                 r=[('stage',), kC], w=[PK[0]])
        S.op(V, lambda e: e.tensor_copy(out=scT[:], in_=PS[0][:, 0:136].rearrange("p (a b) -> p a b", b=17)), r=[PK[0]], w=[('scT',)])
        wa = [WB[:, i * 6144:(i + 1) * 6144] for i in range(2)]
        for l in range(4):
            for kc in range(8):
                sl = (l * 8 + kc) % 2
                S.dma('pool', wa[sl], I["w_ada"][l, kc * 128:(kc + 1) * 128, :], w=[('W', sl)], max_dma_last_dim=4096)
                for fc in range(48):
                    b = 1 + fc // 24
                    o = (fc % 24) * 17
                    S.op(PE_, lambda e, sl=sl, fc=fc, b=b, o=o, kc=kc: e.matmul(PS[b][:, o:o + 17], wa[sl][:, fc * 128:(fc + 1) * 128], scT[:, kc, :],
                                                                            start=(kc == 0 and fc % 24 == 0), stop=(kc == 7), skip_group_check=True),
                         r=[('W', sl), ('scT',)], w=[PK[b]], inc=(fc == 47))
            for b in range(2):
                S.op(V, lambda e, l=l, b=b: e.tensor_tensor(out=MOD[:, l, b * 24:(b + 1) * 24, :], in0=PS[1 + b][:, 0:408].rearrange("p (a b) -> p a b", b=17),
                                                           in1=badaT[:, l, b * 24:(b + 1) * 24].unsqueeze(2).broadcast_to([128, 24, 17]), op=ALU.add),
                     r=[PK[1 + b], kC], w=[kM])
            for (c0, nT) in [(8, n1T), (32, n2T)]:
                S.op(V, lambda e, l=l, c0=c0, nT=nT: e.scalar_tensor_tensor(out=MOD[:, l, c0:c0 + 8, :], in0=MOD[:, l, c0:c0 + 8, :], scalar=1.0,
                                                                           in1=nT[:, l, :].unsqueeze(2).broadcast_to([128, 8, 17]), op0=ALU.add, op1=ALU.mult),
                     r=[kM, kC], w=[kM])

        def fm_rows_out(src_fn, nrows, dst_rows_fn, nchunks, rkeys):
            for c0 in range(0, nchunks, 4):
                n = min(4, nchunks - c0)
                for j in range(n):
                    S.op(PE_, lambda e, j=j, c0=c0: e.transpose(out=PS[3][0:nrows, j * 128:(j + 1) * 128], in_=src_fn(c0 + j), identity=ident[:, :]),
                         r=rkeys + [kC], w=[PK[3]])
                S.op(A_, lambda e, n=n: e.copy(out=stage[0:nrows, 0:n * 128], in_=PS[3][0:nrows, 0:n * 128]), r=[PK[3]], w=[('stage',)])
                S.dma('sync', dst_rows_fn(c0 * 128, n * 128), stage[0:nrows, 0:n * 128], r=[('stage',)], chan=('ostage',))

        def rows_to_fm(src_rows_fn, nrows, dst_fn, nchunks, wkeys):
            for c0 in range(0, nchunks, 8):
                n = min(8, nchunks - c0)
                S.dma('sync', stage[0:nrows, 0:n * 128], src_rows_fn(c0 * 128, n * 128), w=[('stage',)], chan=('istage',))
                for j in range(n):
                    S.op(PE_, lambda e, j=j: e.transpose(out=PS[3][:, j * 32:j * 32 + nrows], in_=stage[0:nrows, j * 128:(j + 1) * 128], identity=ident[0:nrows, 0:nrows]),
                         r=[('stage',), kC], w=[PK[3]])
                for j in range(n):
                    S.op(V, lambda e, j=j, c0=c0: e.tensor_copy(out=dst_fn(c0 + j), in_=PS[3][:, j * 32:j * 32 + nrows]), r=[PK[3]], w=wkeys)

        def run_pass(sample, pi):
            T = 128 if sample else TP
            nseq = 16 if sample else 1
            L = 8 if sample else 128
            LT = 8 if sample else TP
            nch = T // 128
            tok0 = 0 if sample else pi * TP
            last = sample or (pi == NPASS - 1)
            xin = I["xs"] if sample else I["xp"]
            yout = O["ys"] if sample else O["yp"]
            sq0 = 1 if sample else 0
            v3 = lambda ap: ap.rearrange("p (s l) -> p s l", l=LT)

            def modbc(l, ch, kc):
                return MOD[:, l, ch * 8 + kc, sq0:sq0 + nseq].unsqueeze(2).broadcast_to([128, nseq, LT])

            for c in range(nch):
                S.dma('sync', stage[:, :], xin[tok0 + c * 128: tok0 + (c + 1) * 128, :], w=[('stage',)], chan=('istage',))
                for b in range(2):
                    for j in range(4):
                        kc = b * 4 + j
                        S.op(PE_, lambda e, b=b, j=j, kc=kc: e.transpose(out=PS[b][:, j * 128:(j + 1) * 128], in_=stage[:, kc * 128:(kc + 1) * 128], identity=ident[:, :]),
                             r=[('stage',), kC], w=[PK[b]])
                    S.op(V if b == 0 else A_, (lambda e, b=b, c=c: e.tensor_copy(out=xT[:, b * 4:(b + 1) * 4, c * 128:(c + 1) * 128], in_=PS[b][:, :].rearrange("p (a b) -> p a b", b=128))) if b == 0 else
                         (lambda e, b=b, c=c: e.copy(out=xT[:, b * 4:(b + 1) * 4, c * 128:(c + 1) * 128], in_=PS[b][:, :].rearrange("p (a b) -> p a b", b=128))),
                         r=[PK[b]], w=[kX])
            if not sample:
                S.dma('sync', cosT[:, 0:nch, :], I["cosP"][tok0:tok0 + T, :].rearrange("(c p) i -> p c i", p=128), w=[('rope',)])
                S.dma('sync', sinT[:, 0:nch, :], I["sinP"][tok0:tok0 + T, :].rearrange("(c p) i -> p c i", p=128), w=[('rope',)])
            else:
                S.dma('sync', cosT[:, 0, :], I["cosS"], w=[('rope',)])
                S.dma('sync', sinT[:, 0, :], I["sinS"], w=[('rope',)])

            def norm_mod(l, which):
                cA, cB = (1, 0) if which == 1 else (4, 3)
                S.barrier(); AR.reset()
                sqb, ksq = AR.get("sq", [128, 8, T], BF16)
                rs, krs = AR.get("rs", [128, T])
                tmp, ktmp = AR.get("tmp", [128, T])
                S.op(A_, lambda e: e.activation(out=sqb, in_=xT[:, :, 0:T], func=AF.Square), r=[kX], w=[ksq])
                for kc in range(8):
                    S.op(PE_, lambda e, kc=kc: e.matmul(PS[0][:, 0:T], onesb[:, :], sqb[:, kc, :], start=(kc == 0), stop=(kc == 7)), r=[ksq, kC2], w=[PK[0]], inc=(kc == 7))
                S.op(A_, lambda e: e.activation(out=rs, in_=PS[0][:, 0:T], func=AF.Sqrt, bias=EPS, scale=1.0 / D), r=[PK[0]], w=[krs])
                S.op(V, lambda e: e.reciprocal(out=rs, in_=rs), r=[krs], w=[krs])
                for kc in range(8):
                    S.op(V, lambda e, kc=kc: e.tensor_tensor(out=tmp, in0=xT[:, kc, 0:T], in1=rs, op=ALU.mult), r=[kX, krs], w=[ktmp])
                    S.op(V, lambda e, kc=kc: e.tensor_tensor(out=v3(tmp), in0=v3(tmp), in1=modbc(l, cA, kc), op=ALU.mult), r=[ktmp, kM], w=[ktmp])
                    S.op(V, lambda e, kc=kc: e.tensor_tensor(out=v3(hT[:, kc, 0:T]), in0=v3(tmp), in1=modbc(l, cB, kc), op=ALU.add), r=[ktmp, kM], w=[kH])

            def resid_add(l, gch, d, psrc, pkey, tkey_ap):
                tmp, ktmp = tkey_ap
                if not sample:
                    S.op(V, lambda e: e.scalar_tensor_tensor(out=xT[:, d, 0:T], in0=psrc, scalar=MOD[:, l, gch * 8 + d, 0:1], in1=xT[:, d, 0:T], op0=ALU.mult, op1=ALU.add),
                         r=[pkey, kM, kX], w=[kX])
                    return
                S.op(V, lambda e: e.tensor_tensor(out=v3(tmp), in0=v3(psrc), in1=modbc(l, gch, d), op=ALU.mult), r=[pkey, kM], w=[ktmp])
                S.op(V, lambda e: e.tensor_tensor(out=xT[:, d, 0:T], in0=xT[:, d, 0:T], in1=tmp, op=ALU.add), r=[ktmp, kX], w=[kX])

            def conv3(dst3, src3, wfn, keys_r, keys_w, pool=False):
                E1 = P_ if pool else V
                if pool:
                    S.op(E1, lambda e: e.tensor_scalar(out=dst3, in0=src3[:, :, 0:LT], scalar1=wfn(0), scalar2=0.0, op0=ALU.mult, op1=ALU.add), r=keys_r, w=keys_w)
                else:
                    S.op(E1, lambda e: e.tensor_scalar(out=dst3, in0=src3[:, :, 0:LT], scalar1=wfn(0), scalar2=None, op0=ALU.mult), r=keys_r, w=keys_w)
                S.op(V, lambda e: e.scalar_tensor_tensor(out=dst3, in0=src3[:, :, 1:LT + 1], scalar=wfn(1), in1=dst3, op0=ALU.mult, op1=ALU.add), r=keys_r + keys_w, w=keys_w)
                S.op(V, lambda e: e.scalar_tensor_tensor(out=dst3, in0=src3[:, :, 2:LT + 2], scalar=wfn(2), in1=dst3, op0=ALU.mult, op1=ALU.add), r=keys_r + keys_w, w=keys_w)

            def ffn(l):
                gsz = [4, 4, 4, 4, 4, 2]
                gst = [0, 4, 8, 12, 16, 20]

                def wslot(g):
                    base = (g % 3) * SLOT
                    ug = WB[:, base:base + 4096].rearrange("p (k c) -> p k c", c=512)
                    ua = WB[:, base + 4096:base + 8192].rearrange("p (k c) -> p k c", c=512)
                    dn = WB[:, base + 8192:base + 12288].rearrange("p (j d) -> p j d", d=D)
                    return ug, ua, dn

                def issue(g):
                    ug, ua, dn = wslot(g)
                    n = gsz[g]; f0 = gst[g]
                    k = ('W', g % 3)
                    wu = I["f_w_up"][l].rearrange("(k p) c -> p k c", p=128)
                    S.dma('pool', ug[:, :, 0:n * 128], wu[:, :, f0 * 128:(f0 + n) * 128], w=[k], max_dma_last_dim=4096)
                    S.dma('pool', ua[:, :, 0:n * 128], wu[:, :, DFF + f0 * 128:DFF + (f0 + n) * 128], w=[k], max_dma_last_dim=4096)
                    S.dma('pool', dn[:, 0:n, :], I["f_w_down"][l, f0 * 128:(f0 + n) * 128, :].rearrange("(j p) d -> p j d", p=128), w=[k], max_dma_last_dim=4096)
                issue(0); issue(1)
                norm_mod(l, 2)
                if os.environ.get("KD_FFN", "") == "n":
                    return
                S.barrier(); AR.reset()
                UB, kUB = AR.get("UB", [128, 4, nseq * (LT + 2)])
                Y, kY = AR.get("Y", [128, 4, T])
                ACTB, kAB = AR.get("ACTB", [128, 4, T], BF16)
                rt = AR.get("rt", [128, T])
                if sample:
                    car, kcar = AR.get("car", [128, 44, 32])
                    rows_to_fm(lambda o, n: I["stffn"][l, :, o:o + n], 32, lambda c: car[:, c, :], 44, [kcar])
                    carv = lambda idx: car[:, idx, :].rearrange("p (s j) -> p s j", j=2)
                else:
                    kcar = ('ffc',)
                    carv = lambda idx: ffc[:, l, idx, :].unsqueeze(1)
                ub3 = lambda i: UB[:, i, :].rearrange("p (s k) -> p s k", k=LT + 2)

                for g in range(6):
                    if g + 2 < 6:
                        issue(g + 2)
                    ug, ua, dn = wslot(g)
                    kW = ('W', g % 3)
                    n = gsz[g]; f0 = gst[g]
                    for j in range(n):
                        ib = 2 * (j % 2)
                        for part in range(2):
                            i = ib + part
                            wv = ug if part == 0 else ua
                            idx = (0 if part == 0 else 22) + f0 + j
                            for kc in range(8):
                                S.op(PE_, lambda e, i=i, wv=wv, j=j, kc=kc: e.matmul(PS[i][:, 0:T], wv[:, kc, j * 128:(j + 1) * 128], hT[:, kc, 0:T], start=(kc == 0), stop=(kc == 7)),
                                     r=[kW, kH], w=[PK[i]], inc=(kc == 7))
                            S.op(A_, lambda e, i=i, idx=idx: e.copy(out=ub3(i)[:, :, 0:2], in_=carv(idx)), r=[kcar], w=[(kUB, i)])
                            S.op(A_, lambda e, i=i: e.copy(out=ub3(i)[:, :, 2:LT + 2], in_=v3(PS[i][:, 0:T])), r=[PK[i]], w=[(kUB, i)])
                            S.op(A_, lambda e, i=i, idx=idx: e.copy(out=carv(idx), in_=ub3(i)[:, :, LT:LT + 2]), r=[(kUB, i)], w=[kcar])
                            conv3(v3(Y[:, i, :]), ub3(i), lambda t, idx=idx: fcw[:, l, t, idx:idx + 1], [(kUB, i), kC], [(kY, i)], pool=True)
                        S.op(A_, lambda e, ib=ib: e.activation(out=Y[:, ib, :], in_=Y[:, ib, :], func=AF.Silu), r=[(kY, ib)], w=[(kY, ib)])
                        S.op(V, lambda e, j=j, ib=ib: e.tensor_tensor(out=ACTB[:, j, :], in0=Y[:, ib, :], in1=Y[:, ib + 1, :], op=ALU.mult), r=[(kY, ib), (kY, ib + 1)], w=[(kAB, j)])
                    for dh in range(2):
                        for d4 in range(4):
                            d = dh * 4 + d4
                            for j in range(n):
                                S.op(PE_, lambda e, d4=d4, d=d, j=j, dn=dn, n=n: e.matmul(PS[4 + d4][:, 0:T], dn[:, j, d * 128:(d + 1) * 128], ACTB[:, j, :], start=(j == 0), stop=(j == n - 1)),
                                     r=[kW, (kAB, j)], w=[PK[4 + d4]], inc=(j == n - 1))
                            resid_add(l, 5, d, PS[4 + d4][:, 0:T], PK[4 + d4], rt)
                if last:
                    nr = 32 if sample else 2
                    dst = O["sffn"] if sample else O["pffn"]
                    if sample:
                        fm_rows_out(lambda c: car[:, c, :], 32, lambda o, n: dst[l, :, o:o + n], 44, [kcar])
                    else:
                        fm_rows_out(lambda c: ffc[:, l, c, :], 2, lambda o, n: dst[l, :, o:o + n], 44, [kcar])

            def mixer_ab(l):
                i = l // 2
                win = WB[:, 0:8 * INA].rearrange("p (k c) -> p k c", c=INA)
                wout = WB[:, 8 * INA:8 * INA + 8 * D].rearrange("p (k c) -> p k c", c=D)
                kW = [('W', 0), ('W', 1), ('W', 2)]
                for kc in range(8):
                    S.dma('pool', win[:, kc, :], I["a_w_in"][i, kc * 128:(kc + 1) * 128, :], w=kW, chan=('W', 0), max_dma_last_dim=4096)
                S.dma('pool', wout, I["a_w_out"][i].rearrange("(k p) c -> p k c", p=128), w=kW, chan=('W', 0), max_dma_last_dim=4096)
                norm_mod(l, 1)
                S.barrier(); AR.reset()
                qT, kqT = AR.get("qT", [128, 4, T], BF16)
                kTt, kkT = AR.get("kT", [128, 4, T], BF16)
                sgT, ksg = AR.get("sgT", [128, 4, T], BF16)
                scTt, ksc = AR.get("scTt", [128, 4, T], BF16)
                hmT, khm = AR.get("hmT", [128, 4, T], BF16)
                KTOK, kkt = AR.get("KTOK", [128, nch, 512], BF16)
                VT, kvt = AR.get("VT", [128, nch * 4, 129], BF16)
                PB, kPB = AR.get("PB", [128, nseq * (LT + 2)])
                ZX, kZX = AR.get("ZX", [128, T])
                U, kU = AR.get("U", [128, T])
                rt = AR.get("rt", [128, T])
                rows = {}
                for nm in ["ig", "l1", "cs", "aa", "A", "negm", "prev", "lastr"]:
                    rows[nm] = AR.get("r_" + nm, [4, T], parts=4)
                NEGAX, kNX = AR.get("NEGAX", [4, nseq + T], parts=4)
                MIN, kMIN = AR.get("MIN", [4, 16], parts=4)
                COLS, kCOLS = AR.get("COLS", [128, 20])
                SA, kSA = AR.get("SA", [128, nseq + 128])
                WT, kWT = AR.get("WT", [128, 128])
                PT, kPT = AR.get("PT", [128, 128], BF16)
                PVs, kPVs = AR.get("PVs", [128, 129])
                NUM, kNUM = AR.get("NUM", [128, 129])
                JK, kJK = AR.get("JK", [128, 128])
                HN, kHN = AR.get("HN", [128, 128], BF16)
                VS, kVS = AR.get("VS", [128, 129], BF16)
                SM, kSM = AR.get("SM", [128, 16])
                WSI, kWSI = AR.get("WSI", [128, 16])
                WCB, kWCB = AR.get("WCB", [128, 16])
                QPAD, kQP = AR.get("QPAD", [128, nseq * 136])
                if sample:
                    CS_, kCS = AR.get("CSs", [128, 64, 129])
                    S.dma('sync', CS_[:, :, 0:128], I["stC"][i].rearrange("s h k v -> k (s h) v"), w=[kCS])
                    rows_to_fm(lambda o, n: I["stn"][i, :, o:o + n], 64, lambda c: CS_[:, :, 128], 1, [kCS])
                    S.dma('sync', MIN, I["stmT"][:, i, :], w=[kMIN])
                    csv = lambda s_, h: CS_[:, s_ * 4 + h, :]
                    pcar, kpc = AR.get("pcar", [128, 4, 32])
                    rows_to_fm(lambda o, n: I["stsc"][i, :, o:o + n], 32, lambda c: pcar[:, c, :], 4, [kpc])
                    pcv = lambda ch: pcar[:, ch, :].rearrange("p (s j) -> p s j", j=2)
                else:
                    kCS = ('Cst', i)
                    csv = lambda s_, h: Cst[:, i, h, :]
                    kMIN = ('MROW',)
                    kpc = ('scc',)
                    pcv = lambda ch: scc[:, i, ch, :].unsqueeze(1)
                S.op(V, lambda e: e.memset(VT, 1.0), w=[kvt])
                S.op(V, lambda e: e.memset(QPAD, 0.0), w=[kQP])

                def proj_fm(c0, handler, tag):
                    b = proj_fm.n % 2
                    proj_fm.n += 1
                    for kc in range(8):
                        S.op(PE_, lambda e, b=b, kc=kc: e.matmul(PS[b][:, 0:T], win[:, kc, c0:c0 + 128], hT[:, kc, 0:T], start=(kc == 0), stop=(kc == 7)),
                             r=kW + [kH], w=[PK[b]], inc=(kc == 7))
                    handler(PS[b][:, 0:T], PK[b])
                proj_fm.n = 0
                for h in range(4):
                    proj_fm(h * 128, lambda p, k, h=h: S.op(A_, lambda e: e.copy(out=qT[:, h, :], in_=p), r=[k], w=[kqT]), "q")
                    proj_fm(512 + h * 128, lambda p, k, h=h: S.op(A_, lambda e: e.activation(out=kTt[:, h, :], in_=p, func=AF.Copy, scale=128.0 ** -0.5), r=[k], w=[kkT]), "k")
                    proj_fm(1536 + h * 128, lambda p, k, h=h: S.op(A_, lambda e: e.activation(out=sgT[:, h, :], in_=p, func=AF.Sigmoid), r=[k], w=[ksg]), "o")
                pb3 = PB.rearrange("p (s k) -> p s k", k=LT + 2)
                for ch in range(4):
                    proj_fm(3080 + ch * 128, lambda p, k: S.op(A_, lambda e: e.copy(out=ZX, in_=p), r=[k], w=[kZX]), "zx")
                    S.op(A_, lambda e, ch=ch: e.copy(out=pb3[:, :, 0:2], in_=pcv(ch)), r=[kpc], w=[kPB])
                    proj_fm(2568 + ch * 128, lambda p, k: S.op(V, lambda e: e.tensor_tensor(out=pb3[:, :, 2:LT + 2], in0=v3(p), in1=v3(ZX), op=ALU.mult), r=[k, kZX], w=[kPB]), "zc")
                    S.op(A_, lambda e, ch=ch: e.copy(out=pcv(ch), in_=pb3[:, :, LT:LT + 2]), r=[kPB], w=[kpc])
                    conv3(v3(U), pb3, lambda t, ch=ch: acw[:, i, t, ch:ch + 1], [kPB, kC], [kU])
                    proj_fm(2056 + ch * 128, lambda p, k, ch=ch: S.op(V, lambda e: e.tensor_tensor(out=scTt[:, ch, :], in0=p, in1=U, op=ALU.mult), r=[k, kU], w=[ksc]), "zb")
                if last:
                    if sample:
                        fm_rows_out(lambda c: pcar[:, c, :], 32, lambda o, n: O["ssc"][i, :, o:o + n], 4, [kpc])
                    else:
                        fm_rows_out(lambda c: scc[:, i, c, :], 2, lambda o, n: O["psc"][i, :, o:o + n], 4, [kpc])
                rg = lambda nm: rows[nm][0]
                kg = lambda nm: rows[nm][1]
                for gi, (c0, nm) in enumerate([(2048, "ig"), (2052, "l1")]):
                    for kc in range(8):
                        S.op(PE_, lambda e, kc=kc, c0=c0: e.matmul(PS[2][0:4, 0:T], win[:, kc, c0:c0 + 4], hT[:, kc, 0:T], start=(kc == 0), stop=(kc == 7)),
                             r=kW + [kH], w=[PK[2]], inc=(kc == 7))
                    if gi == 0:
                        S.op(A_, lambda e: e.activation(out=rg("ig"), in_=PS[2][0:4, 0:T], func=AF.Identity, bias=bif[:, i, 0:1], scale=1.0), r=[PK[2], kC], w=[kg("ig")])
                    else:
                        S.op(A_, lambda e: e.activation(out=rg("l1"), in_=PS[2][0:4, 0:T], func=AF.Exp, bias=nbif[:, i, 1:2], scale=-1.0), r=[PK[2], kC2], w=[kg("l1")])
                        S.op(A_, lambda e: e.activation(out=rg("l1"), in_=rg("l1"), func=AF.Ln, bias=1.0, scale=1.0), r=[kg("l1")], w=[kg("l1")])
                r3 = lambda ap: ap.rearrange("p (s l) -> p s l", l=LT)
                for s_ in range(nseq):
                    sl = slice(s_ * LT, (s_ + 1) * LT)
                    S.op(V, lambda e, sl=sl: e.tensor_tensor_scan(out=rg("cs")[:, sl], data0=onesrow[:, 0:LT], data1=rg("l1")[:, sl], initial=0.0, op0=ALU.mult, op1=ALU.add),
                         r=[kg("l1"), kC2], w=[kg("cs")])
                S.op(V, lambda e: e.tensor_tensor(out=rg("aa"), in0=rg("ig"), in1=rg("cs"), op=ALU.add), r=[kg("ig"), kg("cs")], w=[kg("aa")])
                minap = (lambda s_: MIN[:, s_:s_ + 1]) if sample else (lambda s_: MROW[:, i:i + 1])
                for s_ in range(nseq):
                    sl = slice(s_ * LT, (s_ + 1) * LT)
                    S.op(V, lambda e, sl=sl, s_=s_: e.tensor_tensor_scan(out=rg("A")[:, sl], data0=rg("aa")[:, sl], data1=rg("aa")[:, sl], initial=minap(s_), op0=ALU.max, op1=ALU.max),
                         r=[kg("aa"), kMIN], w=[kg("A")])
                minall = MIN[:, 0:16] if sample else MROW[:, i:i + 1]
                S.op(V, lambda e: e.tensor_scalar(out=NEGAX[:, 0:nseq], in0=minall, scalar1=-1.0, scalar2=None, op0=ALU.mult), r=[kMIN], w=[kNX])
                S.op(V, lambda e: e.tensor_scalar(out=NEGAX[:, nseq:nseq + T], in0=rg("A"), scalar1=-1.0, scalar2=None, op0=ALU.mult), r=[kg("A")], w=[kNX])
                S.op(V, lambda e: e.tensor_tensor(out=rg("negm"), in0=rg("cs"), in1=rg("A"), op=ALU.subtract), r=[kg("cs"), kg("A")], w=[kg("negm")])
                if sample:
                    S.op(V, lambda e: e.tensor_copy(out=r3(rg("prev")), in_=MIN[:, 0:16].unsqueeze(2).broadcast_to([4, 16, 8])), r=[kMIN], w=[kg("prev")])
                    S.op(V, lambda e: e.tensor_copy(out=r3(rg("lastr")), in_=r3(NEGAX[:, 16:16 + T])[:, :, 7:8].broadcast_to([4, 16, 8])), r=[kNX], w=[kg("lastr")])
                else:
                    c3 = lambda ap: ap.rearrange("p (c l) -> p c l", l=128)
                    S.op(V, lambda e: e.tensor_scalar(out=c3(rg("prev")), in0=c3(NEGAX[:, 0:T])[:, :, 0:1].broadcast_to([4, nch, 128]), scalar1=-1.0, scalar2=None, op0=ALU.mult), r=[kNX], w=[kg("prev")])
                    S.op(V, lambda e: e.tensor_copy(out=c3(rg("lastr")), in_=c3(NEGAX[:, 1:T + 1])[:, :, 127:128].broadcast_to([4, nch, 128])), r=[kNX], w=[kg("lastr")])
                if sample:
                    if last:
                        MOUT, kMO = AR.get("MOUT", [4, 16], parts=4)
                        S.op(V, lambda e: e.tensor_scalar(out=MOUT, in0=r3(rg("negm"))[:, :, 7], scalar1=-1.0, scalar2=None, op0=ALU.mult), r=[kg("negm")], w=[kMO])
                        S.dma('sync', O["smT"][:, i, :], MOUT, r=[kMO], chan=('osm',))
                else:
                    S.op(V, lambda e: e.tensor_scalar(out=MROW[:, i:i + 1], in0=rg("negm")[:, T - 1:T], scalar1=-1.0, scalar2=None, op0=ALU.mult), r=[kg("negm")], w=[kMIN])
                    if last:
                        S.dma('sync', O["pm"][i].unsqueeze(1), MROW[:, i:i + 1], r=[kMIN], chan=('opm',))
                for c in range(nch):
                    cs_ = slice(c * 128, (c + 1) * 128)
                    for part, c0 in [(0, 512), (1, 1024)]:
                        b = 4 + part
                        for kc in range(8):
                            S.op(PE_, lambda e, kc=kc, b=b, c0=c0, cs_=cs_: e.matmul(PS[b][:, 0:512], hT[:, kc, cs_], win[:, kc, c0:c0 + 512], start=(kc == 0), stop=(kc == 7)),
                                 r=kW + [kH], w=[PK[b]], inc=(kc == 7))
                    S.op(A_, lambda e, c=c: e.activation(out=KTOK[:, c, :], in_=PS[4][:, 0:512], func=AF.Copy, scale=128.0 ** -0.5), r=[PK[4]], w=[kkt])
                    S.op(A_, lambda e, c=c: e.copy(out=VT[:, c * 4:(c + 1) * 4, 0:128], in_=PS[5][:, 0:512].rearrange("p (h v) -> p h v", v=128)), r=[PK[5]], w=[kvt])
                maskb = mSb if sample else mPb
                qpd = QPAD.rearrange("p (s k) -> p s k", k=136)[:, :, 0:L]
                for c in range(nch):
                    cs_ = slice(c * 128, (c + 1) * 128)
                    for bi, nm in enumerate(["aa", "A", "negm", "prev", "lastr"]):
                        src = NEGAX[:, nseq + c * 128:nseq + (c + 1) * 128] if nm == "A" else rg(nm)[:, cs_]
                        S.op(PE_, lambda e, bi=bi, src=src: e.transpose(out=PS[3][:, bi * 4:(bi + 1) * 4], in_=src, identity=ident[0:4, 0:4]),
                             r=[kg(nm) if nm != "A" else kNX, kC], w=[PK[3]])
                    S.op(V, lambda e: e.tensor_copy(out=COLS, in_=PS[3][:, 0:20]), r=[PK[3]], w=[kCOLS])
                    for h in range(4):
                        ca = COLS[:, 0 + h:1 + h]; cnA = COLS[:, 4 + h:5 + h]; cnm = COLS[:, 8 + h:9 + h]
                        cpv = COLS[:, 12 + h:13 + h]; cla = COLS[:, 16 + h:17 + h]
                        if sample:
                            S.op(PE_, lambda e, h=h: e.matmul(PS[2][:, 0:16], sel4[:, h, :], NEGAX[:, 0:16], start=True, stop=False, skip_group_check=True), r=[kNX, kC], w=[PK[2]], inc=False)
                            S.op(PE_, lambda e, h=h: e.matmul(PS[2][:, 16:144], sel4[:, h, :], NEGAX[:, 16:144], start=False, stop=True, skip_group_check=True), r=[kNX, kC], w=[PK[2]])
                        else:
                            S.op(PE_, lambda e, h=h, c=c: e.matmul(PS[2][:, 0:129], sel4[:, h, :], NEGAX[:, c * 128:c * 128 + 129], start=True, stop=True), r=[kNX, kC], w=[PK[2]])
                        S.op(PE_, lambda e, h=h, c=c: e.matmul(PS[3][:, 0:128], sel4[:, h, :], NEGAX[:, nseq + c * 128:nseq + (c + 1) * 128], start=True, stop=False), r=[kNX, kC, kCOLS], w=[PK[3]], inc=False)
                        S.op(PE_, lambda e: e.matmul(PS[3][:, 0:128], identb[:, :], maskb[:, :], start=False, stop=True), r=[kC], w=[PK[3]])
                        S.op(PE_, lambda e, h=h, cs_=cs_: e.matmul(PS[4][:, 0:128], kTt[:, h, cs_], qT[:, h, cs_], start=True, stop=True), r=[kkT, kqT], w=[PK[4]])
                        S.op(A_, lambda e: e.copy(out=SA, in_=PS[2][:, 0:nseq + 128]), r=[PK[2]], w=[kSA])
                        S.op(A_, lambda e, ca=ca: e.activation(out=WT, in_=PS[3][:, 0:128], func=AF.Exp, bias=ca, scale=1.0), r=[PK[3], kCOLS], w=[kWT])
                        S.op(V, lambda e: e.tensor_tensor(out=PT, in0=PS[4][:, 0:128], in1=WT, op=ALU.mult), r=[PK[4], kWT], w=[kPT])
                        S.op(PE_, lambda e, c=c, h=h: e.matmul(PS[6][:, 0:129], PT, VT[:, c * 4 + h, :], start=True, stop=True), r=[kPT, kvt], w=[PK[6]])
                        S.op(V, lambda e, h=h, cs_=cs_: e.tensor_copy(out=qpd, in_=qT[:, h, cs_].rearrange("p (s l) -> p s l", l=L)), r=[kqT], w=[kQP])
                        for s_ in range(nseq):
                            S.op(PE_, lambda e, s_=s_, h=h: e.matmul(PS[5][:, 0:129], QPAD[:, s_ * 128:(s_ + 1) * 128], csv(s_, h), start=(s_ == 0), stop=(s_ == nseq - 1)),
                                 r=[kQP, kCS], w=[PK[5]], inc=(s_ == nseq - 1))
                        S.op(A_, lambda e, cnA=cnA, cpv=cpv: e.activation(out=SM[:, 0:1], in_=cnA, func=AF.Exp, bias=cpv, scale=1.0), r=[kCOLS], w=[(kSM, 0)])
                        S.op(A_, lambda e, ca=ca, cla=cla: e.activation(out=SM[:, 1:2], in_=ca, func=AF.Exp, bias=cla, scale=1.0), r=[kCOLS], w=[(kSM, 1)])
                        S.op(A_, lambda e, cnm=cnm: e.activation(out=SM[:, 2:3], in_=cnm, func=AF.Exp), r=[kCOLS], w=[(kSM, 2)])
                        S.op(A_, lambda e: e.copy(out=PVs, in_=PS[6][:, 0:129]), r=[PK[6]], w=[kPVs])
                        S.op(V, lambda e: e.scalar_tensor_tensor(out=NUM, in0=PS[5][:, 0:129], scalar=SM[:, 0:1], in1=PVs, op0=ALU.mult, op1=ALU.add), r=[PK[5], (kSM, 0), kPVs], w=[kNUM])
                        S.op(A_, lambda e: e.activation(out=SM[:, 3:4], in_=NUM[:, 128:129], func=AF.Abs), r=[kNUM], w=[(kSM, 3)])
                        S.op(V, lambda e: e.tensor_tensor(out=SM[:, 3:4], in0=SM[:, 3:4], in1=SM[:, 2:3], op=ALU.max), r=[(kSM, 3), (kSM, 2)], w=[(kSM, 3)])
                        S.op(V, lambda e: e.reciprocal(out=SM[:, 4:5], in_=SM[:, 3:4]), r=[(kSM, 3)], w=[(kSM, 4)])
                        S.op(A_, lambda e: e.activation(out=JK, in_=NUM[:, 0:128], func=AF.Square, scale=SM[:, 4:5], accum_out=SM[:, 5:6]), r=[kNUM, (kSM, 4)], w=[kJK, (kSM, 5)])
                        S.op(A_, lambda e: e.activation(out=SM[:, 6:7], in_=SM[:, 5:6], func=AF.Sqrt, bias=EPS, scale=1.0 / 128), r=[(kSM, 5)], w=[(kSM, 6)])
                        S.op(V, lambda e: e.reciprocal(out=SM[:, 6:7], in_=SM[:, 6:7]), r=[(kSM, 6)], w=[(kSM, 6)])
                        S.op(V, lambda e: e.tensor_tensor(out=SM[:, 7:8], in0=SM[:, 6:7], in1=SM[:, 4:5], op=ALU.mult), r=[(kSM, 6), (kSM, 4)], w=[(kSM, 7)])
                        S.op(V, lambda e: e.tensor_scalar(out=HN, in0=NUM[:, 0:128], scalar1=SM[:, 7:8], scalar2=None, op0=ALU.mult), r=[kNUM, (kSM, 7)], w=[kHN])
                        S.op(PE_, lambda e: e.transpose(out=psb(7)[:, 0:128], in_=HN, identity=identb[:, :]), r=[kHN, kC], w=[PK[7]])
                        S.op(V, lambda e, h=h, cs_=cs_: e.scalar_tensor_tensor(out=hmT[:, h, cs_], in0=psb(7)[:, 0:128], scalar=aon[:, i, h:h + 1], in1=sgT[:, h, cs_], op0=ALU.mult, op1=ALU.mult),
                             r=[PK[7], kC, ksg], w=[khm])
                        S.op(V, lambda e: e.tensor_scalar(out=WSI[:, 0:nseq], in0=onehot[:, 0:nseq] if sample else onesb[:, 0:1], scalar1=SM[:, 1:2], scalar2=None, op0=ALU.mult), r=[(kSM, 1), kC, kC2], w=[kWSI])
                        sa_last = SA[:, nseq:nseq + 128].rearrange("p (s l) -> p s l", l=L)[:, :, L - 1]
                        S.op(V, lambda e, sa_last=sa_last: e.tensor_tensor(out=WCB[:, 0:nseq], in0=sa_last, in1=SA[:, 0:nseq], op=ALU.subtract), r=[kSA], w=[kWCB])
                        S.op(A_, lambda e: e.activation(out=WCB[:, 0:nseq], in_=WCB[:, 0:nseq], func=AF.Exp), r=[kWCB], w=[kWCB])
                        for s_ in range(nseq):
                            S.op(V, lambda e, s_=s_, c=c, h=h: e.tensor_scalar(out=VS, in0=VT[:, c * 4 + h, :], scalar1=WSI[:, s_:s_ + 1], scalar2=None, op0=ALU.mult), r=[kvt, kWSI], w=[kVS])
                            S.op(PE_, lambda e, c=c, h=h: e.matmul(PS[7][:, 0:129], KTOK[:, c, h * 128:(h + 1) * 128], VS, start=True, stop=True), r=[kkt, kVS], w=[PK[7]])
                            S.op(V, lambda e, s_=s_, h=h: e.scalar_tensor_tensor(out=csv(s_, h), in0=csv(s_, h), scalar=WCB[:, s_:s_ + 1], in1=PS[7][:, 0:129], op0=ALU.mult, op1=ALU.add),
                                 r=[PK[7], kWCB, kCS], w=[kCS])
                if last:
                    if sample:
                        S.dma('sync', O["sC"][i].rearrange("s h k v -> k (s h) v"), CS_[:, :, 0:128], r=[kCS], chan=('osC',))
                        fm_rows_out(lambda c: CS_[:, :, 128], 64, lambda o, n: O["sn"][i, :, o:o + n], 1, [kCS])
                    else:
                        S.dma('sync', O["pC"][i].rearrange("h k v -> k h v"), Cst[:, i, :, 0:128], r=[kCS], chan=('opC',))
                        fm_rows_out(lambda c: Cst[:, i, :, 128], 4, lambda o, n: O["pn"][i, :, o:o + n], 1, [kCS])
                for d in range(8):
                    b = d % 2
                    for j in range(8):
                        rhs = hmT[:, j, :] if j < 4 else scTt[:, j - 4, :]
                        S.op(PE_, lambda e, b=b, j=j, d=d, rhs=rhs: e.matmul(PS[b][:, 0:T], wout[:, j, d * 128:(d + 1) * 128], rhs, start=(j == 0), stop=(j == 7)),
                             r=kW + [khm, ksc], w=[PK[b]], inc=(j == 7))
                    resid_add(l, 2, d, PS[b][:, 0:T], PK[b], rt)

            def mixer_c(l):
                jl = l // 2
                wq = WB[:, 0:8 * 1536].rearrange("p (k c) -> p k c", c=1536)
                wo = WB[:, 8 * 1536:8 * 1536 + 8 * D].rearrange("p (k c) -> p k c", c=D)
                kW = [('W', 0), ('W', 1), ('W', 2)]
                for kc in range(8):
                    for b3 in range(3):
                        S.dma('pool', wq[:, kc, b3 * 512:(b3 + 1) * 512], I["c_w_qkv"][jl, kc * 128:(kc + 1) * 128, b3 * 512:(b3 + 1) * 512], w=kW, chan=('W', 0))
                S.dma('pool', wo, I["c_w_out"][jl].rearrange("(k p) c -> p k c", p=128), w=kW, chan=('W', 0), max_dma_last_dim=4096)
                norm_mod(l, 1)
                S.barrier(); AR.reset()
                QK, kQK = AR.get("QK", [128, 1280])
                SQ, kSQ = AR.get("SQ", [128, 1280])
                QR, kQR = AR.get("QR", [128, 1280])
                QRb, kQRb = AR.get("QRb", [128, 1280], BF16)
                T1, kT1 = AR.get("T1", [128, 640])
                T2, kT2 = AR.get("T2", [128, 640])
                VV, kVV = AR.get("VV", [128, 256])
                SS, kSS = AR.get("SS", [128, 20])
                QT, kQT = AR.get("QTz", [128, 16, T], BF16)
                S.op(V, lambda e: e.memset(QT, 0.0), w=[kQT])
                QT5 = QT.rearrange("p (a b g) t -> p a b g t", a=2, b=2)
                KT, kKT = AR.get("KT", [128, 2, T], BF16)
                VTa, kVTa = AR.get("VTa", [128, nch * 4, 65], BF16)
                PTb = [AR.get("PTb%d" % z, [128, 512], BF16) for z in range(2)]
                DEN, kDEN = AR.get("DEN", [128, 4])
                OTOK, kOT = AR.get("OTOK", [128, 16, 64], BF16)
                OTT, kOTT = AR.get("OTT", [128, 8, T], BF16)
                rt = AR.get("rt", [128, T])
                if sample:
                    CKb, kCKb = AR.get("CKb", [128, 16, 256], BF16)
                    KCT, kKCT = AR.get("KCT", [128, 32, 128], BF16)
                    VCs, kVCs = AR.get("VCs", [128, 64, 65], BF16)
                    S.op(V, lambda e: e.memset(VCs, 1.0), w=[kVCs])
                    S.dma('pool', CKb, I["ck"][jl].rearrange("s p c -> p s c"), w=[kCKb], max_dma_last_dim=1024)
                    for s_ in range(16):
                        S.dma('pool', VCs[:, s_ * 4:(s_ + 1) * 4, 0:64], I["cv"][jl, s_].rearrange("p (h d) -> p h d", d=64), w=[kVCs], max_dma_last_dim=256)
                    for s_ in range(16):
                        for j in range(2):
                            S.op(PE_, lambda e, s_=s_, j=j: e.transpose(out=psb(3)[:, j * 128:(j + 1) * 128], in_=CKb[:, s_, j * 128:(j + 1) * 128], identity=identb[:, :]), r=[kCKb, kC], w=[PK[3]])
                        S.op(V, lambda e, s_=s_: e.tensor_copy(out=KCT[:, 2 * s_:2 * s_ + 2, :], in_=psb(3)[:, 0:256].rearrange("p (a b) -> p a b", b=128)), r=[PK[3]], w=[kKCT])
                    S.dma('sync', O["swk"][jl, :, 0:120, :], I["ck"][jl, :, 8:128, :], chan=('dd',))
                    S.dma('sync', O["swv"][jl, :, 0:120, :], I["cv"][jl, :, 8:128, :], chan=('dd',))
                S.op(V, lambda e: e.memset(VTa, 1.0), w=[kVTa])
                if int(os.environ.get('KD_C', 9)) < 1:
                    return
                qk3 = lambda ap: ap.rearrange("p (h d) -> p h d", d=64)
                for c in range(nch):
                    cs_ = slice(c * 128, (c + 1) * 128)
                    for b in range(3):
                        for kc in range(8):
                            S.op(PE_, lambda e, b=b, kc=kc, cs_=cs_: e.matmul(PS[b][:, 0:512], hT[:, kc, cs_], wq[:, kc, b * 512:(b + 1) * 512], start=(kc == 0), stop=(kc == 7)),
                                 r=kW + [kH], w=[PK[b]], inc=(kc == 7))
                    S.op(A_, lambda e: e.copy(out=QK[:, 0:512], in_=PS[0][:, 0:512]), r=[PK[0]], w=[kQK])
                    S.op(A_, lambda e: e.copy(out=QK[:, 512:1024], in_=PS[1][:, 0:512]), r=[PK[1]], w=[kQK])
                    S.op(A_, lambda e: e.copy(out=QK[:, 1024:1280], in_=PS[2][:, 0:256]), r=[PK[2]], w=[kQK])
                    if os.environ.get('KD_X', '') == 'a':
                        continue
                    S.op(A_, lambda e: e.copy(out=VV, in_=PS[2][:, 256:512]), r=[PK[2]], w=[kVV])
                    if os.environ.get('KD_X', '') == 'b':
                        continue
                    S.op(V, lambda e, c=c: e.tensor_copy(out=VTa[:, c * 4:(c + 1) * 4, 0:64], in_=VV.rearrange("p (h d) -> p h d", d=64)), r=[kVV], w=[kVTa])
                    if int(os.environ.get('KD_C', 9)) < 2:
                        continue
                    S.op(A_, lambda e: e.activation(out=SQ, in_=QK, func=AF.Square), r=[kQK], w=[kSQ])
                    S.op(V, lambda e: e.tensor_reduce(out=SS, in_=qk3(SQ), axis=AX.X, op=ALU.add), r=[kSQ], w=[kSS])
                    S.op(A_, lambda e: e.activation(out=SS, in_=SS, func=AF.Sqrt, bias=EPS, scale=1.0 / 64), r=[kSS], w=[kSS])
                    S.op(V, lambda e: e.reciprocal(out=SS, in_=SS), r=[kSS], w=[kSS])
                    S.op(V, lambda e: e.tensor_tensor(out=qk3(QK), in0=qk3(QK), in1=SS.unsqueeze(2).broadcast_to([128, 20, 64]), op=ALU.mult), r=[kSS, kQK], w=[kQK])
                    S.op(V, lambda e: e.tensor_tensor(out=qk3(QK[:, 0:1024]), in0=qk3(QK[:, 0:1024]), in1=GQ[:, jl, :].unsqueeze(1).broadcast_to([128, 16, 64]), op=ALU.mult), r=[kC, kQK], w=[kQK])
                    S.op(V, lambda e: e.tensor_tensor(out=qk3(QK[:, 1024:1280]), in0=qk3(QK[:, 1024:1280]), in1=GK[:, jl, :].unsqueeze(1).broadcast_to([128, 4, 64]), op=ALU.mult), r=[kC, kQK], w=[kQK])
                    if int(os.environ.get('KD_C', 9)) < 3:
                        continue
                    cosb = cosT[:, c, :].unsqueeze(1).broadcast_to([128, 20, 32])
                    sinb = sinT[:, c, :].unsqueeze(1).broadcast_to([128, 20, 32])
                    x1 = qk3(QK)[:, :, 0:32]; x2 = qk3(QK)[:, :, 32:64]
                    t1 = T1.rearrange("p (h d) -> p h d", d=32); t2 = T2.rearrange("p (h d) -> p h d", d=32)
                    S.op(P_, lambda e, x1=x1, cosb=cosb: e.tensor_tensor(out=t1, in0=x1, in1=cosb, op=ALU.mult), r=[kQK, ('rope',)], w=[kT1])
                    S.op(P_, lambda e, x2=x2, sinb=sinb: e.tensor_tensor(out=t2, in0=x2, in1=sinb, op=ALU.mult), r=[kQK, ('rope',)], w=[kT2])
                    S.op(V, lambda e: e.tensor_tensor(out=qk3(QR)[:, :, 0:32], in0=t1, in1=t2, op=ALU.subtract), r=[kT1, kT2], w=[kQR])
                    S.op(P_, lambda e, x2=x2, cosb=cosb: e.tensor_tensor(out=t1, in0=x2, in1=cosb, op=ALU.mult), r=[kQK, ('rope',)], w=[kT1])
                    S.op(P_, lambda e, x1=x1, sinb=sinb: e.tensor_tensor(out=t2, in0=x1, in1=sinb, op=ALU.mult), r=[kQK, ('rope',)], w=[kT2])
                    S.op(V, lambda e: e.tensor_tensor(out=qk3(QR)[:, :, 32:64], in0=t1, in1=t2, op=ALU.add), r=[kT1, kT2], w=[kQR])
                    S.op(A_, lambda e: e.copy(out=QRb, in_=QR), r=[kQR], w=[kQRb])
                    if int(os.environ.get('KD_C', 9)) < 4:
                        continue
                    for j in range(8):
                        S.op(PE_, lambda e, j=j: e.transpose(out=psb(3)[:, j * 128:(j + 1) * 128], in_=QRb[:, j * 128:(j + 1) * 128], identity=identb[:, :]), r=[kQRb, kC], w=[PK[3]])
                    S.op(V, lambda e, cs_=cs_: e.tensor_copy(out=QT5[0:64, :, 0, :, cs_], in_=psb(3)[0:64, 0:1024].rearrange("p (a g t) -> p a g t", a=2, g=4)), r=[PK[3]], w=[kQT])
                    S.op(V, lambda e, cs_=cs_: e.tensor_copy(out=QT5[64:128, :, 1, :, cs_], in_=psb(3)[64:128, 0:1024].rearrange("p (a g t) -> p a g t", a=2, g=4)), r=[PK[3]], w=[kQT])
                    for j in range(2):
                        S.op(PE_, lambda e, j=j: e.transpose(out=psb(4)[:, j * 128:(j + 1) * 128], in_=QRb[:, 1024 + j * 128:1024 + (j + 1) * 128], identity=identb[:, :]), r=[kQRb, kC], w=[PK[4]])
                    S.op(V, lambda e, cs_=cs_: e.tensor_copy(out=KT[:, :, cs_], in_=psb(4)[:, 0:256].rearrange("p (a b) -> p a b", b=128)), r=[PK[4]], w=[kKT])
                    if sample:
                        for s_ in range(16):
                            S.dma('sync', O["swk"][jl, s_, 120:128, :], QR[s_ * 8:(s_ + 1) * 8, 1024:1280], r=[kQR], chan=('oswk',))
                            S.dma('sync', O["swv"][jl, s_, 120:128, :], VV[s_ * 8:(s_ + 1) * 8, :], r=[kVV], chan=('oswv',))
                    elif last and c == nch - 1:
                        S.dma('sync', O["pwk"][jl], QR[:, 1024:1280], r=[kQR], chan=('opwk',))
                        S.dma('sync', O["pwv"][jl], VV, r=[kVV], chan=('opwv',))
                    if int(os.environ.get('KD_C', 9)) < 5:
                        continue
                    for kap in range(4):
                        base = 64 * (kap % 2)
                        pr = kap // 2
                        blocks = []
                        if sample:
                            for s_ in range(16):
                                blocks.append((KCT[:, 2 * s_ + pr, :], VCs[:, s_ * 4 + kap, :],
                                               acacheb[:, s_, :].unsqueeze(1).broadcast_to([128, 4, 128]), [kKCT], [kVCs]))
                            blocks.append((KT[:, pr, cs_], VTa[:, c * 4 + kap, :], anewb[:, :].rearrange("p (g t) -> p g t", t=128), [kKT], [kVTa]))
                        else:
                            blocks.append((KT[:, pr, cs_], VTa[:, c * 4 + kap, :], acurb[:, :].rearrange("p (g t) -> p g t", t=128), [kKT], [kVTa]))
                            if c > 0:
                                ps_ = slice((c - 1) * 128, c * 128)
                                blocks.append((KT[:, pr, ps_], VTa[:, (c - 1) * 4 + kap, :], aprevb[:, :].rearrange("p (g t) -> p g t", t=128), [kKT], [kVTa]))
                            elif pi > 0:
                                blocks.append((KTC[:, jl, pr, :], VC[:, jl, kap, :], aprevb[:, :].rearrange("p (g t) -> p g t", t=128), [('KTC',)], [('VC',)]))
                        qrhs = QT[:, 4 * kap:4 * kap + 4, cs_]
                        for bi, (kb, vb, mb, kr, vr) in enumerate(blocks):
                            pt, kpt = PTb[bi % 2]
                            ps5 = PS[5][:, 0:512].rearrange("p (g t) -> p g t", t=128)
                            S.op(PE_, lambda e, kb=kb, qrhs=qrhs, ps5=ps5: e.matmul(ps5, kb, qrhs, start=True, stop=False), r=kr + [kQT], w=[PK[5]], inc=False)
                            S.op(PE_, lambda e, mb=mb, ps5=ps5: e.matmul(ps5, identb[:, :], mb, start=False, stop=True), r=[kC], w=[PK[5]])
                            S.op(A_, lambda e, pt=pt: e.activation(out=pt, in_=PS[5][:, 0:512], func=AF.Exp, bias=NEGMA[:, jl:jl + 1], scale=0.125), r=[PK[5], kC2], w=[kpt])
                            for g in range(4):
                                S.op(PE_, lambda e, g=g, pt=pt, vb=vb, bi=bi, nb=len(blocks): e.matmul(PS[6][:, g * 65:(g + 1) * 65], pt[:, g * 128:(g + 1) * 128], vb,
                                                                                    start=(bi == 0 and g == 0), stop=(bi == nb - 1), skip_group_check=True),
                                     r=[kpt] + vr, w=[PK[6]], inc=(g == 3))
                        s0 = 8 * pr + (kap % 2)
                        o3 = PS[6][:, 0:260].rearrange("p (g d) -> p g d", d=65)
                        S.op(V, lambda e, s0=s0, o3=o3: e.tensor_tensor(out=DEN, in0=o3[:, :, 64], in1=SINKE[:, jl, s0:s0 + 7:2], op=ALU.add), r=[PK[6], ('SINKE',)], w=[kDEN])
                        S.op(V, lambda e: e.reciprocal(out=DEN, in_=DEN), r=[kDEN], w=[kDEN])
                        S.op(V, lambda e, s0=s0, o3=o3: e.tensor_tensor(out=OTOK[:, s0:s0 + 7:2, :], in0=o3[:, :, 0:64], in1=DEN.unsqueeze(2).broadcast_to([128, 4, 64]), op=ALU.mult),
                             r=[PK[6], kDEN], w=[kOT])
                    if int(os.environ.get('KD_C', 9)) < 6:
                        continue
                    otf = OTOK.rearrange("p h d -> p (h d)")
                    for j in range(8):
                        S.op(PE_, lambda e, j=j: e.transpose(out=psb(7)[:, j * 128:(j + 1) * 128], in_=otf[:, j * 128:(j + 1) * 128], identity=identb[:, :]), r=[kOT, kC], w=[PK[7]])
                    S.op(V, lambda e, cs_=cs_: e.tensor_copy(out=OTT[:, :, cs_], in_=psb(7)[:, 0:1024].rearrange("p (a b) -> p a b", b=128)), r=[PK[7]], w=[kOTT])
                if int(os.environ.get('KD_C', 9)) < 7:
                    return
                if not sample:
                    ls_ = slice((nch - 1) * 128, nch * 128)
                    S.op(V, lambda e: e.tensor_copy(out=KTC[:, jl, :, :], in_=KT[:, :, ls_]), r=[kKT], w=[('KTC',)])
                    S.op(V, lambda e: e.tensor_copy(out=VC[:, jl, :, :], in_=VTa[:, (nch - 1) * 4:nch * 4, :]), r=[kVTa], w=[('VC',)])
                for d in range(8):
                    b = d % 2
                    for j in range(8):
                        S.op(PE_, lambda e, b=b, j=j, d=d: e.matmul(PS[b][:, 0:T], wo[:, j, d * 128:(d + 1) * 128], OTT[:, j, :], start=(j == 0), stop=(j == 7)),
                             r=kW + [kOTT], w=[PK[b]], inc=(j == 7))
                    resid_add(l, 2, d, PS[b][:, 0:T], PK[b], rt)

            for l in range(4):
                if str(l) not in os.environ.get("KD_LAYERS", "0123"):
                    continue
                if "m" in os.environ.get("KD_PARTS", "mf"):
                    if l % 2 == 0:
                        mixer_ab(l)
                    else:
                        mixer_c(l)
                if "f" in os.environ.get("KD_PARTS", "mf"):
                    ffn(l)
            S.barrier()
            for c in range(nch):
                for b in range(2):
                    for j in range(4):
                        kc = b * 4 + j
                        S.op(PE_, lambda e, b=b, j=j, kc=kc, c=c: e.transpose(out=PS[b][:, j * 128:(j + 1) * 128], in_=xT[:, kc, c * 128:(c + 1) * 128], identity=ident[:, :]),
                             r=[kX, kC], w=[PK[b]])
                    S.op(A_, lambda e, b=b: e.copy(out=stage[:, b * 512:(b + 1) * 512], in_=PS[b][:, :]), r=[PK[b]], w=[('stage',)])
                S.dma('sync', yout[tok0 + c * 128: tok0 + (c + 1) * 128, :], stage[:, :], r=[('stage',)], chan=('ostage',))

        for pi in range(int(os.environ.get("KD_NPASS", NPASS))):
            run_pass(False, pi)
        if os.environ.get("KD_SAMPLE", "1") == "1":
            run_pass(True, 0)
        S.finish()

        with nc.Block() as block:
            @block.sync
            def _(e):
                S.replay('sync', e)

            @block.tensor
            def _(e):
                S.replay('pe', e)

            @block.scalar
            def _(e):
                S.replay('act', e)

            @block.vector
            def _(e):
                S.replay('dve', e)

            @block.gpsimd
            def _(e):
                S.replay('pool', e)
    return nc


def _slot_perm():
    perm = np.zeros(16, np.int64)
    for kap in range(4):
        for g in range(4):
            s = 2 * (g + 4 * (kap // 2)) + (kap % 2)
            perm[s] = 4 * kap + g
    return perm


def _consts():
    c = {}
    c["ident"] = np.eye(128, dtype=np.float32)
    s = np.arange(128)[:, None]
    t = np.arange(128)[None, :]
    c["mP"] = np.where(s <= t, 0.0, NEG).astype(np.float32)
    same = (s // 8) == (t // 8)
    c["mS"] = np.where(same & (s <= t), 0.0, NEG).astype(np.float32)
    c["acur"] = np.tile(np.where(s <= t, 0.0, ANEG).astype(np.float32), (1, 4))
    c["aprev"] = np.tile(np.where(s > t, 0.0, ANEG).astype(np.float32), (1, 4))
    c["anew"] = np.tile(np.where(same & (s <= t), 0.0, ANEG).astype(np.float32), (1, 4))
    ac = np.full((128, 16, 128), ANEG, np.float32)
    for i in range(16):
        for j in range(8):
            tt = 8 * i + j
            ac[j + 1:, i, tt] = 0.0
    c["acache"] = ac
    oh = np.zeros((128, 16), np.float32)
    oh[np.arange(128), np.arange(128) // 8] = 1.0
    c["onehot"] = oh
    sel = np.zeros((4, 4, 128), np.float32)
    for h in range(4):
        sel[h, h, :] = 1.0
    c["sel4"] = sel
    inv = 10000.0 ** (-np.arange(32, dtype=np.float64) / 32)
    pos = np.arange(SEQ, dtype=np.float64)[:, None] * inv[None, :]
    c["cosP"] = np.cos(pos).astype(np.float32)
    c["sinP"] = np.sin(pos).astype(np.float32)
    ps = (8192 + (np.arange(128) % 8)).astype(np.float64)[:, None] * inv[None, :]
    c["cosS"] = np.cos(ps).astype(np.float32)
    c["sinS"] = np.sin(ps).astype(np.float32)
    return c


_NC = None


def kernel(x_prompt, x_sample, c_prompt, c_sample, state_mlstm_C, state_mlstm_n, state_mlstm_m,
           state_sconv, cache_win_k, cache_win_v, state_ffn_conv,
           norm1, norm2, w_ada, b_ada, a_w_in, a_b_if, a_out_norm, a_conv_w, a_w_out,
           c_w_qkv, c_q_norm, c_k_norm, c_sink, c_w_out, f_w_up, f_conv_w, f_w_down):
    global _NC
    f = lambda a: np.ascontiguousarray(np.asarray(a, dtype=np.float32))
    perm = _slot_perm()
    common = dict(_consts())
    common["w_ada"] = f(w_ada)
    common["b_adaT"] = f(np.asarray(b_ada).reshape(4, 48, 128).transpose(2, 0, 1))
    common["norm1T"] = f(np.asarray(norm1).reshape(4, 8, 128).transpose(2, 0, 1))
    common["norm2T"] = f(np.asarray(norm2).reshape(4, 8, 128).transpose(2, 0, 1))
    common["a_w_in"] = f(a_w_in)
    common["a_bif"] = f(np.asarray(a_b_if).reshape(2, 2, 4).transpose(2, 0, 1))
    common["a_onT"] = f(np.asarray(a_out_norm).reshape(2, 4, 128).transpose(2, 0, 1))
    common["a_cwT"] = f(np.asarray(a_conv_w).reshape(2, 3, 4, 128).transpose(3, 0, 1, 2))
    common["a_w_out"] = f(a_w_out)
    wq = np.asarray(c_w_qkv)
    qcols = np.concatenate([np.arange(64) + 64 * h for h in perm])
    common["c_w_qkv"] = f(np.concatenate([wq[:, :, qcols], wq[:, :, 1024:]], axis=2))
    common["c_qn"] = f(c_q_norm)
    common["c_kn"] = f(c_k_norm)
    common["c_sink"] = f(np.asarray(c_sink)[:, perm])
    common["c_w_out"] = f(np.asarray(c_w_out)[:, qcols, :])
    common["f_w_up"] = f(f_w_up)
    common["f_cwT"] = f(np.asarray(f_conv_w).reshape(4, 3, 44, 128).transpose(3, 0, 1, 2))
    common["f_w_down"] = f(f_w_down)
    in_maps = []
    for c in range(NCORE):
        b = c // 4
        sl = slice(16 * c, 16 * c + 16)
        m = dict(common)
        m["xp"] = f(np.asarray(x_prompt)[b])
        m["xs"] = f(np.asarray(x_sample)[sl].reshape(128, D))
        m["call"] = f(np.concatenate([np.asarray(c_prompt)[b:b + 1], np.asarray(c_sample)[sl]], axis=0))
        m["stC"] = f(np.asarray(state_mlstm_C)[:, sl])
        m["stn"] = f(np.asarray(state_mlstm_n)[:, sl].reshape(2, 64, 128))
        m["stmT"] = f(np.asarray(state_mlstm_m)[:, sl].transpose(2, 0, 1))
        m["stsc"] = f(np.asarray(state_sconv)[:, sl].reshape(2, 32, 512))
        m["ck"] = f(np.asarray(cache_win_k)[:, sl].reshape(2, 16, 128, 256))
        m["cv"] = f(np.asarray(cache_win_v)[:, sl].reshape(2, 16, 128, 256))
        m["stffn"] = f(np.asarray(state_ffn_conv)[:, sl].reshape(4, 32, 5632))
        in_maps.append(m)
    if _NC is None:
        _NC = build()
    res = run_bass_kernel_spmd(_NC, in_maps, core_ids=list(range(NCORE)))
    R = res.results
    pc = [0, 4]
    cat_p = lambda nm, shp: np.stack([R[c][nm] for c in pc], axis=1).reshape(shp)
    y_prompt = np.stack([R[c]["yp"] for c in pc], axis=0)
    y_sample = np.concatenate([R[c]["ys"].reshape(16, 8, D) for c in range(NCORE)], axis=0)
    p_C = cat_p("pC", (2, 2, 4, 128, 128))
    p_n = cat_p("pn", (2, 2, 4, 128))
    p_m = cat_p("pm", (2, 2, 4))
    p_sc = cat_p("psc", (2, 2, 2, 512))
    p_wk = cat_p("pwk", (2, 2, 128, 4, 64))
    p_wv = cat_p("pwv", (2, 2, 128, 4, 64))
    p_ffn = cat_p("pffn", (4, 2, 2, 5632))
    cat_s = lambda nm, shp: np.concatenate([R[c][nm].reshape(shp) for c in range(NCORE)], axis=1)
    s_C = cat_s("sC", (2, 16, 4, 128, 128))
    s_n = cat_s("sn", (2, 16, 4, 128))
    s_m = np.concatenate([R[c]["smT"].transpose(1, 2, 0) for c in range(NCORE)], axis=1)
    s_sc = cat_s("ssc", (2, 16, 2, 512))
    s_wk = cat_s("swk", (2, 16, 128, 4, 64))
    s_wv = cat_s("swv", (2, 16, 128, 4, 64))
    s_ffn = cat_s("sffn", (4, 16, 2, 5632))
    outs = (y_prompt, y_sample, p_C, p_n, p_m, p_sc, p_wk, p_wv, p_ffn,
            s_C, s_n, s_m, s_sc, s_wk, s_wv, s_ffn)
    return tuple(np.ascontiguousarray(o, dtype=np.float32) for o in outs)
```

```python
import contextlib
import os
import numpy as np
import concourse.bass as bass
import concourse.mybir as mybir
from concourse.bass_utils import run_bass_kernel_spmd

F32 = mybir.dt.float32
BF16 = mybir.dt.bfloat16
AF = mybir.ActivationFunctionType
ALU = mybir.AluOpType
AX = mybir.AxisListType

D = 1024
KC = 8
DFF = 2816
NF = 22
INA = 3592
SEQ = 8192
TP = 512
NPASS = SEQ // TP
EPS = 1e-6
NEG = -1.0e30
ANEG = -1.0e9
NCORE = 8
SLOT = 13312


class Sch:
    ROT = 30000

    def __init__(self, nc, stack):
        self.nc = nc
        self.stack = stack
        self.names = ('pe', 'act', 'dve', 'pool', 'sync')
        self.prog = {e: [] for e in self.names}
        self.cnt = {e: 0 for e in ('pe', 'act', 'dve', 'pool')}
        self.sems = {}
        self.seen = {e: {} for e in self.names}
        self.lastw = {}
        self.readers = {}
        self.dtot = {}

    def sem(self, key):
        if key not in self.sems:
            nm = "s" + str(len(self.sems))
            self.sems[key] = self.stack.enter_context(self.nc.semaphore(nm))
        return self.sems[key]

    def _deps(self, r, w):
        ev = []
        for k in r:
            if k in self.lastw:
                ev.append(self.lastw[k])
            if k[0] == 'P':
                ev.extend(self.readers.get(k, []))
        for k in w:
            if k in self.lastw:
                ev.append(self.lastw[k])
            ev.extend(self.readers.get(k, []))
        return ev

    def _wait(self, en, evs, skip=None):
        need = {}
        for (sk, v) in evs:
            if skip is not None and sk == skip:
                continue
            if en == 'pe' and sk[0] == 'pe':
                continue
            if need.get(sk, 0) < v:
                need[sk] = v
        for sk, v in need.items():
            if self.seen[en].get(sk, 0) < v:
                self.prog[en].append(('w', self.sem(sk), v))
                self.seen[en][sk] = v

    def _commit(self, ev, r, w):
        for k in r:
            self.readers.setdefault(k, []).append(ev)
        for k in w:
            self.lastw[k] = ev
            self.readers[k] = []

    def op(self, en, fn, r=(), w=(), inc=True):
        self._wait(en, self._deps(r, w))
        c = self.cnt[en] + 1
        sk = (en, (c - 1) // self.ROT)
        v = (c - 1) % self.ROT + 1
        if inc:
            self.cnt[en] = c
            self.prog[en].append(('i', fn, self.sem(sk), 1))
        else:
            self.prog[en].append(('i', fn, None, 0))
        self._commit((sk, v), r, w)

    def dma(self, q, out, in_, r=(), w=(), chan=None, **kw):
        if chan is None:
            chan = w[0] if len(w) else ('o',) + tuple(r[0])
        sk = ('d', chan)
        self._wait(q, self._deps(r, w), skip=sk)
        self.dtot[chan] = self.dtot.get(chan, 0) + 16
        fn = (lambda e, out=out, in_=in_, kw=kw: e.dma_start(out=out, in_=in_, allow_slow_non_contiguous=True, **kw))
        self.prog[q].append(('i', fn, self.sem(sk), 16))
        self._commit((sk, self.dtot[chan]), r, w)

    def barrier(self):
        evs = []
        for e in ('pe', 'act', 'dve', 'pool'):
            c = self.cnt[e]
            if c > 0:
                evs.append(((e, (c - 1) // self.ROT), (c - 1) % self.ROT + 1))
        for ch, t in self.dtot.items():
            if ch[0] == 'W':
                continue
            evs.append((('d', ch), t))
        for e in self.names:
            need = [x for x in evs if not (x[0][0] == e)]
            for (sk, v) in need:
                if self.seen[e].get(sk, 0) < v:
                    self.prog[e].append(('w', self.sem(sk), v))
                    self.seen[e][sk] = v

    def finish(self):
        evs = []
        for ch, t in self.dtot.items():
            evs.append((('d', ch), t))
        for e in ('pe', 'act', 'dve', 'pool'):
            c = self.cnt[e]
            if c > 0:
                evs.append(((e, (c - 1) // self.ROT), (c - 1) % self.ROT + 1))
        for (sk, v) in evs:
            self.prog['sync'].append(('w', self.sem(sk), v))

    def replay(self, en, eng):
        for it in self.prog[en]:
            if it[0] == 'w':
                eng.wait_ge(it[1], it[2])
            else:
                ins = it[1](eng)
                if it[2] is not None:
                    ins.then_inc(it[2], it[3])


class Arena:
    def __init__(self, t, words):
        self.t = t
        self.words = words
        self.off = 0
        self.gen = 0

    def reset(self):
        self.off = 0
        self.gen += 1

    def get(self, name, shape, dt=F32, parts=128):
        n = 1
        for s in shape[1:]:
            n *= s
        if dt == BF16:
            w = (n + 1) // 2
        else:
            w = n
        w = (w + 7) // 8 * 8
        assert self.off + w <= self.words, (name, self.off, w, self.words)
        ap = self.t[0:shape[0], self.off:self.off + w]
        self.off += w
        if dt == BF16:
            ap = ap.bitcast(BF16)[:, 0:n]
        else:
            ap = ap[:, 0:n]
        if len(shape) == 3:
            ap = ap.rearrange("p (a b) -> p a b", b=shape[2])
        elif len(shape) == 4:
            ap = ap.rearrange("p (a b c) -> p a b c", b=shape[2], c=shape[3])
        return ap, (name, self.gen)


def build():
    nc = bass.Bass("TRN2", target_bir_lowering=False)
    din = lambda n, s: nc.dram_tensor(n, list(s), F32, kind="ExternalInput").ap()
    dout = lambda n, s: nc.dram_tensor(n, list(s), F32, kind="ExternalOutput").ap()
    I = {}
    for n, s in [("xp", (SEQ, D)), ("xs", (128, D)), ("call", (17, D)),
                 ("stC", (2, 16, 4, 128, 128)), ("stn", (2, 64, 128)), ("stmT", (4, 2, 16)),
                 ("stsc", (2, 32, 512)), ("ck", (2, 16, 128, 256)), ("cv", (2, 16, 128, 256)),
                 ("stffn", (4, 32, 5632)),
                 ("w_ada", (4, D, 6144)), ("b_adaT", (128, 4, 48)), ("norm1T", (128, 4, 8)),
                 ("norm2T", (128, 4, 8)), ("a_w_in", (2, D, INA)), ("a_bif", (4, 2, 2)),
                 ("a_onT", (128, 2, 4)), ("a_cwT", (128, 2, 3, 4)), ("a_w_out", (2, D, D)),
                 ("c_w_qkv", (2, D, 1536)), ("c_qn", (2, 64)), ("c_kn", (2, 64)), ("c_sink", (2, 16)),
                 ("c_w_out", (2, D, D)), ("f_w_up", (4, D, 2 * DFF)), ("f_cwT", (128, 4, 3, 44)),
                 ("f_w_down", (4, DFF, D)),
                 ("ident", (128, 128)), ("mP", (128, 128)), ("mS", (128, 128)),
                 ("acur", (128, 512)), ("aprev", (128, 512)), ("anew", (128, 512)),
                 ("acache", (128, 16, 128)), ("onehot", (128, 16)), ("sel4", (4, 4, 128)),
                 ("cosP", (SEQ, 32)), ("sinP", (SEQ, 32)), ("cosS", (128, 32)), ("sinS", (128, 32))]:
        I[n] = din(n, s)
    O = {}
    for n, s in [("yp", (SEQ, D)), ("ys", (128, D)), ("pC", (2, 4, 128, 128)), ("pn", (2, 4, 128)),
                 ("pm", (2, 4)), ("psc", (2, 2, 512)), ("pwk", (2, 128, 256)), ("pwv", (2, 128, 256)),
                 ("pffn", (4, 2, 5632)), ("sC", (2, 16, 4, 128, 128)), ("sn", (2, 64, 128)),
                 ("smT", (4, 2, 16)), ("ssc", (2, 32, 512)), ("swk", (2, 16, 128, 256)),
                 ("swv", (2, 16, 128, 256)), ("sffn", (4, 32, 5632))]:
        O[n] = dout(n, s)

    with contextlib.ExitStack() as st:
        S = Sch(nc, st)
        sbt = lambda n, s, dt=F32: st.enter_context(nc.sbuf_tensor("sb_" + n, list(s), dt))
        PS = [st.enter_context(nc.psum_tensor("ps%d" % i, [128, 512], F32)) for i in range(8)]
        PK = [('P', i) for i in range(8)]
        psb = lambda i: PS[i][:, :].bitcast(BF16)

        xT = sbt("xT", [128, 8, TP]); kX = ('xT',)
        hT = sbt("hT", [128, 8, TP], BF16); kH = ('hT',)
        MOD = sbt("MOD", [128, 4, 48, 17]); kM = ('MOD',)
        WB = sbt("WB", [128, 3 * SLOT], BF16)
        WKt = sbt("WK", [128, 15360])
        AR = Arena(WKt, 15360)
        ident = sbt("ident", [128, 128]); identb = sbt("identb", [128, 128], BF16)
        onesb = sbt("onesb", [128, 128], BF16)
        mPb = sbt("mPb", [128, 128], BF16); mSb = sbt("mSb", [128, 128], BF16)
        acurb = sbt("acurb", [128, 512], BF16); aprevb = sbt("aprevb", [128, 512], BF16)
        anewb = sbt("anewb", [128, 512], BF16); acacheb = sbt("acacheb", [128, 16, 128], BF16)
        onehot = sbt("onehot", [128, 16]); sel4 = sbt("sel4", [4, 4, 128])
        onesrow = sbt("onesrow", [4, TP])
        cosT = sbt("cosT", [128, 4, 32]); sinT = sbt("sinT", [128, 4, 32])
        n1T = sbt("n1T", [128, 4, 8]); n2T = sbt("n2T", [128, 4, 8]); badaT = sbt("badaT", [128, 4, 48])
        fcw = sbt("fcw", [128, 4, 3, 44]); acw = sbt("acw", [128, 2, 3, 4]); aon = sbt("aon", [128, 2, 4])
        bif = sbt("bif", [4, 2, 2]); nbif = sbt("nbif", [4, 2, 2])
        GQ = sbt("GQ", [128, 2, 64]); GK = sbt("GK", [128, 2, 64]); SK = sbt("SK", [128, 2, 16])
        SINKE = sbt("SINKE", [128, 2, 16]); NEGMA = sbt("NEGMA", [128, 2]); tmpc = sbt("tmpc", [128, 4])
        Cst = sbt("Cst", [128, 2, 4, 129]); MROW = sbt("MROW", [4, 2])
        ffc = sbt("ffc", [128, 4, 44, 2]); scc = sbt("scc", [128, 2, 4, 2])
        KTC = sbt("KTC", [128, 2, 2, 128], BF16); VC = sbt("VC", [128, 2, 4, 65], BF16)
        scT = sbt("scT", [128, 8, 17], BF16)
        stage = sbt("stage", [128, D]); call_sb = stage[0:17, :]
        kC = ('const',)

        V, A_, P_, PE_ = 'dve', 'act', 'pool', 'pe'

        def ld(q, dst, src, key, **kw):
            S.dma(q, dst, src, w=[key], chan=key if key != kC else ('c0',), **kw)
        for dst, nm in [(ident, "ident"), (onehot, "onehot"), (sel4, "sel4"), (n1T, "norm1T"), (n2T, "norm2T"),
                        (badaT, "b_adaT"), (fcw, "f_cwT"), (acw, "a_cwT"), (aon, "a_onT"), (bif, "a_bif")]:
            S.dma('sync', dst[:], I[nm], w=[kC], chan=('c0',))
        S.dma('sync', call_sb, I["call"], w=[('stage',)], chan=('istage',))
        for l in range(2):
            S.dma('sync', GQ[:, l, :], I["c_qn"][l].partition_broadcast(128), w=[kC], chan=('c0',))
            S.dma('sync', GK[:, l, :], I["c_kn"][l].partition_broadcast(128), w=[kC], chan=('c0',))
            S.dma('sync', SK[:, l, :], I["c_sink"][l].partition_broadcast(128), w=[kC], chan=('c0',))
        for dst, nm in [(identb, "ident"), (mPb, "mP"), (mSb, "mS"), (acurb, "acur"), (aprevb, "aprev"),
                        (anewb, "anew"), (acacheb, "acache")]:
            S.dma('pool', dst[:], I[nm], w=[kC], chan=('c1',))
        kC2 = ('const2',)
        S.op(V, lambda e: e.memset(onesb[:], 1.0), r=[kC], w=[kC2])
        S.op(V, lambda e: e.memset(onesrow[:], 1.0), w=[kC2])
        S.op(V, lambda e: e.tensor_scalar(out=nbif[:], in0=bif[:], scalar1=-1.0, scalar2=None, op0=ALU.mult), r=[kC], w=[kC2])
        S.op(V, lambda e: e.memset(Cst[:], 0.0), w=[('Cst', 0), ('Cst', 1)])
        S.op(V, lambda e: e.memset(MROW[:], 0.0), w=[('MROW',)])
        S.op(V, lambda e: e.memset(ffc[:], 0.0), w=[('ffc',)])
        S.op(V, lambda e: e.memset(scc[:], 0.0), w=[('scc',)])
        S.op(V, lambda e: e.memset(VC[:], 1.0), w=[('VC',)])
        for l in range(2):
            S.op(V, lambda e, l=l: e.tensor_reduce(out=tmpc[:, 0:1], in_=GQ[:, l, :], axis=AX.X, op=ALU.max, apply_absolute_value=True), r=[kC], w=[('tmpc',)])
            S.op(V, lambda e, l=l: e.tensor_reduce(out=tmpc[:, 1:2], in_=GK[:, l, :], axis=AX.X, op=ALU.max, apply_absolute_value=True), r=[kC], w=[('tmpc',)])
            S.op(V, lambda e, l=l: e.scalar_tensor_tensor(out=NEGMA[:, l:l + 1], in0=tmpc[:, 0:1], scalar=-8.0, in1=tmpc[:, 1:2], op0=ALU.mult, op1=ALU.mult), r=[('tmpc',)], w=[kC2])
            S.op(A_, lambda e, l=l: e.activation(out=SINKE[:, l, :], in_=SK[:, l, :], func=AF.Exp, bias=NEGMA[:, l:l + 1], scale=1.0), r=[kC, kC2], w=[('SINKE',)])

        S.op(A_, lambda e: e.activation(out=call_sb, in_=call_sb, func=AF.Silu), r=[('stage',)], w=[('stage',)])
        for kc in range(8):
            S.op(PE_, lambda e, kc=kc: e.transpose(out=PS[0][:, kc * 17:(kc + 1) * 17], in_=call_sb[:, kc * 128:(kc + 1) * 128], identity=ident[0:17, 0:17]),
                 r=[('stage',), kC], w=[PK[0]])
        S.op(V, lambda e: e.tensor_copy(out=scT[:], in_=PS[0][:, 0:136].rearrange("p (a b) -> p a b", b=17)), r=[PK[0]], w=[('scT',)])
        wa = [WB[:, i * 6144:(i + 1) * 6144] for i in range(2)]
        for l in range(4):
            for kc in range(8):
                sl = (l * 8 + kc) % 2
                S.dma('pool', wa[sl], I["w_ada"][l, kc * 128:(kc + 1) * 128, :], w=[('W', sl)], max_dma_last_dim=4096)
                for fc in range(48):
                    b = 1 + fc // 24
                    o = (fc % 24) * 17
                    S.op(PE_, lambda e, sl=sl, fc=fc, b=b, o=o, kc=kc: e.matmul(PS[b][:, o:o + 17], wa[sl][:, fc * 128:(fc + 1) * 128], scT[:, kc, :],
                                                                            start=(kc == 0 and fc % 24 == 0), stop=(kc == 7), skip_group_check=True),
                         r=[('W', sl), ('scT',)], w=[PK[b]], inc=(fc == 47))
            for b in range(2):
                S.op(V, lambda e, l=l, b=b: e.tensor_tensor(out=MOD[:, l, b * 24:(b + 1) * 24, :], in0=PS[1 + b][:, 0:408].rearrange("p (a b) -> p a b", b=17),
                                                           in1=badaT[:, l, b * 24:(b + 1) * 24].unsqueeze(2).broadcast_to([128, 24, 17]), op=ALU.add),
                     r=[PK[1 + b], kC], w=[kM])
            for (c0, nT) in [(8, n1T), (32, n2T)]:
                S.op(V, lambda e, l=l, c0=c0, nT=nT: e.scalar_tensor_tensor(out=MOD[:, l, c0:c0 + 8, :], in0=MOD[:, l, c0:c0 + 8, :], scalar=1.0,
                                                                           in1=nT[:, l, :].unsqueeze(2).broadcast_to([128, 8, 17]), op0=ALU.add, op1=ALU.mult),
                     r=[kM, kC], w=[kM])

        def fm_rows_out(src_fn, nrows, dst_rows_fn, nchunks, rkeys):
            for c0 in range(0, nchunks, 4):
                n = min(4, nchunks - c0)
                for j in range(n):
                    S.op(PE_, lambda e, j=j, c0=c0: e.transpose(out=PS[3][0:nrows, j * 128:(j + 1) * 128], in_=src_fn(c0 + j), identity=ident[:, :]),
                         r=rkeys + [kC], w=[PK[3]])
                S.op(A_, lambda e, n=n: e.copy(out=stage[0:nrows, 0:n * 128], in_=PS[3][0:nrows, 0:n * 128]), r=[PK[3]], w=[('stage',)])
                S.dma('sync', dst_rows_fn(c0 * 128, n * 128), stage[0:nrows, 0:n * 128], r=[('stage',)], chan=('ostage',))

        def rows_to_fm(src_rows_fn, nrows, dst_fn, nchunks, wkeys):
            for c0 in range(0, nchunks, 8):
                n = min(8, nchunks - c0)
                S.dma('sync', stage[0:nrows, 0:n * 128], src_rows_fn(c0 * 128, n * 128), w=[('stage',)], chan=('istage',))
                for j in range(n):
                    S.op(PE_, lambda e, j=j: e.transpose(out=PS[3][:, j * 32:j * 32 + nrows], in_=stage[0:nrows, j * 128:(j + 1) * 128], identity=ident[0:nrows, 0:nrows]),
                         r=[('stage',), kC], w=[PK[3]])
                for j in range(n):
                    S.op(V, lambda e, j=j, c0=c0: e.tensor_copy(out=dst_fn(c0 + j), in_=PS[3][:, j * 32:j * 32 + nrows]), r=[PK[3]], w=wkeys)

        def run_pass(sample, pi):
            T = 128 if sample else TP
            nseq = 16 if sample else 1
            L = 8 if sample else 128
            LT = 8 if sample else TP
            nch = T // 128
            tok0 = 0 if sample else pi * TP
            last = sample or (pi == NPASS - 1)
            xin = I["xs"] if sample else I["xp"]
            yout = O["ys"] if sample else O["yp"]
            sq0 = 1 if sample else 0
            v3 = lambda ap: ap.rearrange("p (s l) -> p s l", l=LT)

            def modbc(l, ch, kc):
                return MOD[:, l, ch * 8 + kc, sq0:sq0 + nseq].unsqueeze(2).broadcast_to([128, nseq, LT])

            for c in range(nch):
                S.dma('sync', stage[:, :], xin[tok0 + c * 128: tok0 + (c + 1) * 128, :], w=[('stage',)], chan=('istage',))
                for b in range(2):
                    for j in range(4):
                        kc = b * 4 + j
                        S.op(PE_, lambda e, b=b, j=j, kc=kc: e.transpose(out=PS[b][:, j * 128:(j + 1) * 128], in_=stage[:, kc * 128:(kc + 1) * 128], identity=ident[:, :]),
                             r=[('stage',), kC], w=[PK[b]])
                    S.op(V if b == 0 else A_, (lambda e, b=b, c=c: e.tensor_copy(out=xT[:, b * 4:(b + 1) * 4, c * 128:(c + 1) * 128], in_=PS[b][:, :].rearrange("p (a b) -> p a b", b=128))) if b == 0 else
                         (lambda e, b=b, c=c: e.copy(out=xT[:, b * 4:(b + 1) * 4, c * 128:(c + 1) * 128], in_=PS[b][:, :].rearrange("p (a b) -> p a b", b=128))),
                         r=[PK[b]], w=[kX])
            if not sample:
                S.dma('sync', cosT[:, 0:nch, :], I["cosP"][tok0:tok0 + T, :].rearrange("(c p) i -> p c i", p=128), w=[('rope',)])
                S.dma('sync', sinT[:, 0:nch, :], I["sinP"][tok0:tok0 + T, :].rearrange("(c p) i -> p c i", p=128), w=[('rope',)])
            else:
                S.dma('sync', cosT[:, 0, :], I["cosS"], w=[('rope',)])
                S.dma('sync', sinT[:, 0, :], I["sinS"], w=[('rope',)])

            def norm_mod(l, which):
                cA, cB = (1, 0) if which == 1 else (4, 3)
                S.barrier(); AR.reset()
                sqb, ksq = AR.get("sq", [128, 8, T], BF16)
                rs, krs = AR.get("rs", [128, T])
                tmp, ktmp = AR.get("tmp", [128, T])
                S.op(A_, lambda e: e.activation(out=sqb, in_=xT[:, :, 0:T], func=AF.Square), r=[kX], w=[ksq])
                for kc in range(8):
                    S.op(PE_, lambda e, kc=kc: e.matmul(PS[0][:, 0:T], onesb[:, :], sqb[:, kc, :], start=(kc == 0), stop=(kc == 7)), r=[ksq, kC2], w=[PK[0]], inc=(kc == 7))
                S.op(A_, lambda e: e.activation(out=rs, in_=PS[0][:, 0:T], func=AF.Sqrt, bias=EPS, scale=1.0 / D), r=[PK[0]], w=[krs])
                S.op(V, lambda e: e.reciprocal(out=rs, in_=rs), r=[krs], w=[krs])
                for kc in range(8):
                    S.op(V, lambda e, kc=kc: e.tensor_tensor(out=tmp, in0=xT[:, kc, 0:T], in1=rs, op=ALU.mult), r=[kX, krs], w=[ktmp])
                    S.op(V, lambda e, kc=kc: e.tensor_tensor(out=v3(tmp), in0=v3(tmp), in1=modbc(l, cA, kc), op=ALU.mult), r=[ktmp, kM], w=[ktmp])
                    S.op(V, lambda e, kc=kc: e.tensor_tensor(out=v3(hT[:, kc, 0:T]), in0=v3(tmp), in1=modbc(l, cB, kc), op=ALU.add), r=[ktmp, kM], w=[kH])

            def resid_add(l, gch, d, psrc, pkey, tkey_ap):
                tmp, ktmp = tkey_ap
                if not sample:
                    S.op(V, lambda e: e.scalar_tensor_tensor(out=xT[:, d, 0:T], in0=psrc, scalar=MOD[:, l, gch * 8 + d, 0:1], in1=xT[:, d, 0:T], op0=ALU.mult, op1=ALU.add),
                         r=[pkey, kM, kX], w=[kX])
                    return
                S.op(V, lambda e: e.tensor_tensor(out=v3(tmp), in0=v3(psrc), in1=modbc(l, gch, d), op=ALU.mult), r=[pkey, kM], w=[ktmp])
                S.op(V, lambda e: e.tensor_tensor(out=xT[:, d, 0:T], in0=xT[:, d, 0:T], in1=tmp, op=ALU.add), r=[ktmp, kX], w=[kX])

            def conv3(dst3, src3, wfn, keys_r, keys_w, pool=False):
                E1 = P_ if pool else V
                if pool:
                    S.op(E1, lambda e: e.tensor_scalar(out=dst3, in0=src3[:, :, 0:LT], scalar1=wfn(0), scalar2=0.0, op0=ALU.mult, op1=ALU.add), r=keys_r, w=keys_w)
                else:
                    S.op(E1, lambda e: e.tensor_scalar(out=dst3, in0=src3[:, :, 0:LT], scalar1=wfn(0), scalar2=None, op0=ALU.mult), r=keys_r, w=keys_w)
                S.op(V, lambda e: e.scalar_tensor_tensor(out=dst3, in0=src3[:, :, 1:LT + 1], scalar=wfn(1), in1=dst3, op0=ALU.mult, op1=ALU.add), r=keys_r + keys_w, w=keys_w)
                S.op(V, lambda e: e.scalar_tensor_tensor(out=dst3, in0=src3[:, :, 2:LT + 2], scalar=wfn(2), in1=dst3, op0=ALU.mult, op1=ALU.add), r=keys_r + keys_w, w=keys_w)

            def ffn(l):
                gsz = [4, 4, 4, 4, 4, 2]
                gst = [0, 4, 8, 12, 16, 20]

                def wslot(g):
                    base = (g % 3) * SLOT
                    ug = WB[:, base:base + 4096].rearrange("p (k c) -> p k c", c=512)
                    ua = WB[:, base + 4096:base + 8192].rearrange("p (k c) -> p k c", c=512)
                    dn = WB[:, base + 8192:base + 12288].rearrange("p (j d) -> p j d", d=D)
                    return ug, ua, dn

                def issue(g):
                    ug, ua, dn = wslot(g)
                    n = gsz[g]; f0 = gst[g]
                    k = ('W', g % 3)
                    wu = I["f_w_up"][l].rearrange("(k p) c -> p k c", p=128)
                    S.dma('pool', ug[:, :, 0:n * 128], wu[:, :, f0 * 128:(f0 + n) * 128], w=[k], max_dma_last_dim=4096)
                    S.dma('pool', ua[:, :, 0:n * 128], wu[:, :, DFF + f0 * 128:DFF + (f0 + n) * 128], w=[k], max_dma_last_dim=4096)
                    S.dma('pool', dn[:, 0:n, :], I["f_w_down"][l, f0 * 128:(f0 + n) * 128, :].rearrange("(j p) d -> p j d", p=128), w=[k], max_dma_last_dim=4096)
                issue(0); issue(1)
                norm_mod(l, 2)
                if os.environ.get("KD_FFN", "") == "n":
                    return
                S.barrier(); AR.reset()
                UB, kUB = AR.get("UB", [128, 4, nseq * (LT + 2)])
                Y, kY = AR.get("Y", [128, 4, T])
                ACTB, kAB = AR.get("ACTB", [128, 4, T], BF16)
                rt = AR.get("rt", [128, T])
                if sample:
                    car, kcar = AR.get("car", [128, 44, 32])
                    rows_to_fm(lambda o, n: I["stffn"][l, :, o:o + n], 32, lambda c: car[:, c, :], 44, [kcar])
                    carv = lambda idx: car[:, idx, :].rearrange("p (s j) -> p s j", j=2)
                else:
                    kcar = ('ffc',)
                    carv = lambda idx: ffc[:, l, idx, :].unsqueeze(1)
                ub3 = lambda i: UB[:, i, :].rearrange("p (s k) -> p s k", k=LT + 2)

                for g in range(6):
                    if g + 2 < 6:
                        issue(g + 2)
                    ug, ua, dn = wslot(g)
                    kW = ('W', g % 3)
                    n = gsz[g]; f0 = gst[g]
                    for j in range(n):
                        ib = 2 * (j % 2)
                        for part in range(2):
                            i = ib + part
                            wv = ug if part == 0 else ua
                            idx = (0 if part == 0 else 22) + f0 + j
                            for kc in range(8):
                                S.op(PE_, lambda e, i=i, wv=wv, j=j, kc=kc: e.matmul(PS[i][:, 0:T], wv[:, kc, j * 128:(j + 1) * 128], hT[:, kc, 0:T], start=(kc == 0), stop=(kc == 7)),
                                     r=[kW, kH], w=[PK[i]], inc=(kc == 7))
                            S.op(A_, lambda e, i=i, idx=idx: e.copy(out=ub3(i)[:, :, 0:2], in_=carv(idx)), r=[kcar], w=[(kUB, i)])
                            S.op(A_, lambda e, i=i: e.copy(out=ub3(i)[:, :, 2:LT + 2], in_=v3(PS[i][:, 0:T])), r=[PK[i]], w=[(kUB, i)])
                            S.op(A_, lambda e, i=i, idx=idx: e.copy(out=carv(idx), in_=ub3(i)[:, :, LT:LT + 2]), r=[(kUB, i)], w=[kcar])
                            conv3(v3(Y[:, i, :]), ub3(i), lambda t, idx=idx: fcw[:, l, t, idx:idx + 1], [(kUB, i), kC], [(kY, i)], pool=True)
                        S.op(A_, lambda e, ib=ib: e.activation(out=Y[:, ib, :], in_=Y[:, ib, :], func=AF.Silu), r=[(kY, ib)], w=[(kY, ib)])
                        S.op(V, lambda e, j=j, ib=ib: e.tensor_tensor(out=ACTB[:, j, :], in0=Y[:, ib, :], in1=Y[:, ib + 1, :], op=ALU.mult), r=[(kY, ib), (kY, ib + 1)], w=[(kAB, j)])
                    for dh in range(2):
                        for d4 in range(4):
                            d = dh * 4 + d4
                            for j in range(n):
                                S.op(PE_, lambda e, d4=d4, d=d, j=j, dn=dn, n=n: e.matmul(PS[4 + d4][:, 0:T], dn[:, j, d * 128:(d + 1) * 128], ACTB[:, j, :], start=(j == 0), stop=(j == n - 1)),
                                     r=[kW, (kAB, j)], w=[PK[4 + d4]], inc=(j == n - 1))
                            resid_add(l, 5, d, PS[4 + d4][:, 0:T], PK[4 + d4], rt)
                if last:
                    nr = 32 if sample else 2
                    dst = O["sffn"] if sample else O["pffn"]
                    if sample:
                        fm_rows_out(lambda c: car[:, c, :], 32, lambda o, n: dst[l, :, o:o + n], 44, [kcar])
                    else:
                        fm_rows_out(lambda c: ffc[:, l, c, :], 2, lambda o, n: dst[l, :, o:o + n], 44, [kcar])

            def mixer_ab(l):
                i = l // 2
                win = WB[:, 0:8 * INA].rearrange("p (k c) -> p k c", c=INA)
                wout = WB[:, 8 * INA:8 * INA + 8 * D].rearrange("p (k c) -> p k c", c=D)
                kW = [('W', 0), ('W', 1), ('W', 2)]
                for kc in range(8):
                    S.dma('pool', win[:, kc, :], I["a_w_in"][i, kc * 128:(kc + 1) * 128, :], w=kW, chan=('W', 0), max_dma_last_dim=4096)
                S.dma('pool', wout, I["a_w_out"][i].rearrange("(k p) c -> p k c", p=128), w=kW, chan=('W', 0), max_dma_last_dim=4096)
                norm_mod(l, 1)
                S.barrier(); AR.reset()
                qT, kqT = AR.get("qT", [128, 4, T], BF16)
                kTt, kkT = AR.get("kT", [128, 4, T], BF16)
                sgT, ksg = AR.get("sgT", [128, 4, T], BF16)
                scTt, ksc = AR.get("scTt", [128, 4, T], BF16)
                hmT, khm = AR.get("hmT", [128, 4, T], BF16)
                KTOK, kkt = AR.get("KTOK", [128, nch, 512], BF16)
                VT, kvt = AR.get("VT", [128, nch * 4, 129], BF16)
                PB, kPB = AR.get("PB", [128, nseq * (LT + 2)])
                ZX, kZX = AR.get("ZX", [128, T])
                U, kU = AR.get("U", [128, T])
                rt = AR.get("rt", [128, T]) if sample else (None, None)
                rows = {}
                for nm in ["ig", "l1", "cs", "aa", "A", "negm", "prev", "lastr"]:
                    rows[nm] = AR.get("r_" + nm, [4, T], parts=4)
                NEGAX, kNX = AR.get("NEGAX", [4, nseq + T], parts=4)
                MIN, kMIN = AR.get("MIN", [4, 16], parts=4)
                nset = 1 if sample else 2
                COLSb = [AR.get("COLS%d" % z, [128, 20]) for z in range(2)]
                B = []
                for z in range(nset):
                    d_ = {}
                    d_["SA"] = AR.get("SA%d" % z, [128, nseq + 128])
                    d_["WT"] = AR.get("WT%d" % z, [128, 128], BF16)
                    d_["PT"] = AR.get("PT%d" % z, [128, 128], BF16)
                    d_["PVs"] = AR.get("PVs%d" % z, [128, 129])
                    d_["NUM"] = AR.get("NUM%d" % z, [128, 129])
                    d_["JK"] = AR.get("JK%d" % z, [128, 128], BF16)
                    d_["HN"] = AR.get("HN%d" % z, [128, 128], BF16)
                    d_["VS"] = AR.get("VS%d" % z, [128, 129], BF16)
                    d_["SM"] = AR.get("SM%d" % z, [128, 16])
                    d_["WSI"] = AR.get("WSI%d" % z, [128, 16])
                    d_["WCB"] = AR.get("WCB%d" % z, [128, 16])
                    d_["QPAD"] = AR.get("QPAD%d" % z, [128, nseq * 136])
                    B.append(d_)
                if sample:
                    CS_, kCS = AR.get("CSs", [128, 64, 129])
                    S.dma('sync', CS_[:, :, 0:128], I["stC"][i].rearrange("s h k v -> k (s h) v"), w=[kCS])
                    rows_to_fm(lambda o, n: I["stn"][i, :, o:o + n], 64, lambda c: CS_[:, :, 128], 1, [kCS])
                    S.dma('sync', MIN, I["stmT"][:, i, :], w=[kMIN])
                    csv = lambda s_, h: CS_[:, s_ * 4 + h, :]
                    pcar, kpc = AR.get("pcar", [128, 4, 32])
                    rows_to_fm(lambda o, n: I["stsc"][i, :, o:o + n], 32, lambda c: pcar[:, c, :], 4, [kpc])
                    pcv = lambda ch: pcar[:, ch, :].rearrange("p (s j) -> p s j", j=2)
                else:
                    kCS = ('Cst', i)
                    csv = lambda s_, h: Cst[:, i, h, :]
                    kMIN = ('MROW',)
                    kpc = ('scc',)
                    pcv = lambda ch: scc[:, i, ch, :].unsqueeze(1)
                S.op(V, lambda e: e.memset(VT, 1.0), w=[kvt])
                for z in range(nset):
                    S.op(V, lambda e, z=z: e.memset(B[z]["QPAD"][0], 0.0), w=[B[z]["QPAD"][1]])

                def proj_fm(c0, handler, tag):
                    b = proj_fm.n % 2
                    proj_fm.n += 1
                    for kc in range(8):
                        S.op(PE_, lambda e, b=b, kc=kc: e.matmul(PS[b][:, 0:T], win[:, kc, c0:c0 + 128], hT[:, kc, 0:T], start=(kc == 0), stop=(kc == 7)),
                             r=kW + [kH], w=[PK[b]], inc=(kc == 7))
                    handler(PS[b][:, 0:T], PK[b])
                proj_fm.n = 0
                for h in range(4):
                    proj_fm(h * 128, lambda p, k, h=h: S.op(A_, lambda e: e.copy(out=qT[:, h, :], in_=p), r=[k], w=[kqT]), "q")
                    proj_fm(512 + h * 128, lambda p, k, h=h: S.op(A_, lambda e: e.activation(out=kTt[:, h, :], in_=p, func=AF.Copy, scale=128.0 ** -0.5), r=[k], w=[kkT]), "k")
                    proj_fm(1536 + h * 128, lambda p, k, h=h: S.op(A_, lambda e: e.activation(out=sgT[:, h, :], in_=p, func=AF.Sigmoid), r=[k], w=[ksg]), "o")
                pb3 = PB.rearrange("p (s k) -> p s k", k=LT + 2)
                for ch in range(4):
                    proj_fm(3080 + ch * 128, lambda p, k: S.op(A_, lambda e: e.copy(out=ZX, in_=p), r=[k], w=[kZX]), "zx")
                    S.op(A_, lambda e, ch=ch: e.copy(out=pb3[:, :, 0:2], in_=pcv(ch)), r=[kpc], w=[kPB])
                    proj_fm(2568 + ch * 128, lambda p, k: S.op(V, lambda e: e.tensor_tensor(out=pb3[:, :, 2:LT + 2], in0=v3(p), in1=v3(ZX), op=ALU.mult), r=[k, kZX], w=[kPB]), "zc")
                    S.op(A_, lambda e, ch=ch: e.copy(out=pcv(ch), in_=pb3[:, :, LT:LT + 2]), r=[kPB], w=[kpc])
                    conv3(v3(U), pb3, lambda t, ch=ch: acw[:, i, t, ch:ch + 1], [kPB, kC], [kU])
                    proj_fm(2056 + ch * 128, lambda p, k, ch=ch: S.op(V, lambda e: e.tensor_tensor(out=scTt[:, ch, :], in0=p, in1=U, op=ALU.mult), r=[k, kU], w=[ksc]), "zb")
                if last:
                    if sample:
                        fm_rows_out(lambda c: pcar[:, c, :], 32, lambda o, n: O["ssc"][i, :, o:o + n], 4, [kpc])
                    else:
                        fm_rows_out(lambda c: scc[:, i, c, :], 2, lambda o, n: O["psc"][i, :, o:o + n], 4, [kpc])
                rg = lambda nm: rows[nm][0]
                kg = lambda nm: rows[nm][1]
                for gi, (c0, nm) in enumerate([(2048, "ig"), (2052, "l1")]):
                    for kc in range(8):
                        S.op(PE_, lambda e, kc=kc, c0=c0: e.matmul(PS[2][0:4, 0:T], win[:, kc, c0:c0 + 4], hT[:, kc, 0:T], start=(kc == 0), stop=(kc == 7)),
                             r=kW + [kH], w=[PK[2]], inc=(kc == 7))
                    if gi == 0:
                        S.op(A_, lambda e: e.activation(out=rg("ig"), in_=PS[2][0:4, 0:T], func=AF.Identity, bias=bif[:, i, 0:1], scale=1.0), r=[PK[2], kC], w=[kg("ig")])
                    else:
                        S.op(A_, lambda e: e.activation(out=rg("l1"), in_=PS[2][0:4, 0:T], func=AF.Exp, bias=nbif[:, i, 1:2], scale=-1.0), r=[PK[2], kC2], w=[kg("l1")])
                        S.op(A_, lambda e: e.activation(out=rg("l1"), in_=rg("l1"), func=AF.Ln, bias=1.0, scale=1.0), r=[kg("l1")], w=[kg("l1")])
                r3 = lambda ap: ap.rearrange("p (s l) -> p s l", l=LT)
                for s_ in range(nseq):
                    sl = slice(s_ * LT, (s_ + 1) * LT)
                    S.op(V, lambda e, sl=sl: e.tensor_tensor_scan(out=rg("cs")[:, sl], data0=onesrow[:, 0:LT], data1=rg("l1")[:, sl], initial=0.0, op0=ALU.mult, op1=ALU.add),
                         r=[kg("l1"), kC2], w=[kg("cs")])
                S.op(V, lambda e: e.tensor_tensor(out=rg("aa"), in0=rg("ig"), in1=rg("cs"), op=ALU.add), r=[kg("ig"), kg("cs")], w=[kg("aa")])
                minap = (lambda s_: MIN[:, s_:s_ + 1]) if sample else (lambda s_: MROW[:, i:i + 1])
                for s_ in range(nseq):
                    sl = slice(s_ * LT, (s_ + 1) * LT)
                    S.op(V, lambda e, sl=sl, s_=s_: e.tensor_tensor_scan(out=rg("A")[:, sl], data0=rg("aa")[:, sl], data1=rg("aa")[:, sl], initial=minap(s_), op0=ALU.max, op1=ALU.max),
                         r=[kg("aa"), kMIN], w=[kg("A")])
                minall = MIN[:, 0:16] if sample else MROW[:, i:i + 1]
                S.op(V, lambda e: e.tensor_scalar(out=NEGAX[:, 0:nseq], in0=minall, scalar1=-1.0, scalar2=None, op0=ALU.mult), r=[kMIN], w=[kNX])
                S.op(V, lambda e: e.tensor_scalar(out=NEGAX[:, nseq:nseq + T], in0=rg("A"), scalar1=-1.0, scalar2=None, op0=ALU.mult), r=[kg("A")], w=[kNX])
                S.op(V, lambda e: e.tensor_tensor(out=rg("negm"), in0=rg("cs"), in1=rg("A"), op=ALU.subtract), r=[kg("cs"), kg("A")], w=[kg("negm")])
                if sample:
                    S.op(V, lambda e: e.tensor_copy(out=r3(rg("prev")), in_=MIN[:, 0:16].unsqueeze(2).broadcast_to([4, 16, 8])), r=[kMIN], w=[kg("prev")])
                    S.op(V, lambda e: e.tensor_copy(out=r3(rg("lastr")), in_=r3(NEGAX[:, 16:16 + T])[:, :, 7:8].broadcast_to([4, 16, 8])), r=[kNX], w=[kg("lastr")])
                else:
                    c3 = lambda ap: ap.rearrange("p (c l) -> p c l", l=128)
                    S.op(V, lambda e: e.tensor_scalar(out=c3(rg("prev")), in0=c3(NEGAX[:, 0:T])[:, :, 0:1].broadcast_to([4, nch, 128]), scalar1=-1.0, scalar2=None, op0=ALU.mult), r=[kNX], w=[kg("prev")])
                    S.op(V, lambda e: e.tensor_copy(out=c3(rg("lastr")), in_=c3(NEGAX[:, 1:T + 1])[:, :, 127:128].broadcast_to([4, nch, 128])), r=[kNX], w=[kg("lastr")])
                if sample:
                    if last:
                        MOUT, kMO = AR.get("MOUT", [4, 16], parts=4)
                        S.op(V, lambda e: e.tensor_scalar(out=MOUT, in0=r3(rg("negm"))[:, :, 7], scalar1=-1.0, scalar2=None, op0=ALU.mult), r=[kg("negm")], w=[kMO])
                        S.dma('sync', O["smT"][:, i, :], MOUT, r=[kMO], chan=('osm',))
                else:
                    S.op(V, lambda e: e.tensor_scalar(out=MROW[:, i:i + 1], in0=rg("negm")[:, T - 1:T], scalar1=-1.0, scalar2=None, op0=ALU.mult), r=[kg("negm")], w=[kMIN])
                    if last:
                        S.dma('sync', O["pm"][i].unsqueeze(1), MROW[:, i:i + 1], r=[kMIN], chan=('opm',))
                for c in range(nch):
                    cs_ = slice(c * 128, (c + 1) * 128)
                    for part, c0 in [(0, 512), (1, 1024)]:
                        b = 4 + part
                        for kc in range(8):
                            S.op(PE_, lambda e, kc=kc, b=b, c0=c0, cs_=cs_: e.matmul(PS[b][:, 0:512], hT[:, kc, cs_], win[:, kc, c0:c0 + 512], start=(kc == 0), stop=(kc == 7)),
                                 r=kW + [kH], w=[PK[b]], inc=(kc == 7))
                    S.op(A_, lambda e, c=c: e.activation(out=KTOK[:, c, :], in_=PS[4][:, 0:512], func=AF.Copy, scale=128.0 ** -0.5), r=[PK[4]], w=[kkt])
                    S.op(A_, lambda e, c=c: e.copy(out=VT[:, c * 4:(c + 1) * 4, 0:128], in_=PS[5][:, 0:512].rearrange("p (h v) -> p h v", v=128)), r=[PK[5]], w=[kvt])
                maskb = mSb if sample else mPb
                kCSh = lambda h: (kCS, h)
                S.barrier()
                for c in range(nch):
                    cs_ = slice(c * 128, (c + 1) * 128)
                    COLS, kCOLS = COLSb[c % 2]
                    allp3 = [PK[6]]
                    for bi, nm in enumerate(["aa", "A", "negm", "prev", "lastr"]):
                        src = NEGAX[:, nseq + c * 128:nseq + (c + 1) * 128] if nm == "A" else rg(nm)[:, cs_]
                        S.op(PE_, lambda e, bi=bi, src=src: e.transpose(out=PS[6][:, 256 + bi * 4:256 + (bi + 1) * 4], in_=src, identity=ident[0:4, 0:4]),
                             r=[kg(nm) if nm != "A" else kNX, kC], w=allp3)
                    S.op(V, lambda e, COLS=COLS: e.tensor_copy(out=COLS, in_=PS[6][:, 256:276]), r=allp3, w=[kCOLS])

                    def stage_fns(h, z, c=c, cs_=cs_, COLS=COLS, kCOLS=kCOLS):
                        bz = B[z]
                        SA, kSA = bz["SA"]; WT, kWT = bz["WT"]; PT, kPT = bz["PT"]; PVs, kPVs = bz["PVs"]
                        NUM, kNUM = bz["NUM"]; JK, kJK = bz["JK"]; HN, kHN = bz["HN"]; VS, kVS = bz["VS"]
                        SM, kSM = bz["SM"]; WSI, kWSI = bz["WSI"]; WCB, kWCB = bz["WCB"]; QPAD, kQP = bz["QPAD"]
                        bA, bB, bC, bD = z, z + 2, z + 4, z + 6
                        ca = COLS[:, 0 + h:1 + h]; cnA = COLS[:, 4 + h:5 + h]; cnm = COLS[:, 8 + h:9 + h]
                        cpv = COLS[:, 12 + h:13 + h]; cla = COLS[:, 16 + h:17 + h]
                        qpd = QPAD.rearrange("p (s k) -> p s k", k=136)[:, :, 0:L]
                        fns = []

                        def s1():
                            if sample:
                                S.op(PE_, lambda e: e.matmul(PS[bA][:, 0:16], sel4[:, h, :], NEGAX[:, 0:16], start=True, stop=False, skip_group_check=True), r=[kNX, kC], w=[PK[bA]], inc=False)
                                S.op(PE_, lambda e: e.matmul(PS[bA][:, 16:144], sel4[:, h, :], NEGAX[:, 16:144], start=False, stop=True, skip_group_check=True), r=[kNX, kC], w=[PK[bA]])
                            else:
                                S.op(PE_, lambda e: e.matmul(PS[bA][:, 0:129], sel4[:, h, :], NEGAX[:, c * 128:c * 128 + 129], start=True, stop=True), r=[kNX, kC], w=[PK[bA]])
                            S.op(PE_, lambda e: e.matmul(PS[bB][:, 0:128], sel4[:, h, :], NEGAX[:, nseq + c * 128:nseq + (c + 1) * 128], start=True, stop=False), r=[kNX, kC, kCOLS], w=[PK[bB]], inc=False)
                            S.op(PE_, lambda e: e.matmul(PS[bB][:, 0:128], identb[:, :], maskb[:, :], start=False, stop=True), r=[kC], w=[PK[bB]])
                            S.op(PE_, lambda e: e.matmul(PS[bC][:, 0:128], kTt[:, h, cs_], qT[:, h, cs_], start=True, stop=True), r=[kkT, kqT], w=[PK[bC]])
                        fns.append(s1)

                        def s2():
                            S.op(A_, lambda e: e.copy(out=SA, in_=PS[bA][:, 0:nseq + 128]), r=[PK[bA]], w=[kSA])
                            S.op(A_, lambda e: e.activation(out=WT, in_=PS[bB][:, 0:128], func=AF.Exp, bias=ca, scale=1.0), r=[PK[bB], kCOLS], w=[kWT])
                            S.op(A_, lambda e: e.activation(out=SM[:, 0:1], in_=cnA, func=AF.Exp, bias=cpv, scale=1.0), r=[kCOLS], w=[(kSM, 0)])
                            S.op(A_, lambda e: e.activation(out=SM[:, 1:2], in_=ca, func=AF.Exp, bias=cla, scale=1.0), r=[kCOLS], w=[(kSM, 1)])
                            S.op(A_, lambda e: e.activation(out=SM[:, 2:3], in_=cnm, func=AF.Exp), r=[kCOLS], w=[(kSM, 2)])
                        fns.append(s2)

                        def s3():
                            S.op(V, lambda e: e.tensor_tensor(out=PT, in0=PS[bC][:, 0:128], in1=WT, op=ALU.mult), r=[PK[bC], kWT], w=[kPT])
                            S.op(V, lambda e: e.tensor_copy(out=qpd, in_=qT[:, h, cs_].rearrange("p (s l) -> p s l", l=L)), r=[kqT], w=[kQP])
                            S.op(V, lambda e: e.tensor_scalar(out=WSI[:, 0:nseq], in0=onehot[:, 0:nseq] if sample else onesb[:, 0:1], scalar1=SM[:, 1:2], scalar2=None, op0=ALU.mult), r=[(kSM, 1), kC, kC2], w=[kWSI])
                            sa_last = SA[:, nseq:nseq + 128].rearrange("p (s l) -> p s l", l=L)[:, :, L - 1]
                            S.op(V, lambda e: e.tensor_tensor(out=WCB[:, 0:nseq], in0=sa_last, in1=SA[:, 0:nseq], op=ALU.subtract), r=[kSA], w=[kWCB])
                        fns.append(s3)

                        def s4():
                            S.op(PE_, lambda e: e.matmul(PS[bB][:, 256:385], PT, VT[:, c * 4 + h, :], start=True, stop=True), r=[kPT, kvt], w=[PK[bB]])
                            for s_ in range(nseq):
                                S.op(PE_, lambda e, s_=s_: e.matmul(PS[bA][:, 256:385], QPAD[:, s_ * 128:(s_ + 1) * 128], csv(s_, h), start=(s_ == 0), stop=(s_ == nseq - 1)),
                                     r=[kQP, kCSh(h)], w=[PK[bA]], inc=(s_ == nseq - 1))
                        fns.append(s4)

                        def s5():
                            S.op(A_, lambda e: e.copy(out=PVs, in_=PS[bB][:, 256:385]), r=[PK[bB]], w=[kPVs])
                            S.op(A_, lambda e: e.activation(out=WCB[:, 0:nseq], in_=WCB[:, 0:nseq], func=AF.Exp), r=[kWCB], w=[kWCB])
                        fns.append(s5)

                        def s6():
                            S.op(V, lambda e: e.scalar_tensor_tensor(out=NUM, in0=PS[bA][:, 256:385], scalar=SM[:, 0:1], in1=PVs, op0=ALU.mult, op1=ALU.add), r=[PK[bA], (kSM, 0), kPVs], w=[kNUM])
                        fns.append(s6)

                        def s7():
                            S.op(A_, lambda e: e.activation(out=SM[:, 3:4], in_=NUM[:, 128:129], func=AF.Abs), r=[kNUM], w=[(kSM, 3)])
                        fns.append(s7)

                        def s8():
                            S.op(V, lambda e: e.tensor_tensor(out=SM[:, 3:4], in0=SM[:, 3:4], in1=SM[:, 2:3], op=ALU.max), r=[(kSM, 3), (kSM, 2)], w=[(kSM, 3)])
                            S.op(V, lambda e: e.reciprocal(out=SM[:, 4:5], in_=SM[:, 3:4]), r=[(kSM, 3)], w=[(kSM, 4)])
                        fns.append(s8)

                        def s9():
                            S.op(A_, lambda e: e.activation(out=JK, in_=NUM[:, 0:128], func=AF.Square, scale=SM[:, 4:5], accum_out=SM[:, 5:6]), r=[kNUM, (kSM, 4)], w=[kJK, (kSM, 5)])
                            S.op(A_, lambda e: e.activation(out=SM[:, 6:7], in_=SM[:, 5:6], func=AF.Sqrt, bias=EPS, scale=1.0 / 128), r=[(kSM, 5)], w=[(kSM, 6)])
                        fns.append(s9)

                        def s10():
                            S.op(V, lambda e: e.reciprocal(out=SM[:, 6:7], in_=SM[:, 6:7]), r=[(kSM, 6)], w=[(kSM, 6)])
                            S.op(V, lambda e: e.tensor_tensor(out=SM[:, 7:8], in0=SM[:, 6:7], in1=SM[:, 4:5], op=ALU.mult), r=[(kSM, 6), (kSM, 4)], w=[(kSM, 7)])
                            S.op(V, lambda e: e.tensor_scalar(out=HN, in0=NUM[:, 0:128], scalar1=SM[:, 7:8], scalar2=None, op0=ALU.mult), r=[kNUM, (kSM, 7)], w=[kHN])
                        fns.append(s10)

                        def s11():
                            S.op(PE_, lambda e: e.transpose(out=psb(bC)[:, 512:640], in_=HN, identity=identb[:, :]), r=[kHN, kC], w=[PK[bC]])
                        fns.append(s11)

                        def s12():
                            S.op(V, lambda e: e.scalar_tensor_tensor(out=hmT[:, h, cs_], in0=psb(bC)[:, 512:640], scalar=aon[:, i, h:h + 1], in1=sgT[:, h, cs_], op0=ALU.mult, op1=ALU.mult),
                                 r=[PK[bC], kC, ksg], w=[(khm, h)])
                        fns.append(s12)

                        def s13():
                            for s_ in range(nseq):
                                S.op(V, lambda e, s_=s_: e.tensor_scalar(out=VS, in0=VT[:, c * 4 + h, :], scalar1=WSI[:, s_:s_ + 1], scalar2=None, op0=ALU.mult), r=[kvt, kWSI], w=[kVS])
                                S.op(PE_, lambda e: e.matmul(PS[bD][:, 0:129], KTOK[:, c, h * 128:(h + 1) * 128], VS, start=True, stop=True), r=[kkt, kVS], w=[PK[bD]])
                                S.op(V, lambda e, s_=s_: e.scalar_tensor_tensor(out=csv(s_, h), in0=csv(s_, h), scalar=WCB[:, s_:s_ + 1], in1=PS[bD][:, 0:129], op0=ALU.mult, op1=ALU.add),
                                     r=[PK[bD], kWCB, kCSh(h)], w=[kCSh(h)])
                        fns.append(s13)
                        return fns

                    groups = [[0], [1], [2], [3]] if nset == 1 else [[0, 1], [2, 3]]
                    for grp in groups:
                        fl = [stage_fns(h, z) for z, h in enumerate(grp)]
                        for si in range(len(fl[0])):
                            for f_ in fl:
                                f_[si]()
                S.barrier()
                khm_all = [(khm, h) for h in range(4)]
                kCS_all = [kCSh(h) for h in range(4)]
                if last:
                    if sample:
                        S.dma('sync', O["sC"][i].rearrange("s h k v -> k (s h) v"), CS_[:, :, 0:128], r=kCS_all, chan=('osC',))
                        fm_rows_out(lambda c: CS_[:, :, 128], 64, lambda o, n: O["sn"][i, :, o:o + n], 1, kCS_all)
                    else:
                        S.dma('sync', O["pC"][i].rearrange("h k v -> k h v"), Cst[:, i, :, 0:128], r=kCS_all, chan=('opC',))
                        fm_rows_out(lambda c: Cst[:, i, :, 128], 4, lambda o, n: O["pn"][i, :, o:o + n], 1, kCS_all)
                for d in range(8):
                    b = d % 2
                    for j in range(8):
                        rhs = hmT[:, j, :] if j < 4 else scTt[:, j - 4, :]
                        S.op(PE_, lambda e, b=b, j=j, d=d, rhs=rhs: e.matmul(PS[b][:, 0:T], wout[:, j, d * 128:(d + 1) * 128], rhs, start=(j == 0), stop=(j == 7)),
                             r=kW + khm_all + [ksc], w=[PK[b]], inc=(j == 7))
                    resid_add(l, 2, d, PS[b][:, 0:T], PK[b], rt)

            def mixer_c(l):
                jl = l // 2
                wq = WB[:, 0:8 * 1536].rearrange("p (k c) -> p k c", c=1536)
                wo = WB[:, 8 * 1536:8 * 1536 + 8 * D].rearrange("p (k c) -> p k c", c=D)
                kW = [('W', 0), ('W', 1), ('W', 2)]
                for kc in range(8):
                    for b3 in range(3):
                        S.dma('pool', wq[:, kc, b3 * 512:(b3 + 1) * 512], I["c_w_qkv"][jl, kc * 128:(kc + 1) * 128, b3 * 512:(b3 + 1) * 512], w=kW, chan=('W', 0))
                S.dma('pool', wo, I["c_w_out"][jl].rearrange("(k p) c -> p k c", p=128), w=kW, chan=('W', 0), max_dma_last_dim=4096)
                norm_mod(l, 1)
                S.barrier(); AR.reset()
                QK, kQK = AR.get("QK", [128, 1280])
                SQ, kSQ = AR.get("SQ", [128, 1280])
                QR, kQR = AR.get("QR", [128, 1280])
                QRb, kQRb = AR.get("QRb", [128, 1280], BF16)
                T1, kT1 = AR.get("T1", [128, 640])
                T2, kT2 = AR.get("T2", [128, 640])
                VV, kVV = AR.get("VV", [128, 256])
                SS, kSS = AR.get("SS", [128, 20])
                QT, kQT = AR.get("QTz", [128, 16, T], BF16)
                S.op(V, lambda e: e.memset(QT, 0.0), w=[kQT])
                QT5 = QT.rearrange("p (a b g) t -> p a b g t", a=2, b=2)
                KT, kKT = AR.get("KT", [128, 2, T], BF16)
                VTa, kVTa = AR.get("VTa", [128, nch * 4, 65], BF16)
                PTb = [AR.get("PTb%d" % z, [128, 512], BF16) for z in range(2)]
                DEN, kDEN = AR.get("DEN", [128, 4])
                OTOK, kOT = AR.get("OTOK", [128, 16, 64], BF16)
                OTT, kOTT = AR.get("OTT", [128, 8, T], BF16)
                rt = AR.get("rt", [128, T])
                if sample:
                    CKb, kCKb = AR.get("CKb", [128, 16, 256], BF16)
                    KCT, kKCT = AR.get("KCT", [128, 32, 128], BF16)
                    VCs, kVCs = AR.get("VCs", [128, 64, 65], BF16)
                    S.op(V, lambda e: e.memset(VCs, 1.0), w=[kVCs])
                    S.dma('pool', CKb, I["ck"][jl].rearrange("s p c -> p s c"), w=[kCKb], max_dma_last_dim=1024)
                    for s_ in range(16):
                        S.dma('pool', VCs[:, s_ * 4:(s_ + 1) * 4, 0:64], I["cv"][jl, s_].rearrange("p (h d) -> p h d", d=64), w=[kVCs], max_dma_last_dim=256)
                    for s_ in range(16):
                        for j in range(2):
                            S.op(PE_, lambda e, s_=s_, j=j: e.transpose(out=psb(3)[:, j * 128:(j + 1) * 128], in_=CKb[:, s_, j * 128:(j + 1) * 128], identity=identb[:, :]), r=[kCKb, kC], w=[PK[3]])
                        S.op(V, lambda e, s_=s_: e.tensor_copy(out=KCT[:, 2 * s_:2 * s_ + 2, :], in_=psb(3)[:, 0:256].rearrange("p (a b) -> p a b", b=128)), r=[PK[3]], w=[kKCT])
                    S.dma('sync', O["swk"][jl, :, 0:120, :], I["ck"][jl, :, 8:128, :], chan=('dd',))
                    S.dma('sync', O["swv"][jl, :, 0:120, :], I["cv"][jl, :, 8:128, :], chan=('dd',))
                S.op(V, lambda e: e.memset(VTa, 1.0), w=[kVTa])
                if int(os.environ.get('KD_C', 9)) < 1:
                    return
                qk3 = lambda ap: ap.rearrange("p (h d) -> p h d", d=64)
                for c in range(nch):
                    cs_ = slice(c * 128, (c + 1) * 128)
                    for b in range(3):
                        for kc in range(8):
                            S.op(PE_, lambda e, b=b, kc=kc, cs_=cs_: e.matmul(PS[b][:, 0:512], hT[:, kc, cs_], wq[:, kc, b * 512:(b + 1) * 512], start=(kc == 0), stop=(kc == 7)),
                                 r=kW + [kH], w=[PK[b]], inc=(kc == 7))
                    S.op(A_, lambda e: e.copy(out=QK[:, 0:512], in_=PS[0][:, 0:512]), r=[PK[0]], w=[kQK])
                    S.op(A_, lambda e: e.copy(out=QK[:, 512:1024], in_=PS[1][:, 0:512]), r=[PK[1]], w=[kQK])
                    S.op(A_, lambda e: e.copy(out=QK[:, 1024:1280], in_=PS[2][:, 0:256]), r=[PK[2]], w=[kQK])
                    if os.environ.get('KD_X', '') == 'a':
                        continue
                    S.op(A_, lambda e: e.copy(out=VV, in_=PS[2][:, 256:512]), r=[PK[2]], w=[kVV])
                    if os.environ.get('KD_X', '') == 'b':
                        continue
                    S.op(V, lambda e, c=c: e.tensor_copy(out=VTa[:, c * 4:(c + 1) * 4, 0:64], in_=VV.rearrange("p (h d) -> p h d", d=64)), r=[kVV], w=[kVTa])
                    if int(os.environ.get('KD_C', 9)) < 2:
                        continue
                    S.op(A_, lambda e: e.activation(out=SQ, in_=QK, func=AF.Square), r=[kQK], w=[kSQ])
                    S.op(V, lambda e: e.tensor_reduce(out=SS, in_=qk3(SQ), axis=AX.X, op=ALU.add), r=[kSQ], w=[kSS])
                    S.op(A_, lambda e: e.activation(out=SS, in_=SS, func=AF.Sqrt, bias=EPS, scale=1.0 / 64), r=[kSS], w=[kSS])
                    S.op(V, lambda e: e.reciprocal(out=SS, in_=SS), r=[kSS], w=[kSS])
                    S.op(V, lambda e: e.tensor_tensor(out=qk3(QK), in0=qk3(QK), in1=SS.unsqueeze(2).broadcast_to([128, 20, 64]), op=ALU.mult), r=[kSS, kQK], w=[kQK])
                    S.op(V, lambda e: e.tensor_tensor(out=qk3(QK[:, 0:1024]), in0=qk3(QK[:, 0:1024]), in1=GQ[:, jl, :].unsqueeze(1).broadcast_to([128, 16, 64]), op=ALU.mult), r=[kC, kQK], w=[kQK])
                    S.op(V, lambda e: e.tensor_tensor(out=qk3(QK[:, 1024:1280]), in0=qk3(QK[:, 1024:1280]), in1=GK[:, jl, :].unsqueeze(1).broadcast_to([128, 4, 64]), op=ALU.mult), r=[kC, kQK], w=[kQK])
                    if int(os.environ.get('KD_C', 9)) < 3:
                        continue
                    cosb = cosT[:, c, :].unsqueeze(1).broadcast_to([128, 20, 32])
                    sinb = sinT[:, c, :].unsqueeze(1).broadcast_to([128, 20, 32])
                    x1 = qk3(QK)[:, :, 0:32]; x2 = qk3(QK)[:, :, 32:64]
                    t1 = T1.rearrange("p (h d) -> p h d", d=32); t2 = T2.rearrange("p (h d) -> p h d", d=32)
                    S.op(P_, lambda e, x1=x1, cosb=cosb: e.tensor_tensor(out=t1, in0=x1, in1=cosb, op=ALU.mult), r=[kQK, ('rope',)], w=[kT1])
                    S.op(P_, lambda e, x2=x2, sinb=sinb: e.tensor_tensor(out=t2, in0=x2, in1=sinb, op=ALU.mult), r=[kQK, ('rope',)], w=[kT2])
                    S.op(V, lambda e: e.tensor_tensor(out=qk3(QR)[:, :, 0:32], in0=t1, in1=t2, op=ALU.subtract), r=[kT1, kT2], w=[kQR])
                    S.op(P_, lambda e, x2=x2, cosb=cosb: e.tensor_tensor(out=t1, in0=x2, in1=cosb, op=ALU.mult), r=[kQK, ('rope',)], w=[kT1])
                    S.op(P_, lambda e, x1=x1, sinb=sinb: e.tensor_tensor(out=t2, in0=x1, in1=sinb, op=ALU.mult), r=[kQK, ('rope',)], w=[kT2])
                    S.op(V, lambda e: e.tensor_tensor(out=qk3(QR)[:, :, 32:64], in0=t1, in1=t2, op=ALU.add), r=[kT1, kT2], w=[kQR])
                    S.op(A_, lambda e: e.copy(out=QRb, in_=QR), r=[kQR], w=[kQRb])
                    if int(os.environ.get('KD_C', 9)) < 4:
                        continue
                    for j in range(8):
                        S.op(PE_, lambda e, j=j: e.transpose(out=psb(3)[:, j * 128:(j + 1) * 128], in_=QRb[:, j * 128:(j + 1) * 128], identity=identb[:, :]), r=[kQRb, kC], w=[PK[3]])
                    S.op(V, lambda e, cs_=cs_: e.tensor_copy(out=QT5[0:64, :, 0, :, cs_], in_=psb(3)[0:64, 0:1024].rearrange("p (a g t) -> p a g t", a=2, g=4)), r=[PK[3]], w=[kQT])
                    S.op(V, lambda e, cs_=cs_: e.tensor_copy(out=QT5[64:128, :, 1, :, cs_], in_=psb(3)[64:128, 0:1024].rearrange("p (a g t) -> p a g t", a=2, g=4)), r=[PK[3]], w=[kQT])
                    for j in range(2):
                        S.op(PE_, lambda e, j=j: e.transpose(out=psb(4)[:, j * 128:(j + 1) * 128], in_=QRb[:, 1024 + j * 128:1024 + (j + 1) * 128], identity=identb[:, :]), r=[kQRb, kC], w=[PK[4]])
                    S.op(V, lambda e, cs_=cs_: e.tensor_copy(out=KT[:, :, cs_], in_=psb(4)[:, 0:256].rearrange("p (a b) -> p a b", b=128)), r=[PK[4]], w=[kKT])
                    if sample:
                        for s_ in range(16):
                            S.dma('sync', O["swk"][jl, s_, 120:128, :], QR[s_ * 8:(s_ + 1) * 8, 1024:1280], r=[kQR], chan=('oswk',))
                            S.dma('sync', O["swv"][jl, s_, 120:128, :], VV[s_ * 8:(s_ + 1) * 8, :], r=[kVV], chan=('oswv',))
                    elif last and c == nch - 1:
                        S.dma('sync', O["pwk"][jl], QR[:, 1024:1280], r=[kQR], chan=('opwk',))
                        S.dma('sync', O["pwv"][jl], VV, r=[kVV], chan=('opwv',))
                    if int(os.environ.get('KD_C', 9)) < 5:
                        continue
                    for kap in range(4):
                        base = 64 * (kap % 2)
                        pr = kap // 2
                        blocks = []
                        if sample:
                            for s_ in range(16):
                                blocks.append((KCT[:, 2 * s_ + pr, :], VCs[:, s_ * 4 + kap, :],
                                               acacheb[:, s_, :].unsqueeze(1).broadcast_to([128, 4, 128]), [kKCT], [kVCs]))
                            blocks.append((KT[:, pr, cs_], VTa[:, c * 4 + kap, :], anewb[:, :].rearrange("p (g t) -> p g t", t=128), [kKT], [kVTa]))
                        else:
                            blocks.append((KT[:, pr, cs_], VTa[:, c * 4 + kap, :], acurb[:, :].rearrange("p (g t) -> p g t", t=128), [kKT], [kVTa]))
                            if c > 0:
                                ps_ = slice((c - 1) * 128, c * 128)
                                blocks.append((KT[:, pr, ps_], VTa[:, (c - 1) * 4 + kap, :], aprevb[:, :].rearrange("p (g t) -> p g t", t=128), [kKT], [kVTa]))
                            elif pi > 0:
                                blocks.append((KTC[:, jl, pr, :], VC[:, jl, kap, :], aprevb[:, :].rearrange("p (g t) -> p g t", t=128), [('KTC',)], [('VC',)]))
                        qrhs = QT[:, 4 * kap:4 * kap + 4, cs_]
                        for bi, (kb, vb, mb, kr, vr) in enumerate(blocks):
                            pt, kpt = PTb[bi % 2]
                            ps5 = PS[5][:, 0:512].rearrange("p (g t) -> p g t", t=128)
                            S.op(PE_, lambda e, kb=kb, qrhs=qrhs, ps5=ps5: e.matmul(ps5, kb, qrhs, start=True, stop=False), r=kr + [kQT], w=[PK[5]], inc=False)
                            S.op(PE_, lambda e, mb=mb, ps5=ps5: e.matmul(ps5, identb[:, :], mb, start=False, stop=True), r=[kC], w=[PK[5]])
                            S.op(A_, lambda e, pt=pt: e.activation(out=pt, in_=PS[5][:, 0:512], func=AF.Exp, bias=NEGMA[:, jl:jl + 1], scale=0.125), r=[PK[5], kC2], w=[kpt])
                            for g in range(4):
                                S.op(PE_, lambda e, g=g, pt=pt, vb=vb, bi=bi, nb=len(blocks): e.matmul(PS[6][:, g * 65:(g + 1) * 65], pt[:, g * 128:(g + 1) * 128], vb,
                                                                                    start=(bi == 0 and g == 0), stop=(bi == nb - 1), skip_group_check=True),
                                     r=[kpt] + vr, w=[PK[6]], inc=(g == 3))
                        s0 = 8 * pr + (kap % 2)
                        o3 = PS[6][:, 0:260].rearrange("p (g d) -> p g d", d=65)
                        S.op(V, lambda e, s0=s0, o3=o3: e.tensor_tensor(out=DEN, in0=o3[:, :, 64], in1=SINKE[:, jl, s0:s0 + 7:2], op=ALU.add), r=[PK[6], ('SINKE',)], w=[kDEN])
                        S.op(V, lambda e: e.reciprocal(out=DEN, in_=DEN), r=[kDEN], w=[kDEN])
                        S.op(V, lambda e, s0=s0, o3=o3: e.tensor_tensor(out=OTOK[:, s0:s0 + 7:2, :], in0=o3[:, :, 0:64], in1=DEN.unsqueeze(2).broadcast_to([128, 4, 64]), op=ALU.mult),
                             r=[PK[6], kDEN], w=[kOT])
                    if int(os.environ.get('KD_C', 9)) < 6:
                        continue
                    otf = OTOK.rearrange("p h d -> p (h d)")
                    for j in range(8):
                        S.op(PE_, lambda e, j=j: e.transpose(out=psb(7)[:, j * 128:(j + 1) * 128], in_=otf[:, j * 128:(j + 1) * 128], identity=identb[:, :]), r=[kOT, kC], w=[PK[7]])
                    S.op(V, lambda e, cs_=cs_: e.tensor_copy(out=OTT[:, :, cs_], in_=psb(7)[:, 0:1024].rearrange("p (a b) -> p a b", b=128)), r=[PK[7]], w=[kOTT])
                if int(os.environ.get('KD_C', 9)) < 7:
                    return
                if not sample:
                    ls_ = slice((nch - 1) * 128, nch * 128)
                    S.op(V, lambda e: e.tensor_copy(out=KTC[:, jl, :, :], in_=KT[:, :, ls_]), r=[kKT], w=[('KTC',)])
                    S.op(V, lambda e: e.tensor_copy(out=VC[:, jl, :, :], in_=VTa[:, (nch - 1) * 4:nch * 4, :]), r=[kVTa], w=[('VC',)])
                for d in range(8):
                    b = d % 2
                    for j in range(8):
                        S.op(PE_, lambda e, b=b, j=j, d=d: e.matmul(PS[b][:, 0:T], wo[:, j, d * 128:(d + 1) * 128], OTT[:, j, :], start=(j == 0), stop=(j == 7)),
                             r=kW + [kOTT], w=[PK[b]], inc=(j == 7))
                    resid_add(l, 2, d, PS[b][:, 0:T], PK[b], rt)

            for l in range(4):
                if str(l) not in os.environ.get("KD_LAYERS", "0123"):
                    continue
                if "m" in os.environ.get("KD_PARTS", "mf"):
                    if l % 2 == 0:
                        mixer_ab(l)
                    else:
                        mixer_c(l)
                if "f" in os.environ.get("KD_PARTS", "mf"):
                    ffn(l)
            S.barrier()
            for c in range(nch):
                for b in range(2):
                    for j in range(4):
                        kc = b * 4 + j
                        S.op(PE_, lambda e, b=b, j=j, kc=kc, c=c: e.transpose(out=PS[b][:, j * 128:(j + 1) * 128], in_=xT[:, kc, c * 128:(c + 1) * 128], identity=ident[:, :]),
                             r=[kX, kC], w=[PK[b]])
                    S.op(A_, lambda e, b=b: e.copy(out=stage[:, b * 512:(b + 1) * 512], in_=PS[b][:, :]), r=[PK[b]], w=[('stage',)])
                S.dma('sync', yout[tok0 + c * 128: tok0 + (c + 1) * 128, :], stage[:, :], r=[('stage',)], chan=('ostage',))

        for pi in range(int(os.environ.get("KD_NPASS", NPASS))):
            run_pass(False, pi)
        if os.environ.get("KD_SAMPLE", "1") == "1":
            run_pass(True, 0)
        S.finish()

        with nc.Block() as block:
            @block.sync
            def _(e):
                S.replay('sync', e)

            @block.tensor
            def _(e):
                S.replay('pe', e)

            @block.scalar
            def _(e):
                S.replay('act', e)

            @block.vector
            def _(e):
                S.replay('dve', e)

            @block.gpsimd
            def _(e):
                S.replay('pool', e)
    return nc


def _slot_perm():
    perm = np.zeros(16, np.int64)
    for kap in range(4):
        for g in range(4):
            s = 2 * (g + 4 * (kap // 2)) + (kap % 2)
            perm[s] = 4 * kap + g
    return perm


def _consts():
    c = {}
    c["ident"] = np.eye(128, dtype=np.float32)
    s = np.arange(128)[:, None]
    t = np.arange(128)[None, :]
    c["mP"] = np.where(s <= t, 0.0, NEG).astype(np.float32)
    same = (s // 8) == (t // 8)
    c["mS"] = np.where(same & (s <= t), 0.0, NEG).astype(np.float32)
    c["acur"] = np.tile(np.where(s <= t, 0.0, ANEG).astype(np.float32), (1, 4))
    c["aprev"] = np.tile(np.where(s > t, 0.0, ANEG).astype(np.float32), (1, 4))
    c["anew"] = np.tile(np.where(same & (s <= t), 0.0, ANEG).astype(np.float32), (1, 4))
    ac = np.full((128, 16, 128), ANEG, np.float32)
    for i in range(16):
        for j in range(8):
            tt = 8 * i + j
            ac[j + 1:, i, tt] = 0.0
    c["acache"] = ac
    oh = np.zeros((128, 16), np.float32)
    oh[np.arange(128), np.arange(128) // 8] = 1.0
    c["onehot"] = oh
    sel = np.zeros((4, 4, 128), np.float32)
    for h in range(4):
        sel[h, h, :] = 1.0
    c["sel4"] = sel
    inv = 10000.0 ** (-np.arange(32, dtype=np.float64) / 32)
    pos = np.arange(SEQ, dtype=np.float64)[:, None] * inv[None, :]
    c["cosP"] = np.cos(pos).astype(np.float32)
    c["sinP"] = np.sin(pos).astype(np.float32)
    ps = (8192 + (np.arange(128) % 8)).astype(np.float64)[:, None] * inv[None, :]
    c["cosS"] = np.cos(ps).astype(np.float32)
    c["sinS"] = np.sin(ps).astype(np.float32)
    return c


_NC = None


def kernel(x_prompt, x_sample, c_prompt, c_sample, state_mlstm_C, state_mlstm_n, state_mlstm_m,
           state_sconv, cache_win_k, cache_win_v, state_ffn_conv,
           norm1, norm2, w_ada, b_ada, a_w_in, a_b_if, a_out_norm, a_conv_w, a_w_out,
           c_w_qkv, c_q_norm, c_k_norm, c_sink, c_w_out, f_w_up, f_conv_w, f_w_down):
    global _NC
    f = lambda a: np.ascontiguousarray(np.asarray(a, dtype=np.float32))
    perm = _slot_perm()
    common = dict(_consts())
    common["w_ada"] = f(w_ada)
    common["b_adaT"] = f(np.asarray(b_ada).reshape(4, 48, 128).transpose(2, 0, 1))
    common["norm1T"] = f(np.asarray(norm1).reshape(4, 8, 128).transpose(2, 0, 1))
    common["norm2T"] = f(np.asarray(norm2).reshape(4, 8, 128).transpose(2, 0, 1))
    common["a_w_in"] = f(a_w_in)
    common["a_bif"] = f(np.asarray(a_b_if).reshape(2, 2, 4).transpose(2, 0, 1))
    common["a_onT"] = f(np.asarray(a_out_norm).reshape(2, 4, 128).transpose(2, 0, 1))
    common["a_cwT"] = f(np.asarray(a_conv_w).reshape(2, 3, 4, 128).transpose(3, 0, 1, 2))
    common["a_w_out"] = f(a_w_out)
    wq = np.asarray(c_w_qkv)
    qcols = np.concatenate([np.arange(64) + 64 * h for h in perm])
    common["c_w_qkv"] = f(np.concatenate([wq[:, :, qcols], wq[:, :, 1024:]], axis=2))
    common["c_qn"] = f(c_q_norm)
    common["c_kn"] = f(c_k_norm)
    common["c_sink"] = f(np.asarray(c_sink)[:, perm])
    common["c_w_out"] = f(np.asarray(c_w_out)[:, qcols, :])
    common["f_w_up"] = f(f_w_up)
    common["f_cwT"] = f(np.asarray(f_conv_w).reshape(4, 3, 44, 128).transpose(3, 0, 1, 2))
    common["f_w_down"] = f(f_w_down)
    in_maps = []
    for c in range(NCORE):
        b = c // 4
        sl = slice(16 * c, 16 * c + 16)
        m = dict(common)
        m["xp"] = f(np.asarray(x_prompt)[b])
        m["xs"] = f(np.asarray(x_sample)[sl].reshape(128, D))
        m["call"] = f(np.concatenate([np.asarray(c_prompt)[b:b + 1], np.asarray(c_sample)[sl]], axis=0))
        m["stC"] = f(np.asarray(state_mlstm_C)[:, sl])
        m["stn"] = f(np.asarray(state_mlstm_n)[:, sl].reshape(2, 64, 128))
        m["stmT"] = f(np.asarray(state_mlstm_m)[:, sl].transpose(2, 0, 1))
        m["stsc"] = f(np.asarray(state_sconv)[:, sl].reshape(2, 32, 512))
        m["ck"] = f(np.asarray(cache_win_k)[:, sl].reshape(2, 16, 128, 256))
        m["cv"] = f(np.asarray(cache_win_v)[:, sl].reshape(2, 16, 128, 256))
        m["stffn"] = f(np.asarray(state_ffn_conv)[:, sl].reshape(4, 32, 5632))
        in_maps.append(m)
    if _NC is None:
        _NC = build()
    res = run_bass_kernel_spmd(_NC, in_maps, core_ids=list(range(NCORE)))
    R = res.results
    pc = [0, 4]
    cat_p = lambda nm, shp: np.stack([R[c][nm] for c in pc], axis=1).reshape(shp)
    y_prompt = np.stack([R[c]["yp"] for c in pc], axis=0)
    y_sample = np.concatenate([R[c]["ys"].reshape(16, 8, D) for c in range(NCORE)], axis=0)
    p_C = cat_p("pC", (2, 2, 4, 128, 128))
    p_n = cat_p("pn", (2, 2, 4, 128))
    p_m = cat_p("pm", (2, 2, 4))
    p_sc = cat_p("psc", (2, 2, 2, 512))
    p_wk = cat_p("pwk", (2, 2, 128, 4, 64))
    p_wv = cat_p("pwv", (2, 2, 128, 4, 64))
    p_ffn = cat_p("pffn", (4, 2, 2, 5632))
    cat_s = lambda nm, shp: np.concatenate([R[c][nm].reshape(shp) for c in range(NCORE)], axis=1)
    s_C = cat_s("sC", (2, 16, 4, 128, 128))
    s_n = cat_s("sn", (2, 16, 4, 128))
    s_m = np.concatenate([R[c]["smT"].transpose(1, 2, 0) for c in range(NCORE)], axis=1)
    s_sc = cat_s("ssc", (2, 16, 2, 512))
    s_wk = cat_s("swk", (2, 16, 128, 4, 64))
    s_wv = cat_s("swv", (2, 16, 128, 4, 64))
    s_ffn = cat_s("sffn", (4, 16, 2, 5632))
    outs = (y_prompt, y_sample, p_C, p_n, p_m, p_sc, p_wk, p_wv, p_ffn,
            s_C, s_n, s_m, s_sc, s_wk, s_wv, s_ffn)
    return tuple(np.ascontiguousarray(o, dtype=np.float32) for o in outs)
```

```python
import contextlib
import os
import numpy as np
import concourse.bass as bass
import concourse.mybir as mybir
from concourse.bass_utils import run_bass_kernel_spmd

F32 = mybir.dt.float32
BF16 = mybir.dt.bfloat16
AF = mybir.ActivationFunctionType
ALU = mybir.AluOpType
AX = mybir.AxisListType

D = 1024
KC = 8
DFF = 2816
NF = 22
INA = 3592
SEQ = 8192
TP = 512
NPASS = SEQ // TP
EPS = 1e-6
NEG = -1.0e30
ANEG = -1.0e9
NCORE = 8
SLOT = 13312


class Sch:
    ROT = 30000

    def __init__(self, nc, stack):
        self.nc = nc
        self.stack = stack
        self.names = ('pe', 'act', 'dve', 'pool', 'sync')
        self.prog = {e: [] for e in self.names}
        self.cnt = {e: 0 for e in ('pe', 'act', 'dve', 'pool')}
        self.sems = {}
        self.seen = {e: {} for e in self.names}
        self.lastw = {}
        self.readers = {}
        self.dtot = {}

    def sem(self, key):
        if key not in self.sems:
            nm = "s" + str(len(self.sems))
            self.sems[key] = self.stack.enter_context(self.nc.semaphore(nm))
        return self.sems[key]

    def _deps(self, r, w):
        ev = []
        for k in r:
            if k in self.lastw:
                ev.append(self.lastw[k])
            if k[0] == 'P':
                ev.extend(self.readers.get(k, []))
        for k in w:
            if k in self.lastw:
                ev.append(self.lastw[k])
            ev.extend(self.readers.get(k, []))
        return ev

    def _wait(self, en, evs, skip=None):
        need = {}
        for (sk, v) in evs:
            if skip is not None and sk == skip:
                continue
            if en == 'pe' and sk[0] == 'pe':
                continue
            if need.get(sk, 0) < v:
                need[sk] = v
        for sk, v in need.items():
            if self.seen[en].get(sk, 0) < v:
                self.prog[en].append(('w', self.sem(sk), v))
                self.seen[en][sk] = v

    def _commit(self, ev, r, w):
        for k in r:
            self.readers.setdefault(k, []).append(ev)
        for k in w:
            self.lastw[k] = ev
            self.readers[k] = []

    def op(self, en, fn, r=(), w=(), inc=True):
        self._wait(en, self._deps(r, w))
        c = self.cnt[en] + 1
        sk = (en, (c - 1) // self.ROT)
        v = (c - 1) % self.ROT + 1
        if inc:
            self.cnt[en] = c
            self.prog[en].append(('i', fn, self.sem(sk), 1))
        else:
            self.prog[en].append(('i', fn, None, 0))
        self._commit((sk, v), r, w)

    def dma(self, q, out, in_, r=(), w=(), chan=None, **kw):
        if chan is None:
            chan = w[0] if len(w) else ('o',) + tuple(r[0])
        sk = ('d', chan)
        self._wait(q, self._deps(r, w), skip=sk)
        self.dtot[chan] = self.dtot.get(chan, 0) + 16
        fn = (lambda e, out=out, in_=in_, kw=kw: e.dma_start(out=out, in_=in_, allow_slow_non_contiguous=True, **kw))
        self.prog[q].append(('i', fn, self.sem(sk), 16))
        self._commit((sk, self.dtot[chan]), r, w)

    def barrier(self):
        evs = []
        for e in ('pe', 'act', 'dve', 'pool'):
            c = self.cnt[e]
            if c > 0:
                evs.append(((e, (c - 1) // self.ROT), (c - 1) % self.ROT + 1))
        for ch, t in self.dtot.items():
            if ch[0] == 'W':
                continue
            evs.append((('d', ch), t))
        for e in self.names:
            need = [x for x in evs if not (x[0][0] == e)]
            for (sk, v) in need:
                if self.seen[e].get(sk, 0) < v:
                    self.prog[e].append(('w', self.sem(sk), v))
                    self.seen[e][sk] = v

    def finish(self):
        evs = []
        for ch, t in self.dtot.items():
            evs.append((('d', ch), t))
        for e in ('pe', 'act', 'dve', 'pool'):
            c = self.cnt[e]
            if c > 0:
                evs.append(((e, (c - 1) // self.ROT), (c - 1) % self.ROT + 1))
        for (sk, v) in evs:
            self.prog['sync'].append(('w', self.sem(sk), v))

    def replay(self, en, eng):
        for it in self.prog[en]:
            if it[0] == 'w':
                eng.wait_ge(it[1], it[2])
            else:
                ins = it[1](eng)
                if it[2] is not None:
                    ins.then_inc(it[2], it[3])


class Arena:
    def __init__(self, t, words):
        self.t = t
        self.words = words
        self.off = 0
        self.gen = 0

    def reset(self):
        self.off = 0
        self.gen += 1

    def get(self, name, shape, dt=F32, parts=128):
        n = 1
        for s in shape[1:]:
            n *= s
        if dt == BF16:
            w = (n + 1) // 2
        else:
            w = n
        w = (w + 7) // 8 * 8
        assert self.off + w <= self.words, (name, self.off, w, self.words)
        ap = self.t[0:shape[0], self.off:self.off + w]
        self.off += w
        if dt == BF16:
            ap = ap.bitcast(BF16)[:, 0:n]
        else:
            ap = ap[:, 0:n]
        if len(shape) == 3:
            ap = ap.rearrange("p (a b) -> p a b", b=shape[2])
        elif len(shape) == 4:
            ap = ap.rearrange("p (a b c) -> p a b c", b=shape[2], c=shape[3])
        return ap, (name, self.gen)


def build():
    nc = bass.Bass("TRN2", target_bir_lowering=False)
    din = lambda n, s: nc.dram_tensor(n, list(s), F32, kind="ExternalInput").ap()
    dout = lambda n, s: nc.dram_tensor(n, list(s), F32, kind="ExternalOutput").ap()
    I = {}
    for n, s in [("xp", (SEQ, D)), ("xs", (128, D)), ("call", (17, D)),
                 ("stC", (2, 16, 4, 128, 128)), ("stn", (2, 64, 128)), ("stmT", (4, 2, 16)),
                 ("stsc", (2, 32, 512)), ("ck", (2, 16, 128, 256)), ("cv", (2, 16, 128, 256)),
                 ("stffn", (4, 32, 5632)),
                 ("w_ada", (4, D, 6144)), ("b_adaT", (128, 4, 48)), ("norm1T", (128, 4, 8)),
                 ("norm2T", (128, 4, 8)), ("a_w_in", (2, 128, 8 * INA)), ("a_bif", (4, 2, 2)),
                 ("a_onT", (128, 2, 4)), ("a_cwT", (128, 2, 3, 4)), ("a_w_out", (2, 128, 8 * D)),
                 ("c_w_qkv", (2, 128, 8 * 1536)), ("c_qn", (2, 64)), ("c_kn", (2, 64)), ("c_sink", (2, 16)),
                 ("c_w_out", (2, 128, 8 * D)), ("f_w_g", (4, 6, 128, 12288)), ("f_cwT", (128, 4, 3, 44)),
                 ("ident", (128, 128)), ("mP", (128, 128)), ("mS", (128, 128)),
                 ("acur", (128, 512)), ("aprev", (128, 512)), ("anew", (128, 512)),
                 ("acache", (128, 16, 128)), ("onehot", (128, 16)), ("sel4", (4, 4, 128)),
                 ("cosP", (SEQ, 32)), ("sinP", (SEQ, 32)), ("cosS", (128, 32)), ("sinS", (128, 32))]:
        I[n] = din(n, s)
    O = {}
    for n, s in [("yp", (SEQ, D)), ("ys", (128, D)), ("pC", (2, 4, 128, 128)), ("pn", (2, 4, 128)),
                 ("pm", (2, 4)), ("psc", (2, 2, 512)), ("pwk", (2, 128, 256)), ("pwv", (2, 128, 256)),
                 ("pffn", (4, 2, 5632)), ("sC", (2, 16, 4, 128, 128)), ("sn", (2, 64, 128)),
                 ("smT", (4, 2, 16)), ("ssc", (2, 32, 512)), ("swk", (2, 16, 128, 256)),
                 ("swv", (2, 16, 128, 256)), ("sffn", (4, 32, 5632))]:
        O[n] = dout(n, s)

    with contextlib.ExitStack() as st:
        S = Sch(nc, st)
        sbt = lambda n, s, dt=F32: st.enter_context(nc.sbuf_tensor("sb_" + n, list(s), dt))
        PS = [st.enter_context(nc.psum_tensor("ps%d" % i, [128, 512], F32)) for i in range(8)]
        PK = [('P', i) for i in range(8)]
        psb = lambda i: PS[i][:, :].bitcast(BF16)

        xT = sbt("xT", [128, 8, TP]); kX = ('xT',)
        hT = sbt("hT", [128, 8, TP], BF16); kH = ('hT',)
        MOD = sbt("MOD", [128, 4, 48, 17]); kM = ('MOD',)
        WB = sbt("WB", [128, 3 * SLOT], BF16)
        WKt = sbt("WK", [128, 15360])
        AR = Arena(WKt, 15360)
        ident = sbt("ident", [128, 128]); identb = sbt("identb", [128, 128], BF16)
        onesb = sbt("onesb", [128, 128], BF16)
        mPb = sbt("mPb", [128, 128], BF16); mSb = sbt("mSb", [128, 128], BF16)
        acurb = sbt("acurb", [128, 512], BF16); aprevb = sbt("aprevb", [128, 512], BF16)
        anewb = sbt("anewb", [128, 512], BF16); acacheb = sbt("acacheb", [128, 16, 128], BF16)
        onehot = sbt("onehot", [128, 16]); sel4 = sbt("sel4", [4, 4, 128])
        onesrow = sbt("onesrow", [4, TP])
        cosT = sbt("cosT", [128, 4, 32]); sinT = sbt("sinT", [128, 4, 32])
        n1T = sbt("n1T", [128, 4, 8]); n2T = sbt("n2T", [128, 4, 8]); badaT = sbt("badaT", [128, 4, 48])
        fcw = sbt("fcw", [128, 4, 3, 44]); acw = sbt("acw", [128, 2, 3, 4]); aon = sbt("aon", [128, 2, 4])
        bif = sbt("bif", [4, 2, 2]); nbif = sbt("nbif", [4, 2, 2])
        GQ = sbt("GQ", [128, 2, 64]); GK = sbt("GK", [128, 2, 64]); SK = sbt("SK", [128, 2, 16])
        SINKE = sbt("SINKE", [128, 2, 16]); NEGMA = sbt("NEGMA", [128, 2]); tmpc = sbt("tmpc", [128, 4])
        Cst = sbt("Cst", [128, 2, 4, 129]); MROW = sbt("MROW", [4, 2])
        ffc = sbt("ffc", [128, 4, 44, 2]); scc = sbt("scc", [128, 2, 4, 2])
        KTC = sbt("KTC", [128, 2, 2, 128], BF16); VC = sbt("VC", [128, 2, 4, 65], BF16)
        scT = sbt("scT", [128, 8, 17], BF16)
        stage = sbt("stage", [128, D]); call_sb = stage[0:17, :]
        kC = ('const',)

        V, A_, P_, PE_ = 'dve', 'act', 'pool', 'pe'

        def ld(q, dst, src, key, **kw):
            S.dma(q, dst, src, w=[key], chan=key if key != kC else ('c0',), **kw)
        for dst, nm in [(ident, "ident"), (onehot, "onehot"), (sel4, "sel4"), (n1T, "norm1T"), (n2T, "norm2T"),
                        (badaT, "b_adaT"), (fcw, "f_cwT"), (acw, "a_cwT"), (aon, "a_onT"), (bif, "a_bif")]:
            S.dma('sync', dst[:], I[nm], w=[kC], chan=('c0',))
        S.dma('sync', call_sb, I["call"], w=[('stage',)], chan=('istage',))
        for l in range(2):
            S.dma('sync', GQ[:, l, :], I["c_qn"][l].partition_broadcast(128), w=[kC], chan=('c0',))
            S.dma('sync', GK[:, l, :], I["c_kn"][l].partition_broadcast(128), w=[kC], chan=('c0',))
            S.dma('sync', SK[:, l, :], I["c_sink"][l].partition_broadcast(128), w=[kC], chan=('c0',))
        for dst, nm in [(identb, "ident"), (mPb, "mP"), (mSb, "mS"), (acurb, "acur"), (aprevb, "aprev"),
                        (anewb, "anew"), (acacheb, "acache")]:
            S.dma('pool', dst[:], I[nm], w=[kC], chan=('c1',))
        kC2 = ('const2',)
        S.op(V, lambda e: e.memset(onesb[:], 1.0), r=[kC], w=[kC2])
        S.op(V, lambda e: e.memset(onesrow[:], 1.0), w=[kC2])
        S.op(V, lambda e: e.tensor_scalar(out=nbif[:], in0=bif[:], scalar1=-1.0, scalar2=None, op0=ALU.mult), r=[kC], w=[kC2])
        S.op(V, lambda e: e.memset(Cst[:], 0.0), w=[('Cst', 0), ('Cst', 1)])
        S.op(V, lambda e: e.memset(MROW[:], 0.0), w=[('MROW',)])
        S.op(V, lambda e: e.memset(ffc[:], 0.0), w=[('ffc',)])
        S.op(V, lambda e: e.memset(scc[:], 0.0), w=[('scc',)])
        S.op(V, lambda e: e.memset(VC[:], 1.0), w=[('VC',)])
        for l in range(2):
            S.op(V, lambda e, l=l: e.tensor_reduce(out=tmpc[:, 0:1], in_=GQ[:, l, :], axis=AX.X, op=ALU.max, apply_absolute_value=True), r=[kC], w=[('tmpc',)])
            S.op(V, lambda e, l=l: e.tensor_reduce(out=tmpc[:, 1:2], in_=GK[:, l, :], axis=AX.X, op=ALU.max, apply_absolute_value=True), r=[kC], w=[('tmpc',)])
            S.op(V, lambda e, l=l: e.scalar_tensor_tensor(out=NEGMA[:, l:l + 1], in0=tmpc[:, 0:1], scalar=-8.0, in1=tmpc[:, 1:2], op0=ALU.mult, op1=ALU.mult), r=[('tmpc',)], w=[kC2])
            S.op(A_, lambda e, l=l: e.activation(out=SINKE[:, l, :], in_=SK[:, l, :], func=AF.Exp, bias=NEGMA[:, l:l + 1], scale=1.0), r=[kC, kC2], w=[('SINKE',)])

        S.op(A_, lambda e: e.activation(out=call_sb, in_=call_sb, func=AF.Silu), r=[('stage',)], w=[('stage',)])
        for kc in range(8):
            S.op(PE_, lambda e, kc=kc: e.transpose(out=PS[0][:, kc * 17:(kc + 1) * 17], in_=call_sb[:, kc * 128:(kc + 1) * 128], identity=ident[0:17, 0:17]),
                 r=[('stage',), kC], w=[PK[0]])
        S.op(V, lambda e: e.tensor_copy(out=scT[:], in_=PS[0][:, 0:136].rearrange("p (a b) -> p a b", b=17)), r=[PK[0]], w=[('scT',)])
        wa = [WB[:, i * 6144:(i + 1) * 6144] for i in range(2)]
        for l in range(4):
            for kc in range(8):
                sl = (l * 8 + kc) % 2
                S.dma('pool', wa[sl], I["w_ada"][l, kc * 128:(kc + 1) * 128, :], w=[('W', sl)], max_dma_last_dim=4096)
                for fc in range(48):
                    b = 1 + fc // 24
                    o = (fc % 24) * 17
                    S.op(PE_, lambda e, sl=sl, fc=fc, b=b, o=o, kc=kc: e.matmul(PS[b][:, o:o + 17], wa[sl][:, fc * 128:(fc + 1) * 128], scT[:, kc, :],
                                                                            start=(kc == 0 and fc % 24 == 0), stop=(kc == 7), skip_group_check=True),
                         r=[('W', sl), ('scT',)], w=[PK[b]], inc=(fc == 47))
            for b in range(2):
                S.op(V, lambda e, l=l, b=b: e.tensor_tensor(out=MOD[:, l, b * 24:(b + 1) * 24, :], in0=PS[1 + b][:, 0:408].rearrange("p (a b) -> p a b", b=17),
                                                           in1=badaT[:, l, b * 24:(b + 1) * 24].unsqueeze(2).broadcast_to([128, 24, 17]), op=ALU.add),
                     r=[PK[1 + b], kC], w=[kM])
            for (c0, nT) in [(8, n1T), (32, n2T)]:
                S.op(V, lambda e, l=l, c0=c0, nT=nT: e.scalar_tensor_tensor(out=MOD[:, l, c0:c0 + 8, :], in0=MOD[:, l, c0:c0 + 8, :], scalar=1.0,
                                                                           in1=nT[:, l, :].unsqueeze(2).broadcast_to([128, 8, 17]), op0=ALU.add, op1=ALU.mult),
                     r=[kM, kC], w=[kM])

        def fm_rows_out(src_fn, nrows, dst_rows_fn, nchunks, rkeys):
            for c0 in range(0, nchunks, 4):
                n = min(4, nchunks - c0)
                for j in range(n):
                    S.op(PE_, lambda e, j=j, c0=c0: e.transpose(out=PS[3][0:nrows, j * 128:(j + 1) * 128], in_=src_fn(c0 + j), identity=ident[:, :]),
                         r=rkeys + [kC], w=[PK[3]])
                S.op(A_, lambda e, n=n: e.copy(out=stage[0:nrows, 0:n * 128], in_=PS[3][0:nrows, 0:n * 128]), r=[PK[3]], w=[('stage',)])
                S.dma('sync', dst_rows_fn(c0 * 128, n * 128), stage[0:nrows, 0:n * 128], r=[('stage',)], chan=('ostage',))

        def rows_to_fm(src_rows_fn, nrows, dst_fn, nchunks, wkeys):
            for c0 in range(0, nchunks, 8):
                n = min(8, nchunks - c0)
                S.dma('sync', stage[0:nrows, 0:n * 128], src_rows_fn(c0 * 128, n * 128), w=[('stage',)], chan=('istage',))
                for j in range(n):
                    S.op(PE_, lambda e, j=j: e.transpose(out=PS[3][:, j * 32:j * 32 + nrows], in_=stage[0:nrows, j * 128:(j + 1) * 128], identity=ident[0:nrows, 0:nrows]),
                         r=[('stage',), kC], w=[PK[3]])
                for j in range(n):
                    S.op(V, lambda e, j=j, c0=c0: e.tensor_copy(out=dst_fn(c0 + j), in_=PS[3][:, j * 32:j * 32 + nrows]), r=[PK[3]], w=wkeys)

        def run_pass(sample, pi):
            T = 128 if sample else TP
            nseq = 16 if sample else 1
            L = 8 if sample else 128
            LT = 8 if sample else TP
            nch = T // 128
            tok0 = 0 if sample else pi * TP
            last = sample or (pi == NPASS - 1)
            xin = I["xs"] if sample else I["xp"]
            yout = O["ys"] if sample else O["yp"]
            sq0 = 1 if sample else 0
            v3 = lambda ap: ap.rearrange("p (s l) -> p s l", l=LT)

            def modbc(l, ch, kc):
                return MOD[:, l, ch * 8 + kc, sq0:sq0 + nseq].unsqueeze(2).broadcast_to([128, nseq, LT])

            for c in range(nch):
                S.dma('sync', stage[:, :], xin[tok0 + c * 128: tok0 + (c + 1) * 128, :], w=[('stage',)], chan=('istage',))
                for b in range(2):
                    for j in range(4):
                        kc = b * 4 + j
                        S.op(PE_, lambda e, b=b, j=j, kc=kc: e.transpose(out=PS[b][:, j * 128:(j + 1) * 128], in_=stage[:, kc * 128:(kc + 1) * 128], identity=ident[:, :]),
                             r=[('stage',), kC], w=[PK[b]])
                    S.op(V if b == 0 else A_, (lambda e, b=b, c=c: e.tensor_copy(out=xT[:, b * 4:(b + 1) * 4, c * 128:(c + 1) * 128], in_=PS[b][:, :].rearrange("p (a b) -> p a b", b=128))) if b == 0 else
                         (lambda e, b=b, c=c: e.copy(out=xT[:, b * 4:(b + 1) * 4, c * 128:(c + 1) * 128], in_=PS[b][:, :].rearrange("p (a b) -> p a b", b=128))),
                         r=[PK[b]], w=[kX])
            if not sample:
                S.dma('sync', cosT[:, 0:nch, :], I["cosP"][tok0:tok0 + T, :].rearrange("(c p) i -> p c i", p=128), w=[('rope',)])
                S.dma('sync', sinT[:, 0:nch, :], I["sinP"][tok0:tok0 + T, :].rearrange("(c p) i -> p c i", p=128), w=[('rope',)])
            else:
                S.dma('sync', cosT[:, 0, :], I["cosS"], w=[('rope',)])
                S.dma('sync', sinT[:, 0, :], I["sinS"], w=[('rope',)])

            def norm_mod(l, which):
                cA, cB = (1, 0) if which == 1 else (4, 3)
                S.barrier(); AR.reset()
                sqb, ksq = AR.get("sq", [128, 8, T], BF16)
                rs, krs = AR.get("rs", [128, T])
                tmp, ktmp = AR.get("tmp", [128, T])
                S.op(A_, lambda e: e.activation(out=sqb, in_=xT[:, :, 0:T], func=AF.Square), r=[kX], w=[ksq])
                for kc in range(8):
                    S.op(PE_, lambda e, kc=kc: e.matmul(PS[0][:, 0:T], onesb[:, :], sqb[:, kc, :], start=(kc == 0), stop=(kc == 7)), r=[ksq, kC2], w=[PK[0]], inc=(kc == 7))
                S.op(A_, lambda e: e.activation(out=rs, in_=PS[0][:, 0:T], func=AF.Sqrt, bias=EPS, scale=1.0 / D), r=[PK[0]], w=[krs])
                S.op(V, lambda e: e.reciprocal(out=rs, in_=rs), r=[krs], w=[krs])
                for kc in range(8):
                    S.op(V, lambda e, kc=kc: e.tensor_tensor(out=tmp, in0=xT[:, kc, 0:T], in1=rs, op=ALU.mult), r=[kX, krs], w=[ktmp])
                    S.op(V, lambda e, kc=kc: e.tensor_tensor(out=v3(tmp), in0=v3(tmp), in1=modbc(l, cA, kc), op=ALU.mult), r=[ktmp, kM], w=[ktmp])
                    S.op(V, lambda e, kc=kc: e.tensor_tensor(out=v3(hT[:, kc, 0:T]), in0=v3(tmp), in1=modbc(l, cB, kc), op=ALU.add), r=[ktmp, kM], w=[kH])

            def resid_add(l, gch, d, psrc, pkey, tkey_ap):
                tmp, ktmp = tkey_ap
                if not sample:
                    S.op(V, lambda e: e.scalar_tensor_tensor(out=xT[:, d, 0:T], in0=psrc, scalar=MOD[:, l, gch * 8 + d, 0:1], in1=xT[:, d, 0:T], op0=ALU.mult, op1=ALU.add),
                         r=[pkey, kM, kX], w=[kX])
                    return
                S.op(V, lambda e: e.tensor_tensor(out=v3(tmp), in0=v3(psrc), in1=modbc(l, gch, d), op=ALU.mult), r=[pkey, kM], w=[ktmp])
                S.op(V, lambda e: e.tensor_tensor(out=xT[:, d, 0:T], in0=xT[:, d, 0:T], in1=tmp, op=ALU.add), r=[ktmp, kX], w=[kX])

            def conv3(dst3, src3, wfn, keys_r, keys_w, pool=False):
                E1 = P_ if pool else V
                if pool:
                    S.op(E1, lambda e: e.tensor_scalar(out=dst3, in0=src3[:, :, 0:LT], scalar1=wfn(0), scalar2=0.0, op0=ALU.mult, op1=ALU.add), r=keys_r, w=keys_w)
                else:
                    S.op(E1, lambda e: e.tensor_scalar(out=dst3, in0=src3[:, :, 0:LT], scalar1=wfn(0), scalar2=None, op0=ALU.mult), r=keys_r, w=keys_w)
                S.op(V, lambda e: e.scalar_tensor_tensor(out=dst3, in0=src3[:, :, 1:LT + 1], scalar=wfn(1), in1=dst3, op0=ALU.mult, op1=ALU.add), r=keys_r + keys_w, w=keys_w)
                S.op(V, lambda e: e.scalar_tensor_tensor(out=dst3, in0=src3[:, :, 2:LT + 2], scalar=wfn(2), in1=dst3, op0=ALU.mult, op1=ALU.add), r=keys_r + keys_w, w=keys_w)

            def ffn(l):
                gsz = [4, 4, 4, 4, 4, 2]
                gst = [0, 4, 8, 12, 16, 20]

                def wslot(g):
                    base = (g % 3) * SLOT
                    ug = WB[:, base:base + 4096].rearrange("p (k c) -> p k c", c=512)
                    ua = WB[:, base + 4096:base + 8192].rearrange("p (k c) -> p k c", c=512)
                    dn = WB[:, base + 8192:base + 12288].rearrange("p (j d) -> p j d", d=D)
                    return ug, ua, dn

                def issue(g):
                    ug, ua, dn = wslot(g)
                    n = gsz[g]; f0 = gst[g]
                    k = ('W', g % 3)
                    base = (g % 3) * SLOT
                    S.dma('pool', WB[:, base:base + 12288], I["f_w_g"][l, g], w=[k], max_dma_last_dim=8192)
                issue(0); issue(1)
                norm_mod(l, 2)
                if os.environ.get("KD_FFN", "") == "n":
                    return
                S.barrier(); AR.reset()
                UB, kUB = AR.get("UB", [128, 4, nseq * (LT + 2)])
                Y, kY = AR.get("Y", [128, 4, T])
                ACTB, kAB = AR.get("ACTB", [128, 4, T], BF16)
                rt = AR.get("rt", [128, T])
                if sample:
                    car, kcar = AR.get("car", [128, 44, 32])
                    rows_to_fm(lambda o, n: I["stffn"][l, :, o:o + n], 32, lambda c: car[:, c, :], 44, [kcar])
                    carv = lambda idx: car[:, idx, :].rearrange("p (s j) -> p s j", j=2)
                else:
                    kcar = ('ffc',)
                    carv = lambda idx: ffc[:, l, idx, :].unsqueeze(1)
                ub3 = lambda i: UB[:, i, :].rearrange("p (s k) -> p s k", k=LT + 2)

                for g in range(6):
                    if g + 2 < 6:
                        issue(g + 2)
                    ug, ua, dn = wslot(g)
                    kW = ('W', g % 3)
                    n = gsz[g]; f0 = gst[g]
                    for j in range(n):
                        ib = 2 * (j % 2)
                        for part in range(2):
                            i = ib + part
                            wv = ug if part == 0 else ua
                            idx = (0 if part == 0 else 22) + f0 + j
                            for kc in range(8):
                                S.op(PE_, lambda e, i=i, wv=wv, j=j, kc=kc: e.matmul(PS[i][:, 0:T], wv[:, kc, j * 128:(j + 1) * 128], hT[:, kc, 0:T], start=(kc == 0), stop=(kc == 7)),
                                     r=[kW, kH], w=[PK[i]], inc=(kc == 7))
                            S.op(A_, lambda e, i=i, idx=idx: e.copy(out=ub3(i)[:, :, 0:2], in_=carv(idx)), r=[kcar], w=[(kUB, i)])
                            S.op(A_, lambda e, i=i: e.copy(out=ub3(i)[:, :, 2:LT + 2], in_=v3(PS[i][:, 0:T])), r=[PK[i]], w=[(kUB, i)])
                            S.op(A_, lambda e, i=i, idx=idx: e.copy(out=carv(idx), in_=ub3(i)[:, :, LT:LT + 2]), r=[(kUB, i)], w=[kcar])
                            conv3(v3(Y[:, i, :]), ub3(i), lambda t, idx=idx: fcw[:, l, t, idx:idx + 1], [(kUB, i), kC], [(kY, i)], pool=True)
                        S.op(A_, lambda e, ib=ib: e.activation(out=Y[:, ib, :], in_=Y[:, ib, :], func=AF.Silu), r=[(kY, ib)], w=[(kY, ib)])
                        S.op(V, lambda e, j=j, ib=ib: e.tensor_tensor(out=ACTB[:, j, :], in0=Y[:, ib, :], in1=Y[:, ib + 1, :], op=ALU.mult), r=[(kY, ib), (kY, ib + 1)], w=[(kAB, j)])
                    for dh in range(2):
                        for d4 in range(4):
                            d = dh * 4 + d4
                            for j in range(n):
                                S.op(PE_, lambda e, d4=d4, d=d, j=j, dn=dn, n=n: e.matmul(PS[4 + d4][:, 0:T], dn[:, j, d * 128:(d + 1) * 128], ACTB[:, j, :], start=(j == 0), stop=(j == n - 1)),
                                     r=[kW, (kAB, j)], w=[PK[4 + d4]], inc=(j == n - 1))
                            resid_add(l, 5, d, PS[4 + d4][:, 0:T], PK[4 + d4], rt)
                if last:
                    nr = 32 if sample else 2
                    dst = O["sffn"] if sample else O["pffn"]
                    if sample:
                        fm_rows_out(lambda c: car[:, c, :], 32, lambda o, n: dst[l, :, o:o + n], 44, [kcar])
                    else:
                        fm_rows_out(lambda c: ffc[:, l, c, :], 2, lambda o, n: dst[l, :, o:o + n], 44, [kcar])

            def mixer_ab(l):
                i = l // 2
                win = WB[:, 0:8 * INA].rearrange("p (k c) -> p k c", c=INA)
                wout = WB[:, 8 * INA:8 * INA + 8 * D].rearrange("p (k c) -> p k c", c=D)
                kW = [('W', 0), ('W', 1), ('W', 2)]
                S.dma('pool', WB[:, 0:8 * INA], I["a_w_in"][i], w=kW, chan=('W', 0), max_dma_last_dim=8192)
                S.dma('pool', WB[:, 8 * INA:8 * INA + 8 * D], I["a_w_out"][i], w=kW, chan=('W', 0), max_dma_last_dim=8192)
                norm_mod(l, 1)
                S.barrier(); AR.reset()
                qT, kqT = AR.get("qT", [128, 4, T], BF16)
                kTt, kkT = AR.get("kT", [128, 4, T], BF16)
                sgT, ksg = AR.get("sgT", [128, 4, T], BF16)
                scTt, ksc = AR.get("scTt", [128, 4, T], BF16)
                hmT, khm = AR.get("hmT", [128, 4, T], BF16)
                KTOK, kkt = AR.get("KTOK", [128, nch, 512], BF16)
                VT, kvt = AR.get("VT", [128, nch * 4, 129], BF16)
                PB, kPB = AR.get("PB", [128, nseq * (LT + 2)])
                ZX, kZX = AR.get("ZX", [128, T])
                U, kU = AR.get("U", [128, T])
                rt = AR.get("rt", [128, T]) if sample else (None, None)
                rows = {}
                for nm in ["ig", "l1", "cs", "aa", "A", "negm", "prev", "lastr"]:
                    rows[nm] = AR.get("r_" + nm, [4, T], parts=4)
                NEGAX, kNX = AR.get("NEGAX", [4, nseq + T], parts=4)
                MIN, kMIN = AR.get("MIN", [4, 16], parts=4)
                nset = 1 if sample else 2
                COLSb = [AR.get("COLS%d" % z, [128, 20]) for z in range(2)]
                B = []
                for z in range(nset):
                    d_ = {}
                    d_["SA"] = AR.get("SA%d" % z, [128, nseq + 128])
                    d_["WT"] = AR.get("WT%d" % z, [128, 128], BF16)
                    d_["PT"] = AR.get("PT%d" % z, [128, 128], BF16)
                    d_["PVs"] = AR.get("PVs%d" % z, [128, 129])
                    d_["NUM"] = AR.get("NUM%d" % z, [128, 129])
                    d_["JK"] = AR.get("JK%d" % z, [128, 128], BF16)
                    d_["HN"] = AR.get("HN%d" % z, [128, 128], BF16)
                    d_["VS"] = AR.get("VS%d" % z, [128, 129], BF16)
                    d_["SM"] = AR.get("SM%d" % z, [128, 16])
                    d_["WSI"] = AR.get("WSI%d" % z, [128, 16])
                    d_["WCB"] = AR.get("WCB%d" % z, [128, 16])
                    d_["QPAD"] = AR.get("QPAD%d" % z, [128, nseq * 136])
                    B.append(d_)
                if sample:
                    CS_, kCS = AR.get("CSs", [128, 64, 129])
                    S.dma('sync', CS_[:, :, 0:128], I["stC"][i].rearrange("s h k v -> k (s h) v"), w=[kCS])
                    rows_to_fm(lambda o, n: I["stn"][i, :, o:o + n], 64, lambda c: CS_[:, :, 128], 1, [kCS])
                    S.dma('sync', MIN, I["stmT"][:, i, :], w=[kMIN])
                    csv = lambda s_, h: CS_[:, s_ * 4 + h, :]
                    pcar, kpc = AR.get("pcar", [128, 4, 32])
                    rows_to_fm(lambda o, n: I["stsc"][i, :, o:o + n], 32, lambda c: pcar[:, c, :], 4, [kpc])
                    pcv = lambda ch: pcar[:, ch, :].rearrange("p (s j) -> p s j", j=2)
                else:
                    kCS = ('Cst', i)
                    csv = lambda s_, h: Cst[:, i, h, :]
                    kMIN = ('MROW',)
                    kpc = ('scc',)
                    pcv = lambda ch: scc[:, i, ch, :].unsqueeze(1)
                S.op(V, lambda e: e.memset(VT, 1.0), w=[kvt])
                for z in range(nset):
                    S.op(V, lambda e, z=z: e.memset(B[z]["QPAD"][0], 0.0), w=[B[z]["QPAD"][1]])

                def proj_fm(c0, handler, tag):
                    b = proj_fm.n % 2
                    proj_fm.n += 1
                    for kc in range(8):
                        S.op(PE_, lambda e, b=b, kc=kc: e.matmul(PS[b][:, 0:T], win[:, kc, c0:c0 + 128], hT[:, kc, 0:T], start=(kc == 0), stop=(kc == 7)),
                             r=kW + [kH], w=[PK[b]], inc=(kc == 7))
                    handler(PS[b][:, 0:T], PK[b])
                proj_fm.n = 0
                for h in range(4):
                    proj_fm(h * 128, lambda p, k, h=h: S.op(A_, lambda e: e.copy(out=qT[:, h, :], in_=p), r=[k], w=[kqT]), "q")
                    proj_fm(512 + h * 128, lambda p, k, h=h: S.op(A_, lambda e: e.activation(out=kTt[:, h, :], in_=p, func=AF.Copy, scale=128.0 ** -0.5), r=[k], w=[kkT]), "k")
                    proj_fm(1536 + h * 128, lambda p, k, h=h: S.op(A_, lambda e: e.activation(out=sgT[:, h, :], in_=p, func=AF.Sigmoid), r=[k], w=[ksg]), "o")
                pb3 = PB.rearrange("p (s k) -> p s k", k=LT + 2)
                for ch in range(4):
                    proj_fm(3080 + ch * 128, lambda p, k: S.op(A_, lambda e: e.copy(out=ZX, in_=p), r=[k], w=[kZX]), "zx")
                    S.op(A_, lambda e, ch=ch: e.copy(out=pb3[:, :, 0:2], in_=pcv(ch)), r=[kpc], w=[kPB])
                    proj_fm(2568 + ch * 128, lambda p, k: S.op(V, lambda e: e.tensor_tensor(out=pb3[:, :, 2:LT + 2], in0=v3(p), in1=v3(ZX), op=ALU.mult), r=[k, kZX], w=[kPB]), "zc")
                    S.op(A_, lambda e, ch=ch: e.copy(out=pcv(ch), in_=pb3[:, :, LT:LT + 2]), r=[kPB], w=[kpc])
                    conv3(v3(U), pb3, lambda t, ch=ch: acw[:, i, t, ch:ch + 1], [kPB, kC], [kU])
                    proj_fm(2056 + ch * 128, lambda p, k, ch=ch: S.op(V, lambda e: e.tensor_tensor(out=scTt[:, ch, :], in0=p, in1=U, op=ALU.mult), r=[k, kU], w=[ksc]), "zb")
                if last:
                    if sample:
                        fm_rows_out(lambda c: pcar[:, c, :], 32, lambda o, n: O["ssc"][i, :, o:o + n], 4, [kpc])
                    else:
                        fm_rows_out(lambda c: scc[:, i, c, :], 2, lambda o, n: O["psc"][i, :, o:o + n], 4, [kpc])
                rg = lambda nm: rows[nm][0]
                kg = lambda nm: rows[nm][1]
                for gi, (c0, nm) in enumerate([(2048, "ig"), (2052, "l1")]):
                    for kc in range(8):
                        S.op(PE_, lambda e, kc=kc, c0=c0: e.matmul(PS[2][0:4, 0:T], win[:, kc, c0:c0 + 4], hT[:, kc, 0:T], start=(kc == 0), stop=(kc == 7)),
                             r=kW + [kH], w=[PK[2]], inc=(kc == 7))
                    if gi == 0:
                        S.op(A_, lambda e: e.activation(out=rg("ig"), in_=PS[2][0:4, 0:T], func=AF.Identity, bias=bif[:, i, 0:1], scale=1.0), r=[PK[2], kC], w=[kg("ig")])
                    else:
                        S.op(A_, lambda e: e.activation(out=rg("l1"), in_=PS[2][0:4, 0:T], func=AF.Exp, bias=nbif[:, i, 1:2], scale=-1.0), r=[PK[2], kC2], w=[kg("l1")])
                        S.op(A_, lambda e: e.activation(out=rg("l1"), in_=rg("l1"), func=AF.Ln, bias=1.0, scale=1.0), r=[kg("l1")], w=[kg("l1")])
                r3 = lambda ap: ap.rearrange("p (s l) -> p s l", l=LT)
                for s_ in range(nseq):
                    sl = slice(s_ * LT, (s_ + 1) * LT)
                    S.op(V, lambda e, sl=sl: e.tensor_tensor_scan(out=rg("cs")[:, sl], data0=onesrow[:, 0:LT], data1=rg("l1")[:, sl], initial=0.0, op0=ALU.mult, op1=ALU.add),
                         r=[kg("l1"), kC2], w=[kg("cs")])
                S.op(V, lambda e: e.tensor_tensor(out=rg("aa"), in0=rg("ig"), in1=rg("cs"), op=ALU.add), r=[kg("ig"), kg("cs")], w=[kg("aa")])
                minap = (lambda s_: MIN[:, s_:s_ + 1]) if sample else (lambda s_: MROW[:, i:i + 1])
                for s_ in range(nseq):
                    sl = slice(s_ * LT, (s_ + 1) * LT)
                    S.op(V, lambda e, sl=sl, s_=s_: e.tensor_tensor_scan(out=rg("A")[:, sl], data0=rg("aa")[:, sl], data1=rg("aa")[:, sl], initial=minap(s_), op0=ALU.max, op1=ALU.max),
                         r=[kg("aa"), kMIN], w=[kg("A")])
                minall = MIN[:, 0:16] if sample else MROW[:, i:i + 1]
                S.op(V, lambda e: e.tensor_scalar(out=NEGAX[:, 0:nseq], in0=minall, scalar1=-1.0, scalar2=None, op0=ALU.mult), r=[kMIN], w=[kNX])
                S.op(V, lambda e: e.tensor_scalar(out=NEGAX[:, nseq:nseq + T], in0=rg("A"), scalar1=-1.0, scalar2=None, op0=ALU.mult), r=[kg("A")], w=[kNX])
                S.op(V, lambda e: e.tensor_tensor(out=rg("negm"), in0=rg("cs"), in1=rg("A"), op=ALU.subtract), r=[kg("cs"), kg("A")], w=[kg("negm")])
                if sample:
                    S.op(V, lambda e: e.tensor_copy(out=r3(rg("prev")), in_=MIN[:, 0:16].unsqueeze(2).broadcast_to([4, 16, 8])), r=[kMIN], w=[kg("prev")])
                    S.op(V, lambda e: e.tensor_copy(out=r3(rg("lastr")), in_=r3(NEGAX[:, 16:16 + T])[:, :, 7:8].broadcast_to([4, 16, 8])), r=[kNX], w=[kg("lastr")])
                else:
                    c3 = lambda ap: ap.rearrange("p (c l) -> p c l", l=128)
                    S.op(V, lambda e: e.tensor_scalar(out=c3(rg("prev")), in0=c3(NEGAX[:, 0:T])[:, :, 0:1].broadcast_to([4, nch, 128]), scalar1=-1.0, scalar2=None, op0=ALU.mult), r=[kNX], w=[kg("prev")])
                    S.op(V, lambda e: e.tensor_copy(out=c3(rg("lastr")), in_=c3(NEGAX[:, 1:T + 1])[:, :, 127:128].broadcast_to([4, nch, 128])), r=[kNX], w=[kg("lastr")])
                if sample:
                    if last:
                        MOUT, kMO = AR.get("MOUT", [4, 16], parts=4)
                        S.op(V, lambda e: e.tensor_scalar(out=MOUT, in0=r3(rg("negm"))[:, :, 7], scalar1=-1.0, scalar2=None, op0=ALU.mult), r=[kg("negm")], w=[kMO])
                        S.dma('sync', O["smT"][:, i, :], MOUT, r=[kMO], chan=('osm',))
                else:
                    S.op(V, lambda e: e.tensor_scalar(out=MROW[:, i:i + 1], in0=rg("negm")[:, T - 1:T], scalar1=-1.0, scalar2=None, op0=ALU.mult), r=[kg("negm")], w=[kMIN])
                    if last:
                        S.dma('sync', O["pm"][i].unsqueeze(1), MROW[:, i:i + 1], r=[kMIN], chan=('opm',))
                for c in range(nch):
                    cs_ = slice(c * 128, (c + 1) * 128)
                    for part, c0 in [(0, 512), (1, 1024)]:
                        b = 4 + part
                        for kc in range(8):
                            S.op(PE_, lambda e, kc=kc, b=b, c0=c0, cs_=cs_: e.matmul(PS[b][:, 0:512], hT[:, kc, cs_], win[:, kc, c0:c0 + 512], start=(kc == 0), stop=(kc == 7)),
                                 r=kW + [kH], w=[PK[b]], inc=(kc == 7))
                    S.op(A_, lambda e, c=c: e.activation(out=KTOK[:, c, :], in_=PS[4][:, 0:512], func=AF.Copy, scale=128.0 ** -0.5), r=[PK[4]], w=[kkt])
                    S.op(A_, lambda e, c=c: e.copy(out=VT[:, c * 4:(c + 1) * 4, 0:128], in_=PS[5][:, 0:512].rearrange("p (h v) -> p h v", v=128)), r=[PK[5]], w=[kvt])
                maskb = mSb if sample else mPb
                kCSh = lambda h: (kCS, h)
                S.barrier()
                for c in range(nch):
                    cs_ = slice(c * 128, (c + 1) * 128)
                    COLS, kCOLS = COLSb[c % 2]
                    allp3 = [PK[6]]
                    for bi, nm in enumerate(["aa", "A", "negm", "prev", "lastr"]):
                        src = NEGAX[:, nseq + c * 128:nseq + (c + 1) * 128] if nm == "A" else rg(nm)[:, cs_]
                        S.op(PE_, lambda e, bi=bi, src=src: e.transpose(out=PS[6][:, 256 + bi * 4:256 + (bi + 1) * 4], in_=src, identity=ident[0:4, 0:4]),
                             r=[kg(nm) if nm != "A" else kNX, kC], w=allp3)
                    S.op(V, lambda e, COLS=COLS: e.tensor_copy(out=COLS, in_=PS[6][:, 256:276]), r=allp3, w=[kCOLS])

                    def stage_fns(h, z, c=c, cs_=cs_, COLS=COLS, kCOLS=kCOLS):
                        bz = B[z]
                        SA, kSA = bz["SA"]; WT, kWT = bz["WT"]; PT, kPT = bz["PT"]; PVs, kPVs = bz["PVs"]
                        NUM, kNUM = bz["NUM"]; JK, kJK = bz["JK"]; HN, kHN = bz["HN"]; VS, kVS = bz["VS"]
                        SM, kSM = bz["SM"]; WSI, kWSI = bz["WSI"]; WCB, kWCB = bz["WCB"]; QPAD, kQP = bz["QPAD"]
                        bA, bB, bC, bD = z, z + 2, z + 4, z + 6
                        ca = COLS[:, 0 + h:1 + h]; cnA = COLS[:, 4 + h:5 + h]; cnm = COLS[:, 8 + h:9 + h]
                        cpv = COLS[:, 12 + h:13 + h]; cla = COLS[:, 16 + h:17 + h]
                        qpd = QPAD.rearrange("p (s k) -> p s k", k=136)[:, :, 0:L]
                        fns = []

                        def s1():
                            if sample:
                                S.op(PE_, lambda e: e.matmul(PS[bA][:, 0:16], sel4[:, h, :], NEGAX[:, 0:16], start=True, stop=False, skip_group_check=True), r=[kNX, kC], w=[PK[bA]], inc=False)
                                S.op(PE_, lambda e: e.matmul(PS[bA][:, 16:144], sel4[:, h, :], NEGAX[:, 16:144], start=False, stop=True, skip_group_check=True), r=[kNX, kC], w=[PK[bA]])
                            else:
                                S.op(PE_, lambda e: e.matmul(PS[bA][:, 0:129], sel4[:, h, :], NEGAX[:, c * 128:c * 128 + 129], start=True, stop=True), r=[kNX, kC], w=[PK[bA]])
                            S.op(PE_, lambda e: e.matmul(PS[bB][:, 0:128], sel4[:, h, :], NEGAX[:, nseq + c * 128:nseq + (c + 1) * 128], start=True, stop=False), r=[kNX, kC, kCOLS], w=[PK[bB]], inc=False)
                            S.op(PE_, lambda e: e.matmul(PS[bB][:, 0:128], identb[:, :], maskb[:, :], start=False, stop=True), r=[kC], w=[PK[bB]])
                            S.op(PE_, lambda e: e.matmul(PS[bC][:, 0:128], kTt[:, h, cs_], qT[:, h, cs_], start=True, stop=True), r=[kkT, kqT], w=[PK[bC]])
                        fns.append(s1)

                        def s2():
                            S.op(A_, lambda e: e.copy(out=SA, in_=PS[bA][:, 0:nseq + 128]), r=[PK[bA]], w=[kSA])
                            S.op(A_, lambda e: e.activation(out=WT, in_=PS[bB][:, 0:128], func=AF.Exp, bias=ca, scale=1.0), r=[PK[bB], kCOLS], w=[kWT])
                            S.op(A_, lambda e: e.activation(out=SM[:, 0:1], in_=cnA, func=AF.Exp, bias=cpv, scale=1.0), r=[kCOLS], w=[(kSM, 0)])
                            S.op(A_, lambda e: e.activation(out=SM[:, 1:2], in_=ca, func=AF.Exp, bias=cla, scale=1.0), r=[kCOLS], w=[(kSM, 1)])
                            S.op(A_, lambda e: e.activation(out=SM[:, 2:3], in_=cnm, func=AF.Exp), r=[kCOLS], w=[(kSM, 2)])
                        fns.append(s2)

                        def s3():
                            S.op(V, lambda e: e.tensor_tensor(out=PT, in0=PS[bC][:, 0:128], in1=WT, op=ALU.mult), r=[PK[bC], kWT], w=[kPT])
                            S.op(V, lambda e: e.tensor_copy(out=qpd, in_=qT[:, h, cs_].rearrange("p (s l) -> p s l", l=L)), r=[kqT], w=[kQP])
                            S.op(V, lambda e: e.tensor_scalar(out=WSI[:, 0:nseq], in0=onehot[:, 0:nseq] if sample else onesb[:, 0:1], scalar1=SM[:, 1:2], scalar2=None, op0=ALU.mult), r=[(kSM, 1), kC, kC2], w=[kWSI])
                            sa_last = SA[:, nseq:nseq + 128].rearrange("p (s l) -> p s l", l=L)[:, :, L - 1]
                            S.op(V, lambda e: e.tensor_tensor(out=WCB[:, 0:nseq], in0=sa_last, in1=SA[:, 0:nseq], op=ALU.subtract), r=[kSA], w=[kWCB])
                        fns.append(s3)

                        def s4():
                            S.op(PE_, lambda e: e.matmul(PS[bB][:, 256:385], PT, VT[:, c * 4 + h, :], start=True, stop=True), r=[kPT, kvt], w=[PK[bB]])
                            for s_ in range(nseq):
                                S.op(PE_, lambda e, s_=s_: e.matmul(PS[bA][:, 256:385], QPAD[:, s_ * 128:(s_ + 1) * 128], csv(s_, h), start=(s_ == 0), stop=(s_ == nseq - 1)),
                                     r=[kQP, kCSh(h)], w=[PK[bA]], inc=(s_ == nseq - 1))
                        fns.append(s4)

                        def s5():
                            S.op(A_, lambda e: e.copy(out=PVs, in_=PS[bB][:, 256:385]), r=[PK[bB]], w=[kPVs])
                            S.op(A_, lambda e: e.activation(out=WCB[:, 0:nseq], in_=WCB[:, 0:nseq], func=AF.Exp), r=[kWCB], w=[kWCB])
                        fns.append(s5)

                        def s6():
                            S.op(V, lambda e: e.scalar_tensor_tensor(out=NUM, in0=PS[bA][:, 256:385], scalar=SM[:, 0:1], in1=PVs, op0=ALU.mult, op1=ALU.add), r=[PK[bA], (kSM, 0), kPVs], w=[kNUM])
                        fns.append(s6)

                        def s7():
                            S.op(A_, lambda e: e.activation(out=SM[:, 3:4], in_=NUM[:, 128:129], func=AF.Abs), r=[kNUM], w=[(kSM, 3)])
                        fns.append(s7)

                        def s8():
                            S.op(V, lambda e: e.tensor_tensor(out=SM[:, 3:4], in0=SM[:, 3:4], in1=SM[:, 2:3], op=ALU.max), r=[(kSM, 3), (kSM, 2)], w=[(kSM, 3)])
                            S.op(V, lambda e: e.reciprocal(out=SM[:, 4:5], in_=SM[:, 3:4]), r=[(kSM, 3)], w=[(kSM, 4)])
                        fns.append(s8)

                        def s9():
                            S.op(A_, lambda e: e.activation(out=JK, in_=NUM[:, 0:128], func=AF.Square, scale=SM[:, 4:5], accum_out=SM[:, 5:6]), r=[kNUM, (kSM, 4)], w=[kJK, (kSM, 5)])
                            S.op(A_, lambda e: e.activation(out=SM[:, 6:7], in_=SM[:, 5:6], func=AF.Sqrt, bias=EPS, scale=1.0 / 128), r=[(kSM, 5)], w=[(kSM, 6)])
                        fns.append(s9)

                        def s10():
                            S.op(V, lambda e: e.reciprocal(out=SM[:, 6:7], in_=SM[:, 6:7]), r=[(kSM, 6)], w=[(kSM, 6)])
                            S.op(V, lambda e: e.tensor_tensor(out=SM[:, 7:8], in0=SM[:, 6:7], in1=SM[:, 4:5], op=ALU.mult), r=[(kSM, 6), (kSM, 4)], w=[(kSM, 7)])
                            S.op(V, lambda e: e.tensor_scalar(out=HN, in0=NUM[:, 0:128], scalar1=SM[:, 7:8], scalar2=None, op0=ALU.mult), r=[kNUM, (kSM, 7)], w=[kHN])
                        fns.append(s10)

                        def s11():
                            S.op(PE_, lambda e: e.transpose(out=psb(bC)[:, 512:640], in_=HN, identity=identb[:, :]), r=[kHN, kC], w=[PK[bC]])
                        fns.append(s11)

                        def s12():
                            S.op(V, lambda e: e.scalar_tensor_tensor(out=hmT[:, h, cs_], in0=psb(bC)[:, 512:640], scalar=aon[:, i, h:h + 1], in1=sgT[:, h, cs_], op0=ALU.mult, op1=ALU.mult),
                                 r=[PK[bC], kC, ksg], w=[(khm, h)])
                        fns.append(s12)

                        def s13():
                            for s_ in range(nseq):
                                S.op(V, lambda e, s_=s_: e.tensor_scalar(out=VS, in0=VT[:, c * 4 + h, :], scalar1=WSI[:, s_:s_ + 1], scalar2=None, op0=ALU.mult), r=[kvt, kWSI], w=[kVS])
                                S.op(PE_, lambda e: e.matmul(PS[bD][:, 0:129], KTOK[:, c, h * 128:(h + 1) * 128], VS, start=True, stop=True), r=[kkt, kVS], w=[PK[bD]])
                                S.op(V, lambda e, s_=s_: e.scalar_tensor_tensor(out=csv(s_, h), in0=csv(s_, h), scalar=WCB[:, s_:s_ + 1], in1=PS[bD][:, 0:129], op0=ALU.mult, op1=ALU.add),
                                     r=[PK[bD], kWCB, kCSh(h)], w=[kCSh(h)])
                        fns.append(s13)
                        return fns

                    groups = [[0], [1], [2], [3]] if nset == 1 else [[0, 1], [2, 3]]
                    for grp in groups:
                        fl = [stage_fns(h, z) for z, h in enumerate(grp)]
                        for si in range(len(fl[0])):
                            for f_ in fl:
                                f_[si]()
                S.barrier()
                khm_all = [(khm, h) for h in range(4)]
                kCS_all = [kCSh(h) for h in range(4)]
                if last:
                    if sample:
                        S.dma('sync', O["sC"][i].rearrange("s h k v -> k (s h) v"), CS_[:, :, 0:128], r=kCS_all, chan=('osC',))
                        fm_rows_out(lambda c: CS_[:, :, 128], 64, lambda o, n: O["sn"][i, :, o:o + n], 1, kCS_all)
                    else:
                        S.dma('sync', O["pC"][i].rearrange("h k v -> k h v"), Cst[:, i, :, 0:128], r=kCS_all, chan=('opC',))
                        fm_rows_out(lambda c: Cst[:, i, :, 128], 4, lambda o, n: O["pn"][i, :, o:o + n], 1, kCS_all)
                for d in range(8):
                    b = d % 2
                    for j in range(8):
                        rhs = hmT[:, j, :] if j < 4 else scTt[:, j - 4, :]
                        S.op(PE_, lambda e, b=b, j=j, d=d, rhs=rhs: e.matmul(PS[b][:, 0:T], wout[:, j, d * 128:(d + 1) * 128], rhs, start=(j == 0), stop=(j == 7)),
                             r=kW + khm_all + [ksc], w=[PK[b]], inc=(j == 7))
                    resid_add(l, 2, d, PS[b][:, 0:T], PK[b], rt)

            def mixer_c(l):
                jl = l // 2
                wq = WB[:, 0:8 * 1536].rearrange("p (k c) -> p k c", c=1536)
                wo = WB[:, 8 * 1536:8 * 1536 + 8 * D].rearrange("p (k c) -> p k c", c=D)
                kW = [('W', 0), ('W', 1), ('W', 2)]
                S.dma('pool', WB[:, 0:8 * 1536], I["c_w_qkv"][jl], w=kW, chan=('W', 0), max_dma_last_dim=8192)
                S.dma('pool', WB[:, 8 * 1536:8 * 1536 + 8 * D], I["c_w_out"][jl], w=kW, chan=('W', 0), max_dma_last_dim=8192)
                norm_mod(l, 1)
                S.barrier(); AR.reset()
                QK, kQK = AR.get("QK", [128, 1280])
                SQ, kSQ = AR.get("SQ", [128, 1280])
                QR, kQR = AR.get("QR", [128, 1280])
                QRb, kQRb = AR.get("QRb", [128, 1280], BF16)
                T1, kT1 = AR.get("T1", [128, 640])
                T2, kT2 = AR.get("T2", [128, 640])
                VV, kVV = AR.get("VV", [128, 256])
                SS, kSS = AR.get("SS", [128, 20])
                QT, kQT = AR.get("QTz", [128, 16, T], BF16)
                S.op(V, lambda e: e.memset(QT, 0.0), w=[kQT])
                QT5 = QT.rearrange("p (a b g) t -> p a b g t", a=2, b=2)
                KT, kKT = AR.get("KT", [128, 2, T], BF16)
                VTa, kVTa = AR.get("VTa", [128, nch * 4, 65], BF16)
                PTb = [AR.get("PTb%d" % z, [128, 512], BF16) for z in range(2)]
                DEN, kDEN = AR.get("DEN", [128, 4])
                OTOK, kOT = AR.get("OTOK", [128, 16, 64], BF16)
                OTT, kOTT = AR.get("OTT", [128, 8, T], BF16)
                rt = AR.get("rt", [128, T])
                if sample:
                    CKb, kCKb = AR.get("CKb", [128, 16, 256], BF16)
                    KCT, kKCT = AR.get("KCT", [128, 32, 128], BF16)
                    VCs, kVCs = AR.get("VCs", [128, 64, 65], BF16)
                    S.op(V, lambda e: e.memset(VCs, 1.0), w=[kVCs])
                    S.dma('pool', CKb, I["ck"][jl].rearrange("s p c -> p s c"), w=[kCKb], max_dma_last_dim=1024)
                    for s_ in range(16):
                        S.dma('pool', VCs[:, s_ * 4:(s_ + 1) * 4, 0:64], I["cv"][jl, s_].rearrange("p (h d) -> p h d", d=64), w=[kVCs], max_dma_last_dim=256)
                    for s_ in range(16):
                        for j in range(2):
                            S.op(PE_, lambda e, s_=s_, j=j: e.transpose(out=psb(3)[:, j * 128:(j + 1) * 128], in_=CKb[:, s_, j * 128:(j + 1) * 128], identity=identb[:, :]), r=[kCKb, kC], w=[PK[3]])
                        S.op(V, lambda e, s_=s_: e.tensor_copy(out=KCT[:, 2 * s_:2 * s_ + 2, :], in_=psb(3)[:, 0:256].rearrange("p (a b) -> p a b", b=128)), r=[PK[3]], w=[kKCT])
                    S.dma('sync', O["swk"][jl, :, 0:120, :], I["ck"][jl, :, 8:128, :], chan=('dd',))
                    S.dma('sync', O["swv"][jl, :, 0:120, :], I["cv"][jl, :, 8:128, :], chan=('dd',))
                S.op(V, lambda e: e.memset(VTa, 1.0), w=[kVTa])
                if int(os.environ.get('KD_C', 9)) < 1:
                    return
                qk3 = lambda ap: ap.rearrange("p (h d) -> p h d", d=64)
                for c in range(nch):
                    cs_ = slice(c * 128, (c + 1) * 128)
                    for b in range(3):
                        for kc in range(8):
                            S.op(PE_, lambda e, b=b, kc=kc, cs_=cs_: e.matmul(PS[b][:, 0:512], hT[:, kc, cs_], wq[:, kc, b * 512:(b + 1) * 512], start=(kc == 0), stop=(kc == 7)),
                                 r=kW + [kH], w=[PK[b]], inc=(kc == 7))
                    S.op(A_, lambda e: e.copy(out=QK[:, 0:512], in_=PS[0][:, 0:512]), r=[PK[0]], w=[kQK])
                    S.op(A_, lambda e: e.copy(out=QK[:, 512:1024], in_=PS[1][:, 0:512]), r=[PK[1]], w=[kQK])
                    S.op(A_, lambda e: e.copy(out=QK[:, 1024:1280], in_=PS[2][:, 0:256]), r=[PK[2]], w=[kQK])
                    if os.environ.get('KD_X', '') == 'a':
                        continue
                    S.op(A_, lambda e: e.copy(out=VV, in_=PS[2][:, 256:512]), r=[PK[2]], w=[kVV])
                    if os.environ.get('KD_X', '') == 'b':
                        continue
                    S.op(V, lambda e, c=c: e.tensor_copy(out=VTa[:, c * 4:(c + 1) * 4, 0:64], in_=VV.rearrange("p (h d) -> p h d", d=64)), r=[kVV], w=[kVTa])
                    if int(os.environ.get('KD_C', 9)) < 2:
                        continue
                    S.op(A_, lambda e: e.activation(out=SQ, in_=QK, func=AF.Square), r=[kQK], w=[kSQ])
                    S.op(V, lambda e: e.tensor_reduce(out=SS, in_=qk3(SQ), axis=AX.X, op=ALU.add), r=[kSQ], w=[kSS])
                    S.op(A_, lambda e: e.activation(out=SS, in_=SS, func=AF.Sqrt, bias=EPS, scale=1.0 / 64), r=[kSS], w=[kSS])
                    S.op(V, lambda e: e.reciprocal(out=SS, in_=SS), r=[kSS], w=[kSS])
                    S.op(V, lambda e: e.tensor_tensor(out=qk3(QK), in0=qk3(QK), in1=SS.unsqueeze(2).broadcast_to([128, 20, 64]), op=ALU.mult), r=[kSS, kQK], w=[kQK])
                    S.op(V, lambda e: e.tensor_tensor(out=qk3(QK[:, 0:1024]), in0=qk3(QK[:, 0:1024]), in1=GQ[:, jl, :].unsqueeze(1).broadcast_to([128, 16, 64]), op=ALU.mult), r=[kC, kQK], w=[kQK])
                    S.op(V, lambda e: e.tensor_tensor(out=qk3(QK[:, 1024:1280]), in0=qk3(QK[:, 1024:1280]), in1=GK[:, jl, :].unsqueeze(1).broadcast_to([128, 4, 64]), op=ALU.mult), r=[kC, kQK], w=[kQK])
                    if int(os.environ.get('KD_C', 9)) < 3:
                        continue
                    cosb = cosT[:, c, :].unsqueeze(1).broadcast_to([128, 20, 32])
                    sinb = sinT[:, c, :].unsqueeze(1).broadcast_to([128, 20, 32])
                    x1 = qk3(QK)[:, :, 0:32]; x2 = qk3(QK)[:, :, 32:64]
                    t1 = T1.rearrange("p (h d) -> p h d", d=32); t2 = T2.rearrange("p (h d) -> p h d", d=32)
                    S.op(P_, lambda e, x1=x1, cosb=cosb: e.tensor_tensor(out=t1, in0=x1, in1=cosb, op=ALU.mult), r=[kQK, ('rope',)], w=[kT1])
                    S.op(P_, lambda e, x2=x2, sinb=sinb: e.tensor_tensor(out=t2, in0=x2, in1=sinb, op=ALU.mult), r=[kQK, ('rope',)], w=[kT2])
                    S.op(V, lambda e: e.tensor_tensor(out=qk3(QR)[:, :, 0:32], in0=t1, in1=t2, op=ALU.subtract), r=[kT1, kT2], w=[kQR])
                    S.op(P_, lambda e, x2=x2, cosb=cosb: e.tensor_tensor(out=t1, in0=x2, in1=cosb, op=ALU.mult), r=[kQK, ('rope',)], w=[kT1])
                    S.op(P_, lambda e, x1=x1, sinb=sinb: e.tensor_tensor(out=t2, in0=x1, in1=sinb, op=ALU.mult), r=[kQK, ('rope',)], w=[kT2])
                    S.op(V, lambda e: e.tensor_tensor(out=qk3(QR)[:, :, 32:64], in0=t1, in1=t2, op=ALU.add), r=[kT1, kT2], w=[kQR])
                    S.op(A_, lambda e: e.copy(out=QRb, in_=QR), r=[kQR], w=[kQRb])
                    if int(os.environ.get('KD_C', 9)) < 4:
                        continue
                    for j in range(8):
                        S.op(PE_, lambda e, j=j: e.transpose(out=psb(3)[:, j * 128:(j + 1) * 128], in_=QRb[:, j * 128:(j + 1) * 128], identity=identb[:, :]), r=[kQRb, kC], w=[PK[3]])
                    S.op(V, lambda e, cs_=cs_: e.tensor_copy(out=QT5[0:64, :, 0, :, cs_], in_=psb(3)[0:64, 0:1024].rearrange("p (a g t) -> p a g t", a=2, g=4)), r=[PK[3]], w=[kQT])
                    S.op(V, lambda e, cs_=cs_: e.tensor_copy(out=QT5[64:128, :, 1, :, cs_], in_=psb(3)[64:128, 0:1024].rearrange("p (a g t) -> p a g t", a=2, g=4)), r=[PK[3]], w=[kQT])
                    for j in range(2):
                        S.op(PE_, lambda e, j=j: e.transpose(out=psb(4)[:, j * 128:(j + 1) * 128], in_=QRb[:, 1024 + j * 128:1024 + (j + 1) * 128], identity=identb[:, :]), r=[kQRb, kC], w=[PK[4]])
                    S.op(V, lambda e, cs_=cs_: e.tensor_copy(out=KT[:, :, cs_], in_=psb(4)[:, 0:256].rearrange("p (a b) -> p a b", b=128)), r=[PK[4]], w=[kKT])
                    if sample:
                        for s_ in range(16):
                            S.dma('sync', O["swk"][jl, s_, 120:128, :], QR[s_ * 8:(s_ + 1) * 8, 1024:1280], r=[kQR], chan=('oswk',))
                            S.dma('sync', O["swv"][jl, s_, 120:128, :], VV[s_ * 8:(s_ + 1) * 8, :], r=[kVV], chan=('oswv',))
                    elif last and c == nch - 1:
                        S.dma('sync', O["pwk"][jl], QR[:, 1024:1280], r=[kQR], chan=('opwk',))
                        S.dma('sync', O["pwv"][jl], VV, r=[kVV], chan=('opwv',))
                    if int(os.environ.get('KD_C', 9)) < 5:
                        continue
                    for kap in range(4):
                        base = 64 * (kap % 2)
                        pr = kap // 2
                        blocks = []
                        if sample:
                            for s_ in range(16):
                                blocks.append((KCT[:, 2 * s_ + pr, :], VCs[:, s_ * 4 + kap, :],
                                               acacheb[:, s_, :].unsqueeze(1).broadcast_to([128, 4, 128]), [kKCT], [kVCs]))
                            blocks.append((KT[:, pr, cs_], VTa[:, c * 4 + kap, :], anewb[:, :].rearrange("p (g t) -> p g t", t=128), [kKT], [kVTa]))
                        else:
                            blocks.append((KT[:, pr, cs_], VTa[:, c * 4 + kap, :], acurb[:, :].rearrange("p (g t) -> p g t", t=128), [kKT], [kVTa]))
                            if c > 0:
                                ps_ = slice((c - 1) * 128, c * 128)
                                blocks.append((KT[:, pr, ps_], VTa[:, (c - 1) * 4 + kap, :], aprevb[:, :].rearrange("p (g t) -> p g t", t=128), [kKT], [kVTa]))
                            elif pi > 0:
                                blocks.append((KTC[:, jl, pr, :], VC[:, jl, kap, :], aprevb[:, :].rearrange("p (g t) -> p g t", t=128), [('KTC',)], [('VC',)]))
                        qrhs = QT[:, 4 * kap:4 * kap + 4, cs_]
                        for bi, (kb, vb, mb, kr, vr) in enumerate(blocks):
                            pt, kpt = PTb[bi % 2]
                            ps5 = PS[5][:, 0:512].rearrange("p (g t) -> p g t", t=128)
                            S.op(PE_, lambda e, kb=kb, qrhs=qrhs, ps5=ps5: e.matmul(ps5, kb, qrhs, start=True, stop=False), r=kr + [kQT], w=[PK[5]], inc=False)
                            S.op(PE_, lambda e, mb=mb, ps5=ps5: e.matmul(ps5, identb[:, :], mb, start=False, stop=True), r=[kC], w=[PK[5]])
                            S.op(A_, lambda e, pt=pt: e.activation(out=pt, in_=PS[5][:, 0:512], func=AF.Exp, bias=NEGMA[:, jl:jl + 1], scale=0.125), r=[PK[5], kC2], w=[kpt])
                            for g in range(4):
                                S.op(PE_, lambda e, g=g, pt=pt, vb=vb, bi=bi, nb=len(blocks): e.matmul(PS[6][:, g * 65:(g + 1) * 65], pt[:, g * 128:(g + 1) * 128], vb,
                                                                                    start=(bi == 0 and g == 0), stop=(bi == nb - 1), skip_group_check=True),
                                     r=[kpt] + vr, w=[PK[6]], inc=(g == 3))
                        s0 = 8 * pr + (kap % 2)
                        o3 = PS[6][:, 0:260].rearrange("p (g d) -> p g d", d=65)
                        S.op(V, lambda e, s0=s0, o3=o3: e.tensor_tensor(out=DEN, in0=o3[:, :, 64], in1=SINKE[:, jl, s0:s0 + 7:2], op=ALU.add), r=[PK[6], ('SINKE',)], w=[kDEN])
                        S.op(V, lambda e: e.reciprocal(out=DEN, in_=DEN), r=[kDEN], w=[kDEN])
                        S.op(V, lambda e, s0=s0, o3=o3: e.tensor_tensor(out=OTOK[:, s0:s0 + 7:2, :], in0=o3[:, :, 0:64], in1=DEN.unsqueeze(2).broadcast_to([128, 4, 64]), op=ALU.mult),
                             r=[PK[6], kDEN], w=[kOT])
                    if int(os.environ.get('KD_C', 9)) < 6:
                        continue
                    otf = OTOK.rearrange("p h d -> p (h d)")
                    for j in range(8):
                        S.op(PE_, lambda e, j=j: e.transpose(out=psb(7)[:, j * 128:(j + 1) * 128], in_=otf[:, j * 128:(j + 1) * 128], identity=identb[:, :]), r=[kOT, kC], w=[PK[7]])
                    S.op(V, lambda e, cs_=cs_: e.tensor_copy(out=OTT[:, :, cs_], in_=psb(7)[:, 0:1024].rearrange("p (a b) -> p a b", b=128)), r=[PK[7]], w=[kOTT])
                if int(os.environ.get('KD_C', 9)) < 7:
                    return
                if not sample:
                    ls_ = slice((nch - 1) * 128, nch * 128)
                    S.op(V, lambda e: e.tensor_copy(out=KTC[:, jl, :, :], in_=KT[:, :, ls_]), r=[kKT], w=[('KTC',)])
                    S.op(V, lambda e: e.tensor_copy(out=VC[:, jl, :, :], in_=VTa[:, (nch - 1) * 4:nch * 4, :]), r=[kVTa], w=[('VC',)])
                for d in range(8):
                    b = d % 2
                    for j in range(8):
                        S.op(PE_, lambda e, b=b, j=j, d=d: e.matmul(PS[b][:, 0:T], wo[:, j, d * 128:(d + 1) * 128], OTT[:, j, :], start=(j == 0), stop=(j == 7)),
                             r=kW + [kOTT], w=[PK[b]], inc=(j == 7))
                    resid_add(l, 2, d, PS[b][:, 0:T], PK[b], rt)

            for l in range(4):
                if str(l) not in os.environ.get("KD_LAYERS", "0123"):
                    continue
                if "m" in os.environ.get("KD_PARTS", "mf"):
                    if l % 2 == 0:
                        mixer_ab(l)
                    else:
                        mixer_c(l)
                if "f" in os.environ.get("KD_PARTS", "mf"):
                    ffn(l)
            S.barrier()
            for c in range(nch):
                for b in range(2):
                    for j in range(4):
                        kc = b * 4 + j
                        S.op(PE_, lambda e, b=b, j=j, kc=kc, c=c: e.transpose(out=PS[b][:, j * 128:(j + 1) * 128], in_=xT[:, kc, c * 128:(c + 1) * 128], identity=ident[:, :]),
                             r=[kX, kC], w=[PK[b]])
                    S.op(A_, lambda e, b=b: e.copy(out=stage[:, b * 512:(b + 1) * 512], in_=PS[b][:, :]), r=[PK[b]], w=[('stage',)])
                S.dma('sync', yout[tok0 + c * 128: tok0 + (c + 1) * 128, :], stage[:, :], r=[('stage',)], chan=('ostage',))

        for pi in range(int(os.environ.get("KD_NPASS", NPASS))):
            run_pass(False, pi)
        if os.environ.get("KD_SAMPLE", "1") == "1":
            run_pass(True, 0)
        S.finish()

        with nc.Block() as block:
            @block.sync
            def _(e):
                S.replay('sync', e)

            @block.tensor
            def _(e):
                S.replay('pe', e)

            @block.scalar
            def _(e):
                S.replay('act', e)

            @block.vector
            def _(e):
                S.replay('dve', e)

            @block.gpsimd
            def _(e):
                S.replay('pool', e)
    return nc


def _slot_perm():
    perm = np.zeros(16, np.int64)
    for kap in range(4):
        for g in range(4):
            s = 2 * (g + 4 * (kap // 2)) + (kap % 2)
            perm[s] = 4 * kap + g
    return perm


def _consts():
    c = {}
    c["ident"] = np.eye(128, dtype=np.float32)
    s = np.arange(128)[:, None]
    t = np.arange(128)[None, :]
    c["mP"] = np.where(s <= t, 0.0, NEG).astype(np.float32)
    same = (s // 8) == (t // 8)
    c["mS"] = np.where(same & (s <= t), 0.0, NEG).astype(np.float32)
    c["acur"] = np.tile(np.where(s <= t, 0.0, ANEG).astype(np.float32), (1, 4))
    c["aprev"] = np.tile(np.where(s > t, 0.0, ANEG).astype(np.float32), (1, 4))
    c["anew"] = np.tile(np.where(same & (s <= t), 0.0, ANEG).astype(np.float32), (1, 4))
    ac = np.full((128, 16, 128), ANEG, np.float32)
    for i in range(16):
        for j in range(8):
            tt = 8 * i + j
            ac[j + 1:, i, tt] = 0.0
    c["acache"] = ac
    oh = np.zeros((128, 16), np.float32)
    oh[np.arange(128), np.arange(128) // 8] = 1.0
    c["onehot"] = oh
    sel = np.zeros((4, 4, 128), np.float32)
    for h in range(4):
        sel[h, h, :] = 1.0
    c["sel4"] = sel
    inv = 10000.0 ** (-np.arange(32, dtype=np.float64) / 32)
    pos = np.arange(SEQ, dtype=np.float64)[:, None] * inv[None, :]
    c["cosP"] = np.cos(pos).astype(np.float32)
    c["sinP"] = np.sin(pos).astype(np.float32)
    ps = (8192 + (np.arange(128) % 8)).astype(np.float64)[:, None] * inv[None, :]
    c["cosS"] = np.cos(ps).astype(np.float32)
    c["sinS"] = np.sin(ps).astype(np.float32)
    return c


_NC = None


def kernel(x_prompt, x_sample, c_prompt, c_sample, state_mlstm_C, state_mlstm_n, state_mlstm_m,
           state_sconv, cache_win_k, cache_win_v, state_ffn_conv,
           norm1, norm2, w_ada, b_ada, a_w_in, a_b_if, a_out_norm, a_conv_w, a_w_out,
           c_w_qkv, c_q_norm, c_k_norm, c_sink, c_w_out, f_w_up, f_conv_w, f_w_down):
    global _NC
    f = lambda a: np.ascontiguousarray(np.asarray(a, dtype=np.float32))
    perm = _slot_perm()
    common = dict(_consts())
    common["w_ada"] = f(w_ada)
    common["b_adaT"] = f(np.asarray(b_ada).reshape(4, 48, 128).transpose(2, 0, 1))
    common["norm1T"] = f(np.asarray(norm1).reshape(4, 8, 128).transpose(2, 0, 1))
    common["norm2T"] = f(np.asarray(norm2).reshape(4, 8, 128).transpose(2, 0, 1))
    pk8 = lambda w: f(np.asarray(w).reshape(w.shape[0], 8, 128, w.shape[2]).transpose(0, 2, 1, 3).reshape(w.shape[0], 128, 8 * w.shape[2]))
    common["a_w_in"] = pk8(np.asarray(a_w_in))
    common["a_bif"] = f(np.asarray(a_b_if).reshape(2, 2, 4).transpose(2, 0, 1))
    common["a_onT"] = f(np.asarray(a_out_norm).reshape(2, 4, 128).transpose(2, 0, 1))
    common["a_cwT"] = f(np.asarray(a_conv_w).reshape(2, 3, 4, 128).transpose(3, 0, 1, 2))
    common["a_w_out"] = pk8(np.asarray(a_w_out))
    wq = np.asarray(c_w_qkv)
    qcols = np.concatenate([np.arange(64) + 64 * h for h in perm])
    common["c_w_qkv"] = pk8(np.concatenate([wq[:, :, qcols], wq[:, :, 1024:]], axis=2))
    common["c_qn"] = f(c_q_norm)
    common["c_kn"] = f(c_k_norm)
    common["c_sink"] = f(np.asarray(c_sink)[:, perm])
    common["c_w_out"] = pk8(np.asarray(c_w_out)[:, qcols, :])
    wup = np.asarray(f_w_up, dtype=np.float32).reshape(4, 8, 128, 2, 22, 128)
    wdn = np.asarray(f_w_down, dtype=np.float32).reshape(4, 22, 128, 1024)
    fwg = np.zeros((4, 6, 128, 12288), np.float32)
    for g, (f0, n) in enumerate([(0, 4), (4, 4), (8, 4), (12, 4), (16, 4), (20, 2)]):
        blk = np.zeros((4, 128, 2, 8, 4, 128), np.float32)
        blk[:, :, :, :, 0:n, :] = wup[:, :, :, :, f0:f0 + n, :].transpose(0, 2, 3, 1, 4, 5)
        fwg[:, g, :, 0:8192] = blk.reshape(4, 128, 8192)
        dblk = np.zeros((4, 128, 4, 1024), np.float32)
        dblk[:, :, 0:n, :] = wdn[:, f0:f0 + n].transpose(0, 2, 1, 3)
        fwg[:, g, :, 8192:12288] = dblk.reshape(4, 128, 4096)
    common["f_w_g"] = fwg
    common["f_cwT"] = f(np.asarray(f_conv_w).reshape(4, 3, 44, 128).transpose(3, 0, 1, 2))
    in_maps = []
    for c in range(NCORE):
        b = c // 4
        sl = slice(16 * c, 16 * c + 16)
        m = dict(common)
        m["xp"] = f(np.asarray(x_prompt)[b])
        m["xs"] = f(np.asarray(x_sample)[sl].reshape(128, D))
        m["call"] = f(np.concatenate([np.asarray(c_prompt)[b:b + 1], np.asarray(c_sample)[sl]], axis=0))
        m["stC"] = f(np.asarray(state_mlstm_C)[:, sl])
        m["stn"] = f(np.asarray(state_mlstm_n)[:, sl].reshape(2, 64, 128))
        m["stmT"] = f(np.asarray(state_mlstm_m)[:, sl].transpose(2, 0, 1))
        m["stsc"] = f(np.asarray(state_sconv)[:, sl].reshape(2, 32, 512))
        m["ck"] = f(np.asarray(cache_win_k)[:, sl].reshape(2, 16, 128, 256))
        m["cv"] = f(np.asarray(cache_win_v)[:, sl].reshape(2, 16, 128, 256))
        m["stffn"] = f(np.asarray(state_ffn_conv)[:, sl].reshape(4, 32, 5632))
        in_maps.append(m)
    if _NC is None:
        _NC = build()
    res = run_bass_kernel_spmd(_NC, in_maps, core_ids=list(range(NCORE)))
    R = res.results
    pc = [0, 4]
    cat_p = lambda nm, shp: np.stack([R[c][nm] for c in pc], axis=1).reshape(shp)
    y_prompt = np.stack([R[c]["yp"] for c in pc], axis=0)
    y_sample = np.concatenate([R[c]["ys"].reshape(16, 8, D) for c in range(NCORE)], axis=0)
    p_C = cat_p("pC", (2, 2, 4, 128, 128))
    p_n = cat_p("pn", (2, 2, 4, 128))
    p_m = cat_p("pm", (2, 2, 4))
    p_sc = cat_p("psc", (2, 2, 2, 512))
    p_wk = cat_p("pwk", (2, 2, 128, 4, 64))
    p_wv = cat_p("pwv", (2, 2, 128, 4, 64))
    p_ffn = cat_p("pffn", (4, 2, 2, 5632))
    cat_s = lambda nm, shp: np.concatenate([R[c][nm].reshape(shp) for c in range(NCORE)], axis=1)
    s_C = cat_s("sC", (2, 16, 4, 128, 128))
    s_n = cat_s("sn", (2, 16, 4, 128))
    s_m = np.concatenate([R[c]["smT"].transpose(1, 2, 0) for c in range(NCORE)], axis=1)
    s_sc = cat_s("ssc", (2, 16, 2, 512))
    s_wk = cat_s("swk", (2, 16, 128, 4, 64))
    s_wv = cat_s("swv", (2, 16, 128, 4, 64))
    s_ffn = cat_s("sffn", (4, 16, 2, 5632))
    outs = (y_prompt, y_sample, p_C, p_n, p_m, p_sc, p_wk, p_wv, p_ffn,
            s_C, s_n, s_m, s_sc, s_wk, s_wv, s_ffn)
    return tuple(np.ascontiguousarray(o, dtype=np.float32) for o in outs)
```

```python
import contextlib
import os
import numpy as np
import concourse.bass as bass
import concourse.mybir as mybir
from concourse.bass_utils import run_bass_kernel_spmd

F32 = mybir.dt.float32
BF16 = mybir.dt.bfloat16
AF = mybir.ActivationFunctionType
ALU = mybir.AluOpType
AX = mybir.AxisListType

D = 1024
KC = 8
DFF = 2816
NF = 22
INA = 3592
SEQ = 8192
TP = 512
NPASS = SEQ // TP
EPS = 1e-6
NEG = -1.0e30
ANEG = -1.0e9
NCORE = 8
SLOT = 13312


class Sch:
    ROT = 30000

    def __init__(self, nc, stack):
        self.nc = nc
        self.stack = stack
        self.names = ('pe', 'act', 'dve', 'pool', 'sync')
        self.prog = {e: [] for e in self.names}
        self.cnt = {e: 0 for e in ('pe', 'act', 'dve', 'pool')}
        self.sems = {}
        self.seen = {e: {} for e in self.names}
        self.lastw = {}
        self.readers = {}
        self.dtot = {}

    def sem(self, key):
        if key not in self.sems:
            nm = "s" + str(len(self.sems))
            self.sems[key] = self.stack.enter_context(self.nc.semaphore(nm))
        return self.sems[key]

    def _deps(self, r, w):
        ev = []
        for k in r:
            if k in self.lastw:
                ev.append(self.lastw[k])
            if k[0] == 'P':
                ev.extend(self.readers.get(k, []))
        for k in w:
            if k in self.lastw:
                ev.append(self.lastw[k])
            ev.extend(self.readers.get(k, []))
        return ev

    def _wait(self, en, evs, skip=None):
        need = {}
        for (sk, v) in evs:
            if skip is not None and sk == skip:
                continue
            if en == 'pe' and sk[0] == 'pe':
                continue
            if need.get(sk, 0) < v:
                need[sk] = v
        for sk, v in need.items():
            if self.seen[en].get(sk, 0) < v:
                self.prog[en].append(('w', self.sem(sk), v))
                self.seen[en][sk] = v

    def _commit(self, ev, r, w):
        for k in r:
            self.readers.setdefault(k, []).append(ev)
        for k in w:
            self.lastw[k] = ev
            self.readers[k] = []

    def op(self, en, fn, r=(), w=(), inc=True):
        self._wait(en, self._deps(r, w))
        c = self.cnt[en] + 1
        sk = (en, (c - 1) // self.ROT)
        v = (c - 1) % self.ROT + 1
        if inc:
            self.cnt[en] = c
            self.prog[en].append(('i', fn, self.sem(sk), 1))
        else:
            self.prog[en].append(('i', fn, None, 0))
        self._commit((sk, v), r, w)

    def dma(self, q, out, in_, r=(), w=(), chan=None, **kw):
        if chan is None:
            chan = w[0] if len(w) else ('o',) + tuple(r[0])
        sk = ('d', chan)
        self._wait(q, self._deps(r, w), skip=sk)
        self.dtot[chan] = self.dtot.get(chan, 0) + 16
        fn = (lambda e, out=out, in_=in_, kw=kw: e.dma_start(out=out, in_=in_, allow_slow_non_contiguous=True, **kw))
        self.prog[q].append(('i', fn, self.sem(sk), 16))
        self._commit((sk, self.dtot[chan]), r, w)

    def barrier(self):
        evs = []
        for e in ('pe', 'act', 'dve', 'pool'):
            c = self.cnt[e]
            if c > 0:
                evs.append(((e, (c - 1) // self.ROT), (c - 1) % self.ROT + 1))
        for ch, t in self.dtot.items():
            if ch[0] == 'W':
                continue
            evs.append((('d', ch), t))
        for e in self.names:
            need = [x for x in evs if not (x[0][0] == e)]
            for (sk, v) in need:
                if self.seen[e].get(sk, 0) < v:
                    self.prog[e].append(('w', self.sem(sk), v))
                    self.seen[e][sk] = v

    def finish(self):
        evs = []
        for ch, t in self.dtot.items():
            evs.append((('d', ch), t))
        for e in ('pe', 'act', 'dve', 'pool'):
            c = self.cnt[e]
            if c > 0:
                evs.append(((e, (c - 1) // self.ROT), (c - 1) % self.ROT + 1))
        for (sk, v) in evs:
            self.prog['sync'].append(('w', self.sem(sk), v))

    def replay(self, en, eng):
        for it in self.prog[en]:
            if it[0] == 'w':
                eng.wait_ge(it[1], it[2])
            else:
                ins = it[1](eng)
                if it[2] is not None:
                    ins.then_inc(it[2], it[3])


class Arena:
    def __init__(self, t, words):
        self.t = t
        self.words = words
        self.off = 0
        self.gen = 0

    def reset(self):
        self.off = 0
        self.gen += 1

    def get(self, name, shape, dt=F32, parts=128):
        n = 1
        for s in shape[1:]:
            n *= s
        if dt == BF16:
            w = (n + 1) // 2
        else:
            w = n
        w = (w + 7) // 8 * 8
        assert self.off + w <= self.words, (name, self.off, w, self.words)
        ap = self.t[0:shape[0], self.off:self.off + w]
        self.off += w
        if dt == BF16:
            ap = ap.bitcast(BF16)[:, 0:n]
        else:
            ap = ap[:, 0:n]
        if len(shape) == 3:
            ap = ap.rearrange("p (a b) -> p a b", b=shape[2])
        elif len(shape) == 4:
            ap = ap.rearrange("p (a b c) -> p a b c", b=shape[2], c=shape[3])
        return ap, (name, self.gen)


def build():
    nc = bass.Bass("TRN2", target_bir_lowering=False)
    din = lambda n, s: nc.dram_tensor(n, list(s), F32, kind="ExternalInput").ap()
    dout = lambda n, s: nc.dram_tensor(n, list(s), F32, kind="ExternalOutput").ap()
    I = {}
    for n, s in [("xp", (SEQ, D)), ("xs", (128, D)), ("call", (17, D)),
                 ("stC", (2, 16, 4, 128, 128)), ("stn", (2, 64, 128)), ("stmT", (4, 2, 16)),
                 ("stsc", (2, 32, 512)), ("ck", (2, 16, 128, 256)), ("cv", (2, 16, 128, 256)),
                 ("stffn", (4, 32, 5632)),
                 ("w_ada", (4, D, 6144)), ("b_adaT", (128, 4, 48)), ("norm1T", (128, 4, 8)),
                 ("norm2T", (128, 4, 8)), ("a_w_in", (2, 128, 8 * INA)), ("a_bif", (4, 2, 2)),
                 ("a_onT", (128, 2, 4)), ("a_cwT", (128, 2, 3, 4)), ("a_w_out", (2, 128, 8 * D)),
                 ("c_w_qkv", (2, 128, 8 * 1536)), ("c_qn", (2, 64)), ("c_kn", (2, 64)), ("c_sink", (2, 16)),
                 ("c_w_out", (2, 128, 8 * D)), ("f_w_g", (4, 6, 128, 12288)), ("f_cwT", (128, 4, 3, 44)),
                 ("ident", (128, 128)), ("mP", (128, 128)), ("mS", (128, 128)),
                 ("acur", (128, 512)), ("aprev", (128, 512)), ("anew", (128, 512)),
                 ("acache", (128, 16, 128)), ("onehot", (128, 16)), ("sel4", (4, 4, 128)),
                 ("cosP", (SEQ, 32)), ("sinP", (SEQ, 32)), ("cosS", (128, 32)), ("sinS", (128, 32))]:
        I[n] = din(n, s)
    O = {}
    for n, s in [("yp", (SEQ, D)), ("ys", (128, D)), ("pC", (2, 4, 128, 128)), ("pn", (2, 4, 128)),
                 ("pm", (2, 4)), ("psc", (2, 2, 512)), ("pwk", (2, 128, 256)), ("pwv", (2, 128, 256)),
                 ("pffn", (4, 2, 5632)), ("sC", (2, 16, 4, 128, 128)), ("sn", (2, 64, 128)),
                 ("smT", (4, 2, 16)), ("ssc", (2, 32, 512)), ("swk", (2, 16, 128, 256)),
                 ("swv", (2, 16, 128, 256)), ("sffn", (4, 32, 5632))]:
        O[n] = dout(n, s)

    with contextlib.ExitStack() as st:
        S = Sch(nc, st)
        sbt = lambda n, s, dt=F32: st.enter_context(nc.sbuf_tensor("sb_" + n, list(s), dt))
        PS = [st.enter_context(nc.psum_tensor("ps%d" % i, [128, 512], F32)) for i in range(8)]
        PK = [('P', i) for i in range(8)]
        psb = lambda i: PS[i][:, :].bitcast(BF16)

        xT = sbt("xT", [128, 8, TP]); kX = ('xT',)
        hT = sbt("hT", [128, 8, TP], BF16); kH = ('hT',)
        MOD = sbt("MOD", [128, 4, 48, 17]); kM = ('MOD',)
        WB = sbt("WB", [128, 3 * SLOT], BF16)
        WKt = sbt("WK", [128, 15360])
        AR = Arena(WKt, 15360)
        ident = sbt("ident", [128, 128]); identb = sbt("identb", [128, 128], BF16)
        onesb = sbt("onesb", [128, 128], BF16)
        mPb = sbt("mPb", [128, 128], BF16); mSb = sbt("mSb", [128, 128], BF16)
        acurb = sbt("acurb", [128, 512], BF16); aprevb = sbt("aprevb", [128, 512], BF16)
        anewb = sbt("anewb", [128, 512], BF16); acacheb = sbt("acacheb", [128, 16, 128], BF16)
        onehot = sbt("onehot", [128, 16]); sel4 = sbt("sel4", [4, 4, 128])
        onesrow = sbt("onesrow", [4, TP])
        cosT = sbt("cosT", [128, 4, 32]); sinT = sbt("sinT", [128, 4, 32])
        n1T = sbt("n1T", [128, 4, 8]); n2T = sbt("n2T", [128, 4, 8]); badaT = sbt("badaT", [128, 4, 48])
        fcw = sbt("fcw", [128, 4, 3, 44]); acw = sbt("acw", [128, 2, 3, 4]); aon = sbt("aon", [128, 2, 4])
        bif = sbt("bif", [4, 2, 2]); nbif = sbt("nbif", [4, 2, 2])
        GQ = sbt("GQ", [128, 2, 64]); GK = sbt("GK", [128, 2, 64]); SK = sbt("SK", [128, 2, 16])
        SINKE = sbt("SINKE", [128, 2, 16]); NEGMA = sbt("NEGMA", [128, 2]); tmpc = sbt("tmpc", [128, 4])
        Cst = sbt("Cst", [128, 2, 4, 129]); MROW = sbt("MROW", [4, 2])
        ffc = sbt("ffc", [128, 4, 44, 2]); scc = sbt("scc", [128, 2, 4, 2])
        KTC = sbt("KTC", [128, 2, 2, 128], BF16); VC = sbt("VC", [128, 2, 4, 65], BF16)
        scT = sbt("scT", [128, 8, 17], BF16)
        stage = sbt("stage", [128, D]); call_sb = stage[0:17, :]
        kC = ('const',)

        V, A_, P_, PE_ = 'dve', 'act', 'pool', 'pe'

        def ld(q, dst, src, key, **kw):
            S.dma(q, dst, src, w=[key], chan=key if key != kC else ('c0',), **kw)
        for dst, nm in [(ident, "ident"), (onehot, "onehot"), (sel4, "sel4"), (n1T, "norm1T"), (n2T, "norm2T"),
                        (badaT, "b_adaT"), (fcw, "f_cwT"), (acw, "a_cwT"), (aon, "a_onT"), (bif, "a_bif")]:
            S.dma('sync', dst[:], I[nm], w=[kC], chan=('c0',))
        S.dma('sync', call_sb, I["call"], w=[('stage',)], chan=('istage',))
        for l in range(2):
            S.dma('sync', GQ[:, l, :], I["c_qn"][l].partition_broadcast(128), w=[kC], chan=('c0',))
            S.dma('sync', GK[:, l, :], I["c_kn"][l].partition_broadcast(128), w=[kC], chan=('c0',))
            S.dma('sync', SK[:, l, :], I["c_sink"][l].partition_broadcast(128), w=[kC], chan=('c0',))
        for dst, nm in [(identb, "ident"), (mPb, "mP"), (mSb, "mS"), (acurb, "acur"), (aprevb, "aprev"),
                        (anewb, "anew"), (acacheb, "acache")]:
            S.dma('pool', dst[:], I[nm], w=[kC], chan=('c1',))
        kC2 = ('const2',)
        S.op(V, lambda e: e.memset(onesb[:], 1.0), r=[kC], w=[kC2])
        S.op(V, lambda e: e.memset(onesrow[:], 1.0), w=[kC2])
        S.op(V, lambda e: e.tensor_scalar(out=nbif[:], in0=bif[:], scalar1=-1.0, scalar2=None, op0=ALU.mult), r=[kC], w=[kC2])
        S.op(V, lambda e: e.memset(Cst[:], 0.0), w=[('Cst', 0), ('Cst', 1)])
        S.op(V, lambda e: e.memset(MROW[:], 0.0), w=[('MROW',)])
        S.op(V, lambda e: e.memset(ffc[:], 0.0), w=[('ffc',)])
        S.op(V, lambda e: e.memset(scc[:], 0.0), w=[('scc',)])
        S.op(V, lambda e: e.memset(VC[:], 1.0), w=[('VC',)])
        for l in range(2):
            S.op(V, lambda e, l=l: e.tensor_reduce(out=tmpc[:, 0:1], in_=GQ[:, l, :], axis=AX.X, op=ALU.max, apply_absolute_value=True), r=[kC], w=[('tmpc',)])
            S.op(V, lambda e, l=l: e.tensor_reduce(out=tmpc[:, 1:2], in_=GK[:, l, :], axis=AX.X, op=ALU.max, apply_absolute_value=True), r=[kC], w=[('tmpc',)])
            S.op(V, lambda e, l=l: e.scalar_tensor_tensor(out=NEGMA[:, l:l + 1], in0=tmpc[:, 0:1], scalar=-8.0, in1=tmpc[:, 1:2], op0=ALU.mult, op1=ALU.mult), r=[('tmpc',)], w=[kC2])
            S.op(A_, lambda e, l=l: e.activation(out=SINKE[:, l, :], in_=SK[:, l, :], func=AF.Exp, bias=NEGMA[:, l:l + 1], scale=1.0), r=[kC, kC2], w=[('SINKE',)])

        S.op(A_, lambda e: e.activation(out=call_sb, in_=call_sb, func=AF.Silu), r=[('stage',)], w=[('stage',)])
        for kc in range(8):
            S.op(PE_, lambda e, kc=kc: e.transpose(out=PS[0][:, kc * 17:(kc + 1) * 17], in_=call_sb[:, kc * 128:(kc + 1) * 128], identity=ident[0:17, 0:17]),
                 r=[('stage',), kC], w=[PK[0]])
        S.op(V, lambda e: e.tensor_copy(out=scT[:], in_=PS[0][:, 0:136].rearrange("p (a b) -> p a b", b=17)), r=[PK[0]], w=[('scT',)])
        wa = [WB[:, i * 6144:(i + 1) * 6144] for i in range(2)]
        for l in range(4):
            for kc in range(8):
                sl = (l * 8 + kc) % 2
                S.dma('pool', wa[sl], I["w_ada"][l, kc * 128:(kc + 1) * 128, :], w=[('W', sl)], max_dma_last_dim=4096)
                for fc in range(48):
                    b = 1 + fc // 24
                    o = (fc % 24) * 17
                    S.op(PE_, lambda e, sl=sl, fc=fc, b=b, o=o, kc=kc: e.matmul(PS[b][:, o:o + 17], wa[sl][:, fc * 128:(fc + 1) * 128], scT[:, kc, :],
                                                                            start=(kc == 0 and fc % 24 == 0), stop=(kc == 7), skip_group_check=True),
                         r=[('W', sl), ('scT',)], w=[PK[b]], inc=(fc == 47))
            for b in range(2):
                S.op(V, lambda e, l=l, b=b: e.tensor_tensor(out=MOD[:, l, b * 24:(b + 1) * 24, :], in0=PS[1 + b][:, 0:408].rearrange("p (a b) -> p a b", b=17),
                                                           in1=badaT[:, l, b * 24:(b + 1) * 24].unsqueeze(2).broadcast_to([128, 24, 17]), op=ALU.add),
                     r=[PK[1 + b], kC], w=[kM])
            for (c0, nT) in [(8, n1T), (32, n2T)]:
                S.op(V, lambda e, l=l, c0=c0, nT=nT: e.scalar_tensor_tensor(out=MOD[:, l, c0:c0 + 8, :], in0=MOD[:, l, c0:c0 + 8, :], scalar=1.0,
                                                                           in1=nT[:, l, :].unsqueeze(2).broadcast_to([128, 8, 17]), op0=ALU.add, op1=ALU.mult),
                     r=[kM, kC], w=[kM])

        def fm_rows_out(src_fn, nrows, dst_rows_fn, nchunks, rkeys):
            for c0 in range(0, nchunks, 4):
                n = min(4, nchunks - c0)
                for j in range(n):
                    S.op(PE_, lambda e, j=j, c0=c0: e.transpose(out=PS[3][0:nrows, j * 128:(j + 1) * 128], in_=src_fn(c0 + j), identity=ident[:, :]),
                         r=rkeys + [kC], w=[PK[3]])
                S.op(A_, lambda e, n=n: e.copy(out=stage[0:nrows, 0:n * 128], in_=PS[3][0:nrows, 0:n * 128]), r=[PK[3]], w=[('stage',)])
                S.dma('sync', dst_rows_fn(c0 * 128, n * 128), stage[0:nrows, 0:n * 128], r=[('stage',)], chan=('ostage',))

        def rows_to_fm(src_rows_fn, nrows, dst_fn, nchunks, wkeys):
            for c0 in range(0, nchunks, 8):
                n = min(8, nchunks - c0)
                S.dma('sync', stage[0:nrows, 0:n * 128], src_rows_fn(c0 * 128, n * 128), w=[('stage',)], chan=('istage',))
                for j in range(n):
                    S.op(PE_, lambda e, j=j: e.transpose(out=PS[3][:, j * 32:j * 32 + nrows], in_=stage[0:nrows, j * 128:(j + 1) * 128], identity=ident[0:nrows, 0:nrows]),
                         r=[('stage',), kC], w=[PK[3]])
                for j in range(n):
                    S.op(V, lambda e, j=j, c0=c0: e.tensor_copy(out=dst_fn(c0 + j), in_=PS[3][:, j * 32:j * 32 + nrows]), r=[PK[3]], w=wkeys)

        def run_pass(sample, pi):
            T = 128 if sample else TP
            nseq = 16 if sample else 1
            L = 8 if sample else 128
            LT = 8 if sample else TP
            nch = T // 128
            tok0 = 0 if sample else pi * TP
            last = sample or (pi == NPASS - 1)
            xin = I["xs"] if sample else I["xp"]
            yout = O["ys"] if sample else O["yp"]
            sq0 = 1 if sample else 0
            v3 = lambda ap: ap.rearrange("p (s l) -> p s l", l=LT)

            def modbc(l, ch, kc):
                return MOD[:, l, ch * 8 + kc, sq0:sq0 + nseq].unsqueeze(2).broadcast_to([128, nseq, LT])

            for c in range(nch):
                S.dma('sync', stage[:, :], xin[tok0 + c * 128: tok0 + (c + 1) * 128, :], w=[('stage',)], chan=('istage',))
                for b in range(2):
                    for j in range(4):
                        kc = b * 4 + j
                        S.op(PE_, lambda e, b=b, j=j, kc=kc: e.transpose(out=PS[b][:, j * 128:(j + 1) * 128], in_=stage[:, kc * 128:(kc + 1) * 128], identity=ident[:, :]),
                             r=[('stage',), kC], w=[PK[b]])
                    S.op(V if b == 0 else A_, (lambda e, b=b, c=c: e.tensor_copy(out=xT[:, b * 4:(b + 1) * 4, c * 128:(c + 1) * 128], in_=PS[b][:, :].rearrange("p (a b) -> p a b", b=128))) if b == 0 else
                         (lambda e, b=b, c=c: e.copy(out=xT[:, b * 4:(b + 1) * 4, c * 128:(c + 1) * 128], in_=PS[b][:, :].rearrange("p (a b) -> p a b", b=128))),
                         r=[PK[b]], w=[kX])
            if not sample:
                S.dma('sync', cosT[:, 0:nch, :], I["cosP"][tok0:tok0 + T, :].rearrange("(c p) i -> p c i", p=128), w=[('rope',)])
                S.dma('sync', sinT[:, 0:nch, :], I["sinP"][tok0:tok0 + T, :].rearrange("(c p) i -> p c i", p=128), w=[('rope',)])
            else:
                S.dma('sync', cosT[:, 0, :], I["cosS"], w=[('rope',)])
                S.dma('sync', sinT[:, 0, :], I["sinS"], w=[('rope',)])

            def norm_mod(l, which):
                cA, cB = (1, 0) if which == 1 else (4, 3)
                S.barrier(); AR.reset()
                sqb, ksq = AR.get("sq", [128, 8, T], BF16)
                rs, krs = AR.get("rs", [128, T])
                tmp, ktmp = AR.get("tmp", [128, T])
                S.op(A_, lambda e: e.activation(out=sqb, in_=xT[:, :, 0:T], func=AF.Square), r=[kX], w=[ksq])
                for kc in range(8):
                    S.op(PE_, lambda e, kc=kc: e.matmul(PS[0][:, 0:T], onesb[:, :], sqb[:, kc, :], start=(kc == 0), stop=(kc == 7)), r=[ksq, kC2], w=[PK[0]], inc=(kc == 7))
                S.op(A_, lambda e: e.activation(out=rs, in_=PS[0][:, 0:T], func=AF.Sqrt, bias=EPS, scale=1.0 / D), r=[PK[0]], w=[krs])
                S.op(V, lambda e: e.reciprocal(out=rs, in_=rs), r=[krs], w=[krs])
                for kc in range(8):
                    S.op(V, lambda e, kc=kc: e.tensor_tensor(out=tmp, in0=xT[:, kc, 0:T], in1=rs, op=ALU.mult), r=[kX, krs], w=[ktmp])
                    S.op(V, lambda e, kc=kc: e.tensor_tensor(out=v3(tmp), in0=v3(tmp), in1=modbc(l, cA, kc), op=ALU.mult), r=[ktmp, kM], w=[ktmp])
                    S.op(V, lambda e, kc=kc: e.tensor_tensor(out=v3(hT[:, kc, 0:T]), in0=v3(tmp), in1=modbc(l, cB, kc), op=ALU.add), r=[ktmp, kM], w=[kH])

            def resid_add(l, gch, d, psrc, pkey, tkey_ap):
                tmp, ktmp = tkey_ap
                if not sample:
                    S.op(V, lambda e: e.scalar_tensor_tensor(out=xT[:, d, 0:T], in0=psrc, scalar=MOD[:, l, gch * 8 + d, 0:1], in1=xT[:, d, 0:T], op0=ALU.mult, op1=ALU.add),
                         r=[pkey, kM, kX], w=[kX])
                    return
                S.op(V, lambda e: e.tensor_tensor(out=v3(tmp), in0=v3(psrc), in1=modbc(l, gch, d), op=ALU.mult), r=[pkey, kM], w=[ktmp])
                S.op(V, lambda e: e.tensor_tensor(out=xT[:, d, 0:T], in0=xT[:, d, 0:T], in1=tmp, op=ALU.add), r=[ktmp, kX], w=[kX])

            def conv3(dst3, src3, wfn, keys_r, keys_w, pool=False):
                E1 = P_ if pool else V
                if pool:
                    S.op(E1, lambda e: e.tensor_scalar(out=dst3, in0=src3[:, :, 0:LT], scalar1=wfn(0), scalar2=0.0, op0=ALU.mult, op1=ALU.add), r=keys_r, w=keys_w)
                else:
                    S.op(E1, lambda e: e.tensor_scalar(out=dst3, in0=src3[:, :, 0:LT], scalar1=wfn(0), scalar2=None, op0=ALU.mult), r=keys_r, w=keys_w)
                S.op(V, lambda e: e.scalar_tensor_tensor(out=dst3, in0=src3[:, :, 1:LT + 1], scalar=wfn(1), in1=dst3, op0=ALU.mult, op1=ALU.add), r=keys_r + keys_w, w=keys_w)
                S.op(V, lambda e: e.scalar_tensor_tensor(out=dst3, in0=src3[:, :, 2:LT + 2], scalar=wfn(2), in1=dst3, op0=ALU.mult, op1=ALU.add), r=keys_r + keys_w, w=keys_w)

            def ffn(l):
                gsz = [4, 4, 4, 4, 4, 2]
                gst = [0, 4, 8, 12, 16, 20]

                def wslot(g):
                    base = (g % 3) * SLOT
                    ug = WB[:, base:base + 4096].rearrange("p (k c) -> p k c", c=512)
                    ua = WB[:, base + 4096:base + 8192].rearrange("p (k c) -> p k c", c=512)
                    dn = WB[:, base + 8192:base + 12288].rearrange("p (j d) -> p j d", d=D)
                    return ug, ua, dn

                def issue(g):
                    ug, ua, dn = wslot(g)
                    n = gsz[g]; f0 = gst[g]
                    k = ('W', g % 3)
                    base = (g % 3) * SLOT
                    S.dma('pool', WB[:, base:base + 12288], I["f_w_g"][l, g], w=[k], max_dma_last_dim=8192)
                issue(0); issue(1)
                norm_mod(l, 2)
                if os.environ.get("KD_FFN", "") == "n":
                    return
                S.barrier(); AR.reset()
                UB, kUB = AR.get("UB", [128, 4, nseq * (LT + 2)])
                Y, kY = AR.get("Y", [128, 4, T])
                ACTB, kAB = AR.get("ACTB", [128, 4, T], BF16)
                rt = AR.get("rt", [128, T])
                if sample:
                    car, kcar = AR.get("car", [128, 44, 32])
                    rows_to_fm(lambda o, n: I["stffn"][l, :, o:o + n], 32, lambda c: car[:, c, :], 44, [kcar])
                    carv = lambda idx: car[:, idx, :].rearrange("p (s j) -> p s j", j=2)
                else:
                    kcar = ('ffc',)
                    carv = lambda idx: ffc[:, l, idx, :].unsqueeze(1)
                ub3 = lambda i: UB[:, i, :].rearrange("p (s k) -> p s k", k=LT + 2)

                ACTB2 = [(ACTB, kAB), AR.get("ACTB1", [128, 4, T], BF16)]

                def up_chain(g):
                    ug, ua, dn = wslot(g)
                    kW = ('W', g % 3)
                    n = gsz[g]; f0 = gst[g]
                    AB, kABg = ACTB2[g % 2]
                    for j in range(n):
                        ib = 2 * (j % 2)
                        for part in range(2):
                            i = ib + part
                            wv = ug if part == 0 else ua
                            idx = (0 if part == 0 else 22) + f0 + j
                            for kc in range(8):
                                S.op(PE_, lambda e, i=i, wv=wv, j=j, kc=kc: e.matmul(PS[i][:, 0:T], wv[:, kc, j * 128:(j + 1) * 128], hT[:, kc, 0:T], start=(kc == 0), stop=(kc == 7)),
                                     r=[kW, kH], w=[PK[i]], inc=(kc == 7))
                            S.op(A_, lambda e, i=i, idx=idx: e.copy(out=ub3(i)[:, :, 0:2], in_=carv(idx)), r=[kcar], w=[(kUB, i)])
                            S.op(A_, lambda e, i=i: e.copy(out=ub3(i)[:, :, 2:LT + 2], in_=v3(PS[i][:, 0:T])), r=[PK[i]], w=[(kUB, i)])
                            S.op(A_, lambda e, i=i, idx=idx: e.copy(out=carv(idx), in_=ub3(i)[:, :, LT:LT + 2]), r=[(kUB, i)], w=[kcar])
                            conv3(v3(Y[:, i, :]), ub3(i), lambda t, idx=idx: fcw[:, l, t, idx:idx + 1], [(kUB, i), kC], [(kY, i)], pool=True)
                        S.op(A_, lambda e, ib=ib: e.activation(out=Y[:, ib, :], in_=Y[:, ib, :], func=AF.Silu), r=[(kY, ib)], w=[(kY, ib)])
                        S.op(V, lambda e, j=j, ib=ib, AB=AB: e.tensor_tensor(out=AB[:, j, :], in0=Y[:, ib, :], in1=Y[:, ib + 1, :], op=ALU.mult), r=[(kY, ib), (kY, ib + 1)], w=[(kABg, j)])

                def down(g):
                    ug, ua, dn = wslot(g)
                    kW = ('W', g % 3)
                    n = gsz[g]
                    AB, kABg = ACTB2[g % 2]
                    for dh in range(2):
                        for d4 in range(4):
                            d = dh * 4 + d4
                            for j in range(n):
                                S.op(PE_, lambda e, d4=d4, d=d, j=j, dn=dn, n=n, AB=AB: e.matmul(PS[4 + d4][:, 0:T], dn[:, j, d * 128:(d + 1) * 128], AB[:, j, :], start=(j == 0), stop=(j == n - 1)),
                                     r=[kW, (kABg, j)], w=[PK[4 + d4]], inc=(j == n - 1))
                            resid_add(l, 5, d, PS[4 + d4][:, 0:T], PK[4 + d4], rt)

                up_chain(0)
                for g in range(6):
                    if g + 2 < 6:
                        issue(g + 2)
                    if g + 1 < 6:
                        up_chain(g + 1)
                    down(g)
                if last:
                    nr = 32 if sample else 2
                    dst = O["sffn"] if sample else O["pffn"]
                    if sample:
                        fm_rows_out(lambda c: car[:, c, :], 32, lambda o, n: dst[l, :, o:o + n], 44, [kcar])
                    else:
                        fm_rows_out(lambda c: ffc[:, l, c, :], 2, lambda o, n: dst[l, :, o:o + n], 44, [kcar])

            def mixer_ab(l):
                i = l // 2
                win = WB[:, 0:8 * INA].rearrange("p (k c) -> p k c", c=INA)
                wout = WB[:, 8 * INA:8 * INA + 8 * D].rearrange("p (k c) -> p k c", c=D)
                kW = [('W', 0), ('W', 1), ('W', 2)]
                S.dma('pool', WB[:, 0:8 * INA], I["a_w_in"][i], w=kW, chan=('W', 0), max_dma_last_dim=8192)
                S.dma('pool', WB[:, 8 * INA:8 * INA + 8 * D], I["a_w_out"][i], w=kW, chan=('W', 0), max_dma_last_dim=8192)
                norm_mod(l, 1)
                S.barrier(); AR.reset()
                qT, kqT = AR.get("qT", [128, 4, T], BF16)
                kTt, kkT = AR.get("kT", [128, 4, T], BF16)
                sgT, ksg = AR.get("sgT", [128, 4, T], BF16)
                scTt, ksc = AR.get("scTt", [128, 4, T], BF16)
                hmT, khm = AR.get("hmT", [128, 4, T], BF16)
                KTOK, kkt = AR.get("KTOK", [128, nch, 512], BF16)
                VT, kvt = AR.get("VT", [128, nch * 4, 129], BF16)
                PB, kPB = AR.get("PB", [128, nseq * (LT + 2)])
                ZX, kZX = AR.get("ZX", [128, T])
                U, kU = AR.get("U", [128, T])
                rt = AR.get("rt", [128, T]) if sample else (None, None)
                rows = {}
                for nm in ["ig", "l1", "cs", "aa", "A", "negm", "prev", "lastr"]:
                    rows[nm] = AR.get("r_" + nm, [4, T], parts=4)
                NEGAX, kNX = AR.get("NEGAX", [4, nseq + T], parts=4)
                MIN, kMIN = AR.get("MIN", [4, 16], parts=4)
                nset = 1 if sample else 2
                COLSb = [AR.get("COLS%d" % z, [128, 20]) for z in range(2)]
                B = []
                for z in range(nset):
                    d_ = {}
                    d_["SA"] = AR.get("SA%d" % z, [128, nseq + 128])
                    d_["WT"] = AR.get("WT%d" % z, [128, 128], BF16)
                    d_["PT"] = AR.get("PT%d" % z, [128, 128], BF16)
                    d_["PVs"] = AR.get("PVs%d" % z, [128, 129])
                    d_["NUM"] = AR.get("NUM%d" % z, [128, 129])
                    d_["JK"] = AR.get("JK%d" % z, [128, 128], BF16)
                    d_["HN"] = AR.get("HN%d" % z, [128, 128], BF16)
                    d_["VS"] = AR.get("VS%d" % z, [128, 129], BF16)
                    d_["SM"] = AR.get("SM%d" % z, [128, 16])
                    d_["WSI"] = AR.get("WSI%d" % z, [128, 16])
                    d_["WCB"] = AR.get("WCB%d" % z, [128, 16])
                    d_["QPAD"] = AR.get("QPAD%d" % z, [128, nseq * 136])
                    B.append(d_)
                if sample:
                    CS_, kCS = AR.get("CSs", [128, 64, 129])
                    S.dma('sync', CS_[:, :, 0:128], I["stC"][i].rearrange("s h k v -> k (s h) v"), w=[kCS])
                    rows_to_fm(lambda o, n: I["stn"][i, :, o:o + n], 64, lambda c: CS_[:, :, 128], 1, [kCS])
                    S.dma('sync', MIN, I["stmT"][:, i, :], w=[kMIN])
                    csv = lambda s_, h: CS_[:, s_ * 4 + h, :]
                    pcar, kpc = AR.get("pcar", [128, 4, 32])
                    rows_to_fm(lambda o, n: I["stsc"][i, :, o:o + n], 32, lambda c: pcar[:, c, :], 4, [kpc])
                    pcv = lambda ch: pcar[:, ch, :].rearrange("p (s j) -> p s j", j=2)
                else:
                    kCS = ('Cst', i)
                    csv = lambda s_, h: Cst[:, i, h, :]
                    kMIN = ('MROW',)
                    kpc = ('scc',)
                    pcv = lambda ch: scc[:, i, ch, :].unsqueeze(1)
                S.op(V, lambda e: e.memset(VT, 1.0), w=[kvt])
                for z in range(nset):
                    S.op(V, lambda e, z=z: e.memset(B[z]["QPAD"][0], 0.0), w=[B[z]["QPAD"][1]])

                def proj_fm(c0, handler, tag):
                    b = proj_fm.n % 2
                    proj_fm.n += 1
                    for kc in range(8):
                        S.op(PE_, lambda e, b=b, kc=kc: e.matmul(PS[b][:, 0:T], win[:, kc, c0:c0 + 128], hT[:, kc, 0:T], start=(kc == 0), stop=(kc == 7)),
                             r=kW + [kH], w=[PK[b]], inc=(kc == 7))
                    handler(PS[b][:, 0:T], PK[b])
                proj_fm.n = 0
                for h in range(4):
                    proj_fm(h * 128, lambda p, k, h=h: S.op(A_, lambda e: e.copy(out=qT[:, h, :], in_=p), r=[k], w=[kqT]), "q")
                    proj_fm(512 + h * 128, lambda p, k, h=h: S.op(A_, lambda e: e.activation(out=kTt[:, h, :], in_=p, func=AF.Copy, scale=128.0 ** -0.5), r=[k], w=[kkT]), "k")
                    proj_fm(1536 + h * 128, lambda p, k, h=h: S.op(A_, lambda e: e.activation(out=sgT[:, h, :], in_=p, func=AF.Sigmoid), r=[k], w=[ksg]), "o")
                pb3 = PB.rearrange("p (s k) -> p s k", k=LT + 2)
                for ch in range(4):
                    proj_fm(3080 + ch * 128, lambda p, k: S.op(A_, lambda e: e.copy(out=ZX, in_=p), r=[k], w=[kZX]), "zx")
                    S.op(A_, lambda e, ch=ch: e.copy(out=pb3[:, :, 0:2], in_=pcv(ch)), r=[kpc], w=[kPB])
                    proj_fm(2568 + ch * 128, lambda p, k: S.op(V, lambda e: e.tensor_tensor(out=pb3[:, :, 2:LT + 2], in0=v3(p), in1=v3(ZX), op=ALU.mult), r=[k, kZX], w=[kPB]), "zc")
                    S.op(A_, lambda e, ch=ch: e.copy(out=pcv(ch), in_=pb3[:, :, LT:LT + 2]), r=[kPB], w=[kpc])
                    conv3(v3(U), pb3, lambda t, ch=ch: acw[:, i, t, ch:ch + 1], [kPB, kC], [kU])
                    proj_fm(2056 + ch * 128, lambda p, k, ch=ch: S.op(V, lambda e: e.tensor_tensor(out=scTt[:, ch, :], in0=p, in1=U, op=ALU.mult), r=[k, kU], w=[ksc]), "zb")
                if last:
                    if sample:
                        fm_rows_out(lambda c: pcar[:, c, :], 32, lambda o, n: O["ssc"][i, :, o:o + n], 4, [kpc])
                    else:
                        fm_rows_out(lambda c: scc[:, i, c, :], 2, lambda o, n: O["psc"][i, :, o:o + n], 4, [kpc])
                rg = lambda nm: rows[nm][0]
                kg = lambda nm: rows[nm][1]
                for gi, (c0, nm) in enumerate([(2048, "ig"), (2052, "l1")]):
                    for kc in range(8):
                        S.op(PE_, lambda e, kc=kc, c0=c0: e.matmul(PS[2][0:4, 0:T], win[:, kc, c0:c0 + 4], hT[:, kc, 0:T], start=(kc == 0), stop=(kc == 7)),
                             r=kW + [kH], w=[PK[2]], inc=(kc == 7))
                    if gi == 0:
                        S.op(A_, lambda e: e.activation(out=rg("ig"), in_=PS[2][0:4, 0:T], func=AF.Identity, bias=bif[:, i, 0:1], scale=1.0), r=[PK[2], kC], w=[kg("ig")])
                    else:
                        S.op(A_, lambda e: e.activation(out=rg("l1"), in_=PS[2][0:4, 0:T], func=AF.Exp, bias=nbif[:, i, 1:2], scale=-1.0), r=[PK[2], kC2], w=[kg("l1")])
                        S.op(A_, lambda e: e.activation(out=rg("l1"), in_=rg("l1"), func=AF.Ln, bias=1.0, scale=1.0), r=[kg("l1")], w=[kg("l1")])
                r3 = lambda ap: ap.rearrange("p (s l) -> p s l", l=LT)
                for s_ in range(nseq):
                    sl = slice(s_ * LT, (s_ + 1) * LT)
                    S.op(V, lambda e, sl=sl: e.tensor_tensor_scan(out=rg("cs")[:, sl], data0=onesrow[:, 0:LT], data1=rg("l1")[:, sl], initial=0.0, op0=ALU.mult, op1=ALU.add),
                         r=[kg("l1"), kC2], w=[kg("cs")])
                S.op(V, lambda e: e.tensor_tensor(out=rg("aa"), in0=rg("ig"), in1=rg("cs"), op=ALU.add), r=[kg("ig"), kg("cs")], w=[kg("aa")])
                minap = (lambda s_: MIN[:, s_:s_ + 1]) if sample else (lambda s_: MROW[:, i:i + 1])
                for s_ in range(nseq):
                    sl = slice(s_ * LT, (s_ + 1) * LT)
                    S.op(V, lambda e, sl=sl, s_=s_: e.tensor_tensor_scan(out=rg("A")[:, sl], data0=rg("aa")[:, sl], data1=rg("aa")[:, sl], initial=minap(s_), op0=ALU.max, op1=ALU.max),
                         r=[kg("aa"), kMIN], w=[kg("A")])
                minall = MIN[:, 0:16] if sample else MROW[:, i:i + 1]
                S.op(V, lambda e: e.tensor_scalar(out=NEGAX[:, 0:nseq], in0=minall, scalar1=-1.0, scalar2=None, op0=ALU.mult), r=[kMIN], w=[kNX])
                S.op(V, lambda e: e.tensor_scalar(out=NEGAX[:, nseq:nseq + T], in0=rg("A"), scalar1=-1.0, scalar2=None, op0=ALU.mult), r=[kg("A")], w=[kNX])
                S.op(V, lambda e: e.tensor_tensor(out=rg("negm"), in0=rg("cs"), in1=rg("A"), op=ALU.subtract), r=[kg("cs"), kg("A")], w=[kg("negm")])
                if sample:
                    S.op(V, lambda e: e.tensor_copy(out=r3(rg("prev")), in_=MIN[:, 0:16].unsqueeze(2).broadcast_to([4, 16, 8])), r=[kMIN], w=[kg("prev")])
                    S.op(V, lambda e: e.tensor_copy(out=r3(rg("lastr")), in_=r3(NEGAX[:, 16:16 + T])[:, :, 7:8].broadcast_to([4, 16, 8])), r=[kNX], w=[kg("lastr")])
                else:
                    c3 = lambda ap: ap.rearrange("p (c l) -> p c l", l=128)
                    S.op(V, lambda e: e.tensor_scalar(out=c3(rg("prev")), in0=c3(NEGAX[:, 0:T])[:, :, 0:1].broadcast_to([4, nch, 128]), scalar1=-1.0, scalar2=None, op0=ALU.mult), r=[kNX], w=[kg("prev")])
                    S.op(V, lambda e: e.tensor_copy(out=c3(rg("lastr")), in_=c3(NEGAX[:, 1:T + 1])[:, :, 127:128].broadcast_to([4, nch, 128])), r=[kNX], w=[kg("lastr")])
                if sample:
                    if last:
                        MOUT, kMO = AR.get("MOUT", [4, 16], parts=4)
                        S.op(V, lambda e: e.tensor_scalar(out=MOUT, in0=r3(rg("negm"))[:, :, 7], scalar1=-1.0, scalar2=None, op0=ALU.mult), r=[kg("negm")], w=[kMO])
                        S.dma('sync', O["smT"][:, i, :], MOUT, r=[kMO], chan=('osm',))
                else:
                    S.op(V, lambda e: e.tensor_scalar(out=MROW[:, i:i + 1], in0=rg("negm")[:, T - 1:T], scalar1=-1.0, scalar2=None, op0=ALU.mult), r=[kg("negm")], w=[kMIN])
                    if last:
                        S.dma('sync', O["pm"][i].unsqueeze(1), MROW[:, i:i + 1], r=[kMIN], chan=('opm',))
                for c in range(nch):
                    cs_ = slice(c * 128, (c + 1) * 128)
                    for part, c0 in [(0, 512), (1, 1024)]:
                        b = 4 + part
                        for kc in range(8):
                            S.op(PE_, lambda e, kc=kc, b=b, c0=c0, cs_=cs_: e.matmul(PS[b][:, 0:512], hT[:, kc, cs_], win[:, kc, c0:c0 + 512], start=(kc == 0), stop=(kc == 7)),
                                 r=kW + [kH], w=[PK[b]], inc=(kc == 7))
                    S.op(A_, lambda e, c=c: e.activation(out=KTOK[:, c, :], in_=PS[4][:, 0:512], func=AF.Copy, scale=128.0 ** -0.5), r=[PK[4]], w=[kkt])
                    S.op(A_, lambda e, c=c: e.copy(out=VT[:, c * 4:(c + 1) * 4, 0:128], in_=PS[5][:, 0:512].rearrange("p (h v) -> p h v", v=128)), r=[PK[5]], w=[kvt])
                maskb = mSb if sample else mPb
                kCSh = lambda h: (kCS, h)
                S.barrier()
                for c in range(nch):
                    cs_ = slice(c * 128, (c + 1) * 128)
                    COLS, kCOLS = COLSb[c % 2]
                    allp3 = [PK[6]]
                    for bi, nm in enumerate(["aa", "A", "negm", "prev", "lastr"]):
                        src = NEGAX[:, nseq + c * 128:nseq + (c + 1) * 128] if nm == "A" else rg(nm)[:, cs_]
                        S.op(PE_, lambda e, bi=bi, src=src: e.transpose(out=PS[6][:, 256 + bi * 4:256 + (bi + 1) * 4], in_=src, identity=ident[0:4, 0:4]),
                             r=[kg(nm) if nm != "A" else kNX, kC], w=allp3)
                    S.op(V, lambda e, COLS=COLS: e.tensor_copy(out=COLS, in_=PS[6][:, 256:276]), r=allp3, w=[kCOLS])

                    def stage_fns(h, z, c=c, cs_=cs_, COLS=COLS, kCOLS=kCOLS):
                        bz = B[z]
                        SA, kSA = bz["SA"]; WT, kWT = bz["WT"]; PT, kPT = bz["PT"]; PVs, kPVs = bz["PVs"]
                        NUM, kNUM = bz["NUM"]; JK, kJK = bz["JK"]; HN, kHN = bz["HN"]; VS, kVS = bz["VS"]
                        SM, kSM = bz["SM"]; WSI, kWSI = bz["WSI"]; WCB, kWCB = bz["WCB"]; QPAD, kQP = bz["QPAD"]
                        bA, bB, bC, bD = z, z + 2, z + 4, z + 6
                        ca = COLS[:, 0 + h:1 + h]; cnA = COLS[:, 4 + h:5 + h]; cnm = COLS[:, 8 + h:9 + h]
                        cpv = COLS[:, 12 + h:13 + h]; cla = COLS[:, 16 + h:17 + h]
                        qpd = QPAD.rearrange("p (s k) -> p s k", k=136)[:, :, 0:L]
                        fns = []

                        def s1():
                            if sample:
                                S.op(PE_, lambda e: e.matmul(PS[bA][:, 0:16], sel4[:, h, :], NEGAX[:, 0:16], start=True, stop=False, skip_group_check=True), r=[kNX, kC], w=[PK[bA]], inc=False)
                                S.op(PE_, lambda e: e.matmul(PS[bA][:, 16:144], sel4[:, h, :], NEGAX[:, 16:144], start=False, stop=True, skip_group_check=True), r=[kNX, kC], w=[PK[bA]])
                            else:
                                S.op(PE_, lambda e: e.matmul(PS[bA][:, 0:129], sel4[:, h, :], NEGAX[:, c * 128:c * 128 + 129], start=True, stop=True), r=[kNX, kC], w=[PK[bA]])
                            S.op(PE_, lambda e: e.matmul(PS[bB][:, 0:128], sel4[:, h, :], NEGAX[:, nseq + c * 128:nseq + (c + 1) * 128], start=True, stop=False), r=[kNX, kC, kCOLS], w=[PK[bB]], inc=False)
                            S.op(PE_, lambda e: e.matmul(PS[bB][:, 0:128], identb[:, :], maskb[:, :], start=False, stop=True), r=[kC], w=[PK[bB]])
                            S.op(PE_, lambda e: e.matmul(PS[bC][:, 0:128], kTt[:, h, cs_], qT[:, h, cs_], start=True, stop=True), r=[kkT, kqT], w=[PK[bC]])
                        fns.append(s1)

                        def s2():
                            S.op(A_, lambda e: e.copy(out=SA, in_=PS[bA][:, 0:nseq + 128]), r=[PK[bA]], w=[kSA])
                            S.op(A_, lambda e: e.activation(out=WT, in_=PS[bB][:, 0:128], func=AF.Exp, bias=ca, scale=1.0), r=[PK[bB], kCOLS], w=[kWT])
                            S.op(A_, lambda e: e.activation(out=SM[:, 0:1], in_=cnA, func=AF.Exp, bias=cpv, scale=1.0), r=[kCOLS], w=[(kSM, 0)])
                            S.op(A_, lambda e: e.activation(out=SM[:, 1:2], in_=ca, func=AF.Exp, bias=cla, scale=1.0), r=[kCOLS], w=[(kSM, 1)])
                            S.op(A_, lambda e: e.activation(out=SM[:, 2:3], in_=cnm, func=AF.Exp), r=[kCOLS], w=[(kSM, 2)])
                        fns.append(s2)

                        def s3():
                            S.op(V, lambda e: e.tensor_tensor(out=PT, in0=PS[bC][:, 0:128], in1=WT, op=ALU.mult), r=[PK[bC], kWT], w=[kPT])
                            S.op(V, lambda e: e.tensor_copy(out=qpd, in_=qT[:, h, cs_].rearrange("p (s l) -> p s l", l=L)), r=[kqT], w=[kQP])
                            S.op(V, lambda e: e.tensor_scalar(out=WSI[:, 0:nseq], in0=onehot[:, 0:nseq] if sample else onesb[:, 0:1], scalar1=SM[:, 1:2], scalar2=None, op0=ALU.mult), r=[(kSM, 1), kC, kC2], w=[kWSI])
                            sa_last = SA[:, nseq:nseq + 128].rearrange("p (s l) -> p s l", l=L)[:, :, L - 1]
                            S.op(V, lambda e: e.tensor_tensor(out=WCB[:, 0:nseq], in0=sa_last, in1=SA[:, 0:nseq], op=ALU.subtract), r=[kSA], w=[kWCB])
                        fns.append(s3)

                        def s4():
                            S.op(PE_, lambda e: e.matmul(PS[bB][:, 256:385], PT, VT[:, c * 4 + h, :], start=True, stop=True), r=[kPT, kvt], w=[PK[bB]])
                            for s_ in range(nseq):
                                S.op(PE_, lambda e, s_=s_: e.matmul(PS[bA][:, 256:385], QPAD[:, s_ * 128:(s_ + 1) * 128], csv(s_, h), start=(s_ == 0), stop=(s_ == nseq - 1)),
                                     r=[kQP, kCSh(h)], w=[PK[bA]], inc=(s_ == nseq - 1))
                        fns.append(s4)

                        def s5():
                            S.op(A_, lambda e: e.copy(out=PVs, in_=PS[bB][:, 256:385]), r=[PK[bB]], w=[kPVs])
                            S.op(A_, lambda e: e.activation(out=WCB[:, 0:nseq], in_=WCB[:, 0:nseq], func=AF.Exp), r=[kWCB], w=[kWCB])
                        fns.append(s5)

                        def s6():
                            S.op(V, lambda e: e.scalar_tensor_tensor(out=NUM, in0=PS[bA][:, 256:385], scalar=SM[:, 0:1], in1=PVs, op0=ALU.mult, op1=ALU.add), r=[PK[bA], (kSM, 0), kPVs], w=[kNUM])
                        fns.append(s6)

                        def s7():
                            S.op(A_, lambda e: e.activation(out=SM[:, 3:4], in_=NUM[:, 128:129], func=AF.Abs), r=[kNUM], w=[(kSM, 3)])
                        fns.append(s7)

                        def s8():
                            S.op(V, lambda e: e.tensor_tensor(out=SM[:, 3:4], in0=SM[:, 3:4], in1=SM[:, 2:3], op=ALU.max), r=[(kSM, 3), (kSM, 2)], w=[(kSM, 3)])
                            S.op(V, lambda e: e.reciprocal(out=SM[:, 4:5], in_=SM[:, 3:4]), r=[(kSM, 3)], w=[(kSM, 4)])
                        fns.append(s8)

                        def s9():
                            S.op(A_, lambda e: e.activation(out=JK, in_=NUM[:, 0:128], func=AF.Square, scale=SM[:, 4:5], accum_out=SM[:, 5:6]), r=[kNUM, (kSM, 4)], w=[kJK, (kSM, 5)])
                            S.op(A_, lambda e: e.activation(out=SM[:, 6:7], in_=SM[:, 5:6], func=AF.Sqrt, bias=EPS, scale=1.0 / 128), r=[(kSM, 5)], w=[(kSM, 6)])
                        fns.append(s9)

                        def s10():
                            S.op(V, lambda e: e.reciprocal(out=SM[:, 6:7], in_=SM[:, 6:7]), r=[(kSM, 6)], w=[(kSM, 6)])
                            S.op(V, lambda e: e.tensor_tensor(out=SM[:, 7:8], in0=SM[:, 6:7], in1=SM[:, 4:5], op=ALU.mult), r=[(kSM, 6), (kSM, 4)], w=[(kSM, 7)])
                            S.op(V, lambda e: e.tensor_scalar(out=HN, in0=NUM[:, 0:128], scalar1=SM[:, 7:8], scalar2=None, op0=ALU.mult), r=[kNUM, (kSM, 7)], w=[kHN])
                        fns.append(s10)

                        def s11():
                            S.op(PE_, lambda e: e.transpose(out=psb(bC)[:, 512:640], in_=HN, identity=identb[:, :]), r=[kHN, kC], w=[PK[bC]])
                        fns.append(s11)

                        def s12():
                            S.op(V, lambda e: e.scalar_tensor_tensor(out=hmT[:, h, cs_], in0=psb(bC)[:, 512:640], scalar=aon[:, i, h:h + 1], in1=sgT[:, h, cs_], op0=ALU.mult, op1=ALU.mult),
                                 r=[PK[bC], kC, ksg], w=[(khm, h)])
                        fns.append(s12)

                        def s13():
                            for s_ in range(nseq):
                                S.op(V, lambda e, s_=s_: e.tensor_scalar(out=VS, in0=VT[:, c * 4 + h, :], scalar1=WSI[:, s_:s_ + 1], scalar2=None, op0=ALU.mult), r=[kvt, kWSI], w=[kVS])
                                S.op(PE_, lambda e: e.matmul(PS[bD][:, 0:129], KTOK[:, c, h * 128:(h + 1) * 128], VS, start=True, stop=True), r=[kkt, kVS], w=[PK[bD]])
                                S.op(V, lambda e, s_=s_: e.scalar_tensor_tensor(out=csv(s_, h), in0=csv(s_, h), scalar=WCB[:, s_:s_ + 1], in1=PS[bD][:, 0:129], op0=ALU.mult, op1=ALU.add),
                                     r=[PK[bD], kWCB, kCSh(h)], w=[kCSh(h)])
                        fns.append(s13)
                        return fns

                    groups = [[0], [1], [2], [3]] if nset == 1 else [[0, 1], [2, 3]]
                    for grp in groups:
                        fl = [stage_fns(h, z) for z, h in enumerate(grp)]
                        for si in range(len(fl[0])):
                            for f_ in fl:
                                f_[si]()
                S.barrier()
                khm_all = [(khm, h) for h in range(4)]
                kCS_all = [kCSh(h) for h in range(4)]
                if last:
                    if sample:
                        S.dma('sync', O["sC"][i].rearrange("s h k v -> k (s h) v"), CS_[:, :, 0:128], r=kCS_all, chan=('osC',))
                        fm_rows_out(lambda c: CS_[:, :, 128], 64, lambda o, n: O["sn"][i, :, o:o + n], 1, kCS_all)
                    else:
                        S.dma('sync', O["pC"][i].rearrange("h k v -> k h v"), Cst[:, i, :, 0:128], r=kCS_all, chan=('opC',))
                        fm_rows_out(lambda c: Cst[:, i, :, 128], 4, lambda o, n: O["pn"][i, :, o:o + n], 1, kCS_all)
                for d in range(8):
                    b = d % 2
                    for j in range(8):
                        rhs = hmT[:, j, :] if j < 4 else scTt[:, j - 4, :]
                        S.op(PE_, lambda e, b=b, j=j, d=d, rhs=rhs: e.matmul(PS[b][:, 0:T], wout[:, j, d * 128:(d + 1) * 128], rhs, start=(j == 0), stop=(j == 7)),
                             r=kW + khm_all + [ksc], w=[PK[b]], inc=(j == 7))
                    resid_add(l, 2, d, PS[b][:, 0:T], PK[b], rt)

            def mixer_c(l):
                jl = l // 2
                wq = WB[:, 0:8 * 1536].rearrange("p (k c) -> p k c", c=1536)
                wo = WB[:, 8 * 1536:8 * 1536 + 8 * D].rearrange("p (k c) -> p k c", c=D)
                kW = [('W', 0), ('W', 1), ('W', 2)]
                S.dma('pool', WB[:, 0:8 * 1536], I["c_w_qkv"][jl], w=kW, chan=('W', 0), max_dma_last_dim=8192)
                S.dma('pool', WB[:, 8 * 1536:8 * 1536 + 8 * D], I["c_w_out"][jl], w=kW, chan=('W', 0), max_dma_last_dim=8192)
                norm_mod(l, 1)
                S.barrier(); AR.reset()
                QK, kQK = AR.get("QK", [128, 1280])
                SQ, kSQ = AR.get("SQ", [128, 1280])
                QR, kQR = AR.get("QR", [128, 1280])
                QRb, kQRb = AR.get("QRb", [128, 1280], BF16)
                T1, kT1 = AR.get("T1", [128, 640])
                T2, kT2 = AR.get("T2", [128, 640])
                VV, kVV = AR.get("VV", [128, 256])
                SS, kSS = AR.get("SS", [128, 20])
                QT, kQT = AR.get("QTz", [128, 16, T], BF16)
                S.op(V, lambda e: e.memset(QT, 0.0), w=[kQT])
                QT5 = QT.rearrange("p (a b g) t -> p a b g t", a=2, b=2)
                KT, kKT = AR.get("KT", [128, 2, T], BF16)
                VTa, kVTa = AR.get("VTa", [128, nch * 4, 65], BF16)
                PTb = [AR.get("PTb%d" % z, [128, 512], BF16) for z in range(2)]
                DEN, kDEN = AR.get("DEN", [128, 4])
                OTOK, kOT = AR.get("OTOK", [128, 16, 64], BF16)
                OTT, kOTT = AR.get("OTT", [128, 8, T], BF16)
                rt = AR.get("rt", [128, T])
                if sample:
                    CKb, kCKb = AR.get("CKb", [128, 16, 256], BF16)
                    KCT, kKCT = AR.get("KCT", [128, 32, 128], BF16)
                    VCs, kVCs = AR.get("VCs", [128, 64, 65], BF16)
                    S.op(V, lambda e: e.memset(VCs, 1.0), w=[kVCs])
                    S.dma('pool', CKb, I["ck"][jl].rearrange("s p c -> p s c"), w=[kCKb], max_dma_last_dim=1024)
                    for s_ in range(16):
                        S.dma('pool', VCs[:, s_ * 4:(s_ + 1) * 4, 0:64], I["cv"][jl, s_].rearrange("p (h d) -> p h d", d=64), w=[kVCs], max_dma_last_dim=256)
                    for s_ in range(16):
                        for j in range(2):
                            S.op(PE_, lambda e, s_=s_, j=j: e.transpose(out=psb(3)[:, j * 128:(j + 1) * 128], in_=CKb[:, s_, j * 128:(j + 1) * 128], identity=identb[:, :]), r=[kCKb, kC], w=[PK[3]])
                        S.op(V, lambda e, s_=s_: e.tensor_copy(out=KCT[:, 2 * s_:2 * s_ + 2, :], in_=psb(3)[:, 0:256].rearrange("p (a b) -> p a b", b=128)), r=[PK[3]], w=[kKCT])
                    S.dma('sync', O["swk"][jl, :, 0:120, :], I["ck"][jl, :, 8:128, :], chan=('dd',))
                    S.dma('sync', O["swv"][jl, :, 0:120, :], I["cv"][jl, :, 8:128, :], chan=('dd',))
                S.op(V, lambda e: e.memset(VTa, 1.0), w=[kVTa])
                if int(os.environ.get('KD_C', 9)) < 1:
                    return
                qk3 = lambda ap: ap.rearrange("p (h d) -> p h d", d=64)
                for c in range(nch):
                    cs_ = slice(c * 128, (c + 1) * 128)
                    for b in range(3):
                        for kc in range(8):
                            S.op(PE_, lambda e, b=b, kc=kc, cs_=cs_: e.matmul(PS[b][:, 0:512], hT[:, kc, cs_], wq[:, kc, b * 512:(b + 1) * 512], start=(kc == 0), stop=(kc == 7)),
                                 r=kW + [kH], w=[PK[b]], inc=(kc == 7))
                    S.op(A_, lambda e: e.copy(out=QK[:, 0:512], in_=PS[0][:, 0:512]), r=[PK[0]], w=[kQK])
                    S.op(A_, lambda e: e.copy(out=QK[:, 512:1024], in_=PS[1][:, 0:512]), r=[PK[1]], w=[kQK])
                    S.op(A_, lambda e: e.copy(out=QK[:, 1024:1280], in_=PS[2][:, 0:256]), r=[PK[2]], w=[kQK])
                    if os.environ.get('KD_X', '') == 'a':
                        continue
                    S.op(A_, lambda e: e.copy(out=VV, in_=PS[2][:, 256:512]), r=[PK[2]], w=[kVV])
                    if os.environ.get('KD_X', '') == 'b':
                        continue
                    S.op(V, lambda e, c=c: e.tensor_copy(out=VTa[:, c * 4:(c + 1) * 4, 0:64], in_=VV.rearrange("p (h d) -> p h d", d=64)), r=[kVV], w=[kVTa])
                    if int(os.environ.get('KD_C', 9)) < 2:
                        continue
                    S.op(A_, lambda e: e.activation(out=SQ, in_=QK, func=AF.Square), r=[kQK], w=[kSQ])
                    S.op(V, lambda e: e.tensor_reduce(out=SS, in_=qk3(SQ), axis=AX.X, op=ALU.add), r=[kSQ], w=[kSS])
                    S.op(A_, lambda e: e.activation(out=SS, in_=SS, func=AF.Sqrt, bias=EPS, scale=1.0 / 64), r=[kSS], w=[kSS])
                    S.op(V, lambda e: e.reciprocal(out=SS, in_=SS), r=[kSS], w=[kSS])
                    S.op(V, lambda e: e.tensor_tensor(out=qk3(QK), in0=qk3(QK), in1=SS.unsqueeze(2).broadcast_to([128, 20, 64]), op=ALU.mult), r=[kSS, kQK], w=[kQK])
                    S.op(V, lambda e: e.tensor_tensor(out=qk3(QK[:, 0:1024]), in0=qk3(QK[:, 0:1024]), in1=GQ[:, jl, :].unsqueeze(1).broadcast_to([128, 16, 64]), op=ALU.mult), r=[kC, kQK], w=[kQK])
                    S.op(V, lambda e: e.tensor_tensor(out=qk3(QK[:, 1024:1280]), in0=qk3(QK[:, 1024:1280]), in1=GK[:, jl, :].unsqueeze(1).broadcast_to([128, 4, 64]), op=ALU.mult), r=[kC, kQK], w=[kQK])
                    if int(os.environ.get('KD_C', 9)) < 3:
                        continue
                    cosb = cosT[:, c, :].unsqueeze(1).broadcast_to([128, 20, 32])
                    sinb = sinT[:, c, :].unsqueeze(1).broadcast_to([128, 20, 32])
                    x1 = qk3(QK)[:, :, 0:32]; x2 = qk3(QK)[:, :, 32:64]
                    t1 = T1.rearrange("p (h d) -> p h d", d=32); t2 = T2.rearrange("p (h d) -> p h d", d=32)
                    S.op(P_, lambda e, x1=x1, cosb=cosb: e.tensor_tensor(out=t1, in0=x1, in1=cosb, op=ALU.mult), r=[kQK, ('rope',)], w=[kT1])
                    S.op(P_, lambda e, x2=x2, sinb=sinb: e.tensor_tensor(out=t2, in0=x2, in1=sinb, op=ALU.mult), r=[kQK, ('rope',)], w=[kT2])
                    S.op(V, lambda e: e.tensor_tensor(out=qk3(QR)[:, :, 0:32], in0=t1, in1=t2, op=ALU.subtract), r=[kT1, kT2], w=[kQR])
                    S.op(P_, lambda e, x2=x2, cosb=cosb: e.tensor_tensor(out=t1, in0=x2, in1=cosb, op=ALU.mult), r=[kQK, ('rope',)], w=[kT1])
                    S.op(P_, lambda e, x1=x1, sinb=sinb: e.tensor_tensor(out=t2, in0=x1, in1=sinb, op=ALU.mult), r=[kQK, ('rope',)], w=[kT2])
                    S.op(V, lambda e: e.tensor_tensor(out=qk3(QR)[:, :, 32:64], in0=t1, in1=t2, op=ALU.add), r=[kT1, kT2], w=[kQR])
                    S.op(A_, lambda e: e.copy(out=QRb, in_=QR), r=[kQR], w=[kQRb])
                    if int(os.environ.get('KD_C', 9)) < 4:
                        continue
                    for j in range(8):
                        S.op(PE_, lambda e, j=j: e.transpose(out=psb(3)[:, j * 128:(j + 1) * 128], in_=QRb[:, j * 128:(j + 1) * 128], identity=identb[:, :]), r=[kQRb, kC], w=[PK[3]])
                    S.op(V, lambda e, cs_=cs_: e.tensor_copy(out=QT5[0:64, :, 0, :, cs_], in_=psb(3)[0:64, 0:1024].rearrange("p (a g t) -> p a g t", a=2, g=4)), r=[PK[3]], w=[kQT])
                    S.op(V, lambda e, cs_=cs_: e.tensor_copy(out=QT5[64:128, :, 1, :, cs_], in_=psb(3)[64:128, 0:1024].rearrange("p (a g t) -> p a g t", a=2, g=4)), r=[PK[3]], w=[kQT])
                    for j in range(2):
                        S.op(PE_, lambda e, j=j: e.transpose(out=psb(4)[:, j * 128:(j + 1) * 128], in_=QRb[:, 1024 + j * 128:1024 + (j + 1) * 128], identity=identb[:, :]), r=[kQRb, kC], w=[PK[4]])
                    S.op(V, lambda e, cs_=cs_: e.tensor_copy(out=KT[:, :, cs_], in_=psb(4)[:, 0:256].rearrange("p (a b) -> p a b", b=128)), r=[PK[4]], w=[kKT])
                    if sample:
                        for s_ in range(16):
                            S.dma('sync', O["swk"][jl, s_, 120:128, :], QR[s_ * 8:(s_ + 1) * 8, 1024:1280], r=[kQR], chan=('oswk',))
                            S.dma('sync', O["swv"][jl, s_, 120:128, :], VV[s_ * 8:(s_ + 1) * 8, :], r=[kVV], chan=('oswv',))
                    elif last and c == nch - 1:
                        S.dma('sync', O["pwk"][jl], QR[:, 1024:1280], r=[kQR], chan=('opwk',))
                        S.dma('sync', O["pwv"][jl], VV, r=[kVV], chan=('opwv',))
                    if int(os.environ.get('KD_C', 9)) < 5:
                        continue
                    for kap in range(4):
                        base = 64 * (kap % 2)
                        pr = kap // 2
                        blocks = []
                        if sample:
                            for s_ in range(16):
                                blocks.append((KCT[:, 2 * s_ + pr, :], VCs[:, s_ * 4 + kap, :],
                                               acacheb[:, s_, :].unsqueeze(1).broadcast_to([128, 4, 128]), [kKCT], [kVCs]))
                            blocks.append((KT[:, pr, cs_], VTa[:, c * 4 + kap, :], anewb[:, :].rearrange("p (g t) -> p g t", t=128), [kKT], [kVTa]))
                        else:
                            blocks.append((KT[:, pr, cs_], VTa[:, c * 4 + kap, :], acurb[:, :].rearrange("p (g t) -> p g t", t=128), [kKT], [kVTa]))
                            if c > 0:
                                ps_ = slice((c - 1) * 128, c * 128)
                                blocks.append((KT[:, pr, ps_], VTa[:, (c - 1) * 4 + kap, :], aprevb[:, :].rearrange("p (g t) -> p g t", t=128), [kKT], [kVTa]))
                            elif pi > 0:
                                blocks.append((KTC[:, jl, pr, :], VC[:, jl, kap, :], aprevb[:, :].rearrange("p (g t) -> p g t", t=128), [('KTC',)], [('VC',)]))
                        qrhs = QT[:, 4 * kap:4 * kap + 4, cs_]
                        for bi, (kb, vb, mb, kr, vr) in enumerate(blocks):
                            pt, kpt = PTb[bi % 2]
                            ps5 = PS[5][:, 0:512].rearrange("p (g t) -> p g t", t=128)
                            S.op(PE_, lambda e, kb=kb, qrhs=qrhs, ps5=ps5: e.matmul(ps5, kb, qrhs, start=True, stop=False), r=kr + [kQT], w=[PK[5]], inc=False)
                            S.op(PE_, lambda e, mb=mb, ps5=ps5: e.matmul(ps5, identb[:, :], mb, start=False, stop=True), r=[kC], w=[PK[5]])
                            S.op(A_, lambda e, pt=pt: e.activation(out=pt, in_=PS[5][:, 0:512], func=AF.Exp, bias=NEGMA[:, jl:jl + 1], scale=0.125), r=[PK[5], kC2], w=[kpt])
                            for g in range(4):
                                S.op(PE_, lambda e, g=g, pt=pt, vb=vb, bi=bi, nb=len(blocks): e.matmul(PS[6][:, g * 65:(g + 1) * 65], pt[:, g * 128:(g + 1) * 128], vb,
                                                                                    start=(bi == 0 and g == 0), stop=(bi == nb - 1), skip_group_check=True),
                                     r=[kpt] + vr, w=[PK[6]], inc=(g == 3))
                        s0 = 8 * pr + (kap % 2)
                        o3 = PS[6][:, 0:260].rearrange("p (g d) -> p g d", d=65)
                        S.op(V, lambda e, s0=s0, o3=o3: e.tensor_tensor(out=DEN, in0=o3[:, :, 64], in1=SINKE[:, jl, s0:s0 + 7:2], op=ALU.add), r=[PK[6], ('SINKE',)], w=[kDEN])
                        S.op(V, lambda e: e.reciprocal(out=DEN, in_=DEN), r=[kDEN], w=[kDEN])
                        S.op(V, lambda e, s0=s0, o3=o3: e.tensor_tensor(out=OTOK[:, s0:s0 + 7:2, :], in0=o3[:, :, 0:64], in1=DEN.unsqueeze(2).broadcast_to([128, 4, 64]), op=ALU.mult),
                             r=[PK[6], kDEN], w=[kOT])
                    if int(os.environ.get('KD_C', 9)) < 6:
                        continue
                    otf = OTOK.rearrange("p h d -> p (h d)")
                    for j in range(8):
                        S.op(PE_, lambda e, j=j: e.transpose(out=psb(7)[:, j * 128:(j + 1) * 128], in_=otf[:, j * 128:(j + 1) * 128], identity=identb[:, :]), r=[kOT, kC], w=[PK[7]])
                    S.op(V, lambda e, cs_=cs_: e.tensor_copy(out=OTT[:, :, cs_], in_=psb(7)[:, 0:1024].rearrange("p (a b) -> p a b", b=128)), r=[PK[7]], w=[kOTT])
                if int(os.environ.get('KD_C', 9)) < 7:
                    return
                if not sample:
                    ls_ = slice((nch - 1) * 128, nch * 128)
                    S.op(V, lambda e: e.tensor_copy(out=KTC[:, jl, :, :], in_=KT[:, :, ls_]), r=[kKT], w=[('KTC',)])
                    S.op(V, lambda e: e.tensor_copy(out=VC[:, jl, :, :], in_=VTa[:, (nch - 1) * 4:nch * 4, :]), r=[kVTa], w=[('VC',)])
                for d in range(8):
                    b = d % 2
                    for j in range(8):
                        S.op(PE_, lambda e, b=b, j=j, d=d: e.matmul(PS[b][:, 0:T], wo[:, j, d * 128:(d + 1) * 128], OTT[:, j, :], start=(j == 0), stop=(j == 7)),
                             r=kW + [kOTT], w=[PK[b]], inc=(j == 7))
                    resid_add(l, 2, d, PS[b][:, 0:T], PK[b], rt)

            for l in range(4):
                if str(l) not in os.environ.get("KD_LAYERS", "0123"):
                    continue
                if "m" in os.environ.get("KD_PARTS", "mf"):
                    if l % 2 == 0:
                        mixer_ab(l)
                    else:
                        mixer_c(l)
                if "f" in os.environ.get("KD_PARTS", "mf"):
                    ffn(l)
            S.barrier()
            for c in range(nch):
                for b in range(2):
                    for j in range(4):
                        kc = b * 4 + j
                        S.op(PE_, lambda e, b=b, j=j, kc=kc, c=c: e.transpose(out=PS[b][:, j * 128:(j + 1) * 128], in_=xT[:, kc, c * 128:(c + 1) * 128], identity=ident[:, :]),
                             r=[kX, kC], w=[PK[b]])
                    S.op(A_, lambda e, b=b: e.copy(out=stage[:, b * 512:(b + 1) * 512], in_=PS[b][:, :]), r=[PK[b]], w=[('stage',)])
                S.dma('sync', yout[tok0 + c * 128: tok0 + (c + 1) * 128, :], stage[:, :], r=[('stage',)], chan=('ostage',))

        for pi in range(int(os.environ.get("KD_NPASS", NPASS))):
            run_pass(False, pi)
        if os.environ.get("KD_SAMPLE", "1") == "1":
            run_pass(True, 0)
        S.finish()

        with nc.Block() as block:
            @block.sync
            def _(e):
                S.replay('sync', e)

            @block.tensor
            def _(e):
                S.replay('pe', e)

            @block.scalar
            def _(e):
                S.replay('act', e)

            @block.vector
            def _(e):
                S.replay('dve', e)

            @block.gpsimd
            def _(e):
                S.replay('pool', e)
    return nc


def _slot_perm():
    perm = np.zeros(16, np.int64)
    for kap in range(4):
        for g in range(4):
            s = 2 * (g + 4 * (kap // 2)) + (kap % 2)
            perm[s] = 4 * kap + g
    return perm


def _consts():
    c = {}
    c["ident"] = np.eye(128, dtype=np.float32)
    s = np.arange(128)[:, None]
    t = np.arange(128)[None, :]
    c["mP"] = np.where(s <= t, 0.0, NEG).astype(np.float32)
    same = (s // 8) == (t // 8)
    c["mS"] = np.where(same & (s <= t), 0.0, NEG).astype(np.float32)
    c["acur"] = np.tile(np.where(s <= t, 0.0, ANEG).astype(np.float32), (1, 4))
    c["aprev"] = np.tile(np.where(s > t, 0.0, ANEG).astype(np.float32), (1, 4))
    c["anew"] = np.tile(np.where(same & (s <= t), 0.0, ANEG).astype(np.float32), (1, 4))
    ac = np.full((128, 16, 128), ANEG, np.float32)
    for i in range(16):
        for j in range(8):
            tt = 8 * i + j
            ac[j + 1:, i, tt] = 0.0
    c["acache"] = ac
    oh = np.zeros((128, 16), np.float32)
    oh[np.arange(128), np.arange(128) // 8] = 1.0
    c["onehot"] = oh
    sel = np.zeros((4, 4, 128), np.float32)
    for h in range(4):
        sel[h, h, :] = 1.0
    c["sel4"] = sel
    inv = 10000.0 ** (-np.arange(32, dtype=np.float64) / 32)
    pos = np.arange(SEQ, dtype=np.float64)[:, None] * inv[None, :]
    c["cosP"] = np.cos(pos).astype(np.float32)
    c["sinP"] = np.sin(pos).astype(np.float32)
    ps = (8192 + (np.arange(128) % 8)).astype(np.float64)[:, None] * inv[None, :]
    c["cosS"] = np.cos(ps).astype(np.float32)
    c["sinS"] = np.sin(ps).astype(np.float32)
    return c


_NC = None


def kernel(x_prompt, x_sample, c_prompt, c_sample, state_mlstm_C, state_mlstm_n, state_mlstm_m,
           state_sconv, cache_win_k, cache_win_v, state_ffn_conv,
           norm1, norm2, w_ada, b_ada, a_w_in, a_b_if, a_out_norm, a_conv_w, a_w_out,
           c_w_qkv, c_q_norm, c_k_norm, c_sink, c_w_out, f_w_up, f_conv_w, f_w_down):
    global _NC
    f = lambda a: np.ascontiguousarray(np.asarray(a, dtype=np.float32))
    perm = _slot_perm()
    common = dict(_consts())
    common["w_ada"] = f(w_ada)
    common["b_adaT"] = f(np.asarray(b_ada).reshape(4, 48, 128).transpose(2, 0, 1))
    common["norm1T"] = f(np.asarray(norm1).reshape(4, 8, 128).transpose(2, 0, 1))
    common["norm2T"] = f(np.asarray(norm2).reshape(4, 8, 128).transpose(2, 0, 1))
    pk8 = lambda w: f(np.asarray(w).reshape(w.shape[0], 8, 128, w.shape[2]).transpose(0, 2, 1, 3).reshape(w.shape[0], 128, 8 * w.shape[2]))
    common["a_w_in"] = pk8(np.asarray(a_w_in))
    common["a_bif"] = f(np.asarray(a_b_if).reshape(2, 2, 4).transpose(2, 0, 1))
    common["a_onT"] = f(np.asarray(a_out_norm).reshape(2, 4, 128).transpose(2, 0, 1))
    common["a_cwT"] = f(np.asarray(a_conv_w).reshape(2, 3, 4, 128).transpose(3, 0, 1, 2))
    common["a_w_out"] = pk8(np.asarray(a_w_out))
    wq = np.asarray(c_w_qkv)
    qcols = np.concatenate([np.arange(64) + 64 * h for h in perm])
    common["c_w_qkv"] = pk8(np.concatenate([wq[:, :, qcols], wq[:, :, 1024:]], axis=2))
    common["c_qn"] = f(c_q_norm)
    common["c_kn"] = f(c_k_norm)
    common["c_sink"] = f(np.asarray(c_sink)[:, perm])
    common["c_w_out"] = pk8(np.asarray(c_w_out)[:, qcols, :])
    wup = np.asarray(f_w_up, dtype=np.float32).reshape(4, 8, 128, 2, 22, 128)
    wdn = np.asarray(f_w_down, dtype=np.float32).reshape(4, 22, 128, 1024)
    fwg = np.zeros((4, 6, 128, 12288), np.float32)
    for g, (f0, n) in enumerate([(0, 4), (4, 4), (8, 4), (12, 4), (16, 4), (20, 2)]):
        blk = np.zeros((4, 128, 2, 8, 4, 128), np.float32)
        blk[:, :, :, :, 0:n, :] = wup[:, :, :, :, f0:f0 + n, :].transpose(0, 2, 3, 1, 4, 5)
        fwg[:, g, :, 0:8192] = blk.reshape(4, 128, 8192)
        dblk = np.zeros((4, 128, 4, 1024), np.float32)
        dblk[:, :, 0:n, :] = wdn[:, f0:f0 + n].transpose(0, 2, 1, 3)
        fwg[:, g, :, 8192:12288] = dblk.reshape(4, 128, 4096)
    common["f_w_g"] = fwg
    common["f_cwT"] = f(np.asarray(f_conv_w).reshape(4, 3, 44, 128).transpose(3, 0, 1, 2))
    in_maps = []
    for c in range(NCORE):
        b = c // 4
        sl = slice(16 * c, 16 * c + 16)
        m = dict(common)
        m["xp"] = f(np.asarray(x_prompt)[b])
        m["xs"] = f(np.asarray(x_sample)[sl].reshape(128, D))
        m["call"] = f(np.concatenate([np.asarray(c_prompt)[b:b + 1], np.asarray(c_sample)[sl]], axis=0))
        m["stC"] = f(np.asarray(state_mlstm_C)[:, sl])
        m["stn"] = f(np.asarray(state_mlstm_n)[:, sl].reshape(2, 64, 128))
        m["stmT"] = f(np.asarray(state_mlstm_m)[:, sl].transpose(2, 0, 1))
        m["stsc"] = f(np.asarray(state_sconv)[:, sl].reshape(2, 32, 512))
        m["ck"] = f(np.asarray(cache_win_k)[:, sl].reshape(2, 16, 128, 256))
        m["cv"] = f(np.asarray(cache_win_v)[:, sl].reshape(2, 16, 128, 256))
        m["stffn"] = f(np.asarray(state_ffn_conv)[:, sl].reshape(4, 32, 5632))
        in_maps.append(m)
    if _NC is None:
        _NC = build()
    res = run_bass_kernel_spmd(_NC, in_maps, core_ids=list(range(NCORE)))
    R = res.results
    pc = [0, 4]
    cat_p = lambda nm, shp: np.stack([R[c][nm] for c in pc], axis=1).reshape(shp)
    y_prompt = np.stack([R[c]["yp"] for c in pc], axis=0)
    y_sample = np.concatenate([R[c]["ys"].reshape(16, 8, D) for c in range(NCORE)], axis=0)
    p_C = cat_p("pC", (2, 2, 4, 128, 128))
    p_n = cat_p("pn", (2, 2, 4, 128))
    p_m = cat_p("pm", (2, 2, 4))
    p_sc = cat_p("psc", (2, 2, 2, 512))
    p_wk = cat_p("pwk", (2, 2, 128, 4, 64))
    p_wv = cat_p("pwv", (2, 2, 128, 4, 64))
    p_ffn = cat_p("pffn", (4, 2, 2, 5632))
    cat_s = lambda nm, shp: np.concatenate([R[c][nm].reshape(shp) for c in range(NCORE)], axis=1)
    s_C = cat_s("sC", (2, 16, 4, 128, 128))
    s_n = cat_s("sn", (2, 16, 4, 128))
    s_m = np.concatenate([R[c]["smT"].transpose(1, 2, 0) for c in range(NCORE)], axis=1)
    s_sc = cat_s("ssc", (2, 16, 2, 512))
    s_wk = cat_s("swk", (2, 16, 128, 4, 64))
    s_wv = cat_s("swv", (2, 16, 128, 4, 64))
    s_ffn = cat_s("sffn", (4, 16, 2, 5632))
    outs = (y_prompt, y_sample, p_C, p_n, p_m, p_sc, p_wk, p_wv, p_ffn,
            s_C, s_n, s_m, s_sc, s_wk, s_wv, s_ffn)
    return tuple(np.ascontiguousarray(o, dtype=np.float32) for o in outs)
```

```python
import contextlib
import os
import numpy as np
import concourse.bass as bass
import concourse.mybir as mybir
from concourse.bass_utils import run_bass_kernel_spmd

F32 = mybir.dt.float32
BF16 = mybir.dt.bfloat16
AF = mybir.ActivationFunctionType
ALU = mybir.AluOpType
AX = mybir.AxisListType

D = 1024
KC = 8
DFF = 2816
NF = 22
INA = 3592
SEQ = 8192
TP = 512
NPASS = SEQ // TP
EPS = 1e-6
NEG = -1.0e30
ANEG = -1.0e9
NCORE = 8
SLOT = 13312


class Sch:
    ROT = 30000

    def __init__(self, nc, stack):
        self.nc = nc
        self.stack = stack
        self.names = ('pe', 'act', 'dve', 'pool', 'sync')
        self.prog = {e: [] for e in self.names}
        self.cnt = {e: 0 for e in ('pe', 'act', 'dve', 'pool')}
        self.sems = {}
        self.seen = {e: {} for e in self.names}
        self.lastw = {}
        self.readers = {}
        self.dtot = {}

    def sem(self, key):
        if key not in self.sems:
            nm = "s" + str(len(self.sems))
            self.sems[key] = self.stack.enter_context(self.nc.semaphore(nm))
        return self.sems[key]

    def _deps(self, r, w):
        ev = []
        for k in r:
            if k in self.lastw:
                ev.append(self.lastw[k])
            if k[0] == 'P':
                ev.extend(self.readers.get(k, []))
        for k in w:
            if k in self.lastw:
                ev.append(self.lastw[k])
            ev.extend(self.readers.get(k, []))
        return ev

    def _wait(self, en, evs, skip=None):
        need = {}
        for (sk, v) in evs:
            if skip is not None and sk == skip:
                continue
            if en == 'pe' and sk[0] == 'pe':
                continue
            if need.get(sk, 0) < v:
                need[sk] = v
        for sk, v in need.items():
            if self.seen[en].get(sk, 0) < v:
                self.prog[en].append(('w', self.sem(sk), v))
                self.seen[en][sk] = v

    def _commit(self, ev, r, w):
        for k in r:
            self.readers.setdefault(k, []).append(ev)
        for k in w:
            self.lastw[k] = ev
            self.readers[k] = []

    def op(self, en, fn, r=(), w=(), inc=True):
        self._wait(en, self._deps(r, w))
        c = self.cnt[en] + 1
        sk = (en, (c - 1) // self.ROT)
        v = (c - 1) % self.ROT + 1
        if inc:
            self.cnt[en] = c
            self.prog[en].append(('i', fn, self.sem(sk), 1))
        else:
            self.prog[en].append(('i', fn, None, 0))
        self._commit((sk, v), r, w)

    def dma(self, q, out, in_, r=(), w=(), chan=None, **kw):
        if chan is None:
            chan = w[0] if len(w) else ('o',) + tuple(r[0])
        sk = ('d', chan)
        self._wait(q, self._deps(r, w), skip=sk)
        self.dtot[chan] = self.dtot.get(chan, 0) + 16
        fn = (lambda e, out=out, in_=in_, kw=kw: e.dma_start(out=out, in_=in_, allow_slow_non_contiguous=True, **kw))
        self.prog[q].append(('i', fn, self.sem(sk), 16))
        self._commit((sk, self.dtot[chan]), r, w)

    def barrier(self):
        evs = []
        for e in ('pe', 'act', 'dve', 'pool'):
            c = self.cnt[e]
            if c > 0:
                evs.append(((e, (c - 1) // self.ROT), (c - 1) % self.ROT + 1))
        for ch, t in self.dtot.items():
            if ch[0] == 'W':
                continue
            evs.append((('d', ch), t))
        for e in self.names:
            need = [x for x in evs if not (x[0][0] == e)]
            for (sk, v) in need:
                if self.seen[e].get(sk, 0) < v:
                    self.prog[e].append(('w', self.sem(sk), v))
                    self.seen[e][sk] = v

    def finish(self):
        evs = []
        for ch, t in self.dtot.items():
            evs.append((('d', ch), t))
        for e in ('pe', 'act', 'dve', 'pool'):
            c = self.cnt[e]
            if c > 0:
                evs.append(((e, (c - 1) // self.ROT), (c - 1) % self.ROT + 1))
        for (sk, v) in evs:
            self.prog['sync'].append(('w', self.sem(sk), v))

    def replay(self, en, eng):
        for it in self.prog[en]:
            if it[0] == 'w':
                eng.wait_ge(it[1], it[2])
            else:
                ins = it[1](eng)
                if it[2] is not None:
                    ins.then_inc(it[2], it[3])


class Arena:
    def __init__(self, t, words):
        self.t = t
        self.words = words
        self.off = 0
        self.gen = 0

    def reset(self):
        self.off = 0
        self.gen += 1

    def get(self, name, shape, dt=F32, parts=128):
        n = 1
        for s in shape[1:]:
            n *= s
        if dt == BF16:
            w = (n + 1) // 2
        else:
            w = n
        w = (w + 7) // 8 * 8
        assert self.off + w <= self.words, (name, self.off, w, self.words)
        ap = self.t[0:shape[0], self.off:self.off + w]
        self.off += w
        if dt == BF16:
            ap = ap.bitcast(BF16)[:, 0:n]
        else:
            ap = ap[:, 0:n]
        if len(shape) == 3:
            ap = ap.rearrange("p (a b) -> p a b", b=shape[2])
        elif len(shape) == 4:
            ap = ap.rearrange("p (a b c) -> p a b c", b=shape[2], c=shape[3])
        return ap, (name, self.gen)


def build():
    nc = bass.Bass("TRN2", target_bir_lowering=False)
    din = lambda n, s: nc.dram_tensor(n, list(s), F32, kind="ExternalInput").ap()
    dout = lambda n, s: nc.dram_tensor(n, list(s), F32, kind="ExternalOutput").ap()
    I = {}
    for n, s in [("xp", (SEQ, D)), ("xs", (128, D)), ("call", (17, D)),
                 ("stC", (2, 16, 4, 128, 128)), ("stn", (2, 64, 128)), ("stmT", (4, 2, 16)),
                 ("stsc", (2, 32, 512)), ("ck", (2, 16, 128, 256)), ("cv", (2, 16, 128, 256)),
                 ("stffn", (4, 32, 5632)),
                 ("w_ada", (4, D, 6144)), ("b_adaT", (128, 4, 48)), ("norm1T", (128, 4, 8)),
                 ("norm2T", (128, 4, 8)), ("a_w_in", (2, 128, 8 * INA)), ("a_bif", (4, 2, 2)),
                 ("a_onT", (128, 2, 4)), ("a_cwT", (128, 2, 3, 4)), ("a_w_out", (2, 128, 8 * D)),
                 ("c_w_qkv", (2, 128, 8 * 1536)), ("c_qn", (2, 64)), ("c_kn", (2, 64)), ("c_sink", (2, 16)),
                 ("c_w_out", (2, 128, 8 * D)), ("f_w_g", (4, 6, 128, 12288)), ("f_cwT", (128, 4, 3, 44)),
                 ("ident", (128, 128)), ("mP", (128, 128)), ("mS", (128, 128)),
                 ("acur", (128, 512)), ("aprev", (128, 512)), ("anew", (128, 512)),
                 ("acache", (128, 16, 128)), ("onehot", (128, 16)), ("sel4", (4, 4, 128)),
                 ("cosP", (SEQ, 32)), ("sinP", (SEQ, 32)), ("cosS", (128, 32)), ("sinS", (128, 32))]:
        I[n] = din(n, s)
    O = {}
    for n, s in [("yp", (SEQ, D)), ("ys", (128, D)), ("pC", (2, 4, 128, 128)), ("pn", (2, 4, 128)),
                 ("pm", (2, 4)), ("psc", (2, 2, 512)), ("pwk", (2, 128, 256)), ("pwv", (2, 128, 256)),
                 ("pffn", (4, 2, 5632)), ("sC", (2, 16, 4, 128, 128)), ("sn", (2, 64, 128)),
                 ("smT", (4, 2, 16)), ("ssc", (2, 32, 512)), ("swk", (2, 16, 128, 256)),
                 ("swv", (2, 16, 128, 256)), ("sffn", (4, 32, 5632))]:
        O[n] = dout(n, s)

    with contextlib.ExitStack() as st:
        S = Sch(nc, st)
        sbt = lambda n, s, dt=F32: st.enter_context(nc.sbuf_tensor("sb_" + n, list(s), dt))
        PS = [st.enter_context(nc.psum_tensor("ps%d" % i, [128, 512], F32)) for i in range(8)]
        PK = [('P', i) for i in range(8)]
        psb = lambda i: PS[i][:, :].bitcast(BF16)

        xT = sbt("xT", [128, 8, TP]); kX = ('xT',)
        hT = sbt("hT", [128, 8, TP], BF16); kH = ('hT',)
        MOD = sbt("MOD", [128, 4, 48, 17]); kM = ('MOD',)
        WB = sbt("WB", [128, 3 * SLOT], BF16)
        WKt = sbt("WK", [128, 15360])
        AR = Arena(WKt, 15360)
        ident = sbt("ident", [128, 128]); identb = sbt("identb", [128, 128], BF16)
        onesb = sbt("onesb", [128, 128], BF16)
        mPb = sbt("mPb", [128, 128], BF16); mSb = sbt("mSb", [128, 128], BF16)
        acurb = sbt("acurb", [128, 512], BF16); aprevb = sbt("aprevb", [128, 512], BF16)
        anewb = sbt("anewb", [128, 512], BF16); acacheb = sbt("acacheb", [128, 16, 128], BF16)
        onehot = sbt("onehot", [128, 16]); sel4 = sbt("sel4", [4, 4, 128])
        onesrow = sbt("onesrow", [4, TP])
        cosT = sbt("cosT", [128, 4, 32]); sinT = sbt("sinT", [128, 4, 32])
        n1T = sbt("n1T", [128, 4, 8]); n2T = sbt("n2T", [128, 4, 8]); badaT = sbt("badaT", [128, 4, 48])
        fcw = sbt("fcw", [128, 4, 3, 44]); acw = sbt("acw", [128, 2, 3, 4]); aon = sbt("aon", [128, 2, 4])
        bif = sbt("bif", [4, 2, 2]); nbif = sbt("nbif", [4, 2, 2])
        GQ = sbt("GQ", [128, 2, 64]); GK = sbt("GK", [128, 2, 64]); SK = sbt("SK", [128, 2, 16])
        SINKE = sbt("SINKE", [128, 2, 16]); NEGMA = sbt("NEGMA", [128, 2]); tmpc = sbt("tmpc", [128, 4])
        Cst = sbt("Cst", [128, 2, 4, 129]); MROW = sbt("MROW", [4, 2])
        ffc = sbt("ffc", [128, 4, 44, 2]); scc = sbt("scc", [128, 2, 4, 2])
        KTC = sbt("KTC", [128, 2, 2, 128], BF16); VC = sbt("VC", [128, 2, 4, 65], BF16)
        scT = sbt("scT", [128, 8, 17], BF16)
        stage = sbt("stage", [128, D]); call_sb = stage[0:17, :]
        kC = ('const',)

        V, A_, P_, PE_ = 'dve', 'act', 'pool', 'pe'

        def ld(q, dst, src, key, **kw):
            S.dma(q, dst, src, w=[key], chan=key if key != kC else ('c0',), **kw)
        for dst, nm in [(ident, "ident"), (onehot, "onehot"), (sel4, "sel4"), (n1T, "norm1T"), (n2T, "norm2T"),
                        (badaT, "b_adaT"), (fcw, "f_cwT"), (acw, "a_cwT"), (aon, "a_onT"), (bif, "a_bif")]:
            S.dma('sync', dst[:], I[nm], w=[kC], chan=('c0',))
        S.dma('sync', call_sb, I["call"], w=[('stage',)], chan=('istage',))
        for l in range(2):
            S.dma('sync', GQ[:, l, :], I["c_qn"][l].partition_broadcast(128), w=[kC], chan=('c0',))
            S.dma('sync', GK[:, l, :], I["c_kn"][l].partition_broadcast(128), w=[kC], chan=('c0',))
            S.dma('sync', SK[:, l, :], I["c_sink"][l].partition_broadcast(128), w=[kC], chan=('c0',))
        for dst, nm in [(identb, "ident"), (mPb, "mP"), (mSb, "mS"), (acurb, "acur"), (aprevb, "aprev"),
                        (anewb, "anew"), (acacheb, "acache")]:
            S.dma('pool', dst[:], I[nm], w=[kC], chan=('c1',))
        kC2 = ('const2',)
        S.op(V, lambda e: e.memset(onesb[:], 1.0), r=[kC], w=[kC2])
        S.op(V, lambda e: e.memset(onesrow[:], 1.0), w=[kC2])
        S.op(V, lambda e: e.tensor_scalar(out=nbif[:], in0=bif[:], scalar1=-1.0, scalar2=None, op0=ALU.mult), r=[kC], w=[kC2])
        S.op(V, lambda e: e.memset(Cst[:], 0.0), w=[('Cst', 0), ('Cst', 1)])
        S.op(V, lambda e: e.memset(MROW[:], 0.0), w=[('MROW',)])
        S.op(V, lambda e: e.memset(ffc[:], 0.0), w=[('ffc',)])
        S.op(V, lambda e: e.memset(scc[:], 0.0), w=[('scc',)])
        S.op(V, lambda e: e.memset(VC[:], 1.0), w=[('VC',)])
        for l in range(2):
            S.op(V, lambda e, l=l: e.tensor_reduce(out=tmpc[:, 0:1], in_=GQ[:, l, :], axis=AX.X, op=ALU.max, apply_absolute_value=True), r=[kC], w=[('tmpc',)])
            S.op(V, lambda e, l=l: e.tensor_reduce(out=tmpc[:, 1:2], in_=GK[:, l, :], axis=AX.X, op=ALU.max, apply_absolute_value=True), r=[kC], w=[('tmpc',)])
            S.op(V, lambda e, l=l: e.scalar_tensor_tensor(out=NEGMA[:, l:l + 1], in0=tmpc[:, 0:1], scalar=-8.0, in1=tmpc[:, 1:2], op0=ALU.mult, op1=ALU.mult), r=[('tmpc',)], w=[kC2])
            S.op(A_, lambda e, l=l: e.activation(out=SINKE[:, l, :], in_=SK[:, l, :], func=AF.Exp, bias=NEGMA[:, l:l + 1], scale=1.0), r=[kC, kC2], w=[('SINKE',)])

        S.op(A_, lambda e: e.activation(out=call_sb, in_=call_sb, func=AF.Silu), r=[('stage',)], w=[('stage',)])
        for kc in range(8):
            S.op(PE_, lambda e, kc=kc: e.transpose(out=PS[0][:, kc * 17:(kc + 1) * 17], in_=call_sb[:, kc * 128:(kc + 1) * 128], identity=ident[0:17, 0:17]),
                 r=[('stage',), kC], w=[PK[0]])
        S.op(V, lambda e: e.tensor_copy(out=scT[:], in_=PS[0][:, 0:136].rearrange("p (a b) -> p a b", b=17)), r=[PK[0]], w=[('scT',)])
        wa = [WB[:, i * 6144:(i + 1) * 6144] for i in range(2)]
        for l in range(4):
            for kc in range(8):
                sl = (l * 8 + kc) % 2
                S.dma('pool', wa[sl], I["w_ada"][l, kc * 128:(kc + 1) * 128, :], w=[('W', sl)], max_dma_last_dim=4096)
                for fc in range(48):
                    b = 1 + fc // 24
                    o = (fc % 24) * 17
                    S.op(PE_, lambda e, sl=sl, fc=fc, b=b, o=o, kc=kc: e.matmul(PS[b][:, o:o + 17], wa[sl][:, fc * 128:(fc + 1) * 128], scT[:, kc, :],
                                                                            start=(kc == 0 and fc % 24 == 0), stop=(kc == 7), skip_group_check=True),
                         r=[('W', sl), ('scT',)], w=[PK[b]], inc=(fc == 47))
            for b in range(2):
                S.op(V, lambda e, l=l, b=b: e.tensor_tensor(out=MOD[:, l, b * 24:(b + 1) * 24, :], in0=PS[1 + b][:, 0:408].rearrange("p (a b) -> p a b", b=17),
                                                           in1=badaT[:, l, b * 24:(b + 1) * 24].unsqueeze(2).broadcast_to([128, 24, 17]), op=ALU.add),
                     r=[PK[1 + b], kC], w=[kM])
            for (c0, nT) in [(8, n1T), (32, n2T)]:
                S.op(V, lambda e, l=l, c0=c0, nT=nT: e.scalar_tensor_tensor(out=MOD[:, l, c0:c0 + 8, :], in0=MOD[:, l, c0:c0 + 8, :], scalar=1.0,
                                                                           in1=nT[:, l, :].unsqueeze(2).broadcast_to([128, 8, 17]), op0=ALU.add, op1=ALU.mult),
                     r=[kM, kC], w=[kM])

        def fm_rows_out(src_fn, nrows, dst_rows_fn, nchunks, rkeys):
            for c0 in range(0, nchunks, 4):
                n = min(4, nchunks - c0)
                for j in range(n):
                    S.op(PE_, lambda e, j=j, c0=c0: e.transpose(out=PS[3][0:nrows, j * 128:(j + 1) * 128], in_=src_fn(c0 + j), identity=ident[:, :]),
                         r=rkeys + [kC], w=[PK[3]])
                S.op(A_, lambda e, n=n: e.copy(out=stage[0:nrows, 0:n * 128], in_=PS[3][0:nrows, 0:n * 128]), r=[PK[3]], w=[('stage',)])
                S.dma('sync', dst_rows_fn(c0 * 128, n * 128), stage[0:nrows, 0:n * 128], r=[('stage',)], chan=('ostage',))

        def rows_to_fm(src_rows_fn, nrows, dst_fn, nchunks, wkeys):
            for c0 in range(0, nchunks, 8):
                n = min(8, nchunks - c0)
                S.dma('sync', stage[0:nrows, 0:n * 128], src_rows_fn(c0 * 128, n * 128), w=[('stage',)], chan=('istage',))
                for j in range(n):
                    S.op(PE_, lambda e, j=j: e.transpose(out=PS[3][:, j * 32:j * 32 + nrows], in_=stage[0:nrows, j * 128:(j + 1) * 128], identity=ident[0:nrows, 0:nrows]),
                         r=[('stage',), kC], w=[PK[3]])
                for j in range(n):
                    S.op(V, lambda e, j=j, c0=c0: e.tensor_copy(out=dst_fn(c0 + j), in_=PS[3][:, j * 32:j * 32 + nrows]), r=[PK[3]], w=wkeys)

        def run_pass(sample, pi):
            T = 128 if sample else TP
            nseq = 16 if sample else 1
            L = 8 if sample else 128
            LT = 8 if sample else TP
            nch = T // 128
            tok0 = 0 if sample else pi * TP
            last = sample or (pi == NPASS - 1)
            xin = I["xs"] if sample else I["xp"]
            yout = O["ys"] if sample else O["yp"]
            sq0 = 1 if sample else 0
            v3 = lambda ap: ap.rearrange("p (s l) -> p s l", l=LT)

            def modbc(l, ch, kc):
                return MOD[:, l, ch * 8 + kc, sq0:sq0 + nseq].unsqueeze(2).broadcast_to([128, nseq, LT])

            for c in range(nch):
                S.dma('sync', stage[:, :], xin[tok0 + c * 128: tok0 + (c + 1) * 128, :], w=[('stage',)], chan=('istage',))
                for b in range(2):
                    for j in range(4):
                        kc = b * 4 + j
                        S.op(PE_, lambda e, b=b, j=j, kc=kc: e.transpose(out=PS[b][:, j * 128:(j + 1) * 128], in_=stage[:, kc * 128:(kc + 1) * 128], identity=ident[:, :]),
                             r=[('stage',), kC], w=[PK[b]])
                    S.op(V if b == 0 else A_, (lambda e, b=b, c=c: e.tensor_copy(out=xT[:, b * 4:(b + 1) * 4, c * 128:(c + 1) * 128], in_=PS[b][:, :].rearrange("p (a b) -> p a b", b=128))) if b == 0 else
                         (lambda e, b=b, c=c: e.copy(out=xT[:, b * 4:(b + 1) * 4, c * 128:(c + 1) * 128], in_=PS[b][:, :].rearrange("p (a b) -> p a b", b=128))),
                         r=[PK[b]], w=[kX])
            if not sample:
                S.dma('sync', cosT[:, 0:nch, :], I["cosP"][tok0:tok0 + T, :].rearrange("(c p) i -> p c i", p=128), w=[('rope',)])
                S.dma('sync', sinT[:, 0:nch, :], I["sinP"][tok0:tok0 + T, :].rearrange("(c p) i -> p c i", p=128), w=[('rope',)])
            else:
                S.dma('sync', cosT[:, 0, :], I["cosS"], w=[('rope',)])
                S.dma('sync', sinT[:, 0, :], I["sinS"], w=[('rope',)])

            def norm_mod(l, which):
                cA, cB = (1, 0) if which == 1 else (4, 3)
                S.barrier(); AR.reset()
                sqb, ksq = AR.get("sq", [128, 8, T], BF16)
                rs, krs = AR.get("rs", [128, T])
                tmp, ktmp = AR.get("tmp", [128, T])
                S.op(A_, lambda e: e.activation(out=sqb, in_=xT[:, :, 0:T], func=AF.Square), r=[kX], w=[ksq])
                for kc in range(8):
                    S.op(PE_, lambda e, kc=kc: e.matmul(PS[0][:, 0:T], onesb[:, :], sqb[:, kc, :], start=(kc == 0), stop=(kc == 7)), r=[ksq, kC2], w=[PK[0]], inc=(kc == 7))
                S.op(A_, lambda e: e.activation(out=rs, in_=PS[0][:, 0:T], func=AF.Sqrt, bias=EPS, scale=1.0 / D), r=[PK[0]], w=[krs])
                S.op(V, lambda e: e.reciprocal(out=rs, in_=rs), r=[krs], w=[krs])
                tmp2, ktmp2 = AR.get("tmp2", [128, T])
                for kc in range(8):
                    tb, ktb = (tmp, ktmp) if kc % 2 == 0 else (tmp2, ktmp2)
                    S.op(V, lambda e, kc=kc, tb=tb: e.tensor_tensor(out=tb, in0=xT[:, kc, 0:T], in1=rs, op=ALU.mult), r=[kX, krs], w=[ktb])
                    if not sample:
                        S.op(A_, lambda e, kc=kc, tb=tb: e.activation(out=hT[:, kc, 0:T], in_=tb, func=AF.Identity,
                                                                   scale=MOD[:, l, cA * 8 + kc, 0:1], bias=MOD[:, l, cB * 8 + kc, 0:1]),
                             r=[ktb, kM], w=[kH])
                    else:
                        S.op(V, lambda e, kc=kc, tb=tb: e.tensor_tensor(out=v3(tb), in0=v3(tb), in1=modbc(l, cA, kc), op=ALU.mult), r=[ktb, kM], w=[ktb])
                        S.op(V, lambda e, kc=kc, tb=tb: e.tensor_tensor(out=v3(hT[:, kc, 0:T]), in0=v3(tb), in1=modbc(l, cB, kc), op=ALU.add), r=[ktb, kM], w=[kH])
                S.barrier()

            def resid_add(l, gch, d, psrc, pkey, tkey_ap):
                tmp, ktmp = tkey_ap
                if not sample:
                    S.op(V, lambda e: e.scalar_tensor_tensor(out=xT[:, d, 0:T], in0=psrc, scalar=MOD[:, l, gch * 8 + d, 0:1], in1=xT[:, d, 0:T], op0=ALU.mult, op1=ALU.add),
                         r=[pkey, kM, kX], w=[kX])
                    return
                S.op(V, lambda e: e.tensor_tensor(out=v3(tmp), in0=v3(psrc), in1=modbc(l, gch, d), op=ALU.mult), r=[pkey, kM], w=[ktmp])
                S.op(V, lambda e: e.tensor_tensor(out=xT[:, d, 0:T], in0=xT[:, d, 0:T], in1=tmp, op=ALU.add), r=[ktmp, kX], w=[kX])

            def conv3(dst3, src3, wfn, keys_r, keys_w, pool=False):
                E1 = P_ if pool else V
                if pool:
                    S.op(E1, lambda e: e.tensor_scalar(out=dst3, in0=src3[:, :, 0:LT], scalar1=wfn(0), scalar2=0.0, op0=ALU.mult, op1=ALU.add), r=keys_r, w=keys_w)
                else:
                    S.op(E1, lambda e: e.tensor_scalar(out=dst3, in0=src3[:, :, 0:LT], scalar1=wfn(0), scalar2=None, op0=ALU.mult), r=keys_r, w=keys_w)
                S.op(V, lambda e: e.scalar_tensor_tensor(out=dst3, in0=src3[:, :, 1:LT + 1], scalar=wfn(1), in1=dst3, op0=ALU.mult, op1=ALU.add), r=keys_r + keys_w, w=keys_w)
                S.op(V, lambda e: e.scalar_tensor_tensor(out=dst3, in0=src3[:, :, 2:LT + 2], scalar=wfn(2), in1=dst3, op0=ALU.mult, op1=ALU.add), r=keys_r + keys_w, w=keys_w)

            def ffn(l):
                gsz = [4, 4, 4, 4, 4, 2]
                gst = [0, 4, 8, 12, 16, 20]

                def wslot(g):
                    base = (g % 3) * SLOT
                    ug = WB[:, base:base + 4096].rearrange("p (k c) -> p k c", c=512)
                    ua = WB[:, base + 4096:base + 8192].rearrange("p (k c) -> p k c", c=512)
                    dn = WB[:, base + 8192:base + 12288].rearrange("p (j d) -> p j d", d=D)
                    return ug, ua, dn

                def issue(g):
                    ug, ua, dn = wslot(g)
                    n = gsz[g]; f0 = gst[g]
                    k = ('W', g % 3)
                    base = (g % 3) * SLOT
                    S.dma('pool', WB[:, base:base + 12288], I["f_w_g"][l, g], w=[k], max_dma_last_dim=8192)
                issue(0); issue(1)
                norm_mod(l, 2)
                if os.environ.get("KD_FFN", "") == "n":
                    return
                S.barrier(); AR.reset()
                UB, kUB = AR.get("UB", [128, 4, nseq * (LT + 2)])
                Y, kY = AR.get("Y", [128, 4, T])
                ACTB, kAB = AR.get("ACTB", [128, 4, T], BF16)
                rt = AR.get("rt", [128, T])
                if sample:
                    car, kcar = AR.get("car", [128, 44, 32])
                    rows_to_fm(lambda o, n: I["stffn"][l, :, o:o + n], 32, lambda c: car[:, c, :], 44, [kcar])
                    carv = lambda idx: car[:, idx, :].rearrange("p (s j) -> p s j", j=2)
                else:
                    kcar = ('ffc',)
                    carv = lambda idx: ffc[:, l, idx, :].unsqueeze(1)
                ub3 = lambda i: UB[:, i, :].rearrange("p (s k) -> p s k", k=LT + 2)

                ACTB2 = [(ACTB, kAB), AR.get("ACTB1", [128, 4, T], BF16)]

                def up_chain(g):
                    ug, ua, dn = wslot(g)
                    kW = ('W', g % 3)
                    n = gsz[g]; f0 = gst[g]
                    AB, kABg = ACTB2[g % 2]
                    for j in range(n):
                        ib = 2 * (j % 2)
                        for part in range(2):
                            i = ib + part
                            wv = ug if part == 0 else ua
                            idx = (0 if part == 0 else 22) + f0 + j
                            for kc in range(8):
                                S.op(PE_, lambda e, i=i, wv=wv, j=j, kc=kc: e.matmul(PS[i][:, 0:T], wv[:, kc, j * 128:(j + 1) * 128], hT[:, kc, 0:T], start=(kc == 0), stop=(kc == 7)),
                                     r=[kW, kH], w=[PK[i]], inc=(kc == 7))
                            S.op(A_, lambda e, i=i, idx=idx: e.copy(out=ub3(i)[:, :, 0:2], in_=carv(idx)), r=[kcar], w=[(kUB, i)])
                            S.op(A_, lambda e, i=i: e.copy(out=ub3(i)[:, :, 2:LT + 2], in_=v3(PS[i][:, 0:T])), r=[PK[i]], w=[(kUB, i)])
                            S.op(A_, lambda e, i=i, idx=idx: e.copy(out=carv(idx), in_=ub3(i)[:, :, LT:LT + 2]), r=[(kUB, i)], w=[kcar])
                            conv3(v3(Y[:, i, :]), ub3(i), lambda t, idx=idx: fcw[:, l, t, idx:idx + 1], [(kUB, i), kC], [(kY, i)], pool=True)
                        S.op(A_, lambda e, ib=ib: e.activation(out=Y[:, ib, :], in_=Y[:, ib, :], func=AF.Silu), r=[(kY, ib)], w=[(kY, ib)])
                        S.op(V, lambda e, j=j, ib=ib, AB=AB: e.tensor_tensor(out=AB[:, j, :], in0=Y[:, ib, :], in1=Y[:, ib + 1, :], op=ALU.mult), r=[(kY, ib), (kY, ib + 1)], w=[(kABg, j)])

                def down(g):
                    ug, ua, dn = wslot(g)
                    kW = ('W', g % 3)
                    n = gsz[g]
                    AB, kABg = ACTB2[g % 2]
                    for dh in range(2):
                        for d4 in range(4):
                            d = dh * 4 + d4
                            for j in range(n):
                                S.op(PE_, lambda e, d4=d4, d=d, j=j, dn=dn, n=n, AB=AB: e.matmul(PS[4 + d4][:, 0:T], dn[:, j, d * 128:(d + 1) * 128], AB[:, j, :], start=(j == 0), stop=(j == n - 1)),
                                     r=[kW, (kABg, j)], w=[PK[4 + d4]], inc=(j == n - 1))
                            resid_add(l, 5, d, PS[4 + d4][:, 0:T], PK[4 + d4], rt)

                up_chain(0)
                for g in range(6):
                    if g + 2 < 6:
                        issue(g + 2)
                    if g + 1 < 6:
                        up_chain(g + 1)
                    down(g)
                if last:
                    nr = 32 if sample else 2
                    dst = O["sffn"] if sample else O["pffn"]
                    if sample:
                        fm_rows_out(lambda c: car[:, c, :], 32, lambda o, n: dst[l, :, o:o + n], 44, [kcar])
                    else:
                        fm_rows_out(lambda c: ffc[:, l, c, :], 2, lambda o, n: dst[l, :, o:o + n], 44, [kcar])

            def mixer_ab(l):
                i = l // 2
                win = WB[:, 0:8 * INA].rearrange("p (k c) -> p k c", c=INA)
                wout = WB[:, 8 * INA:8 * INA + 8 * D].rearrange("p (k c) -> p k c", c=D)
                kW = [('W', 0), ('W', 1), ('W', 2)]
                S.dma('pool', WB[:, 0:8 * INA], I["a_w_in"][i], w=kW, chan=('W', 0), max_dma_last_dim=8192)
                S.dma('pool', WB[:, 8 * INA:8 * INA + 8 * D], I["a_w_out"][i], w=kW, chan=('W', 0), max_dma_last_dim=8192)
                norm_mod(l, 1)
                S.barrier(); AR.reset()
                qT, kqT = AR.get("qT", [128, 4, T], BF16)
                kTt, kkT = AR.get("kT", [128, 4, T], BF16)
                sgT, ksg = AR.get("sgT", [128, 4, T], BF16)
                scTt, ksc = AR.get("scTt", [128, 4, T], BF16)
                hmT, khm = AR.get("hmT", [128, 4, T], BF16)
                KTOK, kkt = AR.get("KTOK", [128, nch, 512], BF16)
                VT, kvt = AR.get("VT", [128, nch * 4, 129], BF16)
                PB, kPB = AR.get("PB", [128, nseq * (LT + 2)])
                ZX, kZX = AR.get("ZX", [128, T])
                U, kU = AR.get("U", [128, T])
                rt = AR.get("rt", [128, T]) if sample else (None, None)
                rows = {}
                for nm in ["ig", "l1", "cs", "aa", "A", "negm", "prev", "lastr"]:
                    rows[nm] = AR.get("r_" + nm, [4, T], parts=4)
                NEGAX, kNX = AR.get("NEGAX", [4, nseq + T], parts=4)
                MIN, kMIN = AR.get("MIN", [4, 16], parts=4)
                nset = 1 if sample else 2
                COLSb = [AR.get("COLS%d" % z, [128, 20]) for z in range(2)]
                B = []
                for z in range(nset):
                    d_ = {}
                    d_["SA"] = AR.get("SA%d" % z, [128, nseq + 128])
                    d_["WT"] = AR.get("WT%d" % z, [128, 128], BF16)
                    d_["PT"] = AR.get("PT%d" % z, [128, 128], BF16)
                    d_["PVs"] = AR.get("PVs%d" % z, [128, 129])
                    d_["NUM"] = AR.get("NUM%d" % z, [128, 129])
                    d_["JK"] = AR.get("JK%d" % z, [128, 128], BF16)
                    d_["HN"] = AR.get("HN%d" % z, [128, 128], BF16)
                    d_["VS"] = AR.get("VS%d" % z, [128, 129], BF16)
                    d_["SM"] = AR.get("SM%d" % z, [128, 16])
                    d_["WSI"] = AR.get("WSI%d" % z, [128, 16])
                    d_["WCB"] = AR.get("WCB%d" % z, [128, 16])
                    d_["QPAD"] = AR.get("QPAD%d" % z, [128, nseq * 136])
                    B.append(d_)
                if sample:
                    CS_, kCS = AR.get("CSs", [128, 64, 129])
                    S.dma('sync', CS_[:, :, 0:128], I["stC"][i].rearrange("s h k v -> k (s h) v"), w=[kCS])
                    rows_to_fm(lambda o, n: I["stn"][i, :, o:o + n], 64, lambda c: CS_[:, :, 128], 1, [kCS])
                    S.dma('sync', MIN, I["stmT"][:, i, :], w=[kMIN])
                    csv = lambda s_, h: CS_[:, s_ * 4 + h, :]
                    pcar, kpc = AR.get("pcar", [128, 4, 32])
                    rows_to_fm(lambda o, n: I["stsc"][i, :, o:o + n], 32, lambda c: pcar[:, c, :], 4, [kpc])
                    pcv = lambda ch: pcar[:, ch, :].rearrange("p (s j) -> p s j", j=2)
                else:
                    kCS = ('Cst', i)
                    csv = lambda s_, h: Cst[:, i, h, :]
                    kMIN = ('MROW',)
                    kpc = ('scc',)
                    pcv = lambda ch: scc[:, i, ch, :].unsqueeze(1)
                S.op(V, lambda e: e.memset(VT, 1.0), w=[kvt])
                for z in range(nset):
                    S.op(V, lambda e, z=z: e.memset(B[z]["QPAD"][0], 0.0), w=[B[z]["QPAD"][1]])

                def proj_fm(c0, handler, tag):
                    b = proj_fm.n % 2
                    proj_fm.n += 1
                    for kc in range(8):
                        S.op(PE_, lambda e, b=b, kc=kc: e.matmul(PS[b][:, 0:T], win[:, kc, c0:c0 + 128], hT[:, kc, 0:T], start=(kc == 0), stop=(kc == 7)),
                             r=kW + [kH], w=[PK[b]], inc=(kc == 7))
                    handler(PS[b][:, 0:T], PK[b])
                proj_fm.n = 0
                for h in range(4):
                    proj_fm(h * 128, lambda p, k, h=h: S.op(A_, lambda e: e.copy(out=qT[:, h, :], in_=p), r=[k], w=[kqT]), "q")
                    proj_fm(512 + h * 128, lambda p, k, h=h: S.op(A_, lambda e: e.activation(out=kTt[:, h, :], in_=p, func=AF.Copy, scale=128.0 ** -0.5), r=[k], w=[kkT]), "k")
                    proj_fm(1536 + h * 128, lambda p, k, h=h: S.op(A_, lambda e: e.activation(out=sgT[:, h, :], in_=p, func=AF.Sigmoid), r=[k], w=[ksg]), "o")
                pb3 = PB.rearrange("p (s k) -> p s k", k=LT + 2)
                for ch in range(4):
                    proj_fm(3080 + ch * 128, lambda p, k: S.op(A_, lambda e: e.copy(out=ZX, in_=p), r=[k], w=[kZX]), "zx")
                    S.op(A_, lambda e, ch=ch: e.copy(out=pb3[:, :, 0:2], in_=pcv(ch)), r=[kpc], w=[kPB])
                    proj_fm(2568 + ch * 128, lambda p, k: S.op(V, lambda e: e.tensor_tensor(out=pb3[:, :, 2:LT + 2], in0=v3(p), in1=v3(ZX), op=ALU.mult), r=[k, kZX], w=[kPB]), "zc")
                    S.op(A_, lambda e, ch=ch: e.copy(out=pcv(ch), in_=pb3[:, :, LT:LT + 2]), r=[kPB], w=[kpc])
                    conv3(v3(U), pb3, lambda t, ch=ch: acw[:, i, t, ch:ch + 1], [kPB, kC], [kU])
                    proj_fm(2056 + ch * 128, lambda p, k, ch=ch: S.op(V, lambda e: e.tensor_tensor(out=scTt[:, ch, :], in0=p, in1=U, op=ALU.mult), r=[k, kU], w=[ksc]), "zb")
                if last:
                    if sample:
                        fm_rows_out(lambda c: pcar[:, c, :], 32, lambda o, n: O["ssc"][i, :, o:o + n], 4, [kpc])
                    else:
                        fm_rows_out(lambda c: scc[:, i, c, :], 2, lambda o, n: O["psc"][i, :, o:o + n], 4, [kpc])
                rg = lambda nm: rows[nm][0]
                kg = lambda nm: rows[nm][1]
                for gi, (c0, nm) in enumerate([(2048, "ig"), (2052, "l1")]):
                    for kc in range(8):
                        S.op(PE_, lambda e, kc=kc, c0=c0: e.matmul(PS[2][0:4, 0:T], win[:, kc, c0:c0 + 4], hT[:, kc, 0:T], start=(kc == 0), stop=(kc == 7)),
                             r=kW + [kH], w=[PK[2]], inc=(kc == 7))
                    if gi == 0:
                        S.op(A_, lambda e: e.activation(out=rg("ig"), in_=PS[2][0:4, 0:T], func=AF.Identity, bias=bif[:, i, 0:1], scale=1.0), r=[PK[2], kC], w=[kg("ig")])
                    else:
                        S.op(A_, lambda e: e.activation(out=rg("l1"), in_=PS[2][0:4, 0:T], func=AF.Exp, bias=nbif[:, i, 1:2], scale=-1.0), r=[PK[2], kC2], w=[kg("l1")])
                        S.op(A_, lambda e: e.activation(out=rg("l1"), in_=rg("l1"), func=AF.Ln, bias=1.0, scale=1.0), r=[kg("l1")], w=[kg("l1")])
                r3 = lambda ap: ap.rearrange("p (s l) -> p s l", l=LT)
                for s_ in range(nseq):
                    sl = slice(s_ * LT, (s_ + 1) * LT)
                    S.op(V, lambda e, sl=sl: e.tensor_tensor_scan(out=rg("cs")[:, sl], data0=onesrow[:, 0:LT], data1=rg("l1")[:, sl], initial=0.0, op0=ALU.mult, op1=ALU.add),
                         r=[kg("l1"), kC2], w=[kg("cs")])
                S.op(V, lambda e: e.tensor_tensor(out=rg("aa"), in0=rg("ig"), in1=rg("cs"), op=ALU.add), r=[kg("ig"), kg("cs")], w=[kg("aa")])
                minap = (lambda s_: MIN[:, s_:s_ + 1]) if sample else (lambda s_: MROW[:, i:i + 1])
                for s_ in range(nseq):
                    sl = slice(s_ * LT, (s_ + 1) * LT)
                    S.op(V, lambda e, sl=sl, s_=s_: e.tensor_tensor_scan(out=rg("A")[:, sl], data0=rg("aa")[:, sl], data1=rg("aa")[:, sl], initial=minap(s_), op0=ALU.max, op1=ALU.max),
                         r=[kg("aa"), kMIN], w=[kg("A")])
                minall = MIN[:, 0:16] if sample else MROW[:, i:i + 1]
                S.op(V, lambda e: e.tensor_scalar(out=NEGAX[:, 0:nseq], in0=minall, scalar1=-1.0, scalar2=None, op0=ALU.mult), r=[kMIN], w=[kNX])
                S.op(V, lambda e: e.tensor_scalar(out=NEGAX[:, nseq:nseq + T], in0=rg("A"), scalar1=-1.0, scalar2=None, op0=ALU.mult), r=[kg("A")], w=[kNX])
                S.op(V, lambda e: e.tensor_tensor(out=rg("negm"), in0=rg("cs"), in1=rg("A"), op=ALU.subtract), r=[kg("cs"), kg("A")], w=[kg("negm")])
                if sample:
                    S.op(V, lambda e: e.tensor_copy(out=r3(rg("prev")), in_=MIN[:, 0:16].unsqueeze(2).broadcast_to([4, 16, 8])), r=[kMIN], w=[kg("prev")])
                    S.op(V, lambda e: e.tensor_copy(out=r3(rg("lastr")), in_=r3(NEGAX[:, 16:16 + T])[:, :, 7:8].broadcast_to([4, 16, 8])), r=[kNX], w=[kg("lastr")])
                else:
                    c3 = lambda ap: ap.rearrange("p (c l) -> p c l", l=128)
                    S.op(V, lambda e: e.tensor_scalar(out=c3(rg("prev")), in0=c3(NEGAX[:, 0:T])[:, :, 0:1].broadcast_to([4, nch, 128]), scalar1=-1.0, scalar2=None, op0=ALU.mult), r=[kNX], w=[kg("prev")])
                    S.op(V, lambda e: e.tensor_copy(out=c3(rg("lastr")), in_=c3(NEGAX[:, 1:T + 1])[:, :, 127:128].broadcast_to([4, nch, 128])), r=[kNX], w=[kg("lastr")])
                if sample:
                    if last:
                        MOUT, kMO = AR.get("MOUT", [4, 16], parts=4)
                        S.op(V, lambda e: e.tensor_scalar(out=MOUT, in0=r3(rg("negm"))[:, :, 7], scalar1=-1.0, scalar2=None, op0=ALU.mult), r=[kg("negm")], w=[kMO])
                        S.dma('sync', O["smT"][:, i, :], MOUT, r=[kMO], chan=('osm',))
                else:
                    S.op(V, lambda e: e.tensor_scalar(out=MROW[:, i:i + 1], in0=rg("negm")[:, T - 1:T], scalar1=-1.0, scalar2=None, op0=ALU.mult), r=[kg("negm")], w=[kMIN])
                    if last:
                        S.dma('sync', O["pm"][i].unsqueeze(1), MROW[:, i:i + 1], r=[kMIN], chan=('opm',))
                for c in range(nch):
                    cs_ = slice(c * 128, (c + 1) * 128)
                    for part, c0 in [(0, 512), (1, 1024)]:
                        b = 4 + part
                        for kc in range(8):
                            S.op(PE_, lambda e, kc=kc, b=b, c0=c0, cs_=cs_: e.matmul(PS[b][:, 0:512], hT[:, kc, cs_], win[:, kc, c0:c0 + 512], start=(kc == 0), stop=(kc == 7)),
                                 r=kW + [kH], w=[PK[b]], inc=(kc == 7))
                    S.op(A_, lambda e, c=c: e.activation(out=KTOK[:, c, :], in_=PS[4][:, 0:512], func=AF.Copy, scale=128.0 ** -0.5), r=[PK[4]], w=[kkt])
                    S.op(A_, lambda e, c=c: e.copy(out=VT[:, c * 4:(c + 1) * 4, 0:128], in_=PS[5][:, 0:512].rearrange("p (h v) -> p h v", v=128)), r=[PK[5]], w=[kvt])
                maskb = mSb if sample else mPb
                kCSh = lambda h: (kCS, h)
                S.barrier()
                for c in range(nch):
                    cs_ = slice(c * 128, (c + 1) * 128)
                    COLS, kCOLS = COLSb[c % 2]
                    allp3 = [PK[6]]
                    for bi, nm in enumerate(["aa", "A", "negm", "prev", "lastr"]):
                        src = NEGAX[:, nseq + c * 128:nseq + (c + 1) * 128] if nm == "A" else rg(nm)[:, cs_]
                        S.op(PE_, lambda e, bi=bi, src=src: e.transpose(out=PS[6][:, 256 + bi * 4:256 + (bi + 1) * 4], in_=src, identity=ident[0:4, 0:4]),
                             r=[kg(nm) if nm != "A" else kNX, kC], w=allp3)
                    S.op(V, lambda e, COLS=COLS: e.tensor_copy(out=COLS, in_=PS[6][:, 256:276]), r=allp3, w=[kCOLS])

                    def stage_fns(h, z, c=c, cs_=cs_, COLS=COLS, kCOLS=kCOLS):
                        bz = B[z]
                        SA, kSA = bz["SA"]; WT, kWT = bz["WT"]; PT, kPT = bz["PT"]; PVs, kPVs = bz["PVs"]
                        NUM, kNUM = bz["NUM"]; JK, kJK = bz["JK"]; HN, kHN = bz["HN"]; VS, kVS = bz["VS"]
                        SM, kSM = bz["SM"]; WSI, kWSI = bz["WSI"]; WCB, kWCB = bz["WCB"]; QPAD, kQP = bz["QPAD"]
                        bA, bB, bC, bD = z, z + 2, z + 4, z + 6
                        ca = COLS[:, 0 + h:1 + h]; cnA = COLS[:, 4 + h:5 + h]; cnm = COLS[:, 8 + h:9 + h]
                        cpv = COLS[:, 12 + h:13 + h]; cla = COLS[:, 16 + h:17 + h]
                        qpd = QPAD.rearrange("p (s k) -> p s k", k=136)[:, :, 0:L]
                        fns = []

                        def s1():
                            if sample:
                                S.op(PE_, lambda e: e.matmul(PS[bA][:, 0:16], sel4[:, h, :], NEGAX[:, 0:16], start=True, stop=False, skip_group_check=True), r=[kNX, kC], w=[PK[bA]], inc=False)
                                S.op(PE_, lambda e: e.matmul(PS[bA][:, 16:144], sel4[:, h, :], NEGAX[:, 16:144], start=False, stop=True, skip_group_check=True), r=[kNX, kC], w=[PK[bA]])
                            else:
                                S.op(PE_, lambda e: e.matmul(PS[bA][:, 0:129], sel4[:, h, :], NEGAX[:, c * 128:c * 128 + 129], start=True, stop=True), r=[kNX, kC], w=[PK[bA]])
                            S.op(PE_, lambda e: e.matmul(PS[bB][:, 0:128], sel4[:, h, :], NEGAX[:, nseq + c * 128:nseq + (c + 1) * 128], start=True, stop=False), r=[kNX, kC, kCOLS], w=[PK[bB]], inc=False)
                            S.op(PE_, lambda e: e.matmul(PS[bB][:, 0:128], identb[:, :], maskb[:, :], start=False, stop=True), r=[kC], w=[PK[bB]])
                            S.op(PE_, lambda e: e.matmul(PS[bC][:, 0:128], kTt[:, h, cs_], qT[:, h, cs_], start=True, stop=True), r=[kkT, kqT], w=[PK[bC]])
                        fns.append(s1)

                        def s2():
                            S.op(A_, lambda e: e.copy(out=SA, in_=PS[bA][:, 0:nseq + 128]), r=[PK[bA]], w=[kSA])
                            S.op(A_, lambda e: e.activation(out=WT, in_=PS[bB][:, 0:128], func=AF.Exp, bias=ca, scale=1.0), r=[PK[bB], kCOLS], w=[kWT])
                            S.op(A_, lambda e: e.activation(out=SM[:, 0:1], in_=cnA, func=AF.Exp, bias=cpv, scale=1.0), r=[kCOLS], w=[(kSM, 0)])
                            S.op(A_, lambda e: e.activation(out=SM[:, 1:2], in_=ca, func=AF.Exp, bias=cla, scale=1.0), r=[kCOLS], w=[(kSM, 1)])
                            S.op(A_, lambda e: e.activation(out=SM[:, 2:3], in_=cnm, func=AF.Exp), r=[kCOLS], w=[(kSM, 2)])
                        fns.append(s2)

                        def s3():
                            S.op(V, lambda e: e.tensor_tensor(out=PT, in0=PS[bC][:, 0:128], in1=WT, op=ALU.mult), r=[PK[bC], kWT], w=[kPT])
                            S.op(V, lambda e: e.tensor_copy(out=qpd, in_=qT[:, h, cs_].rearrange("p (s l) -> p s l", l=L)), r=[kqT], w=[kQP])
                            S.op(V, lambda e: e.tensor_scalar(out=WSI[:, 0:nseq], in0=onehot[:, 0:nseq] if sample else onesb[:, 0:1], scalar1=SM[:, 1:2], scalar2=None, op0=ALU.mult), r=[(kSM, 1), kC, kC2], w=[kWSI])
                            sa_last = SA[:, nseq:nseq + 128].rearrange("p (s l) -> p s l", l=L)[:, :, L - 1]
                            S.op(V, lambda e: e.tensor_tensor(out=WCB[:, 0:nseq], in0=sa_last, in1=SA[:, 0:nseq], op=ALU.subtract), r=[kSA], w=[kWCB])
                        fns.append(s3)

                        def s4():
                            S.op(PE_, lambda e: e.matmul(PS[bB][:, 256:385], PT, VT[:, c * 4 + h, :], start=True, stop=True), r=[kPT, kvt], w=[PK[bB]])
                            for s_ in range(nseq):
                                S.op(PE_, lambda e, s_=s_: e.matmul(PS[bA][:, 256:385], QPAD[:, s_ * 128:(s_ + 1) * 128], csv(s_, h), start=(s_ == 0), stop=(s_ == nseq - 1)),
                                     r=[kQP, kCSh(h)], w=[PK[bA]], inc=(s_ == nseq - 1))
                        fns.append(s4)

                        def s5():
                            S.op(A_, lambda e: e.copy(out=PVs, in_=PS[bB][:, 256:385]), r=[PK[bB]], w=[kPVs])
                            S.op(A_, lambda e: e.activation(out=WCB[:, 0:nseq], in_=WCB[:, 0:nseq], func=AF.Exp), r=[kWCB], w=[kWCB])
                        fns.append(s5)

                        def s6():
                            S.op(V, lambda e: e.scalar_tensor_tensor(out=NUM, in0=PS[bA][:, 256:385], scalar=SM[:, 0:1], in1=PVs, op0=ALU.mult, op1=ALU.add), r=[PK[bA], (kSM, 0), kPVs], w=[kNUM])
                        fns.append(s6)

                        def s7():
                            S.op(A_, lambda e: e.activation(out=SM[:, 3:4], in_=NUM[:, 128:129], func=AF.Abs), r=[kNUM], w=[(kSM, 3)])
                        fns.append(s7)

                        def s8():
                            S.op(V, lambda e: e.tensor_tensor(out=SM[:, 3:4], in0=SM[:, 3:4], in1=SM[:, 2:3], op=ALU.max), r=[(kSM, 3), (kSM, 2)], w=[(kSM, 3)])
                            S.op(V, lambda e: e.reciprocal(out=SM[:, 4:5], in_=SM[:, 3:4]), r=[(kSM, 3)], w=[(kSM, 4)])
                        fns.append(s8)

                        def s9():
                            S.op(A_, lambda e: e.activation(out=JK, in_=NUM[:, 0:128], func=AF.Square, scale=SM[:, 4:5], accum_out=SM[:, 5:6]), r=[kNUM, (kSM, 4)], w=[kJK, (kSM, 5)])
                            S.op(A_, lambda e: e.activation(out=SM[:, 6:7], in_=SM[:, 5:6], func=AF.Sqrt, bias=EPS, scale=1.0 / 128), r=[(kSM, 5)], w=[(kSM, 6)])
                        fns.append(s9)

                        def s10():
                            S.op(V, lambda e: e.reciprocal(out=SM[:, 6:7], in_=SM[:, 6:7]), r=[(kSM, 6)], w=[(kSM, 6)])
                            S.op(V, lambda e: e.tensor_tensor(out=SM[:, 7:8], in0=SM[:, 6:7], in1=SM[:, 4:5], op=ALU.mult), r=[(kSM, 6), (kSM, 4)], w=[(kSM, 7)])
                            S.op(V, lambda e: e.tensor_scalar(out=HN, in0=NUM[:, 0:128], scalar1=SM[:, 7:8], scalar2=None, op0=ALU.mult), r=[kNUM, (kSM, 7)], w=[kHN])
                        fns.append(s10)

                        def s11():
                            S.op(PE_, lambda e: e.transpose(out=psb(bC)[:, 512:640], in_=HN, identity=identb[:, :]), r=[kHN, kC], w=[PK[bC]])
                        fns.append(s11)

                        def s12():
                            S.op(V, lambda e: e.scalar_tensor_tensor(out=hmT[:, h, cs_], in0=psb(bC)[:, 512:640], scalar=aon[:, i, h:h + 1], in1=sgT[:, h, cs_], op0=ALU.mult, op1=ALU.mult),
                                 r=[PK[bC], kC, ksg], w=[(khm, h)])
                        fns.append(s12)

                        def s13():
                            for s_ in range(nseq):
                                S.op(V, lambda e, s_=s_: e.tensor_scalar(out=VS, in0=VT[:, c * 4 + h, :], scalar1=WSI[:, s_:s_ + 1], scalar2=None, op0=ALU.mult), r=[kvt, kWSI], w=[kVS])
                                S.op(PE_, lambda e: e.matmul(PS[bD][:, 0:129], KTOK[:, c, h * 128:(h + 1) * 128], VS, start=True, stop=True), r=[kkt, kVS], w=[PK[bD]])
                                S.op(V, lambda e, s_=s_: e.scalar_tensor_tensor(out=csv(s_, h), in0=csv(s_, h), scalar=WCB[:, s_:s_ + 1], in1=PS[bD][:, 0:129], op0=ALU.mult, op1=ALU.add),
                                     r=[PK[bD], kWCB, kCSh(h)], w=[kCSh(h)])
                        fns.append(s13)
                        return fns

                    groups = [[0], [1], [2], [3]] if nset == 1 else [[0, 1], [2, 3]]
                    for grp in groups:
                        fl = [stage_fns(h, z) for z, h in enumerate(grp)]
                        for si in range(len(fl[0])):
                            for f_ in fl:
                                f_[si]()
                S.barrier()
                khm_all = [(khm, h) for h in range(4)]
                kCS_all = [kCSh(h) for h in range(4)]
                if last:
                    if sample:
                        S.dma('sync', O["sC"][i].rearrange("s h k v -> k (s h) v"), CS_[:, :, 0:128], r=kCS_all, chan=('osC',))
                        fm_rows_out(lambda c: CS_[:, :, 128], 64, lambda o, n: O["sn"][i, :, o:o + n], 1, kCS_all)
                    else:
                        S.dma('sync', O["pC"][i].rearrange("h k v -> k h v"), Cst[:, i, :, 0:128], r=kCS_all, chan=('opC',))
                        fm_rows_out(lambda c: Cst[:, i, :, 128], 4, lambda o, n: O["pn"][i, :, o:o + n], 1, kCS_all)
                for d in range(8):
                    b = d % 2
                    for j in range(8):
                        rhs = hmT[:, j, :] if j < 4 else scTt[:, j - 4, :]
                        S.op(PE_, lambda e, b=b, j=j, d=d, rhs=rhs: e.matmul(PS[b][:, 0:T], wout[:, j, d * 128:(d + 1) * 128], rhs, start=(j == 0), stop=(j == 7)),
                             r=kW + khm_all + [ksc], w=[PK[b]], inc=(j == 7))
                    resid_add(l, 2, d, PS[b][:, 0:T], PK[b], rt)

            def mixer_c(l):
                jl = l // 2
                wq = WB[:, 0:8 * 1536].rearrange("p (k c) -> p k c", c=1536)
                wo = WB[:, 8 * 1536:8 * 1536 + 8 * D].rearrange("p (k c) -> p k c", c=D)
                kW = [('W', 0), ('W', 1), ('W', 2)]
                S.dma('pool', WB[:, 0:8 * 1536], I["c_w_qkv"][jl], w=kW, chan=('W', 0), max_dma_last_dim=8192)
                S.dma('pool', WB[:, 8 * 1536:8 * 1536 + 8 * D], I["c_w_out"][jl], w=kW, chan=('W', 0), max_dma_last_dim=8192)
                norm_mod(l, 1)
                S.barrier(); AR.reset()
                QK, kQK = AR.get("QK", [128, 1280])
                SQ, kSQ = AR.get("SQ", [128, 1280])
                QR, kQR = AR.get("QR", [128, 1280])
                QRb, kQRb = AR.get("QRb", [128, 1280], BF16)
                T1, kT1 = AR.get("T1", [128, 640])
                T2, kT2 = AR.get("T2", [128, 640])
                VV, kVV = AR.get("VV", [128, 256])
                SS, kSS = AR.get("SS", [128, 20])
                QT, kQT = AR.get("QTz", [128, 16, T], BF16)
                S.op(V, lambda e: e.memset(QT, 0.0), w=[kQT])
                QT5 = QT.rearrange("p (a b g) t -> p a b g t", a=2, b=2)
                KT, kKT = AR.get("KT", [128, 2, T], BF16)
                VTa, kVTa = AR.get("VTa", [128, nch * 4, 65], BF16)
                PTb = [AR.get("PTb%d" % z, [128, 512], BF16) for z in range(2)]
                DEN, kDEN = AR.get("DEN", [128, 4])
                OTOK, kOT = AR.get("OTOK", [128, 16, 64], BF16)
                OTT, kOTT = AR.get("OTT", [128, 8, T], BF16)
                rt = AR.get("rt", [128, T])
                if sample:
                    CKb, kCKb = AR.get("CKb", [128, 16, 256], BF16)
                    KCT, kKCT = AR.get("KCT", [128, 32, 128], BF16)
                    VCs, kVCs = AR.get("VCs", [128, 64, 65], BF16)
                    S.op(V, lambda e: e.memset(VCs, 1.0), w=[kVCs])
                    S.dma('pool', CKb, I["ck"][jl].rearrange("s p c -> p s c"), w=[kCKb], max_dma_last_dim=1024)
                    for s_ in range(16):
                        S.dma('pool', VCs[:, s_ * 4:(s_ + 1) * 4, 0:64], I["cv"][jl, s_].rearrange("p (h d) -> p h d", d=64), w=[kVCs], max_dma_last_dim=256)
                    for s_ in range(16):
                        for j in range(2):
                            S.op(PE_, lambda e, s_=s_, j=j: e.transpose(out=psb(3)[:, j * 128:(j + 1) * 128], in_=CKb[:, s_, j * 128:(j + 1) * 128], identity=identb[:, :]), r=[kCKb, kC], w=[PK[3]])
                        S.op(V, lambda e, s_=s_: e.tensor_copy(out=KCT[:, 2 * s_:2 * s_ + 2, :], in_=psb(3)[:, 0:256].rearrange("p (a b) -> p a b", b=128)), r=[PK[3]], w=[kKCT])
                    S.dma('sync', O["swk"][jl, :, 0:120, :], I["ck"][jl, :, 8:128, :], chan=('dd',))
                    S.dma('sync', O["swv"][jl, :, 0:120, :], I["cv"][jl, :, 8:128, :], chan=('dd',))
                S.op(V, lambda e: e.memset(VTa, 1.0), w=[kVTa])
                if int(os.environ.get('KD_C', 9)) < 1:
                    return
                qk3 = lambda ap: ap.rearrange("p (h d) -> p h d", d=64)
                for c in range(nch):
                    cs_ = slice(c * 128, (c + 1) * 128)
                    for b in range(3):
                        for kc in range(8):
                            S.op(PE_, lambda e, b=b, kc=kc, cs_=cs_: e.matmul(PS[b][:, 0:512], hT[:, kc, cs_], wq[:, kc, b * 512:(b + 1) * 512], start=(kc == 0), stop=(kc == 7)),
                                 r=kW + [kH], w=[PK[b]], inc=(kc == 7))
                    S.op(A_, lambda e: e.copy(out=QK[:, 0:512], in_=PS[0][:, 0:512]), r=[PK[0]], w=[kQK])
                    S.op(A_, lambda e: e.copy(out=QK[:, 512:1024], in_=PS[1][:, 0:512]), r=[PK[1]], w=[kQK])
                    S.op(A_, lambda e: e.copy(out=QK[:, 1024:1280], in_=PS[2][:, 0:256]), r=[PK[2]], w=[kQK])
                    if os.environ.get('KD_X', '') == 'a':
                        continue
                    S.op(A_, lambda e: e.copy(out=VV, in_=PS[2][:, 256:512]), r=[PK[2]], w=[kVV])
                    if os.environ.get('KD_X', '') == 'b':
                        continue
                    S.op(V, lambda e, c=c: e.tensor_copy(out=VTa[:, c * 4:(c + 1) * 4, 0:64], in_=VV.rearrange("p (h d) -> p h d", d=64)), r=[kVV], w=[kVTa])
                    if int(os.environ.get('KD_C', 9)) < 2:
                        continue
                    S.op(A_, lambda e: e.activation(out=SQ, in_=QK, func=AF.Square), r=[kQK], w=[kSQ])
                    S.op(V, lambda e: e.tensor_reduce(out=SS, in_=qk3(SQ), axis=AX.X, op=ALU.add), r=[kSQ], w=[kSS])
                    S.op(A_, lambda e: e.activation(out=SS, in_=SS, func=AF.Sqrt, bias=EPS, scale=1.0 / 64), r=[kSS], w=[kSS])
                    S.op(V, lambda e: e.reciprocal(out=SS, in_=SS), r=[kSS], w=[kSS])
                    S.op(V, lambda e: e.tensor_tensor(out=qk3(QK), in0=qk3(QK), in1=SS.unsqueeze(2).broadcast_to([128, 20, 64]), op=ALU.mult), r=[kSS, kQK], w=[kQK])
                    S.op(V, lambda e: e.tensor_tensor(out=qk3(QK[:, 0:1024]), in0=qk3(QK[:, 0:1024]), in1=GQ[:, jl, :].unsqueeze(1).broadcast_to([128, 16, 64]), op=ALU.mult), r=[kC, kQK], w=[kQK])
                    S.op(V, lambda e: e.tensor_tensor(out=qk3(QK[:, 1024:1280]), in0=qk3(QK[:, 1024:1280]), in1=GK[:, jl, :].unsqueeze(1).broadcast_to([128, 4, 64]), op=ALU.mult), r=[kC, kQK], w=[kQK])
                    if int(os.environ.get('KD_C', 9)) < 3:
                        continue
                    cosb = cosT[:, c, :].unsqueeze(1).broadcast_to([128, 20, 32])
                    sinb = sinT[:, c, :].unsqueeze(1).broadcast_to([128, 20, 32])
                    x1 = qk3(QK)[:, :, 0:32]; x2 = qk3(QK)[:, :, 32:64]
                    t1 = T1.rearrange("p (h d) -> p h d", d=32); t2 = T2.rearrange("p (h d) -> p h d", d=32)
                    S.op(P_, lambda e, x1=x1, cosb=cosb: e.tensor_tensor(out=t1, in0=x1, in1=cosb, op=ALU.mult), r=[kQK, ('rope',)], w=[kT1])
                    S.op(P_, lambda e, x2=x2, sinb=sinb: e.tensor_tensor(out=t2, in0=x2, in1=sinb, op=ALU.mult), r=[kQK, ('rope',)], w=[kT2])
                    S.op(V, lambda e: e.tensor_tensor(out=qk3(QR)[:, :, 0:32], in0=t1, in1=t2, op=ALU.subtract), r=[kT1, kT2], w=[kQR])
                    S.op(P_, lambda e, x2=x2, cosb=cosb: e.tensor_tensor(out=t1, in0=x2, in1=cosb, op=ALU.mult), r=[kQK, ('rope',)], w=[kT1])
                    S.op(P_, lambda e, x1=x1, sinb=sinb: e.tensor_tensor(out=t2, in0=x1, in1=sinb, op=ALU.mult), r=[kQK, ('rope',)], w=[kT2])
                    S.op(V, lambda e: e.tensor_tensor(out=qk3(QR)[:, :, 32:64], in0=t1, in1=t2, op=ALU.add), r=[kT1, kT2], w=[kQR])
                    S.op(A_, lambda e: e.copy(out=QRb, in_=QR), r=[kQR], w=[kQRb])
                    if int(os.environ.get('KD_C', 9)) < 4:
                        continue
                    for j in range(8):
                        S.op(PE_, lambda e, j=j: e.transpose(out=psb(3)[:, j * 128:(j + 1) * 128], in_=QRb[:, j * 128:(j + 1) * 128], identity=identb[:, :]), r=[kQRb, kC], w=[PK[3]])
                    S.op(V, lambda e, cs_=cs_: e.tensor_copy(out=QT5[0:64, :, 0, :, cs_], in_=psb(3)[0:64, 0:1024].rearrange("p (a g t) -> p a g t", a=2, g=4)), r=[PK[3]], w=[kQT])
                    S.op(V, lambda e, cs_=cs_: e.tensor_copy(out=QT5[64:128, :, 1, :, cs_], in_=psb(3)[64:128, 0:1024].rearrange("p (a g t) -> p a g t", a=2, g=4)), r=[PK[3]], w=[kQT])
                    for j in range(2):
                        S.op(PE_, lambda e, j=j: e.transpose(out=psb(4)[:, j * 128:(j + 1) * 128], in_=QRb[:, 1024 + j * 128:1024 + (j + 1) * 128], identity=identb[:, :]), r=[kQRb, kC], w=[PK[4]])
                    S.op(V, lambda e, cs_=cs_: e.tensor_copy(out=KT[:, :, cs_], in_=psb(4)[:, 0:256].rearrange("p (a b) -> p a b", b=128)), r=[PK[4]], w=[kKT])
                    if sample:
                        for s_ in range(16):
                            S.dma('sync', O["swk"][jl, s_, 120:128, :], QR[s_ * 8:(s_ + 1) * 8, 1024:1280], r=[kQR], chan=('oswk',))
                            S.dma('sync', O["swv"][jl, s_, 120:128, :], VV[s_ * 8:(s_ + 1) * 8, :], r=[kVV], chan=('oswv',))
                    elif last and c == nch - 1:
                        S.dma('sync', O["pwk"][jl], QR[:, 1024:1280], r=[kQR], chan=('opwk',))
                        S.dma('sync', O["pwv"][jl], VV, r=[kVV], chan=('opwv',))
                    if int(os.environ.get('KD_C', 9)) < 5:
                        continue
                    for kap in range(4):
                        base = 64 * (kap % 2)
                        pr = kap // 2
                        blocks = []
                        if sample:
                            for s_ in range(16):
                                blocks.append((KCT[:, 2 * s_ + pr, :], VCs[:, s_ * 4 + kap, :],
                                               acacheb[:, s_, :].unsqueeze(1).broadcast_to([128, 4, 128]), [kKCT], [kVCs]))
                            blocks.append((KT[:, pr, cs_], VTa[:, c * 4 + kap, :], anewb[:, :].rearrange("p (g t) -> p g t", t=128), [kKT], [kVTa]))
                        else:
                            blocks.append((KT[:, pr, cs_], VTa[:, c * 4 + kap, :], acurb[:, :].rearrange("p (g t) -> p g t", t=128), [kKT], [kVTa]))
                            if c > 0:
                                ps_ = slice((c - 1) * 128, c * 128)
                                blocks.append((KT[:, pr, ps_], VTa[:, (c - 1) * 4 + kap, :], aprevb[:, :].rearrange("p (g t) -> p g t", t=128), [kKT], [kVTa]))
                            elif pi > 0:
                                blocks.append((KTC[:, jl, pr, :], VC[:, jl, kap, :], aprevb[:, :].rearrange("p (g t) -> p g t", t=128), [('KTC',)], [('VC',)]))
                        qrhs = QT[:, 4 * kap:4 * kap + 4, cs_]
                        for bi, (kb, vb, mb, kr, vr) in enumerate(blocks):
                            pt, kpt = PTb[bi % 2]
                            ps5 = PS[5][:, 0:512].rearrange("p (g t) -> p g t", t=128)
                            S.op(PE_, lambda e, kb=kb, qrhs=qrhs, ps5=ps5: e.matmul(ps5, kb, qrhs, start=True, stop=False), r=kr + [kQT], w=[PK[5]], inc=False)
                            S.op(PE_, lambda e, mb=mb, ps5=ps5: e.matmul(ps5, identb[:, :], mb, start=False, stop=True), r=[kC], w=[PK[5]])
                            S.op(A_, lambda e, pt=pt: e.activation(out=pt, in_=PS[5][:, 0:512], func=AF.Exp, bias=NEGMA[:, jl:jl + 1], scale=0.125), r=[PK[5], kC2], w=[kpt])
                            for g in range(4):
                                S.op(PE_, lambda e, g=g, pt=pt, vb=vb, bi=bi, nb=len(blocks): e.matmul(PS[6][:, g * 65:(g + 1) * 65], pt[:, g * 128:(g + 1) * 128], vb,
                                                                                    start=(bi == 0 and g == 0), stop=(bi == nb - 1), skip_group_check=True),
                                     r=[kpt] + vr, w=[PK[6]], inc=(g == 3))
                        s0 = 8 * pr + (kap % 2)
                        o3 = PS[6][:, 0:260].rearrange("p (g d) -> p g d", d=65)
                        S.op(V, lambda e, s0=s0, o3=o3: e.tensor_tensor(out=DEN, in0=o3[:, :, 64], in1=SINKE[:, jl, s0:s0 + 7:2], op=ALU.add), r=[PK[6], ('SINKE',)], w=[kDEN])
                        S.op(V, lambda e: e.reciprocal(out=DEN, in_=DEN), r=[kDEN], w=[kDEN])
                        S.op(V, lambda e, s0=s0, o3=o3: e.tensor_tensor(out=OTOK[:, s0:s0 + 7:2, :], in0=o3[:, :, 0:64], in1=DEN.unsqueeze(2).broadcast_to([128, 4, 64]), op=ALU.mult),
                             r=[PK[6], kDEN], w=[kOT])
                    if int(os.environ.get('KD_C', 9)) < 6:
                        continue
                    otf = OTOK.rearrange("p h d -> p (h d)")
                    for j in range(8):
                        S.op(PE_, lambda e, j=j: e.transpose(out=psb(7)[:, j * 128:(j + 1) * 128], in_=otf[:, j * 128:(j + 1) * 128], identity=identb[:, :]), r=[kOT, kC], w=[PK[7]])
                    S.op(V, lambda e, cs_=cs_: e.tensor_copy(out=OTT[:, :, cs_], in_=psb(7)[:, 0:1024].rearrange("p (a b) -> p a b", b=128)), r=[PK[7]], w=[kOTT])
                if int(os.environ.get('KD_C', 9)) < 7:
                    return
                if not sample:
                    ls_ = slice((nch - 1) * 128, nch * 128)
                    S.op(V, lambda e: e.tensor_copy(out=KTC[:, jl, :, :], in_=KT[:, :, ls_]), r=[kKT], w=[('KTC',)])
                    S.op(V, lambda e: e.tensor_copy(out=VC[:, jl, :, :], in_=VTa[:, (nch - 1) * 4:nch * 4, :]), r=[kVTa], w=[('VC',)])
                for d in range(8):
                    b = d % 2
                    for j in range(8):
                        S.op(PE_, lambda e, b=b, j=j, d=d: e.matmul(PS[b][:, 0:T], wo[:, j, d * 128:(d + 1) * 128], OTT[:, j, :], start=(j == 0), stop=(j == 7)),
                             r=kW + [kOTT], w=[PK[b]], inc=(j == 7))
                    resid_add(l, 2, d, PS[b][:, 0:T], PK[b], rt)

            for l in range(4):
                if str(l) not in os.environ.get("KD_LAYERS", "0123"):
                    continue
                if "m" in os.environ.get("KD_PARTS", "mf"):
                    if l % 2 == 0:
                        mixer_ab(l)
                    else:
                        mixer_c(l)
                if "f" in os.environ.get("KD_PARTS", "mf"):
                    ffn(l)
            S.barrier()
            for c in range(nch):
                for b in range(2):
                    for j in range(4):
                        kc = b * 4 + j
                        S.op(PE_, lambda e, b=b, j=j, kc=kc, c=c: e.transpose(out=PS[b][:, j * 128:(j + 1) * 128], in_=xT[:, kc, c * 128:(c + 1) * 128], identity=ident[:, :]),
                             r=[kX, kC], w=[PK[b]])
                    S.op(A_, lambda e, b=b: e.copy(out=stage[:, b * 512:(b + 1) * 512], in_=PS[b][:, :]), r=[PK[b]], w=[('stage',)])
                S.dma('sync', yout[tok0 + c * 128: tok0 + (c + 1) * 128, :], stage[:, :], r=[('stage',)], chan=('ostage',))

        for pi in range(int(os.environ.get("KD_NPASS", NPASS))):
            run_pass(False, pi)
        if os.environ.get("KD_SAMPLE", "1") == "1":
            run_pass(True, 0)
        S.finish()

        with nc.Block() as block:
            @block.sync
            def _(e):
                S.replay('sync', e)

            @block.tensor
            def _(e):
                S.replay('pe', e)

            @block.scalar
            def _(e):
                S.replay('act', e)

            @block.vector
            def _(e):
                S.replay('dve', e)

            @block.gpsimd
            def _(e):
                S.replay('pool', e)
    return nc


def _slot_perm():
    perm = np.zeros(16, np.int64)
    for kap in range(4):
        for g in range(4):
            s = 2 * (g + 4 * (kap // 2)) + (kap % 2)
            perm[s] = 4 * kap + g
    return perm


def _consts():
    c = {}
    c["ident"] = np.eye(128, dtype=np.float32)
    s = np.arange(128)[:, None]
    t = np.arange(128)[None, :]
    c["mP"] = np.where(s <= t, 0.0, NEG).astype(np.float32)
    same = (s // 8) == (t // 8)
    c["mS"] = np.where(same & (s <= t), 0.0, NEG).astype(np.float32)
    c["acur"] = np.tile(np.where(s <= t, 0.0, ANEG).astype(np.float32), (1, 4))
    c["aprev"] = np.tile(np.where(s > t, 0.0, ANEG).astype(np.float32), (1, 4))
    c["anew"] = np.tile(np.where(same & (s <= t), 0.0, ANEG).astype(np.float32), (1, 4))
    ac = np.full((128, 16, 128), ANEG, np.float32)
    for i in range(16):
        for j in range(8):
            tt = 8 * i + j
            ac[j + 1:, i, tt] = 0.0
    c["acache"] = ac
    oh = np.zeros((128, 16), np.float32)
    oh[np.arange(128), np.arange(128) // 8] = 1.0
    c["onehot"] = oh
    sel = np.zeros((4, 4, 128), np.float32)
    for h in range(4):
        sel[h, h, :] = 1.0
    c["sel4"] = sel
    inv = 10000.0 ** (-np.arange(32, dtype=np.float64) / 32)
    pos = np.arange(SEQ, dtype=np.float64)[:, None] * inv[None, :]
    c["cosP"] = np.cos(pos).astype(np.float32)
    c["sinP"] = np.sin(pos).astype(np.float32)
    ps = (8192 + (np.arange(128) % 8)).astype(np.float64)[:, None] * inv[None, :]
    c["cosS"] = np.cos(ps).astype(np.float32)
    c["sinS"] = np.sin(ps).astype(np.float32)
    return c


_NC = None


def kernel(x_prompt, x_sample, c_prompt, c_sample, state_mlstm_C, state_mlstm_n, state_mlstm_m,
           state_sconv, cache_win_k, cache_win_v, state_ffn_conv,
           norm1, norm2, w_ada, b_ada, a_w_in, a_b_if, a_out_norm, a_conv_w, a_w_out,
           c_w_qkv, c_q_norm, c_k_norm, c_sink, c_w_out, f_w_up, f_conv_w, f_w_down):
    global _NC
    f = lambda a: np.ascontiguousarray(np.asarray(a, dtype=np.float32))
    perm = _slot_perm()
    common = dict(_consts())
    common["w_ada"] = f(w_ada)
    common["b_adaT"] = f(np.asarray(b_ada).reshape(4, 48, 128).transpose(2, 0, 1))
    common["norm1T"] = f(np.asarray(norm1).reshape(4, 8, 128).transpose(2, 0, 1))
    common["norm2T"] = f(np.asarray(norm2).reshape(4, 8, 128).transpose(2, 0, 1))
    pk8 = lambda w: f(np.asarray(w).reshape(w.shape[0], 8, 128, w.shape[2]).transpose(0, 2, 1, 3).reshape(w.shape[0], 128, 8 * w.shape[2]))
    common["a_w_in"] = pk8(np.asarray(a_w_in))
    common["a_bif"] = f(np.asarray(a_b_if).reshape(2, 2, 4).transpose(2, 0, 1))
    common["a_onT"] = f(np.asarray(a_out_norm).reshape(2, 4, 128).transpose(2, 0, 1))
    common["a_cwT"] = f(np.asarray(a_conv_w).reshape(2, 3, 4, 128).transpose(3, 0, 1, 2))
    common["a_w_out"] = pk8(np.asarray(a_w_out))
    wq = np.asarray(c_w_qkv)
    qcols = np.concatenate([np.arange(64) + 64 * h for h in perm])
    common["c_w_qkv"] = pk8(np.concatenate([wq[:, :, qcols], wq[:, :, 1024:]], axis=2))
    common["c_qn"] = f(c_q_norm)
    common["c_kn"] = f(c_k_norm)
    common["c_sink"] = f(np.asarray(c_sink)[:, perm])
    common["c_w_out"] = pk8(np.asarray(c_w_out)[:, qcols, :])
    wup = np.asarray(f_w_up, dtype=np.float32).reshape(4, 8, 128, 2, 22, 128)
    wdn = np.asarray(f_w_down, dtype=np.float32).reshape(4, 22, 128, 1024)
    fwg = np.zeros((4, 6, 128, 12288), np.float32)
    for g, (f0, n) in enumerate([(0, 4), (4, 4), (8, 4), (12, 4), (16, 4), (20, 2)]):
        blk = np.zeros((4, 128, 2, 8, 4, 128), np.float32)
        blk[:, :, :, :, 0:n, :] = wup[:, :, :, :, f0:f0 + n, :].transpose(0, 2, 3, 1, 4, 5)
        fwg[:, g, :, 0:8192] = blk.reshape(4, 128, 8192)
        dblk = np.zeros((4, 128, 4, 1024), np.float32)
        dblk[:, :, 0:n, :] = wdn[:, f0:f0 + n].transpose(0, 2, 1, 3)
        fwg[:, g, :, 8192:12288] = dblk.reshape(4, 128, 4096)
    common["f_w_g"] = fwg
    common["f_cwT"] = f(np.asarray(f_conv_w).reshape(4, 3, 44, 128).transpose(3, 0, 1, 2))
    in_maps = []
    for c in range(NCORE):
        b = c // 4
        sl = slice(16 * c, 16 * c + 16)
        m = dict(common)
        m["xp"] = f(np.asarray(x_prompt)[b])
        m["xs"] = f(np.asarray(x_sample)[sl].reshape(128, D))
        m["call"] = f(np.concatenate([np.asarray(c_prompt)[b:b + 1], np.asarray(c_sample)[sl]], axis=0))
        m["stC"] = f(np.asarray(state_mlstm_C)[:, sl])
        m["stn"] = f(np.asarray(state_mlstm_n)[:, sl].reshape(2, 64, 128))
        m["stmT"] = f(np.asarray(state_mlstm_m)[:, sl].transpose(2, 0, 1))
        m["stsc"] = f(np.asarray(state_sconv)[:, sl].reshape(2, 32, 512))
        m["ck"] = f(np.asarray(cache_win_k)[:, sl].reshape(2, 16, 128, 256))
        m["cv"] = f(np.asarray(cache_win_v)[:, sl].reshape(2, 16, 128, 256))
        m["stffn"] = f(np.asarray(state_ffn_conv)[:, sl].reshape(4, 32, 5632))
        in_maps.append(m)
    if _NC is None:
        _NC = build()
    res = run_bass_kernel_spmd(_NC, in_maps, core_ids=list(range(NCORE)))
    R = res.results
    pc = [0, 4]
    cat_p = lambda nm, shp: np.stack([R[c][nm] for c in pc], axis=1).reshape(shp)
    y_prompt = np.stack([R[c]["yp"] for c in pc], axis=0)
    y_sample = np.concatenate([R[c]["ys"].reshape(16, 8, D) for c in range(NCORE)], axis=0)
    p_C = cat_p("pC", (2, 2, 4, 128, 128))
    p_n = cat_p("pn", (2, 2, 4, 128))
    p_m = cat_p("pm", (2, 2, 4))
    p_sc = cat_p("psc", (2, 2, 2, 512))
    p_wk = cat_p("pwk", (2, 2, 128, 4, 64))
    p_wv = cat_p("pwv", (2, 2, 128, 4, 64))
    p_ffn = cat_p("pffn", (4, 2, 2, 5632))
    cat_s = lambda nm, shp: np.concatenate([R[c][nm].reshape(shp) for c in range(NCORE)], axis=1)
    s_C = cat_s("sC", (2, 16, 4, 128, 128))
    s_n = cat_s("sn", (2, 16, 4, 128))
    s_m = np.concatenate([R[c]["smT"].transpose(1, 2, 0) for c in range(NCORE)], axis=1)
    s_sc = cat_s("ssc", (2, 16, 2, 512))
    s_wk = cat_s("swk", (2, 16, 128, 4, 64))
    s_wv = cat_s("swv", (2, 16, 128, 4, 64))
    s_ffn = cat_s("sffn", (4, 16, 2, 5632))
    outs = (y_prompt, y_sample, p_C, p_n, p_m, p_sc, p_wk, p_wv, p_ffn,
            s_C, s_n, s_m, s_sc, s_wk, s_wv, s_ffn)
    return tuple(np.ascontiguousarray(o, dtype=np.float32) for o in outs)
```
